# Optimizing a Trainium2 kernel written in Bass

```python
import jax, jax.numpy as jnp
from jax import lax
import numpy as np

D_MODEL = 1024
BATCH = 16
SEQ = 256
DEPTH = 2
DEC_BATCH = 4
DEC_SEQ = 1024
PAST_LEN = 512

GRID_W = 64
HEAD_DIM = 64
N_Q_HEADS = D_MODEL // 128
N_KV_HEADS = 2
GROUP = N_Q_HEADS // N_KV_HEADS
WINDOW = 128
ATTN_BLOCK = 128
ROPE_THETA = 10000.0
ROPE_AXIS_DIM = HEAD_DIM // 2
LRU_WIDTH = D_MODEL // 2
LRU_BLOCKS = 8
LRU_BLOCK_DIM = LRU_WIDTH // LRU_BLOCKS
LRU_CONV = 4
LRU_C = 8.0
CONF_WIDTH = D_MODEL // 2
CONF_CONV = 31
N_BRANCH = 3
BRANCH_WIDTH = D_MODEL // 2
Q_WIDTH = N_Q_HEADS * HEAD_DIM
KV_WIDTH = N_KV_HEADS * HEAD_DIM
IN_WIDTH = Q_WIDTH + 2 * KV_WIDTH + LRU_WIDTH + 2 * CONF_WIDTH
IN_SPLITS = (Q_WIDTH, Q_WIDTH + KV_WIDTH, Q_WIDTH + 2 * KV_WIDTH, Q_WIDTH + 2 * KV_WIDTH + LRU_WIDTH)
D_FF = 2816
FFN_CONV = 3
N_MOD = 6
ALPHA = (2 * DEPTH) ** 0.25
BETA = (8 * DEPTH) ** -0.25
LN_EPS = 1e-5
NEG_INF = -1e30

kernel_name = 'hybrid_diffusion_prefix_step'


def _layer_norm(x, g, b):
    xf = x.astype(jnp.float32)
    mu = jnp.mean(xf, axis=-1, keepdims=True)
    var = jnp.mean(jnp.square(xf - mu), axis=-1, keepdims=True)
    y = (xf - mu) * lax.rsqrt(var + LN_EPS)
    return (y * g.astype(jnp.float32) + b.astype(jnp.float32)).astype(x.dtype)


def _dwconv(x, w, b, pad_left):
    K = w.shape[0]
    y = lax.conv_general_dilated(x, w[:, None, :].astype(x.dtype), (1,), [(pad_left, K - 1 - pad_left)],
                                 dimension_numbers=('NWC', 'WIO', 'NWC'), feature_group_count=x.shape[-1])
    return y + b.astype(x.dtype)


def _axial_rope(x):
    T = x.shape[1]
    rows = T // GRID_W
    row = jnp.repeat(jnp.arange(rows), GRID_W)
    col = jnp.tile(jnp.arange(GRID_W), rows)
    n_freq = ROPE_AXIS_DIM // 2
    inv = ROPE_THETA ** (-jnp.arange(n_freq, dtype=jnp.float32) / n_freq)

    def rot(xa, pos):
        ang = pos.astype(jnp.float32)[:, None] * inv[None, :]
        cos = jnp.cos(ang)[None, :, None, :]
        sin = jnp.sin(ang)[None, :, None, :]
        x1, x2 = jnp.split(xa.astype(jnp.float32), 2, axis=-1)
        return jnp.concatenate([x1 * cos - x2 * sin, x2 * cos + x1 * sin], axis=-1)

    out = jnp.concatenate([rot(x[..., :ROPE_AXIS_DIM], row), rot(x[..., ROPE_AXIS_DIM:], col)], axis=-1)
    return out.astype(x.dtype)


def _band_blocks(x, nblk):
    B = x.shape[0]
    xp = jnp.pad(x, ((0, 0), (ATTN_BLOCK, ATTN_BLOCK), (0, 0), (0, 0)))
    xp = xp.reshape(B, nblk + 2, ATTN_BLOCK, N_KV_HEADS, HEAD_DIM)
    xb = jnp.concatenate([xp[:, :-2], xp[:, 1:-1], xp[:, 2:]], axis=2)
    return xb.transpose(1, 0, 2, 3, 4)


def _band_mask(nblk, T):
    j = jnp.arange(nblk)[:, None, None]
    i = jnp.arange(ATTN_BLOCK)[None, :, None]
    m = jnp.arange(3 * ATTN_BLOCK)[None, None, :]
    kpos = (j - 1) * ATTN_BLOCK + m
    qpos = j * ATTN_BLOCK + i
    return (jnp.abs(kpos - qpos) <= WINDOW) & (kpos >= 0) & (kpos < T)


def _attend(q, k_ctx, v_ctx, sink, k_lat=None, v_lat=None):
    B, T = q.shape[0], q.shape[1]
    nblk = T // ATTN_BLOCK
    scale = HEAD_DIM ** -0.5
    qb = (q.astype(jnp.float32) * scale).reshape(B, nblk, ATTN_BLOCK, N_KV_HEADS, GROUP, HEAD_DIM)
    qb = qb.transpose(1, 0, 2, 3, 4, 5)
    kc = k_ctx.astype(jnp.float32)
    vc = v_ctx.astype(jnp.float32)
    n_ctx = kc.shape[1]
    sink_logit = jnp.broadcast_to(sink.astype(jnp.float32).reshape(N_KV_HEADS, GROUP)[None, :, :, None, None],
                                  (B, N_KV_HEADS, GROUP, ATTN_BLOCK, 1))

    if k_lat is None:
        def block(qj):
            s = jnp.concatenate([jnp.einsum('bqhgd,bkhd->bhgqk', qj, kc), sink_logit], axis=-1)
            p = jax.nn.softmax(s, axis=-1)
            return jnp.einsum('bhgqk,bkhd->bqhgd', p[..., :n_ctx], vc)
        out = lax.map(block, qb)
    else:
        kband = _band_blocks(k_lat.astype(jnp.float32), nblk)
        vband = _band_blocks(v_lat.astype(jnp.float32), nblk)
        mask = _band_mask(nblk, T)

        def block(args):
            qj, kj, vj, mj = args
            s_ctx = jnp.einsum('bqhgd,bkhd->bhgqk', qj, kc)
            s_band = jnp.where(mj[None, None, None], jnp.einsum('bqhgd,bkhd->bhgqk', qj, kj), NEG_INF)
            p = jax.nn.softmax(jnp.concatenate([s_ctx, s_band, sink_logit], axis=-1), axis=-1)
            return (jnp.einsum('bhgqk,bkhd->bqhgd', p[..., :n_ctx], vc)
                    + jnp.einsum('bhgqk,bkhd->bqhgd', p[..., n_ctx:n_ctx + 3 * ATTN_BLOCK], vj))
        out = lax.map(block, (qb, kband, vband, mask))
    return out.transpose(1, 0, 2, 3, 4, 5).reshape(B, T, Q_WIDTH).astype(q.dtype)


def _linear_scan(a, b, h0, reverse):
    idx = -1 if reverse else 0
    b = b.at[:, idx].add(a[:, idx] * h0)

    def combine(e1, e2):
        a1, b1 = e1
        a2, b2 = e2
        return a1 * a2, a2 * b1 + b2

    _, h = lax.associative_scan(combine, (a, b), reverse=reverse, axis=1)
    return h


def _bi_rglru(x, lp, h0):
    B, T, W = x.shape
    xf = x.astype(jnp.float32)
    xblk = xf.reshape(B, T, LRU_BLOCKS, LRU_BLOCK_DIM)
    outs, finals = [], []
    for d in range(2):
        r = jax.nn.sigmoid(jnp.einsum('btnc,ncd->btnd', xblk, lp['lru_wa'][d].astype(jnp.float32)).reshape(B, T, W)
                           + lp['lru_ba'][d].astype(jnp.float32))
        i = jax.nn.sigmoid(jnp.einsum('btnc,ncd->btnd', xblk, lp['lru_wx'][d].astype(jnp.float32)).reshape(B, T, W)
                           + lp['lru_bx'][d].astype(jnp.float32))
        log_a = -LRU_C * jax.nn.softplus(-lp['lru_lambda'][d].astype(jnp.float32)) * r
        a = jnp.exp(log_a)
        b = jnp.sqrt(-jnp.expm1(2.0 * log_a)) * (i * xf)
        h = _linear_scan(a, b, h0[:, d].astype(jnp.float32), reverse=(d == 1))
        outs.append(h)
        finals.append(h[:, 0] if d == 1 else h[:, -1])
    return (outs[0] + outs[1]).astype(x.dtype), jnp.stack(finals, axis=1)


def _conformer_conv(x, lp):
    a, g = jnp.split(x, 2, axis=-1)
    u = a * jax.nn.sigmoid(g)
    u = _dwconv(u, lp['conf_dw_w'], lp['conf_dw_b'], CONF_CONV // 2)
    u = _layer_norm(u, lp['conf_ln_g'], lp['conf_ln_b'])
    return jax.nn.silu(u)


def _mixers(h, lp, ctx):
    B, T, _ = h.shape
    proj = h @ lp['w_in']
    q, k, v, xb, xc = jnp.split(proj, IN_SPLITS, axis=-1)
    q = q.reshape(B, T, N_Q_HEADS, HEAD_DIM)
    k = k.reshape(B, T, N_KV_HEADS, HEAD_DIM)
    v = v.reshape(B, T, N_KV_HEADS, HEAD_DIM)
    xb = _dwconv(xb, lp['lru_conv_w'], lp['lru_conv_b'], LRU_CONV // 2)
    if ctx is None:
        attn = _attend(q, k, v, lp['attn_sink'])
        h0 = jnp.zeros((B, 2, LRU_WIDTH), jnp.float32)
    else:
        k_ctx, v_ctx, h0 = ctx
        q = _axial_rope(q)
        k = _axial_rope(k)
        attn = _attend(q, k_ctx, v_ctx, lp['attn_sink'], k, v)
    lru, lru_final = _bi_rglru(xb, lp, h0)
    conv = _conformer_conv(xc, lp)
    branches = jnp.stack([attn, lru, conv], axis=2)
    proj_b = jnp.einsum('btnc,ncd->btnd', branches, lp['w_branch'])
    gates = jax.nn.sigmoid(h @ lp['w_merge'] + lp['b_merge']).reshape(B, T, N_BRANCH, D_MODEL)
    y = jnp.sum(gates * proj_b, axis=2) @ lp['w_out']
    cache = (k, v, lru_final) if ctx is None else None
    return y, cache


def _conv_ffn(h, lp):
    u = h @ lp['ffn_w_up']
    u = _dwconv(u, lp['ffn_conv_w'], lp['ffn_conv_b'], FFN_CONV // 2)
    g, v = jnp.split(u, 2, axis=-1)
    return (jax.nn.gelu(g, approximate=True) * v) @ lp['ffn_w_down']


def _layer(x, mod, lp, ctx):
    sh1, sc1, g1, sh2, sc2, g2 = jnp.split(mod, N_MOD, axis=-1)
    y, cache = _mixers(x * (1 + sc1) + sh1, lp, ctx)
    x = _layer_norm(ALPHA * x + g1 * y, lp['ln1_g'], lp['ln1_b'])
    f = _conv_ffn(x * (1 + sc2) + sh2, lp)
    x = _layer_norm(ALPHA * x + g2 * f, lp['ln2_g'], lp['ln2_b'])
    return x, cache


def setup_inputs(seed: int = 0) -> dict:
    key = jax.random.key(seed)
    ks = iter(jax.random.split(key, 48))
    f32 = jnp.float32

    def nrm(shape, scale):
        return jax.random.normal(next(ks), shape, f32) * scale

    u = jax.random.uniform(next(ks), (DEPTH, 2, LRU_WIDTH), f32, 0.9, 0.999)
    a_base = u ** (1.0 / LRU_C)
    lru_lambda = jnp.log(a_base) - jnp.log1p(-a_base)
    D = D_MODEL
    return {
        'x_prompt': nrm((BATCH, SEQ, D), 1.0),
        'x_sample': nrm((DEC_BATCH, DEC_SEQ, D), 1.0),
        'cache_k': nrm((DEC_BATCH, DEPTH, PAST_LEN, N_KV_HEADS, HEAD_DIM), 1.0),
        'cache_v': nrm((DEC_BATCH, DEPTH, PAST_LEN, N_KV_HEADS, HEAD_DIM), 1.0),
        'state_lru': nrm((DEC_BATCH, DEPTH, 2, LRU_WIDTH), 0.5),
        'c': nrm((DEC_BATCH, D), 1.0),
        'c_ctx': nrm((D,), 1.0),
        'w_mod': nrm((DEPTH, D, N_MOD * D), 0.5 * D ** -0.5),
        'b_mod': nrm((DEPTH, N_MOD * D), 0.01),
        'w_in': nrm((DEPTH, D, IN_WIDTH), D ** -0.5),
        'attn_sink': nrm((DEPTH, N_Q_HEADS), 1.0),
        'lru_conv_w': nrm((DEPTH, LRU_CONV, LRU_WIDTH), LRU_CONV ** -0.5),
        'lru_conv_b': nrm((DEPTH, LRU_WIDTH), 0.02),
        'lru_wa': nrm((DEPTH, 2, LRU_BLOCKS, LRU_BLOCK_DIM, LRU_BLOCK_DIM), LRU_BLOCK_DIM ** -0.5),
        'lru_ba': nrm((DEPTH, 2, LRU_WIDTH), 0.02),
        'lru_wx': nrm((DEPTH, 2, LRU_BLOCKS, LRU_BLOCK_DIM, LRU_BLOCK_DIM), LRU_BLOCK_DIM ** -0.5),
        'lru_bx': nrm((DEPTH, 2, LRU_WIDTH), 0.02),
        'lru_lambda': lru_lambda,
        'conf_dw_w': nrm((DEPTH, CONF_CONV, CONF_WIDTH), CONF_CONV ** -0.5),
        'conf_dw_b': nrm((DEPTH, CONF_WIDTH), 0.02),
        'conf_ln_g': 1.0 + nrm((DEPTH, CONF_WIDTH), 0.02),
        'conf_ln_b': nrm((DEPTH, CONF_WIDTH), 0.02),
        'w_branch': nrm((DEPTH, N_BRANCH, BRANCH_WIDTH, D), BRANCH_WIDTH ** -0.5),
        'w_merge': nrm((DEPTH, D, N_BRANCH * D), D ** -0.5),
        'b_merge': nrm((DEPTH, N_BRANCH * D), 0.02),
        'w_out': nrm((DEPTH, D, D), BETA * D ** -0.5),
        'ln1_g': 1.0 + nrm((DEPTH, D), 0.02),
        'ln1_b': nrm((DEPTH, D), 0.02),
        'ffn_w_up': nrm((DEPTH, D, 2 * D_FF), D ** -0.5),
        'ffn_conv_w': nrm((DEPTH, FFN_CONV, 2 * D_FF), FFN_CONV ** -0.5),
        'ffn_conv_b': nrm((DEPTH, 2 * D_FF), 0.02),
        'ffn_w_down': nrm((DEPTH, D_FF, D), BETA * D_FF ** -0.5),
        'ln2_g': 1.0 + nrm((DEPTH, D), 0.02),
        'ln2_b': nrm((DEPTH, D), 0.02),
    }


def reference(x_prompt, x_sample, cache_k, cache_v, state_lru, c, c_ctx, w_mod, b_mod, w_in, attn_sink,
              lru_conv_w, lru_conv_b, lru_wa, lru_ba, lru_wx, lru_bx, lru_lambda, conf_dw_w, conf_dw_b,
              conf_ln_g, conf_ln_b, w_branch, w_merge, b_merge, w_out, ln1_g, ln1_b, ffn_w_up, ffn_conv_w,
              ffn_conv_b, ffn_w_down, ln2_g, ln2_b):
    xp = x_prompt
    xs = x_sample
    new_k, new_v, new_h = [], [], []
    for l in range(DEPTH):
        lp = {
            'w_in': w_in[l], 'attn_sink': attn_sink[l],
            'lru_conv_w': lru_conv_w[l], 'lru_conv_b': lru_conv_b[l],
            'lru_wa': lru_wa[l], 'lru_ba': lru_ba[l], 'lru_wx': lru_wx[l], 'lru_bx': lru_bx[l],
            'lru_lambda': lru_lambda[l],
            'conf_dw_w': conf_dw_w[l], 'conf_dw_b': conf_dw_b[l],
            'conf_ln_g': conf_ln_g[l], 'conf_ln_b': conf_ln_b[l],
            'w_branch': w_branch[l], 'w_merge': w_merge[l], 'b_merge': b_merge[l], 'w_out': w_out[l],
            'ln1_g': ln1_g[l], 'ln1_b': ln1_b[l],
            'ffn_w_up': ffn_w_up[l], 'ffn_conv_w': ffn_conv_w[l], 'ffn_conv_b': ffn_conv_b[l],
            'ffn_w_down': ffn_w_down[l], 'ln2_g': ln2_g[l], 'ln2_b': ln2_b[l],
        }
        mod_ctx = (jax.nn.silu(c_ctx) @ w_mod[l] + b_mod[l])[None, None, :]
        mod_lat = (jax.nn.silu(c) @ w_mod[l] + b_mod[l])[:, None, :]
        xp, (k_l, v_l, h_l) = _layer(xp, mod_ctx, lp, None)
        new_k.append(k_l)
        new_v.append(v_l)
        new_h.append(h_l)
        xs, _ = _layer(xs, mod_lat, lp, (cache_k[:, l], cache_v[:, l], state_lru[:, l]))
    return (xp, xs, jnp.stack(new_k, axis=1), jnp.stack(new_v, axis=1), jnp.stack(new_h, axis=1))
```

```python
from contextlib import ExitStack
import numpy as np
import concourse.bass as bass
import concourse.mybir as mybir
from concourse.bass_utils import run_bass_kernel_spmd

F32 = mybir.dt.float32
BF16 = mybir.dt.bfloat16
AF = mybir.ActivationFunctionType
ALU = mybir.AluOpType

D = 1024
T = 1024
DEPTH = 2
NSEG = 4
SEG = 256
DFF = 2816
NFF = 22
ALPHA = (2 * DEPTH) ** 0.25
LN_EPS = 1e-5
EPS_F = LN_EPS / (ALPHA * ALPHA)
NEGM = -30000.0
NSLOT = 4
SLOT_ELEMS = 4096
XBP = 259
UPP = 286
FPP = 258

_VOFF = {}
_o = 0
for _n, _w in [("bmod", 48), ("lcw", 16), ("lcb", 4), ("lba", 8), ("lbx", 8), ("llam", 8), ("cw", 124),
               ("cb", 4), ("clg", 4), ("clb", 4), ("bm", 24), ("l1g", 8), ("l1b", 8), ("fcw", 132),
               ("fcb", 44), ("l2g", 8), ("l2b", 8), ("sink", 8)]:
    _VOFF[_n] = _o
    _o += _w
NV = _o

WIN_CHUNKS = ["q0", "qp0", "q1", "qp1", "q2", "qp2", "q3", "qp3", "kA", "kAp",
              "v", "xb0", "xb1", "xb2", "xb3", "a0", "g0", "a1", "g1", "a2", "g2", "a3", "g3"]
NWIN = len(WIN_CHUNKS) * 128
WIN_TILES = [(0, 512), (512, 512), (1024, 256), (1280, 128), (1408, 512), (1920, 512), (2432, 512)]


def band_tiles(th):
    out = []
    for kb in range(max(0, 4 * th - 1), min(8, 4 * th + 5)):
        jl = max(4 * th, kb - 1)
        jh = min(4 * th + 3, kb + 1)
        out.append((kb, jl, jh))
    return out


MASK_OFF = {}
_m = 512
for _th in range(2):
    for (_kb, _jl, _jh) in band_tiles(_th):
        MASK_OFF[(_th, _kb)] = _m
        _m += 128 * (_jh - _jl + 1)
MASK_COLS = _m


class Op:
    __slots__ = ("eng", "fn", "deps", "chan", "cord", "idx", "sig", "sigcount", "gid")


class Prog:
    ENGS = ("pe", "act", "dve", "pool", "sp")

    def __init__(self):
        self.ops = []
        self.kw = {}
        self.kr = {}
        self.chan_last = {}
        self.chan_count = {}

    def add(self, eng, fn, r=(), w=(), chan=None):
        op = Op()
        op.eng = eng
        op.fn = fn
        op.chan = chan
        op.sig = False
        op.gid = len(self.ops)
        deps = set()
        psr = [k for k in r if isinstance(k, tuple) and k and k[0] == "ps"]
        if psr:
            r = [k for k in r if k not in psr]
            w = list(w) + psr
        for k in r:
            p = self.kw.get(k)
            if p is not None:
                deps.add(p)
        for k in w:
            p = self.kw.get(k)
            if p is not None:
                deps.add(p)
            for q in self.kr.get(k, ()):
                deps.add(q)
        if chan is not None:
            p = self.chan_last.get(chan)
            if p is not None:
                deps.add(p)
            self.chan_last[chan] = op.gid
            self.chan_count[chan] = self.chan_count.get(chan, 0) + 1
            op.cord = self.chan_count[chan]
        for k in r:
            self.kr.setdefault(k, []).append(op.gid)
        for k in w:
            self.kw[k] = op.gid
            self.kr[k] = []
        op.deps = deps
        self.ops.append(op)
        return op.gid

    def emit(self, nc, es, block):
        ops = self.ops
        for op in ops:
            for d in op.deps:
                if ops[d].chan is None:
                    ops[d].sig = True
        cnt = {e: 0 for e in self.ENGS}
        for op in ops:
            if op.chan is None and op.sig:
                cnt[op.eng] += 1
                op.sigcount = cnt[op.eng]
        esem = {e: es.enter_context(nc.semaphore("s_" + e)) for e in ("pe", "act", "dve", "pool")}
        csem = {c: es.enter_context(nc.semaphore("c_" + c)) for c in self.chan_count}
        per_eng = {e: [op for op in ops if op.eng == e] for e in self.ENGS}

        def run(e, name):
            known = {}
            for op in per_eng[name]:
                waits = {}
                for d in op.deps:
                    p = ops[d]
                    if p.chan is not None:
                        key = ("c", p.chan)
                        val = 16 * p.cord
                    else:
                        if p.eng == "pe" and name == "pe":
                            continue
                        key = ("e", p.eng)
                        val = p.sigcount
                    if known.get(key, 0) >= val:
                        continue
                    if waits.get(key, 0) < val:
                        waits[key] = val
                for key, val in waits.items():
                    sem = csem[key[1]] if key[0] == "c" else esem[key[1]]
                    e.wait_ge(sem, val)
                    known[key] = val
                ins = op.fn(e)
                if op.chan is not None:
                    ins.then_inc(csem[op.chan], 16)
                elif op.sig:
                    ins.then_inc(esem[name], 1)

        @block.tensor
        def _(e):
            run(e, "pe")

        @block.scalar
        def _(e):
            run(e, "act")

        @block.vector
        def _(e):
            run(e, "dve")

        @block.gpsimd
        def _(e):
            run(e, "pool")

        @block.sync
        def _(e):
            run(e, "sp")


def build_program(stop_after=None, taps=()):
    nc = bass.Bass("TRN2", target_bir_lowering=False)
    es = ExitStack()
    P = Prog()

    def din(name, shape, dt=F32):
        return nc.dram_tensor(name, list(shape), dt, kind="ExternalInput").ap()

    def dout(name, shape, dt=F32):
        return nc.dram_tensor(name, list(shape), dt, kind="ExternalOutput").ap()

    def sb(name, shape, dt):
        return es.enter_context(nc.sbuf_tensor("sb_" + name, list(shape), dt))

    x_d = din("x", [T, D])
    cond_d = din("cond", [128, 8])
    ck_d = din("ck", [DEPTH, 512, 128])
    cv_d = din("cv", [DEPTH, 512, 128])
    h0_d = din("h0", [128, DEPTH * 8])
    flag_d = din("flag", [128, 1])
    ropec_d = din("ropec", [128, T])
    ropes_d = din("ropes", [128, T])
    maskb_d = din("maskb", [128, MASK_COLS])
    ident_d = din("ident", [128, 128])
    vecs_d = din("vecs", [DEPTH, 128, NV])
    lruw_d = din("lruw", [DEPTH, 128, 16, 128])
    wmod_d = din("w_mod", [DEPTH, D, 6 * D])
    win_d = din("w_in_ext", [DEPTH, D, NWIN])
    wbr_d = din("w_branch", [DEPTH, 3, 512, D])
    wmg_d = din("w_merge", [DEPTH, D, 3 * D])
    wout_d = din("w_out", [DEPTH, D, D])
    wup_d = din("ffn_w_up", [DEPTH, D, 2 * DFF])
    wdn_d = din("ffn_w_down", [DEPTH, DFF, D])

    y_d = dout("y", [T, D])
    nk_d = dout("nk", [DEPTH, T, 128])
    nv_d = dout("nv", [DEPTH, T, 128])
    nh_d = dout("nh", [DEPTH, 32, 128])

    xT = sb("xT", [128, 8, T], F32)
    hT = sb("hT", [128, 8, T], BF16)
    wsl = sb("wsl", [128, NSLOT, SLOT_ELEMS], BF16)
    ident = sb("ident", [128, 128], F32)
    identb = sb("identb", [128, 128], BF16)
    onesb = sb("onesb", [128, 128], BF16)
    vecs = sb("vecs", [128, DEPTH, NV], F32)
    condt = sb("condt", [128, 8], F32)
    scond = sb("scond", [128, 8], BF16)
    flag = sb("flag", [128, 1], F32)
    ctxb = sb("ctxb", [128, 1], F32)
    h0t = sb("h0t", [128, DEPTH * 8], F32)
    modT = sb("modT", [128, 48], F32)
    der = sb("der", [128, 64], F32)
    ropec = sb("ropec", [128, T], BF16)
    ropes = sb("ropes", [128, T], BF16)
    maskb = sb("maskb", [128, MASK_COLS], BF16)
    lruw = sb("lruw", [128, 16, 128], BF16)
    lrudg = sb("lrudg", [128, 16, 128], BF16)
    esink = sb("esink", [128, 8], F32)
    fin = sb("fin", [128, 32], F32)
    fint = sb("fint", [32, 128], F32)
    initb = sb("initb", [128, 8], F32)
    attnT = sb("attnT", [128, 4, T], BF16)
    lruT = sb("lruT", [128, 4, T], BF16)
    confT = sb("confT", [128, 4, T], BF16)
    xbpad2 = sb("xbpad", [128, 4 * NSEG * XBP], BF16)
    xbpad = xbpad2[:, :].rearrange("p (c x) -> p c x", c=4)
    upad2 = sb("upad", [128, 4 * NSEG * UPP], BF16)
    upad = upad2[:, :].rearrange("p (c x) -> p c x", c=4)
    SBYTES = 60 * 1024
    arena = sb("arena", [128, SBYTES // 4], F32)
    arena_b = arena[:].bitcast(BF16) if hasattr(arena[:], "bitcast") else None

    psum = es.enter_context(nc.psum_tensor("ps", [128, 8, 512], F32))

    def AF32(off_bytes, shape):
        n = int(np.prod(shape[1:]))
        o = off_bytes // 4
        ap = arena[0:shape[0], o:o + n]
        return ap if len(shape) == 2 else _reshape(ap, shape[1:])

    def ABF(off_bytes, shape):
        n = int(np.prod(shape[1:]))
        o = off_bytes // 2
        ap = arena_b[0:shape[0], o:o + n]
        return ap if len(shape) == 2 else _reshape(ap, shape[1:])

    def _reshape(ap, free):
        if len(free) == 2:
            return ap.rearrange("p (a b) -> p a b", a=free[0])
        if len(free) == 3:
            return ap.rearrange("p (a b c) -> p a b c", a=free[0], b=free[1])
        raise ValueError

    arena_keys = []

    def akey(name):
        arena_keys.append(name)
        return name

    wtiles = []

    def wt_add(kind, parts):
        wtiles.append((kind, parts))

    def slot_view(s, kc, ncols, np_=128):
        return wsl[0:np_, s, 0:kc * ncols].rearrange("p (k n) -> p k n", k=kc)

    def wt_mod(l, j):
        src = wmod_d[l, :, j * 512:(j + 1) * 512].rearrange("(k p) n -> p k n", p=128)
        wt_add(("mod", l, j), [(lambda s: slot_view(s, 8, 512), src)])

    for l in range(DEPTH):
        if l == 0:
            for j in range(4):
                wt_mod(0, j)
        def wt_win(j):
            c0, ncols = WIN_TILES[j]
            src = win_d[l, :, c0:c0 + ncols].rearrange("(k p) n -> p k n", p=128)
            wt_add(("win", l, j), [(lambda s, ncols=ncols: slot_view(s, 8, ncols), src)])
        for j in range(4):
            wt_win(j)
        for j in range(4, 12):
            wt_mod(l, j)
        for j in range(4, 7):
            wt_win(j)
        for n in range(8):
            parts = []
            for b in range(3):
                src = wmg_d[l, :, b * D + n * 128: b * D + (n + 1) * 128].rearrange("(k p) n -> p k n", p=128)
                parts.append((lambda s, b=b: slot_view(s, 24, 128)[:, b * 8:(b + 1) * 8, :], src))
            wt_add(("mg", l, n), parts)
            parts = []
            for b in range(3):
                src = wbr_d[l, b, :, n * 128:(n + 1) * 128].rearrange("(k p) n -> p k n", p=128)
                parts.append((lambda s, b=b: slot_view(s, 12, 128)[:, 4 * b:4 * b + 4, :], src))
            wt_add(("br", l, n), parts)
        for j in range(2):
            src = wout_d[l, :, j * 512:(j + 1) * 512].rearrange("(k p) n -> p k n", p=128)
            wt_add(("out", l, j), [(lambda s: slot_view(s, 8, 512), src)])
        for j in range(11):
            parts = []
            for hv in range(2):
                c0 = hv * DFF + j * 256
                src = wup_d[l, :, c0:c0 + 256].rearrange("(k p) n -> p k n", p=128)
                parts.append((lambda s, hv=hv: slot_view(s, 8, 512)[:, :, hv * 256:(hv + 1) * 256], src))
            wt_add(("up", l, j), parts)
        for n in range(8):
            src = wdn_d[l, :, n * 128:(n + 1) * 128].rearrange("(k p) n -> p k n", p=128)
            wt_add(("dn", l, n), [(lambda s: slot_view(s, NFF, 128), src)])
            if l + 1 < DEPTH and n % 2 == 1:
                wt_mod(l + 1, n // 2)

    wstate = {"issued": 0, "next": 0}

    def w_issue_upto(j):
        while wstate["issued"] <= min(j, len(wtiles) - 1):
            i = wstate["issued"]
            kind, parts = wtiles[i]
            s = i % NSLOT
            for pi, (dst_fn, src) in enumerate(parts):
                dst = dst_fn(s)
                P.add("pool", (lambda e, dst=dst, src=src: e.dma_start(out=dst, in_=src)),
                      w=[("w", s, pi)],
                      chan="w%d_%d" % (s, pi))
            wstate["issued"] += 1

    def w_next(kind, la=NSLOT - 1):
        j = wstate["next"]
        assert wtiles[j][0] == kind, (wtiles[j][0], kind)
        w_issue_upto(j + la)
        wstate["next"] += 1
        s = j % NSLOT
        return s, [("w", s, q) for q in range(3)]

    def PB(b):
        return psum[:, b, :]

    pskey = lambda b: ("ps", b)
    rot = {"i": 0}

    MODB = 7

    def next_bank():
        b = rot["i"] % 7
        rot["i"] += 1
        return b

    def act(out, in_, func, r, w, bias=None, scale=None):
        kw = {}
        if bias is not None:
            kw["bias"] = bias
        if scale is not None:
            kw["scale"] = scale
        return P.add("act", lambda e: e.activation(out=out, in_=in_, func=func, **kw), r=r, w=w)

    def dve_tt(out, in0, in1, op, r, w, eng="dve"):
        return P.add(eng, lambda e: e.tensor_tensor(out=out, in0=in0, in1=in1, op=op), r=r, w=w)

    def dve_ts(out, in0, s1, op0, r, w, s2=None, op1=None, eng="dve"):
        if op1 is None:
            return P.add(eng, lambda e: e.tensor_scalar(out=out, in0=in0, scalar1=s1, scalar2=0.0, op0=op0, op1=ALU.add),
                         r=r, w=w)
        return P.add(eng, lambda e: e.tensor_scalar(out=out, in0=in0, scalar1=s1, scalar2=s2, op0=op0, op1=op1), r=r, w=w)

    def dve_stt(out, in0, scalar, in1, op0, op1, r, w, eng="dve"):
        return P.add(eng, lambda e: e.scalar_tensor_tensor(out=out, in0=in0, scalar=scalar, in1=in1, op0=op0, op1=op1),
                     r=r, w=w)

    def dma(q, out, in_, r, w, chan):
        return P.add(q, lambda e: e.dma_start(out=out, in_=in_), r=r, w=w, chan=chan)

    def mm_group(bank, steps, r, cols=None, extra_w=()):
        outap = PB(bank) if cols is None else PB(bank)[:, cols[0]:cols[1]]

        def fn(e):
            ins = None
            n = len(steps)
            for i, (lt, rh) in enumerate(steps):
                ins = e.matmul(outap, lt, rh, start=(i == 0), stop=(i == n - 1))
            return ins
        return P.add("pe", fn, r=r, w=[pskey(bank)] + list(extra_w))

    stopped = {"v": False}
    tap_list = []

    def phase_end(name):
        if stop_after == name:
            stopped["v"] = True
        return stopped["v"]

    dma("sp", ident[:], ident_d, [], ["ident"], "ld0")
    dma("sp", vecs[:], vecs_d.rearrange("l p n -> p l n"), [], ["vecs"], "ld1")
    dma("sp", condt[:], cond_d, [], ["condt"], "ld2")
    dma("sp", flag[:], flag_d, [], ["flag"], "ld3")
    dma("sp", h0t[:], h0_d, [], ["h0t"], "ld0")

    P.add("dve", lambda e: e.memset(onesb[:], 1.0), w=["onesb"])
    act(identb[:], ident[:], AF.Copy, ["ident"], ["identb"])
    P.add("dve", lambda e: e.tensor_scalar(out=ctxb[:], in0=flag[:], scalar1=-1.0, scalar2=-NEGM, op0=ALU.add,
                                           op1=ALU.mult), r=["flag"], w=["ctxb"])
    act(scond[:], condt[:], AF.Silu, ["condt"], ["scond"])

    XIN = [AF32(i * 4096, [128, 1024]) for i in range(8)]
    for tb in range(8):
        dma("sp", XIN[tb], x_d[tb * 128:(tb + 1) * 128, :], [], [akey(("xin", tb))], "xin%d" % tb)
    for tb in range(8):
        st = XIN[tb]
        k = ("xin", tb)
        for half in range(2):
            b = next_bank()
            def fn(e, st=st, half=half, b=b):
                ins = None
                for q in range(4):
                    c = half * 4 + q
                    ins = e.transpose(out=PB(b)[:, q * 128:(q + 1) * 128], in_=st[:, c * 128:(c + 1) * 128],
                                      identity=ident[:])
                return ins
            P.add("pe", fn, r=[k, "ident"], w=[pskey(b)])
            src = PB(b).rearrange("p (q t) -> p q t", q=4)
            dst = xT[:, half * 4:(half + 1) * 4, tb * 128:(tb + 1) * 128]
            eng = "act" if half == 0 else "dve"
            if eng == "act":
                P.add("act", lambda e, dst=dst, src=src: e.activation(out=dst, in_=src, func=AF.Copy),
                      r=[pskey(b)], w=[("xT", c_, tb // 4) for c_ in range(half * 4, half * 4 + 4)])
            else:
                P.add("dve", lambda e, dst=dst, src=src: e.tensor_copy(out=dst, in_=src),
                      r=[pskey(b)], w=[("xT", c_, tb // 4) for c_ in range(half * 4, half * 4 + 4)])

    def V(l, name, col=0, n=1):
        o = _VOFF[name] + col
        return vecs[:, l, o:o + n]

    for l in range(DEPTH):
        if stopped["v"]:
            break
        def mod_tile(ll, j):
            s, wk = w_next(("mod", ll, j))
            wv = slot_view(s, 8, 512)

            def fn(e, wv=wv, j=j):
                ins = None
                for n4 in range(4):
                    col = j * 4 + n4
                    for kc in range(8):
                        ins = e.matmul(PB(MODB)[:, col:col + 1], wv[:, kc, n4 * 128:(n4 + 1) * 128],
                                       scond[:, kc:kc + 1], start=(kc == 0), stop=(kc == 7))
                return ins
            P.add("pe", fn, r=wk + ["scond"], w=[pskey(MODB)])

        def mod_finish_A(ll):
            dve_tt(modT[:, 0:16], PB(MODB)[:, 0:16], V(ll, "bmod", 0, 16), ALU.add, [pskey(MODB), "vecs"], ["modA"])
            dve_ts(der[:, 0:8], modT[:, 8:16], 1.0, ALU.add, ["modA"], [("der", 0)])

        def mod_finish_B(ll):
            dve_tt(modT[:, 16:48], PB(MODB)[:, 16:48], V(ll, "bmod", 16, 32), ALU.add, [pskey(MODB), "vecs"], ["modB"])
            dve_ts(der[:, 8:16], modT[:, 16:24], 1.0 / ALPHA, ALU.mult, ["modB"], [("der", 1)])
            dve_ts(der[:, 16:24], modT[:, 32:40], 1.0, ALU.add, ["modB"], [("der", 2)])
            dve_ts(der[:, 24:32], modT[:, 40:48], 1.0 / ALPHA, ALU.mult, ["modB"], [("der", 3)])

        if l == 0:
            w_issue_upto(NSLOT - 1)
            for j in range(4):
                mod_tile(0, j)
            mod_finish_A(0)
            dma("pool", ropec[:], ropec_d, [], ["ropec"], "ldr0")
            dma("pool", ropes[:], ropes_d, [], ["ropes"], "ldr1")
            dma("pool", maskb[:], maskb_d, [], ["maskb"], "ldm2")
        dma("pool", lruw[:], lruw_d[l], [], ["lruw"], "ldm")
        act(der[:, 40:48], V(l, "llam", 0, 8), AF.Exp, ["vecs"], [("der", 5)], scale=-1.0)
        dve_ts(der[:, 40:48], der[:, 40:48], 1.0, ALU.add, [("der", 5)], [("der", 5)])
        act(der[:, 32:40], der[:, 40:48], AF.Ln, [("der", 5)], [("der", 4)])
        dve_ts(der[:, 32:40], der[:, 32:40], -4.0, ALU.mult, [("der", 4)], [("der", 4)])
        dve_ts(der[:, 48:64], V(l, "lba", 0, 16), 0.5, ALU.mult, ["vecs"], [("der", 6)])
        act(esink[:], V(l, "sink", 0, 8), AF.Exp, ["vecs"], ["esink"])
        for k in range(4):
            for c in range(4):
                i = k * 4 + c
                dve_ts(lrudg[:, i, :], identb[:], V(l, "lcw", i, 1), ALU.mult, ["identb", "vecs"], [("lrudg", i)],
                       eng="pool")

        for c in range(8 if l == 0 else 0):
            if c % 2 == 0:
                act(hT[:, c, :], xT[:, c, :], AF.Identity, [("xT", c, 0), ("xT", c, 1), ("der", 0), "modA"],
                    [("hT", c, 0), ("hT", c, 1)], bias=modT[:, c:c + 1], scale=der[:, c:c + 1])
            else:
                dve_ts(hT[:, c, :], xT[:, c, :], der[:, c:c + 1], ALU.mult,
                       [("xT", c, 0), ("xT", c, 1), ("der", 0), "modA"], [("hT", c, 0), ("hT", c, 1)],
                       s2=modT[:, c:c + 1], op1=ALU.add)
        gtk = lambda lo, hi: [("gT", c_, th_) for c_ in range(lo, hi) for th_ in range(2)]
        P.add("dve", lambda e: e.memset(xbpad2[:, :], 0.0), r=[("hT", 7, 1)], w=["xbpad_init"] + gtk(12, 16))
        P.add("dve", lambda e: e.memset(upad2[:, :], 0.0), r=[("hT", 7, 1)], w=["upad_init"] + gtk(16, 20))
        if phase_end("A%d" % l):
            break

        fence = list(arena_keys)
        del arena_keys[:]
        o = 0
        qT = ABF(o, [128, 4, T]); o += 8192
        kz = ABF(o, [128, 4, T]); o += 8192
        ktok = AF32(o, [128, 8, 128]); o += 4096
        vtok = AF32(o, [128, 8, 128]); o += 4096
        vaug = ABF(o, [128, 4, 8, 128]); o += 8192
        ckf = AF32(o, [128, 4, 128]); o += 2048
        cksf = AF32(o, [128, 4, 128]); o += 2048
        ckz = ABF(o, [128, 4, 512]); o += 4096
        cvaug = ABF(o, [128, 4, 4, 128]); o += 4096
        NPT = 4
        vraw = AF32(o, [128, T])
        pt = ABF(o, [128, NPT, 512]); o += NPT * 1024
        kraw = AF32(o, [128, T])
        rd = AF32(o, [128, 2, 512]); o += 4096
        rt1 = AF32(o, [128, 2, 512]); o += 4096
        rt2 = AF32(o, [128, 2, 512]); o += 4096
        assert o <= SBYTES, o
        first_fence = {"v": fence}

        def FW():
            f = first_fence["v"]
            first_fence["v"] = []
            return f

        P.add("dve", lambda e, kz=kz: e.memset(kz, 0.0), w=FW() + [akey("kz_init")])
        P.add("dve", lambda e, ckz=ckz: e.memset(ckz, 0.0), r=["kz_init"], w=[akey("ckz_init")])
        P.add("dve", lambda e, vaug=vaug: e.memset(vaug, 1.0), r=["kz_init"], w=[akey("vaug_init")])
        P.add("dve", lambda e, cvaug=cvaug: e.memset(cvaug, 1.0), r=["kz_init"], w=[akey("cvaug_init")])
        dma("sp", ckf, ck_d[l].rearrange("(k p) n -> p k n", p=128), ["kz_init"], [akey("ckf")], "ldc0")
        for g_ in range(2):
            for e__ in range(2):
                dma("pool", cvaug[:, g_ * 2 + e__, :, e__ * 64:(e__ + 1) * 64],
                    cv_d[l][:, g_ * 64:(g_ + 1) * 64].rearrange("(k p) n -> p k n", p=128), ["cvaug_init"],
                    [akey(("cvaug", g_ * 2 + e__))], "ldv%d" % (g_ * 2 + e__))
        hkeys = lambda th: [("hT", c, th) for c in range(8)]
        pend = {}
        wcur = {"j": -1}

        def win_chunk(cname, sgsel=None):
            ci = WIN_CHUNKS.index(cname)
            col = ci * 128
            j = [i for i, (c0, nc_) in enumerate(WIN_TILES) if c0 <= col < c0 + nc_][0]
            off = col - WIN_TILES[j][0]
            if j != wcur["j"]:
                s_cur, wk_c = w_next(("win", l, j))
                wcur["j"] = j
                wcur["wk"] = wk_c
                wcur["wv"] = slot_view(s_cur, 8, WIN_TILES[j][1])
            wk_cur, wv_cur = wcur["wk"], wcur["wv"]
            banks = []
            for th in range(2):
                b = next_bank()
                banks.append(b)
                mm_group(b, [(wv_cur[:, kc, off:off + 128], hT[:, kc, th * 512:(th + 1) * 512]) for kc in range(8)],
                         r=wk_cur + hkeys(th))
            pend[cname] = banks
            if cname.startswith("qp") or cname in ("kAp", "kBp"):
                base = cname.replace("p", "")
                for th in range(2):
                    b0 = pend[base][th]
                    b1 = banks[th]
                    tsl = slice(th * 512, (th + 1) * 512)
                    t1 = rt1[:, th, :]
                    t2 = rt2[:, th, :]
                    dve_tt(t1, PB(b0), ropec[:, tsl], ALU.mult, [pskey(b0), "ropec"], [akey(("rt1", th))])
                    dve_tt(t2, PB(b1), ropes[:, tsl], ALU.mult, [pskey(b1), "ropes"], [akey(("rt2", th))])
                    if base.startswith("q"):
                        c = int(base[1])
                        dve_tt(qT[:, c, tsl], t1, t2, ALU.add, [("rt1", th), ("rt2", th)], [akey(("qT", c, th))])
                    else:
                        lo_idx, hi_idx = (0, 3) if base == "kA" else (2, 1)
                        dve_tt(kz[0:64, lo_idx, tsl], t1[0:64, :], t2[0:64, :], ALU.add,
                               [("rt1", th), ("rt2", th), "kz_init"], [akey(("kz", lo_idx, th))])
                        dve_tt(kz[64:128, hi_idx, tsl], t1[64:128, :], t2[64:128, :], ALU.add,
                               [("rt1", th), ("rt2", th), "kz_init"], [akey(("kz", hi_idx, th))])
                    if base == "kA":
                        act(kraw[:, tsl], PB(b0), AF.Copy, [pskey(b0), "kz_init"], [akey(("kraw", th))])
                        for (do_, di_, so_, si_) in ((slice(64, 128), 1, slice(0, 64), 0), (slice(0, 64), 2, slice(64, 128), 3)):
                            dst_ = kz[do_, di_, tsl]
                            src_ = kz[so_, si_, tsl]
                            P.add("dve", lambda e, dst_=dst_, src_=src_: e.tensor_copy(out=dst_, in_=src_),
                                  r=[("kz", si_, th), "kz_init"], w=[akey(("kz", di_, th))])
            elif cname == "v":
                for th in range(2):
                    tsl = slice(th * 512, (th + 1) * 512)
                    act(vraw[:, tsl], PB(banks[th]), AF.Copy, [pskey(banks[th]), "kz_init"], [akey(("vraw", th))])
                for (raw, rkey, tok, tkey) in ((kraw, "kraw", ktok, "ktok"), (vraw, "vraw", vtok, "vtok")):
                    for th in range(2):
                        b = next_bank()

                        def fn(e, raw=raw, th=th, b=b):
                            ins = None
                            for q in range(4):
                                blk = th * 4 + q
                                ins = e.transpose(out=PB(b)[:, q * 128:(q + 1) * 128],
                                                  in_=raw[:, blk * 128:(blk + 1) * 128], identity=ident[:])
                            return ins
                        P.add("pe", fn, r=[(rkey, th), "ident"], w=[pskey(b)])
                        src = PB(b).rearrange("p (q t) -> p q t", q=4)
                        dst = tok[:, th * 4:(th + 1) * 4, :]
                        P.add("dve", lambda e, dst=dst, src=src: e.tensor_copy(out=dst, in_=src),
                              r=[pskey(b), "kz_init"], w=[akey((tkey, th))])
                        if tkey == "vtok":
                            for g_ in range(2):
                                for e__ in range(2):
                                    dstb = vaug[:, g_ * 2 + e__, th * 4:(th + 1) * 4, e__ * 64:(e__ + 1) * 64]
                                    srcb = src[:, :, g_ * 64:(g_ + 1) * 64]
                                    if e__ == 0:
                                        P.add("act", lambda e, dstb=dstb, srcb=srcb: e.activation(
                                            out=dstb, in_=srcb, func=AF.Copy),
                                            r=[pskey(b), "vaug_init"], w=[akey(("vaug", g_ * 2 + e__, th))])
                                    else:
                                        P.add("dve", lambda e, dstb=dstb, srcb=srcb: e.tensor_copy(out=dstb, in_=srcb),
                                              r=[pskey(b), "vaug_init"], w=[akey(("vaug", g_ * 2 + e__, th))])
                dma("sp", nk_d[l].rearrange("(b p) n -> p b n", p=128), ktok, [("ktok", 0), ("ktok", 1)], [], "st_k")
                dma("sp", nv_d[l].rearrange("(b p) n -> p b n", p=128), vtok, [("vtok", 0), ("vtok", 1)], [], "st_v")
            elif cname.startswith("xb"):
                c = int(cname[2])
                for th in range(2):
                    src = PB(banks[th]).rearrange("p (s t) -> p s t", s=2)
                    dst = xbpad[:, c, :].rearrange("p (s t) -> p s t", s=NSEG)[:, 2 * th:2 * th + 2, 2:2 + SEG]
                    P.add("act", lambda e, dst=dst, src=src: e.activation(out=dst, in_=src, func=AF.Copy),
                          r=[pskey(banks[th]), "xbpad_init"], w=[("xbpad", c, th)])
                xv = xbpad[:, c, :].rearrange("p (s t) -> p s t", s=NSEG)
                dve_ts(xv[:, 1:4, 0:2], xv[:, 0:3, SEG:SEG + 2], flag[:, 0:1], ALU.mult,
                       [("xbpad", c, 0), ("xbpad", c, 1), "flag"], [("xbpad_h", c, 0)])
                dve_ts(xv[:, 0:3, SEG + 2:SEG + 3], xv[:, 1:4, 2:3], flag[:, 0:1], ALU.mult,
                       [("xbpad", c, 0), ("xbpad", c, 1), "flag"], [("xbpad_h", c, 1)])
            elif cname.startswith("g"):
                c = int(cname[1])
                ab = pend["a%d" % c]
                for th in range(2):
                    sg, sgk, sgx = sgsel(th)
                    act(sg, PB(banks[th]), AF.Sigmoid, [pskey(banks[th])], sgx + [akey(sgk)])
                    src = PB(ab[th]).rearrange("p (s t) -> p s t", s=2)
                    dst = upad[:, c, :].rearrange("p (s t) -> p s t", s=NSEG)[:, 2 * th:2 * th + 2, 15:15 + SEG]
                    sgv = sg.rearrange("p (s t) -> p s t", s=2)
                    P.add("dve", lambda e, dst=dst, src=src, sgv=sgv: e.tensor_tensor(out=dst, in0=src, in1=sgv,
                                                                                       op=ALU.mult),
                          r=[pskey(ab[th]), sgk, "upad_init"], w=[("upad", c, th)])
                uv = upad[:, c, :].rearrange("p (s t) -> p s t", s=NSEG)
                dve_ts(uv[:, 1:4, 0:15], uv[:, 0:3, SEG:SEG + 15], flag[:, 0:1], ALU.mult,
                       [("upad", c, 0), ("upad", c, 1), "flag"], [("upad_h", c, 0)])
                dve_ts(uv[:, 0:3, SEG + 15:SEG + 30], uv[:, 1:4, 15:30], flag[:, 0:1], ALU.mult,
                       [("upad", c, 0), ("upad", c, 1), "flag"], [("upad_h", c, 1)])
        for cname in WIN_CHUNKS[:11]:
            win_chunk(cname)
        b = next_bank()

        def fn(e, srcb=ckf, b=b):
            ins = None
            for kc in range(4):
                ins = e.transpose(out=PB(b)[:, kc * 128:(kc + 1) * 128], in_=srcb[:, kc, :], identity=ident[:])
            return ins
        P.add("pe", fn, r=["ckf", "ident"], w=[pskey(b)])
        for (do_, di_, so_) in ((slice(0, 64), 0, slice(0, 64)), (slice(64, 128), 3, slice(64, 128)),
                                (slice(64, 128), 1, slice(0, 64)), (slice(0, 64), 2, slice(64, 128))):
            act(ckz[do_, di_, :], PB(b)[so_, :], AF.Copy, [pskey(b), "ckz_init"], [akey(("ckz", di_))])

        if phase_end("B%d" % l):
            tap_list.extend([("qT", qT, [128, 4, T], BF16), ("kz", kz, [128, 4, T], BF16),
                             ("ckz", ckz, [128, 4, 512], BF16)])
            break

        sched = []
        for h in range(8):
            for th in range(2):
                tl = [("ctx", kc) for kc in range(4)] + [("band",) + bt for bt in band_tiles(th)]
                for ti, t_ in enumerate(tl):
                    sched.append((h, th, ti, len(tl), t_))
        LAG = 3
        st_bank = {}
        it_idx = {}
        for i, (h, th) in enumerate([(h, th) for h in range(8) for th in range(2)]):
            it_idx[(h, th)] = i

        def emit_S(gi):
            h, th, ti, ntl, t_ = sched[gi]
            c, e_, g = h // 2, h % 2, h // 4
            b = gi % 4
            st_bank[gi] = b
            if t_[0] == "ctx":
                kc = t_[1]
                lt = ckz[:, g * 2 + e_, kc * 128:(kc + 1) * 128]
                q0, q1 = th * 512, (th + 1) * 512
                mk = None
                rk = [("ckz", g * 2 + e_), "ckz_init"]
            else:
                _, kb, jl, jh = t_
                lt = kz[:, g * 2 + e_, kb * 128:(kb + 1) * 128]
                q0, q1 = jl * 128, (jh + 1) * 128
                mo = MASK_OFF[(th, kb)]
                mk = maskb[:, mo:mo + (q1 - q0)]
                rk = [("kz", g * 2 + e_, kb // 4), "kz_init"]
            n = q1 - q0
            rk += [("qT", c, th), "maskb", "identb"]
            slot = gi % NPT
            if mk is None:
                mm_group(b, [(lt, qT[:, c, q0:q1])], r=rk, cols=(0, n))
                act(pt[:, slot, 0:n], PB(b)[:, 0:n], AF.Exp, [pskey(b), "ctxb"], [akey(("pt", slot))], scale=0.125,
                    bias=ctxb[:, 0:1])
            else:
                mm_group(b, [(lt, qT[:, c, q0:q1]), (identb[:], mk)], r=rk, cols=(0, n))
                act(pt[:, slot, 0:n], PB(b)[:, 0:n], AF.Exp, [pskey(b)], [akey(("pt", slot))], scale=0.125)

        def emit_PV(gi):
            h, th, ti, ntl, t_ = sched[gi]
            c, e_, g = h // 2, h % 2, h // 4
            par = it_idx[(h, th)] % 2
            ab = 4 + par
            slot = gi % NPT
            if t_[0] == "ctx":
                kc = t_[1]
                vv = cvaug[:, g * 2 + e_, kc, :]
                q0, q1 = 0, 512
                rk = [("cvaug", g * 2 + e_), "cvaug_init"]
            else:
                _, kb, jl, jh = t_
                vv = vaug[:, g * 2 + e_, kb, :]
                q0, q1 = jl * 128 - th * 512, (jh + 1) * 128 - th * 512
                rk = [("vaug", g * 2 + e_, kb // 4), "vaug_init"]
            n = q1 - q0
            first, last = (ti == 0), (ti == ntl - 1)
            pr = slice(e_ * 64, e_ * 64 + 64)
            po = slice((1 - e_) * 64, (1 - e_) * 64 + 64)
            ptv = pt[:, slot, 0:n]
            o_a = PB(ab)[:, q0:q1]

            def fn(e, vv=vv, ptv=ptv, o_a=o_a, first=first, last=last):
                return e.matmul(o_a, vv, ptv, start=first, stop=last)
            P.add("pe", fn, r=rk + [("pt", slot)], w=[pskey(ab)])
            if last:
                r_ = rd[pr, par, :]
                dve_ts(r_, PB(ab)[po, :], esink[pr, h:h + 1], ALU.add, [pskey(ab), "esink"], [akey(("rd", par))])
                P.add("dve", lambda e, r_=r_: e.reciprocal(out=r_, in_=r_), r=[("rd", par)], w=[("rd", par)])
                dve_tt(attnT[pr, c, th * 512:(th + 1) * 512], PB(ab)[pr, :], r_, ALU.mult,
                       [pskey(ab), ("rd", par)], [("attnT", c, th, e_)])

        NT = len(sched)
        modj = 4
        for gi in range(NT + LAG):
            if gi < NT:
                emit_S(gi)
                if sched[gi][2] == 0 and it_idx[(sched[gi][0], sched[gi][1])] % 2 == 1 and modj < 12:
                    mod_tile(l, modj)
                    modj += 1
            if gi - LAG >= 0:
                emit_PV(gi - LAG)
        assert modj == 12
        mod_finish_B(l)
        if phase_end("C%d" % l):
            break

        att_fence = list(arena_keys)
        del arena_keys[:]
        first_fence["v"] = []
        o = 0
        xl = AF32(o, [128, T]); o += 4096
        xlb = ABF(o, [128, T]); o += 2048
        Rb = AF32(o, [128, T]); o += 4096
        Ab = AF32(o, [128, T]); o += 4096
        Sb = AF32(o, [128, T]); o += 4096
        Ib = AF32(o, [128, T]); o += 4096
        H0 = AF32(o, [128, T]); o += 4096
        H1 = Rb
        lru_end = o
        cdg = ABF(o, [128, 2, 31, 128]); o += 2 * 31 * 256
        cvo = AF32(o, [128, 4, T]); o += 16384
        assert o <= SBYTES, o

        def cdg_build(c, fence=()):
            o0 = _VOFF["cw"] + c
            wv_ = vecs[:, l, o0:o0 + 4 * 31:4].unsqueeze(2).broadcast_to([128, 31, 128])
            iv_ = identb[:].unsqueeze(1).broadcast_to([128, 31, 128])
            dve_tt(cdg[:, c % 2, :, :], iv_, wv_, ALU.mult, ["identb", "vecs"],
                   list(fence) + [akey(("cdg", c % 2, k)) for k in range(31)], eng="pool")

        def conf_conv(c):
            uv = upad[:, c, :].rearrange("p (s t) -> p s t", s=NSEG)
            for th in range(2):
                b = next_bank()
                mm_group(b, [(cdg[:, c % 2, k, :], uv[:, 2 * th:2 * th + 2, k:k + SEG]) for k in range(31)],
                         r=[("upad", c, 0), ("upad", c, 1), ("upad_h", c, 0), ("upad_h", c, 1)] +
                           [("cdg", c % 2, k) for k in range(31)])
                tsl = slice(th * 512, (th + 1) * 512)
                dve_ts(cvo[:, c, tsl], PB(b), V(l, "cb", c, 1), ALU.add, [pskey(b), "vecs", ("cdg", 0, 0)],
                       [akey(("cvo", c, th))])
            if c + 2 < 4:
                cdg_build(c + 2)

        cdg_build(0, att_fence)
        cdg_build(1)
        cf32 = confT[:].bitcast(F32)
        A1 = cf32[:, 0:2, :].rearrange("p a b -> p (a b)")
        S1 = cf32[:, 2:4, :].rearrange("p a b -> p (a b)")
        sg_first = {"v": True}

        def sgsel(th):
            fx = list(att_fence) if sg_first["v"] else []
            sg_first["v"] = False
            return Sb[:, th * 512:(th + 1) * 512], ("Sg", th), fx + [("S", 0)]
        for cname in ("xb0", "xb1", "xb2", "xb3"):
            win_chunk(cname)
        for c in range(4):
            win_chunk("a%d" % c, sgsel)
            win_chunk("g%d" % c, sgsel)
            xv = xbpad[:, c, :].rearrange("p (s t) -> p s t", s=NSEG)
            cb = []
            for th in range(2):
                b = next_bank()
                cb.append(b)
                mm_group(b, [(lrudg[:, k * 4 + c, :], xv[:, 2 * th:2 * th + 2, k:k + SEG]) for k in range(4)],
                         r=[("xbpad", c, 0), ("xbpad", c, 1), ("xbpad_h", c, 0), ("xbpad_h", c, 1)] +
                           [("lrudg", k * 4 + c) for k in range(4)])
            for th in range(2):
                tsl = slice(th * 512, (th + 1) * 512)
                act(xl[:, tsl], PB(cb[th]), AF.Identity, [pskey(cb[th]), "vecs"],
                    (att_fence if (c == 0 and th == 0) else []) + [akey(("xl", th))], bias=V(l, "lcb", c, 1))
                xs_, xd_ = xl[:, tsl], xlb[:, tsl]
                P.add("dve", lambda e, xs_=xs_, xd_=xd_: e.tensor_copy(out=xd_, in_=xs_), r=[("xl", th)],
                      w=[akey(("xlb", th))])
            for d in range(2):
                Hd = H0 if d == 0 else H1
                hk = "H%d" % d
                gb = {}
                for gt in range(2):
                    for th in range(2):
                        b = next_bank()
                        gb[(gt, th)] = b
                        mm_group(b, [(lruw[:, (d * 2 + gt) * 4 + c, :], xlb[:, th * 512:(th + 1) * 512])],
                                 r=["lruw", ("xlb", th)])
                RK = [akey(("R", 0)), akey(("R", 1))]
                for th in range(2):
                    tsl = slice(th * 512, (th + 1) * 512)
                    act(Rb[:, tsl], PB(gb[(0, th)]), AF.Tanh, [pskey(gb[(0, th)]), ("der", 6)],
                        [("R", th)] + ([("H1", s_) for s_ in range(NSEG)] if th == 0 else []),
                        bias=der[:, 48 + d * 4 + c:49 + d * 4 + c], scale=0.5)
                    act(Ib[:, tsl], PB(gb[(1, th)]), AF.Tanh, [pskey(gb[(1, th)]), ("der", 6)], [akey(("I", th))],
                        bias=der[:, 56 + d * 4 + c:57 + d * 4 + c], scale=0.5)
                clh = der[:, 32 + d * 4 + c:33 + d * 4 + c]
                act(Rb, Rb, AF.Identity, RK + [("der", 4)], RK, bias=clh, scale=clh)
                Ab_, Sb_ = (Ab, Sb) if d == 0 else (A1, S1)
                kA, kS = akey(("A", d)), akey(("S", d))
                act(Ab_, Rb, AF.Exp, RK, [kA] + ([("confT", c_, th_) for c_ in range(4) for th_ in range(2)]
                                                  if d == 1 else []))
                act(Sb_, Rb, AF.Exp, RK, [kS, ("Sg", 0), ("Sg", 1)], scale=2.0)
                act(Sb_, Sb_, AF.Sqrt, [kS], [kS], bias=0.25, scale=-0.25)
                IK = [("I", 0), ("I", 1)]
                dve_stt(Ib, Ib, 1.0, xl, ALU.add, ALU.mult, IK + [("xl", 0), ("xl", 1)], IK)
                dve_tt(Sb_, Sb_, Ib, ALU.mult, [kS] + IK, [kS])
                order = range(NSEG) if d == 0 else range(NSEG - 1, -1, -1)
                prev = None
                for s_ in order:
                    seg = slice(s_ * SEG, (s_ + 1) * SEG)
                    if prev is None:
                        init = h0t[:, l * 8 + d * 4 + c:l * 8 + d * 4 + c + 1]
                        ik = ["h0t"]
                    else:
                        col = (prev + 1) * SEG - 1 if d == 0 else prev * SEG
                        init = initb[:, s_:s_ + 1] if d == 0 else initb[:, 4 + s_:5 + s_]
                        dve_ts(init, Hd[:, col:col + 1], flag[:, 0:1], ALU.mult, [akey((hk, prev)), "flag"],
                               [("initb", d, s_)])
                        ik = [("initb", d, s_)]
                    if d == 0:
                        o_, a_, b_ = Hd[:, seg], Ab_[:, seg], Sb_[:, seg]
                    else:
                        lo, hi = s_ * SEG, (s_ + 1) * SEG
                        o_ = Hd[:, lo:hi][:, ::-1]
                        a_ = Ab_[:, lo:hi][:, ::-1]
                        b_ = Sb_[:, lo:hi][:, ::-1]
                    P.add("dve", lambda e, o_=o_, a_=a_, b_=b_, init=init: e.tensor_tensor_scan(
                        out=o_, data0=a_, data1=b_, initial=init, op0=ALU.mult, op1=ALU.add),
                        r=[kA, kS] + ik, w=[akey((hk, s_))] + (RK if d == 1 else []))
                    prev = s_
                hv = Hd.rearrange("p (s t) -> p s t", s=NSEG)
                colsel = SEG - 1 if d == 0 else 0
                fv = fin[:, :].rearrange("p (s x) -> p s x", s=NSEG)[:, :, d * 4 + c:d * 4 + c + 1]
                P.add("dve", lambda e, fv=fv, hv=hv, colsel=colsel: e.tensor_copy(out=fv, in_=hv[:, :, colsel:colsel + 1]),
                      r=[(hk, s_) for s_ in range(NSEG)], w=[("fin", d, c)])
            dve_tt(lruT[:, c, :], H0, H1, ALU.add, [("H0", s_) for s_ in range(4)] + [("H1", s_) for s_ in range(4)],
                   [("lruT", c)])
            conf_conv(c)
        fb = next_bank()
        P.add("pe", lambda e, fb=fb: e.transpose(out=PB(fb)[0:32, 0:128], in_=fin[:, :], identity=ident[:]),
              r=[("fin", d, c) for d in range(2) for c in range(4)] + ["ident"], w=[pskey(fb)])
        act(fint[:, :], PB(fb)[0:32, 0:128], AF.Copy, [pskey(fb)], ["fint"])
        dma("sp", nh_d[l], fint[:, :], ["fint"], [], "st_h")
        if phase_end("D%d" % l):
            tap_list.extend([("lruT", lruT[:], [128, 4, T], BF16)])
            break

        lru_fence = [k_ for k_ in arena_keys if not (isinstance(k_, tuple) and k_[0] in ("cvo", "cdg"))]
        first_fence["v"] = lru_fence
        o = 0
        mean = AF32(o, [128, T]); o += 4096
        rstd = AF32(o, [128, T]); o += 4096
        sqb = ABF(o, [128, 2, T]); o += 4096
        cvb = ABF(o, [128, 2, T]); o += 4096
        assert o <= lru_end
        ln_stats_and_norm = None

        def layer_norm(src_fn, nch, keys_fn, eps, tag):
            sb_ = [next_bank(), next_bank(), next_bank(), next_bank()]
            for c in range(nch):
                slot = c % 2
                fw_ = FW()
                cvs_ = cvb[:, slot, :]
                P.add("dve", lambda e, cvs_=cvs_, src_=src_fn(c): e.tensor_copy(out=cvs_, in_=src_),
                      r=keys_fn(c), w=fw_ + [akey(("cvb", slot))])
                act(sqb[:, slot, :], src_fn(c), AF.Square, keys_fn(c), fw_ + [akey(("sqb", slot))])
                for th in range(2):
                    tsl = slice(th * 512, (th + 1) * 512)
                    r1_ = cvb[:, slot, tsl]
                    r2_ = sqb[:, slot, tsl]

                    def fn(e, c=c, th=th, r1_=r1_, r2_=r2_, sb_=sb_):
                        e.matmul(PB(sb_[th]), onesb[:, :], r1_, start=(c == 0), stop=(c == nch - 1))
                        return e.matmul(PB(sb_[2 + th]), onesb[:, :], r2_, start=(c == 0), stop=(c == nch - 1))
                    P.add("pe", fn, r=[("cvb", slot), ("sqb", slot), "onesb"], w=[pskey(sb_[th]), pskey(sb_[2 + th])])
            inv = 1.0 / (nch * 128)
            for th in range(2):
                tsl = slice(th * 512, (th + 1) * 512)
                act(mean[:, tsl], PB(sb_[th]), AF.Copy, [pskey(sb_[th]), ("sqb", 0), ("sqb", 1)], [akey(("mean", th))],
                    scale=inv)
                act(rstd[:, tsl], PB(sb_[th]), AF.Square, [pskey(sb_[th])], [akey(("rstd", th))], scale=inv)
                dve_stt(rstd[:, tsl], PB(sb_[2 + th]), inv, rstd[:, tsl], ALU.mult, ALU.subtract,
                        [pskey(sb_[2 + th]), ("rstd", th)], [("rstd", th)])
                dve_ts(rstd[:, tsl], rstd[:, tsl], 0.0, ALU.max, [("rstd", th)], [("rstd", th)], s2=eps, op1=ALU.add)
                act(rstd[:, tsl], rstd[:, tsl], AF.Ln, [("rstd", th)], [("rstd", th)])
                act(rstd[:, tsl], rstd[:, tsl], AF.Exp, [("rstd", th)], [("rstd", th)], scale=-0.5)

        MK = [("mean", 0), ("mean", 1)]
        RSK = [("rstd", 0), ("rstd", 1)]
        for th in range(2):
            tsl = slice(th * 512, (th + 1) * 512)
            b1, b2 = next_bank(), next_bank()
            for c in range(4):
                slot = c % 2
                cv_ = cvb[:, slot, tsl]
                sq_ = sqb[:, slot, tsl]
                xs_ = cvo[:, c, tsl]
                fw_ = FW()
                P.add("dve", lambda e, cv_=cv_, xs_=xs_: e.tensor_copy(out=cv_, in_=xs_), r=[("cvo", c, th)],
                      w=fw_ + [akey(("cvb", slot, th))])
                act(sq_, xs_, AF.Square, [("cvo", c, th)], fw_ + [akey(("sqb", slot, th))])

                def fn(e, c=c, cv_=cv_, sq_=sq_, b1=b1, b2=b2):
                    e.matmul(PB(b1), onesb[:, :], cv_, start=(c == 0), stop=(c == 3))
                    return e.matmul(PB(b2), onesb[:, :], sq_, start=(c == 0), stop=(c == 3))
                P.add("pe", fn, r=[("cvb", slot, th), ("sqb", slot, th), "onesb"], w=[pskey(b1), pskey(b2)])
            inv = 1.0 / 512.0
            mk_, rk_ = akey(("mean", th)), akey(("rstd", th))
            act(mean[:, tsl], PB(b1), AF.Copy, [pskey(b1)], [mk_], scale=inv)
            act(rstd[:, tsl], PB(b1), AF.Square, [pskey(b1)], [rk_], scale=inv)
            dve_stt(rstd[:, tsl], PB(b2), inv, rstd[:, tsl], ALU.mult, ALU.subtract, [pskey(b2), rk_], [rk_])
            dve_ts(rstd[:, tsl], rstd[:, tsl], 0.0, ALU.max, [rk_], [rk_], s2=LN_EPS, op1=ALU.add)
            act(rstd[:, tsl], rstd[:, tsl], AF.Ln, [rk_], [rk_])
            act(rstd[:, tsl], rstd[:, tsl], AF.Exp, [rk_], [rk_], scale=-0.5)
            for c in range(4):
                ck_ = [("cvo", c, th)]
                xc_ = cvo[:, c, tsl]
                dve_tt(xc_, xc_, mean[:, tsl], ALU.subtract, ck_ + [mk_], ck_)
                dve_tt(xc_, xc_, rstd[:, tsl], ALU.mult, ck_ + [rk_], ck_)
                act(confT[:, c, tsl], xc_, AF.Silu, ck_ + ["vecs"], [("confT", c, th)], bias=V(l, "clb", c, 1),
                    scale=V(l, "clg", c, 1))
        if phase_end("E%d" % l):
            tap_list.extend([("confT", confT[:], [128, 4, T], BF16), ("attnT", attnT[:], [128, 4, T], BF16)])
            break

        fence = list(arena_keys)
        del arena_keys[:]
        first_fence["v"] = fence
        o = 0
        mergeT = ABF(o, [128, 8, T]); o += 16384
        sgb = AF32(o, [128, 2, 3, 512]); o += 12288
        mt = AF32(o, [128, 2, 2, 512]); o += 8192
        it = 0
        for n in range(8):
            sg_, wkg = w_next(("mg", l, n))
            wg = slot_view(sg_, 24, 128)
            sb2, wkb = w_next(("br", l, n), la=NSLOT - 2)
            wb = slot_view(sb2, 12, 128)
            for th in range(2):
                tsl = slice(th * 512, (th + 1) * 512)
                par = it % 2
                it += 1
                G = []
                for b_ in range(3):
                    bk = next_bank()
                    G.append(bk)
                    mm_group(bk, [(wg[:, b_ * 8 + kc, :], hT[:, kc, tsl]) for kc in range(8)], r=wkg + hkeys(th))
                Pb = []
                for b_, (src, skf) in ((0, (attnT, lambda kc: [("attnT", kc, th, 0), ("attnT", kc, th, 1)])),
                                       (1, (lruT, lambda kc: [("lruT", kc)])),
                                       (2, (confT, lambda kc: [("confT", kc, th)]))):
                    bk = next_bank()
                    Pb.append(bk)
                    mm_group(bk, [(wb[:, 4 * b_ + kc, :], src[:, kc, tsl]) for kc in range(4)],
                             r=wkb + [k_ for kc in range(4) for k_ in skf(kc)])
                for b_ in range(3):
                    act(sgb[:, par, b_, :], PB(G[b_]), AF.Sigmoid, [pskey(G[b_]), "vecs"],
                        FW() + [akey(("sgb", par, b_))], bias=V(l, "bm", b_ * 8 + n, 1))
                dve_tt(mt[:, par, 0, :], PB(Pb[0]), sgb[:, par, 0, :], ALU.mult, [pskey(Pb[0]), ("sgb", par, 0)],
                       [akey(("mt", par, 0))])
                dve_tt(mt[:, par, 1, :], PB(Pb[1]), sgb[:, par, 1, :], ALU.mult, [pskey(Pb[1]), ("sgb", par, 1)],
                       [akey(("mt", par, 1))])
                dve_tt(mt[:, par, 0, :], mt[:, par, 0, :], mt[:, par, 1, :], ALU.add,
                       [("mt", par, 0), ("mt", par, 1)], [("mt", par, 0)])
                dve_tt(mt[:, par, 1, :], PB(Pb[2]), sgb[:, par, 2, :], ALU.mult, [pskey(Pb[2]), ("sgb", par, 2)],
                       [("mt", par, 1)])
                dve_tt(mergeT[:, n, tsl], mt[:, par, 0, :], mt[:, par, 1, :], ALU.add,
                       [("mt", par, 0), ("mt", par, 1)], [akey(("mergeT", n, th))])
        if phase_end("F%d" % l):
            tap_list.extend([("mergeT", mergeT, [128, 8, T], BF16)])
            break

        o = 16384 + 12288 + 8192
        mean = AF32(o, [128, T]); o += 4096
        rstd = AF32(o, [128, T]); o += 4096
        sqb = ABF(o, [128, 2, T]); o += 4096
        cvb = ABF(o, [128, 2, T]); o += 4096
        assert o <= SBYTES

        pending_mod = []

        def proj_residual(kind, src, nk, srckeys, gcol):
            for n in range(8):
                if kind == "out":
                    if n % 4 == 0:
                        s_, wk_ = w_next(("out", l, n // 4))
                        wv_ = slot_view(s_, 8, 512)
                    lts = [wv_[:, kc, (n % 4) * 128:(n % 4 + 1) * 128] for kc in range(nk)]
                else:
                    s_, wk_ = w_next(("dn", l, n))
                    wv_ = slot_view(s_, NFF, 128)
                    lts = [wv_[:, kc, :] for kc in range(nk)]
                    if l + 1 < DEPTH and n % 2 == 1:
                        pending_mod.append(n // 2)
                for th in range(2):
                    tsl = slice(th * 512, (th + 1) * 512)
                    bk = next_bank()
                    rh = (lambda kc: src(kc)[:, tsl]) if callable(src) else (lambda kc: src[:, kc, tsl])
                    mm_group(bk, [(lts[kc], rh(kc)) for kc in range(nk)], r=wk_ + srckeys(th))
                    dve_stt(xT[:, n, tsl], PB(bk), der[:, gcol + n:gcol + n + 1], xT[:, n, tsl], ALU.mult, ALU.add,
                            [pskey(bk), ("xT", n, th), ("der", 1), ("der", 3)], [("xT", n, th)])
                while pending_mod:
                    mod_tile(l + 1, pending_mod.pop(0))


        def ln_apply(gname, bname, nxt=False, h2=False):
            layer_norm(lambda c: xT[:, c, :], 8, lambda c: [("xT", c, 0), ("xT", c, 1)], EPS_F, gname)
            for c in range(8):
                xk = [("xT", c, 0), ("xT", c, 1)]
                dve_tt(xT[:, c, :], xT[:, c, :], mean, ALU.subtract, xk + MK, xk)
                dve_tt(xT[:, c, :], xT[:, c, :], rstd, ALU.mult, xk + RSK, xk)
                act(xT[:, c, :], xT[:, c, :], AF.Identity, xk + ["vecs"], xk, bias=V(l, bname, c, 1),
                    scale=V(l, gname, c, 1))
                if nxt:
                    act(hT[:, c, :], xT[:, c, :], AF.Identity, xk + [("der", 0), "modA"],
                        [("hT", c, 0), ("hT", c, 1)], bias=modT[:, c:c + 1], scale=der[:, c:c + 1])
                if h2:
                    act(hT[:, c, :], xT[:, c, :], AF.Identity, xk + [("der", 2), "modB"],
                        [("hT", c, 0), ("hT", c, 1)], bias=modT[:, 24 + c:25 + c], scale=der[:, 16 + c:17 + c])

        def out_ln1_thmajor():
            s0_, wk0_ = w_next(("out", l, 0))
            s1_, wk1_ = w_next(("out", l, 1), la=NSLOT - 2)
            wvs = [slot_view(s0_, 8, 512), slot_view(s1_, 8, 512)]
            wks = [wk0_, wk1_]
            inv = 1.0 / 1024.0
            for th in range(2):
                tsl = slice(th * 512, (th + 1) * 512)
                for n in range(8):
                    wv_ = wvs[n // 4]
                    bk = next_bank()
                    mm_group(bk, [(wv_[:, kc, (n % 4) * 128:(n % 4 + 1) * 128], mergeT[:, kc, tsl]) for kc in range(8)],
                             r=wks[n // 4] + [("mergeT", kc, th) for kc in range(8)])
                    dve_stt(xT[:, n, tsl], PB(bk), der[:, 8 + n:9 + n], xT[:, n, tsl], ALU.mult, ALU.add,
                            [pskey(bk), ("xT", n, th), ("der", 1), ("der", 3)], [("xT", n, th)])
                b1, b2 = next_bank(), next_bank()
                for c in range(8):
                    slot = c % 2
                    cv_ = cvb[:, slot, tsl]
                    sq_ = sqb[:, slot, tsl]
                    xs_ = xT[:, c, tsl]
                    fw_ = FW()
                    P.add("dve", lambda e, cv_=cv_, xs_=xs_: e.tensor_copy(out=cv_, in_=xs_), r=[("xT", c, th)],
                          w=fw_ + [akey(("cvb", slot, th))])
                    act(sq_, xs_, AF.Square, [("xT", c, th)], fw_ + [akey(("sqb", slot, th))])

                    def fn(e, c=c, cv_=cv_, sq_=sq_, b1=b1, b2=b2):
                        e.matmul(PB(b1), onesb[:, :], cv_, start=(c == 0), stop=(c == 7))
                        return e.matmul(PB(b2), onesb[:, :], sq_, start=(c == 0), stop=(c == 7))
                    P.add("pe", fn, r=[("cvb", slot, th), ("sqb", slot, th), "onesb"], w=[pskey(b1), pskey(b2)])
                mk_, rk_ = akey(("mean", th)), akey(("rstd", th))
                act(mean[:, tsl], PB(b1), AF.Copy, [pskey(b1)], [mk_], scale=inv)
                act(rstd[:, tsl], PB(b1), AF.Square, [pskey(b1)], [rk_], scale=inv)
                dve_stt(rstd[:, tsl], PB(b2), inv, rstd[:, tsl], ALU.mult, ALU.subtract, [pskey(b2), rk_], [rk_])
                dve_ts(rstd[:, tsl], rstd[:, tsl], 0.0, ALU.max, [rk_], [rk_], s2=EPS_F, op1=ALU.add)
                act(rstd[:, tsl], rstd[:, tsl], AF.Ln, [rk_], [rk_])
                act(rstd[:, tsl], rstd[:, tsl], AF.Exp, [rk_], [rk_], scale=-0.5)
                for c in range(8):
                    xk = [("xT", c, th)]
                    xc_ = xT[:, c, tsl]
                    dve_tt(xc_, xc_, mean[:, tsl], ALU.subtract, xk + [mk_], xk)
                    dve_tt(xc_, xc_, rstd[:, tsl], ALU.mult, xk + [rk_], xk)
                    act(xc_, xc_, AF.Identity, xk + ["vecs"], xk, bias=V(l, "l1b", c, 1), scale=V(l, "l1g", c, 1))
                    act(hT[:, c, tsl], xc_, AF.Identity, xk + [("der", 2), "modB"], [("hT", c, th)],
                        bias=modT[:, 24 + c:25 + c], scale=der[:, 16 + c:17 + c])

        out_ln1_thmajor()
        if phase_end("G%d" % l):
            break

        fence = list(arena_keys)
        del arena_keys[:]
        first_fence["v"] = fence
        o = 0
        gtail = ABF(o, [128, 2, T]); o += 4096

        def gTc(c):
            if c < 4:
                return lruT[:, c, :]
            if c < 8:
                return confT[:, c - 4, :]
            if c < 12:
                return attnT[:, c - 8, :]
            if c < 16:
                return xbpad2[:, (c - 12) * T:(c - 11) * T]
            if c < 20:
                return upad2[:, (c - 16) * T:(c - 15) * T]
            return gtail[:, c - 20, :]
        NUR = 2
        ur = ABF(o, [128, NUR, 2, NSEG * FPP]); o += NUR * 2 * NSEG * FPP * 2
        cen = AF32(o, [128, NUR, 4, 512]); o += NUR * 4 * 2048
        gl = ABF(o, [128, 2, 512]); o += 2048
        assert o <= SBYTES, o
        oldk = [("lruT", c_) for c_ in range(4)] + [("confT", c_, th_) for c_ in range(4) for th_ in range(2)] + \
               [("attnT", c_, th_, e__) for c_ in range(4) for th_ in range(2) for e__ in range(2)] + \
               [(nm_, c_, th_) for nm_ in ("xbpad", "xbpad_h", "upad", "upad_h") for c_ in range(4) for th_ in range(2)]
        P.add("pool", lambda e, ur=ur: e.memset(ur, 0.0), w=FW() + oldk + [akey("ur_init")])
        ffn_items = []
        for j in range(11):
            for pi in range(2):
                ffn_items.append((j, pi, j * 2 + pi))

        def ffn_up(item):
            j, pi, c = item
            if pi == 0:
                s_, wk_ = w_next(("up", l, j))
                ffn_up.cur = (slot_view(s_, 8, 512), wk_)
            wv_, wk_ = ffn_up.cur
            slot = c % NUR
            for th in range(2):
                for hv in range(2):
                    bk = (c % 2) * 4 + (hv * 2 + th)
                    co = hv * 256 + pi * 128
                    mm_group(bk, [(wv_[:, kc, co:co + 128], hT[:, kc, th * 512:(th + 1) * 512]) for kc in range(8)],
                             r=wk_ + hkeys(th))
                    src = PB(bk).rearrange("p (s t) -> p s t", s=2)
                    dst = ur[:, slot, hv, :].rearrange("p (s t) -> p s t", s=NSEG)[:, 2 * th:2 * th + 2, 1:1 + SEG]
                    P.add("act", lambda e, dst=dst, src=src: e.activation(out=dst, in_=src, func=AF.Copy),
                          r=[pskey(bk), "ur_init"], w=[akey(("ur", slot, hv, th))])
                    act(cen[:, slot, hv * 2 + th, :], PB(bk), AF.Identity, [pskey(bk), "vecs", "ur_init"],
                        [akey(("cen", slot, hv, th))], bias=V(l, "fcb", hv * 22 + c, 1),
                        scale=V(l, "fcw", 44 + hv * 22 + c, 1))

        def ffn_halo(item):
            j, pi, c = item
            slot = c % NUR
            for hv in range(2):
                uvv = ur[:, slot, hv, :].rearrange("p (s t) -> p s t", s=NSEG)
                rk_ = [("ur", slot, hv, 0), ("ur", slot, hv, 1), "flag"]
                dve_ts(uvv[:, 1:4, 0:1], uvv[:, 0:3, SEG:SEG + 1], flag[:, 0:1], ALU.mult, rk_,
                       [akey(("urh", slot, hv, 0))])
                dve_ts(uvv[:, 0:3, SEG + 1:SEG + 2], uvv[:, 1:4, 1:2], flag[:, 0:1], ALU.mult, rk_,
                       [akey(("urh", slot, hv, 1))])

        def ffn_conv(item):
            j, pi, c = item
            slot = c % NUR
            for k in (0, 2):
                for th in range(2):
                    for hv in range(2):
                        uvv = ur[:, slot, hv, :].rearrange("p (s t) -> p s t", s=NSEG)
                        acc = cen[:, slot, hv * 2 + th, :].rearrange("p (s t) -> p s t", s=2)
                        ck_ = ("cen", slot, hv, th)
                        rk_ = [("ur", slot, hv, 0), ("ur", slot, hv, 1), ("urh", slot, hv, 0), ("urh", slot, hv, 1),
                               ck_, "vecs"]
                        dve_stt(acc, uvv[:, 2 * th:2 * th + 2, k:k + SEG], V(l, "fcw", k * 44 + hv * 22 + c, 1), acc,
                                ALU.mult, ALU.add, rk_, [ck_])
            for th in range(2):
                act(gl[:, th, :], cen[:, slot, th, :], AF.Gelu_apprx_tanh, [("cen", slot, 0, th)], [akey(("gl", th))])
                dve_tt(gTc(c)[:, th * 512:(th + 1) * 512], cen[:, slot, 2 + th, :], gl[:, th, :], ALU.mult,
                       [("cen", slot, 1, th), ("gl", th), "ur_init"], [akey(("gT", c, th))])

        for i in range(len(ffn_items) + 1):
            if i < len(ffn_items):
                ffn_up(ffn_items[i])
            if i >= 1:
                ffn_conv(ffn_items[i - 1])
            if i < len(ffn_items):
                ffn_halo(ffn_items[i])
        rot["i"] = 0
        if phase_end("H%d" % l):
            tap_list.extend([("gtail", gtail, [128, 2, T], BF16), ("lruT", lruT[:], [128, 4, T], BF16)])
            break
        mean = AF32(o, [128, T]); o += 4096
        rstd = AF32(o, [128, T]); o += 4096
        sqb = ABF(o, [128, 2, T]); o += 4096
        cvb = ABF(o, [128, 2, T]); o += 4096
        assert o <= SBYTES, o
        proj_residual("dn", gTc, NFF, lambda th: [("gT", kc, th) for kc in range(NFF)], 24)
        if l + 1 < DEPTH:
            mod_finish_A(l + 1)
        ln_apply("l2g", "l2b", nxt=(l + 1 < DEPTH))
        if phase_end("I%d" % l):
            break

    if not stopped["v"]:
        fence = list(arena_keys)
        del arena_keys[:]
        YT = [AF32(i * 4096, [128, 1024]) for i in range(8)]
        YALL = AF32(0, [128, 8, 1024])
        for c in range(8):
            for tbh in range(2):
                b = next_bank()

                def fn(e, c=c, tbh=tbh, b=b):
                    ins = None
                    for q in range(4):
                        tb = tbh * 4 + q
                        ins = e.transpose(out=PB(b)[:, q * 128:(q + 1) * 128], in_=xT[:, c, tb * 128:(tb + 1) * 128],
                                          identity=ident[:])
                    return ins
                P.add("pe", fn, r=[("xT", c, tbh), "ident"], w=[pskey(b)])
                dst = YALL[:, tbh * 4:(tbh + 1) * 4, c * 128:(c + 1) * 128]
                src = PB(b).rearrange("p (q t) -> p q t", q=4)
                wk = [("yt", tbh, c)] + fence
                if (c + tbh) % 2 == 0:
                    P.add("act", lambda e, dst=dst, src=src: e.activation(out=dst, in_=src, func=AF.Copy),
                          r=[pskey(b)], w=wk)
                else:
                    P.add("dve", lambda e, dst=dst, src=src: e.tensor_copy(out=dst, in_=src), r=[pskey(b)], w=wk)
        for tb in range(8):
            yk = [("yt", tb // 4, c) for c in range(8)]
            dma("sp", y_d[tb * 128:(tb + 1) * 128, :], YT[tb], yk, [], "st_y%d" % tb)
    tap_outs = []
    for ti, (name, ap, shape, dt) in enumerate(tap_list):
        dd = dout("tap_" + name, shape, dt)
        tap_outs.append("tap_" + name)
        P.add("sp", lambda e, dd=dd, ap=ap: e.dma_start(out=dd, in_=ap), r=list(P.kw.keys()), w=[], chan="tap%d" % ti)
    if stopped["v"]:
        dd = dout("tap_xT", [128, 8, T], F32)
        P.add("sp", lambda e, dd=dd: e.dma_start(out=dd, in_=xT[:]), r=list(P.kw.keys()), w=[], chan="tapx")
        dd2 = dout("tap_hT", [128, 8, T], BF16)
        P.add("sp", lambda e, dd2=dd2: e.dma_start(out=dd2, in_=hT[:]), r=list(P.kw.keys()), w=[], chan="taph")
        dd3 = dout("tap_modT", [128, 48], F32)
        P.add("sp", lambda e, dd3=dd3: e.dma_start(out=dd3, in_=modT[:]), r=list(P.kw.keys()), w=[], chan="tapm")
    last_dma = [P.chan_last[c] for c in P.chan_last]
    fin_op = Op()
    fin_op.eng = "sp"
    fin_op.fn = lambda e: e.nop()
    fin_op.chan = None
    fin_op.sig = False
    fin_op.gid = len(P.ops)
    fin_op.deps = set(last_dma)
    P.ops.append(fin_op)

    block = es.enter_context(nc.Block())
    P.emit(nc, es, block)
    es.close()
    return nc


def _fm(v):
    v = np.asarray(v, np.float32)
    return np.ascontiguousarray(v.reshape(-1, 128).T)


def _rope_tables(active):
    c = np.ones((128, T), np.float32)
    s = np.zeros((128, T), np.float32)
    if active:
        t = np.arange(T)
        row = (t // 64).astype(np.float32)
        colp = (t % 64).astype(np.float32)
        inv = (np.float32(10000.0) ** (-np.arange(16, dtype=np.float32) / np.float32(16))).astype(np.float32)
        for p in range(128):
            dd = p % 64
            a = dd // 32
            jj = dd % 32
            f = jj % 16
            pos = row if a == 0 else colp
            ang = (pos * inv[f]).astype(np.float32)
            c[p] = np.cos(ang)
            s[p] = -np.sin(ang) if jj < 16 else np.sin(ang)
    return c, s


def _mask_table(sample):
    m = np.zeros((128, MASK_COLS), np.float32)
    if not sample:
        m[:, 0:512] = NEGM
    k = np.arange(128)[:, None]
    for th in range(2):
        for (kb, jl, jh) in band_tiles(th):
            off = MASK_OFF[(th, kb)]
            for j in range(jl, jh + 1):
                q = np.arange(128)[None, :]
                if sample:
                    ok = np.abs((kb * 128 + k) - (j * 128 + q)) <= 128
                else:
                    ok = np.broadcast_to(np.array(kb // 2 == j // 2), (128, 128))
                m[:, off + (j - jl) * 128: off + (j - jl + 1) * 128] = np.where(ok, 0.0, NEGM)
    return m


def _partner(cols64):
    cols64 = np.asarray(cols64)
    idx = np.arange(64)
    j = idx % 32
    p = np.where(j < 16, idx + 16, idx - 16)
    return cols64[p]


def _prep_shared(inp):
    sh = {}
    w_in = np.asarray(inp["w_in"], np.float32)
    cols = []
    qc = lambda h: np.arange(h * 64, (h + 1) * 64)
    kc = lambda g: 512 + np.arange(g * 64, (g + 1) * 64)
    for name in WIN_CHUNKS:
        if name.startswith("qp"):
            c = int(name[2])
            cols.append(np.concatenate([_partner(qc(2 * c)), _partner(qc(2 * c + 1))]))
        elif name.startswith("q"):
            c = int(name[1])
            cols.append(np.concatenate([qc(2 * c), qc(2 * c + 1)]))
        elif name == "kA":
            cols.append(np.concatenate([kc(0), kc(1)]))
        elif name == "kB":
            cols.append(np.concatenate([kc(1), kc(0)]))
        elif name == "kAp":
            cols.append(np.concatenate([_partner(kc(0)), _partner(kc(1))]))
        elif name == "kBp":
            cols.append(np.concatenate([_partner(kc(1)), _partner(kc(0))]))
        elif name == "v":
            cols.append(640 + np.arange(128))
        elif name.startswith("xb"):
            c = int(name[2])
            cols.append(768 + c * 128 + np.arange(128))
        elif name.startswith("a"):
            c = int(name[1])
            cols.append(1280 + c * 128 + np.arange(128))
        elif name.startswith("g"):
            c = int(name[1])
            cols.append(1280 + 512 + c * 128 + np.arange(128))
    cols = np.concatenate(cols)
    sh["w_in_ext"] = np.ascontiguousarray(w_in[:, :, cols])
    vec = np.zeros((DEPTH, 128, NV), np.float32)
    f = lambda a: np.asarray(a, np.float32)
    for l in range(DEPTH):
        def put(name, arr):
            arr = np.asarray(arr, np.float32)
            vec[l, :, _VOFF[name]:_VOFF[name] + arr.shape[1]] = arr
        put("bmod", _fm(f(inp["b_mod"])[l]))
        put("lcw", np.concatenate([_fm(f(inp["lru_conv_w"])[l, k]) for k in range(4)], axis=1))
        put("lcb", _fm(f(inp["lru_conv_b"])[l]))
        put("lba", np.concatenate([_fm(f(inp["lru_ba"])[l, d]) for d in range(2)], axis=1))
        put("lbx", np.concatenate([_fm(f(inp["lru_bx"])[l, d]) for d in range(2)], axis=1))
        put("llam", np.concatenate([_fm(f(inp["lru_lambda"])[l, d]) for d in range(2)], axis=1))
        put("cw", np.concatenate([_fm(f(inp["conf_dw_w"])[l, k]) for k in range(31)], axis=1))
        put("cb", _fm(f(inp["conf_dw_b"])[l]))
        put("clg", _fm(f(inp["conf_ln_g"])[l]))
        put("clb", _fm(f(inp["conf_ln_b"])[l]))
        put("bm", _fm(f(inp["b_merge"])[l]))
        put("l1g", _fm(f(inp["ln1_g"])[l]))
        put("l1b", _fm(f(inp["ln1_b"])[l]))
        put("fcw", np.concatenate([_fm(f(inp["ffn_conv_w"])[l, k]) for k in range(3)], axis=1))
        put("fcb", _fm(f(inp["ffn_conv_b"])[l]))
        put("l2g", _fm(f(inp["ln2_g"])[l]))
        put("l2b", _fm(f(inp["ln2_b"])[l]))
        put("sink", np.broadcast_to(f(inp["attn_sink"])[l][None, :], (128, 8)))
    sh["vecs"] = vec
    lw = np.zeros((DEPTH, 128, 16, 128), np.float32)
    for l in range(DEPTH):
        for d in range(2):
            for gt, nm in enumerate(("lru_wa", "lru_wx")):
                wsrc = f(inp[nm])[l, d]
                for c in range(4):
                    i = (d * 2 + gt) * 4 + c
                    for bb in range(2):
                        lw[l, bb * 64:(bb + 1) * 64, i, bb * 64:(bb + 1) * 64] = wsrc[2 * c + bb]
    sh["lruw"] = lw
    sh["ident"] = np.eye(128, dtype=np.float32)
    for k in ("w_mod", "w_branch", "w_merge", "w_out", "ffn_w_up", "ffn_w_down"):
        sh[k] = np.ascontiguousarray(np.asarray(inp[k], np.float32))
    return sh


def make_in_maps(inp):
    sh = _prep_shared(inp)
    xs = np.asarray(inp["x_sample"], np.float32)
    xp = np.asarray(inp["x_prompt"], np.float32)
    ck = np.asarray(inp["cache_k"], np.float32)
    cv = np.asarray(inp["cache_v"], np.float32)
    st = np.asarray(inp["state_lru"], np.float32)
    cc = np.asarray(inp["c"], np.float32)
    cctx = np.asarray(inp["c_ctx"], np.float32)
    rc_s, rs_s = _rope_tables(True)
    rc_p, rs_p = _rope_tables(False)
    mk_s = _mask_table(True)
    mk_p = _mask_table(False)
    maps = []
    for core in range(8):
        m = dict(sh)
        if core < 4:
            b = core
            m["x"] = np.ascontiguousarray(xs[b])
            m["cond"] = _fm(cc[b])
            m["ck"] = np.ascontiguousarray(ck[b].reshape(DEPTH, 512, 128))
            m["cv"] = np.ascontiguousarray(cv[b].reshape(DEPTH, 512, 128))
            h0 = np.zeros((128, DEPTH * 8), np.float32)
            for l in range(DEPTH):
                for d in range(2):
                    h0[:, l * 8 + d * 4:l * 8 + d * 4 + 4] = _fm(st[b, l, d])
            m["h0"] = h0
            m["flag"] = np.ones((128, 1), np.float32)
            m["ropec"], m["ropes"], m["maskb"] = rc_s, rs_s, mk_s
        else:
            i = core - 4
            m["x"] = np.ascontiguousarray(xp[4 * i:4 * i + 4].reshape(T, D))
            m["cond"] = _fm(cctx)
            m["ck"] = np.zeros((DEPTH, 512, 128), np.float32)
            m["cv"] = np.zeros((DEPTH, 512, 128), np.float32)
            m["h0"] = np.zeros((128, DEPTH * 8), np.float32)
            m["flag"] = np.zeros((128, 1), np.float32)
            m["ropec"], m["ropes"], m["maskb"] = rc_p, rs_p, mk_p
        maps.append(m)
    return maps


_NC_CACHE = {}


def kernel(**inputs):
    if "nc" not in _NC_CACHE:
        _NC_CACHE["nc"] = build_program()
    nc = _NC_CACHE["nc"]
    maps = make_in_maps(inputs)
    res = run_bass_kernel_spmd(nc, maps, core_ids=list(range(8)))
    R = res.results
    y_prompt = np.zeros((16, 256, D), np.float32)
    y_sample = np.zeros((4, T, D), np.float32)
    nk = np.zeros((16, DEPTH, 256, 2, 64), np.float32)
    nv = np.zeros((16, DEPTH, 256, 2, 64), np.float32)
    nh = np.zeros((16, DEPTH, 2, 512), np.float32)
    for core in range(8):
        r = R[core]
        if core < 4:
            y_sample[core] = r["y"]
        else:
            i = core - 4
            y_prompt[4 * i:4 * i + 4] = r["y"].reshape(4, 256, D)
            for s in range(4):
                for l in range(DEPTH):
                    nk[4 * i + s, l] = r["nk"][l, s * 256:(s + 1) * 256].reshape(256, 2, 64)
                    nv[4 * i + s, l] = r["nv"][l, s * 256:(s + 1) * 256].reshape(256, 2, 64)
                    nh[4 * i + s, l] = r["nh"][l].reshape(4, 2, 512)[s]
    return (y_prompt, y_sample, nk, nv, nh)
```

```python
from contextlib import ExitStack
import numpy as np
import concourse.bass as bass
import concourse.mybir as mybir
from concourse.bass_utils import run_bass_kernel_spmd

F32 = mybir.dt.float32
BF16 = mybir.dt.bfloat16
AF = mybir.ActivationFunctionType
ALU = mybir.AluOpType

D = 1024
T = 1024
DEPTH = 2
NSEG = 4
SEG = 256
DFF = 2816
NFF = 22
ALPHA = (2 * DEPTH) ** 0.25
LN_EPS = 1e-5
EPS_F = LN_EPS / (ALPHA * ALPHA)
NEGM = -30000.0
NSLOT = 4
SLOT_ELEMS = 4096
XBP = 259
UPP = 286
FPP = 258

_VOFF = {}
_o = 0
for _n, _w in [("bmod", 48), ("lcw", 16), ("lcb", 4), ("lba", 8), ("lbx", 8), ("llam", 8), ("cw", 124),
               ("cb", 4), ("clg", 4), ("clb", 4), ("bm", 24), ("l1g", 8), ("l1b", 8), ("fcw", 132),
               ("fcb", 44), ("l2g", 8), ("l2b", 8), ("sink", 8)]:
    _VOFF[_n] = _o
    _o += _w
NV = _o

WIN_CHUNKS = ["q0", "qp0", "q1", "qp1", "q2", "qp2", "q3", "qp3", "kA", "kAp",
              "v", "xb0", "xb1", "xb2", "xb3", "a0", "g0", "a1", "g1", "a2", "g2", "a3", "g3"]
NWIN = len(WIN_CHUNKS) * 128
WIN_TILES = [(0, 512), (512, 512), (1024, 256), (1280, 128), (1408, 512), (1920, 512), (2432, 512)]


def band_tiles(th):
    out = []
    for kb in range(max(0, 4 * th - 1), min(8, 4 * th + 5)):
        jl = max(4 * th, kb - 1)
        jh = min(4 * th + 3, kb + 1)
        out.append((kb, jl, jh))
    return out


MASK_OFF = {}
_m = 512
for _th in range(2):
    for (_kb, _jl, _jh) in band_tiles(_th):
        MASK_OFF[(_th, _kb)] = _m
        _m += 128 * (_jh - _jl + 1)
MASK_COLS = _m


class Op:
    __slots__ = ("eng", "fn", "deps", "chan", "cord", "idx", "sig", "sigcount", "gid")


class Prog:
    ENGS = ("pe", "act", "dve", "pool", "sp")

    def __init__(self):
        self.ops = []
        self.kw = {}
        self.kr = {}
        self.chan_last = {}
        self.chan_count = {}

    def add(self, eng, fn, r=(), w=(), chan=None):
        op = Op()
        op.eng = eng
        op.fn = fn
        op.chan = chan
        op.sig = False
        op.gid = len(self.ops)
        deps = set()
        psr = [k for k in r if isinstance(k, tuple) and k and k[0] == "ps"]
        if psr:
            r = [k for k in r if k not in psr]
            w = list(w) + psr
        for k in r:
            p = self.kw.get(k)
            if p is not None:
                deps.add(p)
        for k in w:
            p = self.kw.get(k)
            if p is not None:
                deps.add(p)
            for q in self.kr.get(k, ()):
                deps.add(q)
        if chan is not None:
            p = self.chan_last.get(chan)
            if p is not None:
                deps.add(p)
            self.chan_last[chan] = op.gid
            self.chan_count[chan] = self.chan_count.get(chan, 0) + 1
            op.cord = self.chan_count[chan]
        for k in r:
            self.kr.setdefault(k, []).append(op.gid)
        for k in w:
            self.kw[k] = op.gid
            self.kr[k] = []
        op.deps = deps
        self.ops.append(op)
        return op.gid

    def emit(self, nc, es, block):
        ops = self.ops
        for op in ops:
            for d in op.deps:
                if ops[d].chan is None:
                    ops[d].sig = True
        cnt = {e: 0 for e in self.ENGS}
        for op in ops:
            if op.chan is None and op.sig:
                cnt[op.eng] += 1
                op.sigcount = cnt[op.eng]
        esem = {e: es.enter_context(nc.semaphore("s_" + e)) for e in ("pe", "act", "dve", "pool")}
        csem = {c: es.enter_context(nc.semaphore("c_" + c)) for c in self.chan_count}
        per_eng = {e: [op for op in ops if op.eng == e] for e in self.ENGS}

        def run(e, name):
            known = {}
            for op in per_eng[name]:
                waits = {}
                for d in op.deps:
                    p = ops[d]
                    if p.chan is not None:
                        key = ("c", p.chan)
                        val = 16 * p.cord
                    else:
                        if p.eng == "pe" and name == "pe":
                            continue
                        key = ("e", p.eng)
                        val = p.sigcount
                    if known.get(key, 0) >= val:
                        continue
                    if waits.get(key, 0) < val:
                        waits[key] = val
                for key, val in waits.items():
                    sem = csem[key[1]] if key[0] == "c" else esem[key[1]]
                    e.wait_ge(sem, val)
                    known[key] = val
                ins = op.fn(e)
                if op.chan is not None:
                    ins.then_inc(csem[op.chan], 16)
                elif op.sig:
                    ins.then_inc(esem[name], 1)

        @block.tensor
        def _(e):
            run(e, "pe")

        @block.scalar
        def _(e):
            run(e, "act")

        @block.vector
        def _(e):
            run(e, "dve")

        @block.gpsimd
        def _(e):
            run(e, "pool")

        @block.sync
        def _(e):
            run(e, "sp")


def build_program(stop_after=None, taps=()):
    nc = bass.Bass("TRN2", target_bir_lowering=False)
    es = ExitStack()
    P = Prog()

    def din(name, shape, dt=F32):
        return nc.dram_tensor(name, list(shape), dt, kind="ExternalInput").ap()

    def dout(name, shape, dt=F32):
        return nc.dram_tensor(name, list(shape), dt, kind="ExternalOutput").ap()

    def sb(name, shape, dt):
        return es.enter_context(nc.sbuf_tensor("sb_" + name, list(shape), dt))

    x_d = din("x", [T, D])
    cond_d = din("cond", [128, 8])
    ck_d = din("ck", [DEPTH, 512, 128])
    cv_d = din("cv", [DEPTH, 512, 128])
    h0_d = din("h0", [128, DEPTH * 8])
    flag_d = din("flag", [128, 1])
    ropec_d = din("ropec", [128, T])
    ropes_d = din("ropes", [128, T])
    maskb_d = din("maskb", [128, MASK_COLS])
    ident_d = din("ident", [128, 128])
    vecs_d = din("vecs", [DEPTH, 128, NV])
    lruw_d = din("lruw", [DEPTH, 128, 16, 128])
    wmod_d = din("w_mod", [DEPTH, D, 6 * D])
    win_d = din("w_in_ext", [DEPTH, D, NWIN])
    wbr_d = din("w_branch", [DEPTH, 3, 512, D])
    wmg_d = din("w_merge", [DEPTH, D, 3 * D])
    wout_d = din("w_out", [DEPTH, D, D])
    wup_d = din("ffn_w_up", [DEPTH, D, 2 * DFF])
    wdn_d = din("ffn_w_down", [DEPTH, DFF, D])

    y_d = dout("y", [T, D])
    nk_d = dout("nk", [DEPTH, T, 128])
    nv_d = dout("nv", [DEPTH, T, 128])
    nh_d = dout("nh", [DEPTH, 32, 128])

    xT = sb("xT", [128, 8, T], F32)
    hT = sb("hT", [128, 8, T], BF16)
    wsl = sb("wsl", [128, NSLOT, SLOT_ELEMS], BF16)
    ident = sb("ident", [128, 128], F32)
    identb = sb("identb", [128, 128], BF16)
    onesb = sb("onesb", [128, 128], BF16)
    vecs = sb("vecs", [128, DEPTH, NV], F32)
    condt = sb("condt", [128, 8], F32)
    scond = sb("scond", [128, 8], BF16)
    flag = sb("flag", [128, 1], F32)
    ctxb = sb("ctxb", [128, 1], F32)
    h0t = sb("h0t", [128, DEPTH * 8], F32)
    modT = sb("modT", [128, 48], F32)
    der = sb("der", [128, 64], F32)
    ropec = sb("ropec", [128, T], BF16)
    ropes = sb("ropes", [128, T], BF16)
    maskb = sb("maskb", [128, MASK_COLS], BF16)
    lruw = sb("lruw", [128, 16, 128], BF16)
    lrudg = sb("lrudg", [128, 16, 128], BF16)
    esink = sb("esink", [128, 8], F32)
    fin = sb("fin", [128, 32], F32)
    fint = sb("fint", [32, 128], F32)
    initb = sb("initb", [128, 8], F32)
    attnT = sb("attnT", [128, 4, T], BF16)
    lruT = sb("lruT", [128, 4, T], BF16)
    confT = sb("confT", [128, 4, T], BF16)
    xbpad2 = sb("xbpad", [128, 4 * NSEG * XBP], BF16)
    xbpad = xbpad2[:, :].rearrange("p (c x) -> p c x", c=4)
    upad2 = sb("upad", [128, 4 * NSEG * UPP], BF16)
    upad = upad2[:, :].rearrange("p (c x) -> p c x", c=4)
    SBYTES = 60 * 1024
    arena = sb("arena", [128, SBYTES // 4], F32)
    arena_b = arena[:].bitcast(BF16) if hasattr(arena[:], "bitcast") else None

    psum = es.enter_context(nc.psum_tensor("ps", [128, 8, 512], F32))

    def AF32(off_bytes, shape):
        n = int(np.prod(shape[1:]))
        o = off_bytes // 4
        ap = arena[0:shape[0], o:o + n]
        return ap if len(shape) == 2 else _reshape(ap, shape[1:])

    def ABF(off_bytes, shape):
        n = int(np.prod(shape[1:]))
        o = off_bytes // 2
        ap = arena_b[0:shape[0], o:o + n]
        return ap if len(shape) == 2 else _reshape(ap, shape[1:])

    def _reshape(ap, free):
        if len(free) == 2:
            return ap.rearrange("p (a b) -> p a b", a=free[0])
        if len(free) == 3:
            return ap.rearrange("p (a b c) -> p a b c", a=free[0], b=free[1])
        raise ValueError

    arena_keys = []

    def akey(name):
        arena_keys.append(name)
        return name

    wtiles = []

    def wt_add(kind, parts):
        wtiles.append((kind, parts))

    def slot_view(s, kc, ncols, np_=128):
        return wsl[0:np_, s, 0:kc * ncols].rearrange("p (k n) -> p k n", k=kc)

    def wt_mod(l, j):
        src = wmod_d[l, :, j * 512:(j + 1) * 512].rearrange("(k p) n -> p k n", p=128)
        wt_add(("mod", l, j), [(lambda s: slot_view(s, 8, 512), src)])

    for l in range(DEPTH):
        if l == 0:
            for j in range(4):
                wt_mod(0, j)
        def wt_win(j):
            c0, ncols = WIN_TILES[j]
            src = win_d[l, :, c0:c0 + ncols].rearrange("(k p) n -> p k n", p=128)
            wt_add(("win", l, j), [(lambda s, ncols=ncols: slot_view(s, 8, ncols), src)])
        for j in range(4):
            wt_win(j)
        for j in range(4, 12):
            wt_mod(l, j)
        for j in range(4, 7):
            wt_win(j)
        for n in range(8):
            parts = []
            for b in range(3):
                src = wmg_d[l, :, b * D + n * 128: b * D + (n + 1) * 128].rearrange("(k p) n -> p k n", p=128)
                parts.append((lambda s, b=b: slot_view(s, 24, 128)[:, b * 8:(b + 1) * 8, :], src))
            wt_add(("mg", l, n), parts)
            parts = []
            for b in range(3):
                src = wbr_d[l, b, :, n * 128:(n + 1) * 128].rearrange("(k p) n -> p k n", p=128)
                parts.append((lambda s, b=b: slot_view(s, 12, 128)[:, 4 * b:4 * b + 4, :], src))
            wt_add(("br", l, n), parts)
            if l + 1 < DEPTH and n % 2 == 1:
                wt_mod(l + 1, n // 2)
        for j in range(2):
            src = wout_d[l, :, j * 512:(j + 1) * 512].rearrange("(k p) n -> p k n", p=128)
            wt_add(("out", l, j), [(lambda s: slot_view(s, 8, 512), src)])
        for j in range(11):
            parts = []
            for hv in range(2):
                c0 = hv * DFF + j * 256
                src = wup_d[l, :, c0:c0 + 256].rearrange("(k p) n -> p k n", p=128)
                parts.append((lambda s, hv=hv: slot_view(s, 8, 512)[:, :, hv * 256:(hv + 1) * 256], src))
            wt_add(("up", l, j), parts)
        for n in range(8):
            src = wdn_d[l, :, n * 128:(n + 1) * 128].rearrange("(k p) n -> p k n", p=128)
            wt_add(("dn", l, n), [(lambda s: slot_view(s, NFF, 128), src)])
        for n in range(8):
            src = wdn_d[l, :, n * 128:(n + 1) * 128].rearrange("(k p) n -> p k n", p=128)
            wt_add(("dn2", l, n), [(lambda s: slot_view(s, NFF, 128), src)])

    wstate = {"issued": 0, "next": 0}

    def w_issue_upto(j):
        while wstate["issued"] <= min(j, len(wtiles) - 1):
            i = wstate["issued"]
            kind, parts = wtiles[i]
            s = i % NSLOT
            for pi, (dst_fn, src) in enumerate(parts):
                dst = dst_fn(s)
                P.add("pool", (lambda e, dst=dst, src=src: e.dma_start(out=dst, in_=src)),
                      w=[("w", s, pi)],
                      chan="w%d_%d" % (s, pi))
            wstate["issued"] += 1

    def w_next(kind, la=NSLOT - 1):
        j = wstate["next"]
        assert wtiles[j][0] == kind, (wtiles[j][0], kind)
        w_issue_upto(j + la)
        wstate["next"] += 1
        s = j % NSLOT
        return s, [("w", s, q) for q in range(3)]

    def PB(b):
        return psum[:, b, :]

    pskey = lambda b: ("ps", b)
    rot = {"i": 0}

    MODB = 7

    def next_bank():
        b = rot["i"] % 7
        rot["i"] += 1
        return b

    def act(out, in_, func, r, w, bias=None, scale=None):
        kw = {}
        if bias is not None:
            kw["bias"] = bias
        if scale is not None:
            kw["scale"] = scale
        return P.add("act", lambda e: e.activation(out=out, in_=in_, func=func, **kw), r=r, w=w)

    def dve_tt(out, in0, in1, op, r, w, eng="dve"):
        return P.add(eng, lambda e: e.tensor_tensor(out=out, in0=in0, in1=in1, op=op), r=r, w=w)

    def dve_ts(out, in0, s1, op0, r, w, s2=None, op1=None, eng="dve"):
        if op1 is None:
            return P.add(eng, lambda e: e.tensor_scalar(out=out, in0=in0, scalar1=s1, scalar2=0.0, op0=op0, op1=ALU.add),
                         r=r, w=w)
        return P.add(eng, lambda e: e.tensor_scalar(out=out, in0=in0, scalar1=s1, scalar2=s2, op0=op0, op1=op1), r=r, w=w)

    def dve_stt(out, in0, scalar, in1, op0, op1, r, w, eng="dve"):
        return P.add(eng, lambda e: e.scalar_tensor_tensor(out=out, in0=in0, scalar=scalar, in1=in1, op0=op0, op1=op1),
                     r=r, w=w)

    def dma(q, out, in_, r, w, chan):
        return P.add(q, lambda e: e.dma_start(out=out, in_=in_), r=r, w=w, chan=chan)

    def mm_group(bank, steps, r, cols=None, extra_w=()):
        outap = PB(bank) if cols is None else PB(bank)[:, cols[0]:cols[1]]

        def fn(e):
            ins = None
            n = len(steps)
            for i, (lt, rh) in enumerate(steps):
                ins = e.matmul(outap, lt, rh, start=(i == 0), stop=(i == n - 1))
            return ins
        return P.add("pe", fn, r=r, w=[pskey(bank)] + list(extra_w))

    stopped = {"v": False}
    tap_list = []

    def phase_end(name):
        if stop_after == name:
            stopped["v"] = True
        return stopped["v"]

    dma("sp", ident[:], ident_d, [], ["ident"], "ld0")
    dma("sp", vecs[:], vecs_d.rearrange("l p n -> p l n"), [], ["vecs"], "ld1")
    dma("sp", condt[:], cond_d, [], ["condt"], "ld2")
    dma("sp", flag[:], flag_d, [], ["flag"], "ld3")
    dma("sp", h0t[:], h0_d, [], ["h0t"], "ld0")

    P.add("dve", lambda e: e.memset(onesb[:], 1.0), w=["onesb"])
    act(identb[:], ident[:], AF.Copy, ["ident"], ["identb"])
    P.add("dve", lambda e: e.tensor_scalar(out=ctxb[:], in0=flag[:], scalar1=-1.0, scalar2=-NEGM, op0=ALU.add,
                                           op1=ALU.mult), r=["flag"], w=["ctxb"])
    act(scond[:], condt[:], AF.Silu, ["condt"], ["scond"])

    XIN = [AF32(i * 4096, [128, 1024]) for i in range(8)]
    for tb in range(8):
        dma("sp", XIN[tb], x_d[tb * 128:(tb + 1) * 128, :], [], [akey(("xin", tb))], "xin%d" % tb)
    for tb in range(8):
        st = XIN[tb]
        k = ("xin", tb)
        for half in range(2):
            b = next_bank()
            def fn(e, st=st, half=half, b=b):
                ins = None
                for q in range(4):
                    c = half * 4 + q
                    ins = e.transpose(out=PB(b)[:, q * 128:(q + 1) * 128], in_=st[:, c * 128:(c + 1) * 128],
                                      identity=ident[:])
                return ins
            P.add("pe", fn, r=[k, "ident"], w=[pskey(b)])
            src = PB(b).rearrange("p (q t) -> p q t", q=4)
            dst = xT[:, half * 4:(half + 1) * 4, tb * 128:(tb + 1) * 128]
            eng = "act" if half == 0 else "dve"
            if eng == "act":
                P.add("act", lambda e, dst=dst, src=src: e.activation(out=dst, in_=src, func=AF.Copy),
                      r=[pskey(b)], w=[("xT", c_, tb // 4) for c_ in range(half * 4, half * 4 + 4)])
            else:
                P.add("dve", lambda e, dst=dst, src=src: e.tensor_copy(out=dst, in_=src),
                      r=[pskey(b)], w=[("xT", c_, tb // 4) for c_ in range(half * 4, half * 4 + 4)])

    def V(l, name, col=0, n=1):
        o = _VOFF[name] + col
        return vecs[:, l, o:o + n]

    for l in range(DEPTH):
        if stopped["v"]:
            break
        def mod_tile(ll, j):
            s, wk = w_next(("mod", ll, j))
            wv = slot_view(s, 8, 512)

            def fn(e, wv=wv, j=j):
                ins = None
                for n4 in range(4):
                    col = j * 4 + n4
                    for kc in range(8):
                        ins = e.matmul(PB(MODB)[:, col:col + 1], wv[:, kc, n4 * 128:(n4 + 1) * 128],
                                       scond[:, kc:kc + 1], start=(kc == 0), stop=(kc == 7))
                return ins
            P.add("pe", fn, r=wk + ["scond"], w=[pskey(MODB)])

        def mod_finish_A(ll):
            dve_tt(modT[:, 0:16], PB(MODB)[:, 0:16], V(ll, "bmod", 0, 16), ALU.add, [pskey(MODB), "vecs"], ["modA"])
            dve_ts(der[:, 0:8], modT[:, 8:16], 1.0, ALU.add, ["modA"], [("der", 0)])

        def mod_finish_B(ll):
            dve_tt(modT[:, 16:48], PB(MODB)[:, 16:48], V(ll, "bmod", 16, 32), ALU.add, [pskey(MODB), "vecs"], ["modB"])
            dve_ts(der[:, 8:16], modT[:, 16:24], 1.0 / ALPHA, ALU.mult, ["modB"], [("der", 1)])
            dve_ts(der[:, 16:24], modT[:, 32:40], 1.0, ALU.add, ["modB"], [("der", 2)])
            dve_ts(der[:, 24:32], modT[:, 40:48], 1.0 / ALPHA, ALU.mult, ["modB"], [("der", 3)])

        if l == 0:
            w_issue_upto(NSLOT - 1)
            for j in range(4):
                mod_tile(0, j)
            mod_finish_A(0)
            dma("pool", ropec[:], ropec_d, [], ["ropec"], "ldr0")
            dma("pool", ropes[:], ropes_d, [], ["ropes"], "ldr1")
            dma("pool", maskb[:], maskb_d, [], ["maskb"], "ldm2")
        dma("pool", lruw[:], lruw_d[l], [], ["lruw"], "ldm")
        act(der[:, 40:48], V(l, "llam", 0, 8), AF.Exp, ["vecs"], [("der", 5)], scale=-1.0)
        dve_ts(der[:, 40:48], der[:, 40:48], 1.0, ALU.add, [("der", 5)], [("der", 5)])
        act(der[:, 32:40], der[:, 40:48], AF.Ln, [("der", 5)], [("der", 4)])
        dve_ts(der[:, 32:40], der[:, 32:40], -4.0, ALU.mult, [("der", 4)], [("der", 4)])
        dve_ts(der[:, 48:64], V(l, "lba", 0, 16), 0.5, ALU.mult, ["vecs"], [("der", 6)])
        act(esink[:], V(l, "sink", 0, 8), AF.Exp, ["vecs"], ["esink"])
        for k in range(4):
            for c in range(4):
                i = k * 4 + c
                dve_ts(lrudg[:, i, :], identb[:], V(l, "lcw", i, 1), ALU.mult, ["identb", "vecs"], [("lrudg", i)],
                       eng="pool")

        for c in range(8 if l == 0 else 0):
            if c % 2 == 0:
                act(hT[:, c, :], xT[:, c, :], AF.Identity, [("xT", c, 0), ("xT", c, 1), ("der", 0), "modA"],
                    [("hT", c, 0), ("hT", c, 1)], bias=modT[:, c:c + 1], scale=der[:, c:c + 1])
            else:
                dve_ts(hT[:, c, :], xT[:, c, :], der[:, c:c + 1], ALU.mult,
                       [("xT", c, 0), ("xT", c, 1), ("der", 0), "modA"], [("hT", c, 0), ("hT", c, 1)],
                       s2=modT[:, c:c + 1], op1=ALU.add)
        gtk = lambda lo, hi: [("gT", c_, th_) for c_ in range(lo, hi) for th_ in range(2)]
        P.add("dve", lambda e: e.memset(xbpad2[:, :], 0.0), r=[("hT", 7, 1)], w=["xbpad_init"] + gtk(12, 16))
        P.add("dve", lambda e: e.memset(upad2[:, :], 0.0), r=[("hT", 7, 1)], w=["upad_init"] + gtk(16, 20))
        if phase_end("A%d" % l):
            break

        fence = list(arena_keys)
        del arena_keys[:]
        o = 0
        qT = ABF(o, [128, 4, T]); o += 8192
        kz = ABF(o, [128, 4, T]); o += 8192
        ktok = AF32(o, [128, 8, 128]); o += 4096
        vtok = AF32(o, [128, 8, 128]); o += 4096
        vaug = ABF(o, [128, 4, 8, 128]); o += 8192
        ckf = AF32(o, [128, 4, 128]); o += 2048
        cksf = AF32(o, [128, 4, 128]); o += 2048
        ckz = ABF(o, [128, 4, 512]); o += 4096
        cvaug = ABF(o, [128, 4, 4, 128]); o += 4096
        NPT = 4
        vraw = AF32(o, [128, T])
        pt = ABF(o, [128, NPT, 512]); o += NPT * 1024
        kraw = AF32(o, [128, T])
        rd = AF32(o, [128, 2, 512]); o += 4096
        rt1 = AF32(o, [128, 2, 512]); o += 4096
        rt2 = AF32(o, [128, 2, 512]); o += 4096
        assert o <= SBYTES, o
        first_fence = {"v": fence}

        def FW():
            f = first_fence["v"]
            first_fence["v"] = []
            return f

        P.add("dve", lambda e, kz=kz: e.memset(kz, 0.0), w=FW() + [akey("kz_init")])
        P.add("dve", lambda e, ckz=ckz: e.memset(ckz, 0.0), r=["kz_init"], w=[akey("ckz_init")])
        P.add("dve", lambda e, vaug=vaug: e.memset(vaug, 1.0), r=["kz_init"], w=[akey("vaug_init")])
        P.add("dve", lambda e, cvaug=cvaug: e.memset(cvaug, 1.0), r=["kz_init"], w=[akey("cvaug_init")])
        dma("sp", ckf, ck_d[l].rearrange("(k p) n -> p k n", p=128), ["kz_init"], [akey("ckf")], "ldc0")
        for g_ in range(2):
            for e__ in range(2):
                dma("pool", cvaug[:, g_ * 2 + e__, :, e__ * 64:(e__ + 1) * 64],
                    cv_d[l][:, g_ * 64:(g_ + 1) * 64].rearrange("(k p) n -> p k n", p=128), ["cvaug_init"],
                    [akey(("cvaug", g_ * 2 + e__))], "ldv%d" % (g_ * 2 + e__))
        hkeys = lambda th: [("hT", c, th) for c in range(8)]
        pend = {}
        wcur = {"j": -1}

        def win_chunk(cname, sgsel=None):
            ci = WIN_CHUNKS.index(cname)
            col = ci * 128
            j = [i for i, (c0, nc_) in enumerate(WIN_TILES) if c0 <= col < c0 + nc_][0]
            off = col - WIN_TILES[j][0]
            if j != wcur["j"]:
                s_cur, wk_c = w_next(("win", l, j))
                wcur["j"] = j
                wcur["wk"] = wk_c
                wcur["wv"] = slot_view(s_cur, 8, WIN_TILES[j][1])
            wk_cur, wv_cur = wcur["wk"], wcur["wv"]
            banks = []
            for th in range(2):
                b = next_bank()
                banks.append(b)
                mm_group(b, [(wv_cur[:, kc, off:off + 128], hT[:, kc, th * 512:(th + 1) * 512]) for kc in range(8)],
                         r=wk_cur + hkeys(th))
            pend[cname] = banks
            if cname.startswith("qp") or cname in ("kAp", "kBp"):
                base = cname.replace("p", "")
                for th in range(2):
                    b0 = pend[base][th]
                    b1 = banks[th]
                    tsl = slice(th * 512, (th + 1) * 512)
                    t1 = rt1[:, th, :]
                    t2 = rt2[:, th, :]
                    dve_tt(t1, PB(b0), ropec[:, tsl], ALU.mult, [pskey(b0), "ropec"], [akey(("rt1", th))])
                    dve_tt(t2, PB(b1), ropes[:, tsl], ALU.mult, [pskey(b1), "ropes"], [akey(("rt2", th))])
                    if base.startswith("q"):
                        c = int(base[1])
                        dve_tt(qT[:, c, tsl], t1, t2, ALU.add, [("rt1", th), ("rt2", th)], [akey(("qT", c, th))])
                    else:
                        lo_idx, hi_idx = (0, 3) if base == "kA" else (2, 1)
                        dve_tt(kz[0:64, lo_idx, tsl], t1[0:64, :], t2[0:64, :], ALU.add,
                               [("rt1", th), ("rt2", th), "kz_init"], [akey(("kz", lo_idx, th))])
                        dve_tt(kz[64:128, hi_idx, tsl], t1[64:128, :], t2[64:128, :], ALU.add,
                               [("rt1", th), ("rt2", th), "kz_init"], [akey(("kz", hi_idx, th))])
                    if base == "kA":
                        act(kraw[:, tsl], PB(b0), AF.Copy, [pskey(b0), "kz_init"], [akey(("kraw", th))])
                        for (do_, di_, so_, si_) in ((slice(64, 128), 1, slice(0, 64), 0), (slice(0, 64), 2, slice(64, 128), 3)):
                            dst_ = kz[do_, di_, tsl]
                            src_ = kz[so_, si_, tsl]
                            P.add("dve", lambda e, dst_=dst_, src_=src_: e.tensor_copy(out=dst_, in_=src_),
                                  r=[("kz", si_, th), "kz_init"], w=[akey(("kz", di_, th))])
            elif cname == "v":
                for th in range(2):
                    tsl = slice(th * 512, (th + 1) * 512)
                    act(vraw[:, tsl], PB(banks[th]), AF.Copy, [pskey(banks[th]), "kz_init"], [akey(("vraw", th))])
                for (raw, rkey, tok, tkey) in ((kraw, "kraw", ktok, "ktok"), (vraw, "vraw", vtok, "vtok")):
                    for th in range(2):
                        b = next_bank()

                        def fn(e, raw=raw, th=th, b=b):
                            ins = None
                            for q in range(4):
                                blk = th * 4 + q
                                ins = e.transpose(out=PB(b)[:, q * 128:(q + 1) * 128],
                                                  in_=raw[:, blk * 128:(blk + 1) * 128], identity=ident[:])
                            return ins
                        P.add("pe", fn, r=[(rkey, th), "ident"], w=[pskey(b)])
                        src = PB(b).rearrange("p (q t) -> p q t", q=4)
                        dst = tok[:, th * 4:(th + 1) * 4, :]
                        P.add("dve", lambda e, dst=dst, src=src: e.tensor_copy(out=dst, in_=src),
                              r=[pskey(b), "kz_init"], w=[akey((tkey, th))])
                        if tkey == "vtok":
                            for g_ in range(2):
                                for e__ in range(2):
                                    dstb = vaug[:, g_ * 2 + e__, th * 4:(th + 1) * 4, e__ * 64:(e__ + 1) * 64]
                                    srcb = src[:, :, g_ * 64:(g_ + 1) * 64]
                                    if e__ == 0:
                                        P.add("act", lambda e, dstb=dstb, srcb=srcb: e.activation(
                                            out=dstb, in_=srcb, func=AF.Copy),
                                            r=[pskey(b), "vaug_init"], w=[akey(("vaug", g_ * 2 + e__, th))])
                                    else:
                                        P.add("dve", lambda e, dstb=dstb, srcb=srcb: e.tensor_copy(out=dstb, in_=srcb),
                                              r=[pskey(b), "vaug_init"], w=[akey(("vaug", g_ * 2 + e__, th))])
                dma("sp", nk_d[l].rearrange("(b p) n -> p b n", p=128), ktok, [("ktok", 0), ("ktok", 1)], [], "st_k")
                dma("sp", nv_d[l].rearrange("(b p) n -> p b n", p=128), vtok, [("vtok", 0), ("vtok", 1)], [], "st_v")
            elif cname.startswith("xb"):
                c = int(cname[2])
                for th in range(2):
                    src = PB(banks[th]).rearrange("p (s t) -> p s t", s=2)
                    dst = xbpad[:, c, :].rearrange("p (s t) -> p s t", s=NSEG)[:, 2 * th:2 * th + 2, 2:2 + SEG]
                    P.add("act", lambda e, dst=dst, src=src: e.activation(out=dst, in_=src, func=AF.Copy),
                          r=[pskey(banks[th]), "xbpad_init"], w=[("xbpad", c, th)])
                xv = xbpad[:, c, :].rearrange("p (s t) -> p s t", s=NSEG)
                dve_ts(xv[:, 1:4, 0:2], xv[:, 0:3, SEG:SEG + 2], flag[:, 0:1], ALU.mult,
                       [("xbpad", c, 0), ("xbpad", c, 1), "flag"], [("xbpad_h", c, 0)])
                dve_ts(xv[:, 0:3, SEG + 2:SEG + 3], xv[:, 1:4, 2:3], flag[:, 0:1], ALU.mult,
                       [("xbpad", c, 0), ("xbpad", c, 1), "flag"], [("xbpad_h", c, 1)])
            elif cname.startswith("g"):
                c = int(cname[1])
                ab = pend["a%d" % c]
                for th in range(2):
                    sg, sgk, sgx = sgsel(th)
                    act(sg, PB(banks[th]), AF.Sigmoid, [pskey(banks[th])], sgx + [akey(sgk)])
                    src = PB(ab[th]).rearrange("p (s t) -> p s t", s=2)
                    dst = upad[:, c, :].rearrange("p (s t) -> p s t", s=NSEG)[:, 2 * th:2 * th + 2, 15:15 + SEG]
                    sgv = sg.rearrange("p (s t) -> p s t", s=2)
                    P.add("dve", lambda e, dst=dst, src=src, sgv=sgv: e.tensor_tensor(out=dst, in0=src, in1=sgv,
                                                                                       op=ALU.mult),
                          r=[pskey(ab[th]), sgk, "upad_init"], w=[("upad", c, th)])
                uv = upad[:, c, :].rearrange("p (s t) -> p s t", s=NSEG)
                dve_ts(uv[:, 1:4, 0:15], uv[:, 0:3, SEG:SEG + 15], flag[:, 0:1], ALU.mult,
                       [("upad", c, 0), ("upad", c, 1), "flag"], [("upad_h", c, 0)])
                dve_ts(uv[:, 0:3, SEG + 15:SEG + 30], uv[:, 1:4, 15:30], flag[:, 0:1], ALU.mult,
                       [("upad", c, 0), ("upad", c, 1), "flag"], [("upad_h", c, 1)])
        for cname in WIN_CHUNKS[:11]:
            win_chunk(cname)
        b = next_bank()

        def fn(e, srcb=ckf, b=b):
            ins = None
            for kc in range(4):
                ins = e.transpose(out=PB(b)[:, kc * 128:(kc + 1) * 128], in_=srcb[:, kc, :], identity=ident[:])
            return ins
        P.add("pe", fn, r=["ckf", "ident"], w=[pskey(b)])
        for (do_, di_, so_) in ((slice(0, 64), 0, slice(0, 64)), (slice(64, 128), 3, slice(64, 128)),
                                (slice(64, 128), 1, slice(0, 64)), (slice(0, 64), 2, slice(64, 128))):
            act(ckz[do_, di_, :], PB(b)[so_, :], AF.Copy, [pskey(b), "ckz_init"], [akey(("ckz", di_))])

        if phase_end("B%d" % l):
            tap_list.extend([("qT", qT, [128, 4, T], BF16), ("kz", kz, [128, 4, T], BF16),
                             ("ckz", ckz, [128, 4, 512], BF16)])
            break

        sched = []
        for h in range(8):
            for th in range(2):
                tl = [("ctx", kc) for kc in range(4)] + [("band",) + bt for bt in band_tiles(th)]
                for ti, t_ in enumerate(tl):
                    sched.append((h, th, ti, len(tl), t_))
        LAG = 3
        st_bank = {}
        it_idx = {}
        for i, (h, th) in enumerate([(h, th) for h in range(8) for th in range(2)]):
            it_idx[(h, th)] = i

        def emit_S(gi):
            h, th, ti, ntl, t_ = sched[gi]
            c, e_, g = h // 2, h % 2, h // 4
            b = gi % 4
            st_bank[gi] = b
            if t_[0] == "ctx":
                kc = t_[1]
                lt = ckz[:, g * 2 + e_, kc * 128:(kc + 1) * 128]
                q0, q1 = th * 512, (th + 1) * 512
                mk = None
                rk = [("ckz", g * 2 + e_), "ckz_init"]
            else:
                _, kb, jl, jh = t_
                lt = kz[:, g * 2 + e_, kb * 128:(kb + 1) * 128]
                q0, q1 = jl * 128, (jh + 1) * 128
                mo = MASK_OFF[(th, kb)]
                mk = maskb[:, mo:mo + (q1 - q0)]
                rk = [("kz", g * 2 + e_, kb // 4), "kz_init"]
            n = q1 - q0
            rk += [("qT", c, th), "maskb", "identb"]
            slot = gi % NPT
            if mk is None:
                mm_group(b, [(lt, qT[:, c, q0:q1])], r=rk, cols=(0, n))
                act(pt[:, slot, 0:n], PB(b)[:, 0:n], AF.Exp, [pskey(b), "ctxb"], [akey(("pt", slot))], scale=0.125,
                    bias=ctxb[:, 0:1])
            else:
                mm_group(b, [(lt, qT[:, c, q0:q1]), (identb[:], mk)], r=rk, cols=(0, n))
                act(pt[:, slot, 0:n], PB(b)[:, 0:n], AF.Exp, [pskey(b)], [akey(("pt", slot))], scale=0.125)

        def emit_PV(gi):
            h, th, ti, ntl, t_ = sched[gi]
            c, e_, g = h // 2, h % 2, h // 4
            par = it_idx[(h, th)] % 2
            ab = 4 + par
            slot = gi % NPT
            if t_[0] == "ctx":
                kc = t_[1]
                vv = cvaug[:, g * 2 + e_, kc, :]
                q0, q1 = 0, 512
                rk = [("cvaug", g * 2 + e_), "cvaug_init"]
            else:
                _, kb, jl, jh = t_
                vv = vaug[:, g * 2 + e_, kb, :]
                q0, q1 = jl * 128 - th * 512, (jh + 1) * 128 - th * 512
                rk = [("vaug", g * 2 + e_, kb // 4), "vaug_init"]
            n = q1 - q0
            first, last = (ti == 0), (ti == ntl - 1)
            pr = slice(e_ * 64, e_ * 64 + 64)
            po = slice((1 - e_) * 64, (1 - e_) * 64 + 64)
            ptv = pt[:, slot, 0:n]
            o_a = PB(ab)[:, q0:q1]

            def fn(e, vv=vv, ptv=ptv, o_a=o_a, first=first, last=last):
                return e.matmul(o_a, vv, ptv, start=first, stop=last)
            P.add("pe", fn, r=rk + [("pt", slot)], w=[pskey(ab)])
            if last:
                r_ = rd[pr, par, :]
                dve_ts(r_, PB(ab)[po, :], esink[pr, h:h + 1], ALU.add, [pskey(ab), "esink"], [akey(("rd", par))])
                P.add("dve", lambda e, r_=r_: e.reciprocal(out=r_, in_=r_), r=[("rd", par)], w=[("rd", par)])
                dve_tt(attnT[pr, c, th * 512:(th + 1) * 512], PB(ab)[pr, :], r_, ALU.mult,
                       [pskey(ab), ("rd", par)], [("attnT", c, th, e_)])

        NT = len(sched)
        modj = 4
        for gi in range(NT + LAG):
            if gi < NT:
                emit_S(gi)
                if sched[gi][2] == 0 and it_idx[(sched[gi][0], sched[gi][1])] % 2 == 1 and modj < 12:
                    mod_tile(l, modj)
                    modj += 1
            if gi - LAG >= 0:
                emit_PV(gi - LAG)
        assert modj == 12
        mod_finish_B(l)
        if phase_end("C%d" % l):
            break

        att_fence = list(arena_keys)
        del arena_keys[:]
        first_fence["v"] = []
        o = 0
        xl = AF32(o, [128, T]); o += 4096
        xlb = ABF(o, [128, T]); o += 2048
        Rb = AF32(o, [128, T]); o += 4096
        Ab = AF32(o, [128, T]); o += 4096
        Sb = AF32(o, [128, T]); o += 4096
        Ib = AF32(o, [128, T]); o += 4096
        H0 = AF32(o, [128, T]); o += 4096
        H1 = Rb
        lru_end = o
        cdg = ABF(o, [128, 2, 31, 128]); o += 2 * 31 * 256
        cvo = AF32(o, [128, 4, T]); o += 16384
        assert o <= SBYTES, o

        def cdg_build(c, fence=()):
            o0 = _VOFF["cw"] + c
            wv_ = vecs[:, l, o0:o0 + 4 * 31:4].unsqueeze(2).broadcast_to([128, 31, 128])
            iv_ = identb[:].unsqueeze(1).broadcast_to([128, 31, 128])
            dve_tt(cdg[:, c % 2, :, :], iv_, wv_, ALU.mult, ["identb", "vecs"],
                   list(fence) + [akey(("cdg", c % 2, k)) for k in range(31)], eng="pool")

        def conf_conv(c):
            uv = upad[:, c, :].rearrange("p (s t) -> p s t", s=NSEG)
            for th in range(2):
                b = next_bank()
                mm_group(b, [(cdg[:, c % 2, k, :], uv[:, 2 * th:2 * th + 2, k:k + SEG]) for k in range(31)],
                         r=[("upad", c, 0), ("upad", c, 1), ("upad_h", c, 0), ("upad_h", c, 1)] +
                           [("cdg", c % 2, k) for k in range(31)])
                tsl = slice(th * 512, (th + 1) * 512)
                dve_ts(cvo[:, c, tsl], PB(b), V(l, "cb", c, 1), ALU.add, [pskey(b), "vecs", ("cdg", 0, 0)],
                       [akey(("cvo", c, th))])
            if c + 2 < 4:
                cdg_build(c + 2)

        cdg_build(0, att_fence)
        cdg_build(1)
        cf32 = confT[:].bitcast(F32)
        A1 = cf32[:, 0:2, :].rearrange("p a b -> p (a b)")
        S1 = cf32[:, 2:4, :].rearrange("p a b -> p (a b)")
        sg_first = {"v": True}

        def sgsel(th):
            fx = list(att_fence) if sg_first["v"] else []
            sg_first["v"] = False
            return Sb[:, th * 512:(th + 1) * 512], ("Sg", th), fx + [("S", 0)]
        for cname in ("xb0", "xb1", "xb2", "xb3"):
            win_chunk(cname)
        for c in range(4):
            win_chunk("a%d" % c, sgsel)
            win_chunk("g%d" % c, sgsel)
            xv = xbpad[:, c, :].rearrange("p (s t) -> p s t", s=NSEG)
            cb = []
            for th in range(2):
                b = next_bank()
                cb.append(b)
                mm_group(b, [(lrudg[:, k * 4 + c, :], xv[:, 2 * th:2 * th + 2, k:k + SEG]) for k in range(4)],
                         r=[("xbpad", c, 0), ("xbpad", c, 1), ("xbpad_h", c, 0), ("xbpad_h", c, 1)] +
                           [("lrudg", k * 4 + c) for k in range(4)])
            for th in range(2):
                tsl = slice(th * 512, (th + 1) * 512)
                act(xl[:, tsl], PB(cb[th]), AF.Identity, [pskey(cb[th]), "vecs"],
                    (att_fence if (c == 0 and th == 0) else []) + [akey(("xl", th))], bias=V(l, "lcb", c, 1))
                xs_, xd_ = xl[:, tsl], xlb[:, tsl]
                P.add("dve", lambda e, xs_=xs_, xd_=xd_: e.tensor_copy(out=xd_, in_=xs_), r=[("xl", th)],
                      w=[akey(("xlb", th))])
            for d in range(2):
                Hd = H0 if d == 0 else H1
                hk = "H%d" % d
                gb = {}
                for gt in range(2):
                    for th in range(2):
                        b = next_bank()
                        gb[(gt, th)] = b
                        mm_group(b, [(lruw[:, (d * 2 + gt) * 4 + c, :], xlb[:, th * 512:(th + 1) * 512])],
                                 r=["lruw", ("xlb", th)])
                RK = [akey(("R", 0)), akey(("R", 1))]
                for th in range(2):
                    tsl = slice(th * 512, (th + 1) * 512)
                    act(Rb[:, tsl], PB(gb[(0, th)]), AF.Tanh, [pskey(gb[(0, th)]), ("der", 6)],
                        [("R", th)] + ([("H1", s_) for s_ in range(NSEG)] if th == 0 else []),
                        bias=der[:, 48 + d * 4 + c:49 + d * 4 + c], scale=0.5)
                    act(Ib[:, tsl], PB(gb[(1, th)]), AF.Tanh, [pskey(gb[(1, th)]), ("der", 6)], [akey(("I", th))],
                        bias=der[:, 56 + d * 4 + c:57 + d * 4 + c], scale=0.5)
                clh = der[:, 32 + d * 4 + c:33 + d * 4 + c]
                act(Rb, Rb, AF.Identity, RK + [("der", 4)], RK, bias=clh, scale=clh)
                Ab_, Sb_ = (Ab, Sb) if d == 0 else (A1, S1)
                kA, kS = akey(("A", d)), akey(("S", d))
                act(Ab_, Rb, AF.Exp, RK, [kA] + ([("confT", c_) for c_ in range(4)] if d == 1 else []))
                act(Sb_, Rb, AF.Exp, RK, [kS, ("Sg", 0), ("Sg", 1)], scale=2.0)
                act(Sb_, Sb_, AF.Sqrt, [kS], [kS], bias=0.25, scale=-0.25)
                IK = [("I", 0), ("I", 1)]
                dve_stt(Ib, Ib, 1.0, xl, ALU.add, ALU.mult, IK + [("xl", 0), ("xl", 1)], IK)
                dve_tt(Sb_, Sb_, Ib, ALU.mult, [kS] + IK, [kS])
                order = range(NSEG) if d == 0 else range(NSEG - 1, -1, -1)
                prev = None
                for s_ in order:
                    seg = slice(s_ * SEG, (s_ + 1) * SEG)
                    if prev is None:
                        init = h0t[:, l * 8 + d * 4 + c:l * 8 + d * 4 + c + 1]
                        ik = ["h0t"]
                    else:
                        col = (prev + 1) * SEG - 1 if d == 0 else prev * SEG
                        init = initb[:, s_:s_ + 1] if d == 0 else initb[:, 4 + s_:5 + s_]
                        dve_ts(init, Hd[:, col:col + 1], flag[:, 0:1], ALU.mult, [akey((hk, prev)), "flag"],
                               [("initb", d, s_)])
                        ik = [("initb", d, s_)]
                    if d == 0:
                        o_, a_, b_ = Hd[:, seg], Ab_[:, seg], Sb_[:, seg]
                    else:
                        lo, hi = s_ * SEG, (s_ + 1) * SEG
                        o_ = Hd[:, lo:hi][:, ::-1]
                        a_ = Ab_[:, lo:hi][:, ::-1]
                        b_ = Sb_[:, lo:hi][:, ::-1]
                    P.add("dve", lambda e, o_=o_, a_=a_, b_=b_, init=init: e.tensor_tensor_scan(
                        out=o_, data0=a_, data1=b_, initial=init, op0=ALU.mult, op1=ALU.add),
                        r=[kA, kS] + ik, w=[akey((hk, s_))] + (RK if d == 1 else []))
                    prev = s_
                hv = Hd.rearrange("p (s t) -> p s t", s=NSEG)
                colsel = SEG - 1 if d == 0 else 0
                fv = fin[:, :].rearrange("p (s x) -> p s x", s=NSEG)[:, :, d * 4 + c:d * 4 + c + 1]
                P.add("dve", lambda e, fv=fv, hv=hv, colsel=colsel: e.tensor_copy(out=fv, in_=hv[:, :, colsel:colsel + 1]),
                      r=[(hk, s_) for s_ in range(NSEG)], w=[("fin", d, c)])
            dve_tt(lruT[:, c, :], H0, H1, ALU.add, [("H0", s_) for s_ in range(4)] + [("H1", s_) for s_ in range(4)],
                   [("lruT", c)])
            conf_conv(c)
        fb = next_bank()
        P.add("pe", lambda e, fb=fb: e.transpose(out=PB(fb)[0:32, 0:128], in_=fin[:, :], identity=ident[:]),
              r=[("fin", d, c) for d in range(2) for c in range(4)] + ["ident"], w=[pskey(fb)])
        act(fint[:, :], PB(fb)[0:32, 0:128], AF.Copy, [pskey(fb)], ["fint"])
        dma("sp", nh_d[l], fint[:, :], ["fint"], [], "st_h")
        if phase_end("D%d" % l):
            tap_list.extend([("lruT", lruT[:], [128, 4, T], BF16)])
            break

        lru_fence = [k_ for k_ in arena_keys if not (isinstance(k_, tuple) and k_[0] in ("cvo", "cdg"))]
        first_fence["v"] = lru_fence
        o = 0
        mean = AF32(o, [128, T]); o += 4096
        rstd = AF32(o, [128, T]); o += 4096
        sqb = ABF(o, [128, 2, T]); o += 4096
        cvb = ABF(o, [128, 2, T]); o += 4096
        assert o <= lru_end
        ln_stats_and_norm = None

        def layer_norm(src_fn, nch, keys_fn, eps, tag):
            sb_ = [next_bank(), next_bank(), next_bank(), next_bank()]
            for c in range(nch):
                slot = c % 2
                fw_ = FW()
                cvs_ = cvb[:, slot, :]
                P.add("dve", lambda e, cvs_=cvs_, src_=src_fn(c): e.tensor_copy(out=cvs_, in_=src_),
                      r=keys_fn(c), w=fw_ + [akey(("cvb", slot))])
                act(sqb[:, slot, :], src_fn(c), AF.Square, keys_fn(c), fw_ + [akey(("sqb", slot))])
                for th in range(2):
                    tsl = slice(th * 512, (th + 1) * 512)
                    r1_ = cvb[:, slot, tsl]
                    r2_ = sqb[:, slot, tsl]

                    def fn(e, c=c, th=th, r1_=r1_, r2_=r2_, sb_=sb_):
                        e.matmul(PB(sb_[th]), onesb[:, :], r1_, start=(c == 0), stop=(c == nch - 1))
                        return e.matmul(PB(sb_[2 + th]), onesb[:, :], r2_, start=(c == 0), stop=(c == nch - 1))
                    P.add("pe", fn, r=[("cvb", slot), ("sqb", slot), "onesb"], w=[pskey(sb_[th]), pskey(sb_[2 + th])])
            inv = 1.0 / (nch * 128)
            for th in range(2):
                tsl = slice(th * 512, (th + 1) * 512)
                act(mean[:, tsl], PB(sb_[th]), AF.Copy, [pskey(sb_[th]), ("sqb", 0), ("sqb", 1)], [akey(("mean", th))],
                    scale=inv)
                act(rstd[:, tsl], PB(sb_[th]), AF.Square, [pskey(sb_[th])], [akey(("rstd", th))], scale=inv)
                dve_stt(rstd[:, tsl], PB(sb_[2 + th]), inv, rstd[:, tsl], ALU.mult, ALU.subtract,
                        [pskey(sb_[2 + th]), ("rstd", th)], [("rstd", th)])
                dve_ts(rstd[:, tsl], rstd[:, tsl], 0.0, ALU.max, [("rstd", th)], [("rstd", th)], s2=eps, op1=ALU.add)
                act(rstd[:, tsl], rstd[:, tsl], AF.Ln, [("rstd", th)], [("rstd", th)])
                act(rstd[:, tsl], rstd[:, tsl], AF.Exp, [("rstd", th)], [("rstd", th)], scale=-0.5)

        layer_norm(lambda c: cvo[:, c, :], 4, lambda c: [("cvo", c, 0), ("cvo", c, 1)], LN_EPS, "conf")
        MK = [("mean", 0), ("mean", 1)]
        RSK = [("rstd", 0), ("rstd", 1)]
        for c in range(4):
            ck_ = [("cvo", c, 0), ("cvo", c, 1)]
            dve_tt(cvo[:, c, :], cvo[:, c, :], mean, ALU.subtract, ck_ + MK, ck_)
            dve_tt(cvo[:, c, :], cvo[:, c, :], rstd, ALU.mult, ck_ + RSK, ck_)
            act(confT[:, c, :], cvo[:, c, :], AF.Silu, ck_ + ["vecs"], [("confT", c)], bias=V(l, "clb", c, 1),
                scale=V(l, "clg", c, 1))
        if phase_end("E%d" % l):
            tap_list.extend([("confT", confT[:], [128, 4, T], BF16), ("attnT", attnT[:], [128, 4, T], BF16)])
            break

        fence = list(arena_keys)
        del arena_keys[:]
        first_fence["v"] = fence
        o = 0
        mergeT = ABF(o, [128, 8, T]); o += 16384
        sgb = AF32(o, [128, 2, 3, 512]); o += 12288
        mt = AF32(o, [128, 2, 2, 512]); o += 8192
        it = 0
        for n in range(8):
            sg_, wkg = w_next(("mg", l, n))
            wg = slot_view(sg_, 24, 128)
            sb2, wkb = w_next(("br", l, n), la=NSLOT - 2)
            wb = slot_view(sb2, 12, 128)
            for th in range(2):
                tsl = slice(th * 512, (th + 1) * 512)
                par = it % 2
                it += 1
                G = []
                for b_ in range(3):
                    bk = next_bank()
                    G.append(bk)
                    mm_group(bk, [(wg[:, b_ * 8 + kc, :], hT[:, kc, tsl]) for kc in range(8)], r=wkg + hkeys(th))
                Pb = []
                for b_, (src, skf) in ((0, (attnT, lambda kc: [("attnT", kc, th, 0), ("attnT", kc, th, 1)])),
                                       (1, (lruT, lambda kc: [("lruT", kc)])),
                                       (2, (confT, lambda kc: [("confT", kc)]))):
                    bk = next_bank()
                    Pb.append(bk)
                    mm_group(bk, [(wb[:, 4 * b_ + kc, :], src[:, kc, tsl]) for kc in range(4)],
                             r=wkb + [k_ for kc in range(4) for k_ in skf(kc)])
                for b_ in range(3):
                    act(sgb[:, par, b_, :], PB(G[b_]), AF.Sigmoid, [pskey(G[b_]), "vecs"],
                        FW() + [akey(("sgb", par, b_))], bias=V(l, "bm", b_ * 8 + n, 1))
                dve_tt(mt[:, par, 0, :], PB(Pb[0]), sgb[:, par, 0, :], ALU.mult, [pskey(Pb[0]), ("sgb", par, 0)],
                       [akey(("mt", par, 0))])
                dve_tt(mt[:, par, 1, :], PB(Pb[1]), sgb[:, par, 1, :], ALU.mult, [pskey(Pb[1]), ("sgb", par, 1)],
                       [akey(("mt", par, 1))])
                dve_tt(mt[:, par, 0, :], mt[:, par, 0, :], mt[:, par, 1, :], ALU.add,
                       [("mt", par, 0), ("mt", par, 1)], [("mt", par, 0)])
                dve_tt(mt[:, par, 1, :], PB(Pb[2]), sgb[:, par, 2, :], ALU.mult, [pskey(Pb[2]), ("sgb", par, 2)],
                       [("mt", par, 1)])
                dve_tt(mergeT[:, n, tsl], mt[:, par, 0, :], mt[:, par, 1, :], ALU.add,
                       [("mt", par, 0), ("mt", par, 1)], [akey(("mergeT", n, th))])
            if l + 1 < DEPTH and n % 2 == 1:
                mod_tile(l + 1, n // 2)
        if l + 1 < DEPTH:
            mod_finish_A(l + 1)
        if phase_end("F%d" % l):
            tap_list.extend([("mergeT", mergeT, [128, 8, T], BF16)])
            break

        o = 16384 + 12288 + 8192
        mean = AF32(o, [128, T]); o += 4096
        rstd = AF32(o, [128, T]); o += 4096
        sqb = ABF(o, [128, 2, T]); o += 4096
        cvb = ABF(o, [128, 2, T]); o += 4096
        assert o <= SBYTES

        pending_mod = []

        def proj_residual(kind, src, nk, srckeys, gcol):
            for n in range(8):
                if kind == "out":
                    if n % 4 == 0:
                        s_, wk_ = w_next(("out", l, n // 4))
                        wv_ = slot_view(s_, 8, 512)
                    lts = [wv_[:, kc, (n % 4) * 128:(n % 4 + 1) * 128] for kc in range(nk)]
                else:
                    s_, wk_ = w_next(("dn", l, n))
                    wv_ = slot_view(s_, NFF, 128)
                    lts = [wv_[:, kc, :] for kc in range(nk)]
                    if l + 1 < DEPTH and n % 2 == 1:
                        pending_mod.append(n // 2)
                for th in range(2):
                    tsl = slice(th * 512, (th + 1) * 512)
                    bk = next_bank()
                    rh = (lambda kc: src(kc)[:, tsl]) if callable(src) else (lambda kc: src[:, kc, tsl])
                    mm_group(bk, [(lts[kc], rh(kc)) for kc in range(nk)], r=wk_ + srckeys(th))
                    dve_stt(xT[:, n, tsl], PB(bk), der[:, gcol + n:gcol + n + 1], xT[:, n, tsl], ALU.mult, ALU.add,
                            [pskey(bk), ("xT", n, th), ("der", 1), ("der", 3)], [("xT", n, th)])
                while pending_mod:
                    mod_tile(l + 1, pending_mod.pop(0))


        def ln_apply(gname, bname, nxt=False, h2=False):
            layer_norm(lambda c: xT[:, c, :], 8, lambda c: [("xT", c, 0), ("xT", c, 1)], EPS_F, gname)
            for c in range(8):
                xk = [("xT", c, 0), ("xT", c, 1)]
                dve_tt(xT[:, c, :], xT[:, c, :], mean, ALU.subtract, xk + MK, xk)
                dve_tt(xT[:, c, :], xT[:, c, :], rstd, ALU.mult, xk + RSK, xk)
                act(xT[:, c, :], xT[:, c, :], AF.Identity, xk + ["vecs"], xk, bias=V(l, bname, c, 1),
                    scale=V(l, gname, c, 1))
                if nxt:
                    act(hT[:, c, :], xT[:, c, :], AF.Identity, xk + [("der", 0), "modA"],
                        [("hT", c, 0), ("hT", c, 1)], bias=modT[:, c:c + 1], scale=der[:, c:c + 1])
                if h2:
                    act(hT[:, c, :], xT[:, c, :], AF.Identity, xk + [("der", 2), "modB"],
                        [("hT", c, 0), ("hT", c, 1)], bias=modT[:, 24 + c:25 + c], scale=der[:, 16 + c:17 + c])

        def out_ln1_thmajor():
            s0_, wk0_ = w_next(("out", l, 0))
            s1_, wk1_ = w_next(("out", l, 1), la=NSLOT - 2)
            wvs = [slot_view(s0_, 8, 512), slot_view(s1_, 8, 512)]
            wks = [wk0_, wk1_]
            inv = 1.0 / 1024.0
            for th in range(2):
                tsl = slice(th * 512, (th + 1) * 512)
                for n in range(8):
                    wv_ = wvs[n // 4]
                    bk = next_bank()
                    mm_group(bk, [(wv_[:, kc, (n % 4) * 128:(n % 4 + 1) * 128], mergeT[:, kc, tsl]) for kc in range(8)],
                             r=wks[n // 4] + [("mergeT", kc, th) for kc in range(8)])
                    dve_stt(xT[:, n, tsl], PB(bk), der[:, 8 + n:9 + n], xT[:, n, tsl], ALU.mult, ALU.add,
                            [pskey(bk), ("xT", n, th), ("der", 1), ("der", 3)], [("xT", n, th)])
                b1, b2 = next_bank(), next_bank()
                for c in range(8):
                    slot = c % 2
                    cv_ = cvb[:, slot, tsl]
                    sq_ = sqb[:, slot, tsl]
                    xs_ = xT[:, c, tsl]
                    fw_ = FW()
                    P.add("dve", lambda e, cv_=cv_, xs_=xs_: e.tensor_copy(out=cv_, in_=xs_), r=[("xT", c, th)],
                          w=fw_ + [akey(("cvb", slot, th))])
                    act(sq_, xs_, AF.Square, [("xT", c, th)], fw_ + [akey(("sqb", slot, th))])

                    def fn(e, c=c, cv_=cv_, sq_=sq_, b1=b1, b2=b2):
                        e.matmul(PB(b1), onesb[:, :], cv_, start=(c == 0), stop=(c == 7))
                        return e.matmul(PB(b2), onesb[:, :], sq_, start=(c == 0), stop=(c == 7))
                    P.add("pe", fn, r=[("cvb", slot, th), ("sqb", slot, th), "onesb"], w=[pskey(b1), pskey(b2)])
                mk_, rk_ = akey(("mean", th)), akey(("rstd", th))
                act(mean[:, tsl], PB(b1), AF.Copy, [pskey(b1)], [mk_], scale=inv)
                act(rstd[:, tsl], PB(b1), AF.Square, [pskey(b1)], [rk_], scale=inv)
                dve_stt(rstd[:, tsl], PB(b2), inv, rstd[:, tsl], ALU.mult, ALU.subtract, [pskey(b2), rk_], [rk_])
                dve_ts(rstd[:, tsl], rstd[:, tsl], 0.0, ALU.max, [rk_], [rk_], s2=EPS_F, op1=ALU.add)
                act(rstd[:, tsl], rstd[:, tsl], AF.Ln, [rk_], [rk_])
                act(rstd[:, tsl], rstd[:, tsl], AF.Exp, [rk_], [rk_], scale=-0.5)
                for c in range(8):
                    xk = [("xT", c, th)]
                    xc_ = xT[:, c, tsl]
                    dve_tt(xc_, xc_, mean[:, tsl], ALU.subtract, xk + [mk_], xk)
                    dve_tt(xc_, xc_, rstd[:, tsl], ALU.mult, xk + [rk_], xk)
                    act(xc_, xc_, AF.Identity, xk + ["vecs"], xk, bias=V(l, "l1b", c, 1), scale=V(l, "l1g", c, 1))
                    act(hT[:, c, tsl], xc_, AF.Identity, xk + [("der", 2), "modB"], [("hT", c, th)],
                        bias=modT[:, 24 + c:25 + c], scale=der[:, 16 + c:17 + c])

        out_ln1_thmajor()
        if phase_end("G%d" % l):
            break

        fence = list(arena_keys)
        del arena_keys[:]
        first_fence["v"] = fence
        o = 0
        gtail = ABF(o, [128, 2, T]); o += 4096

        def gTc(c):
            if c < 4:
                return lruT[:, c, :]
            if c < 8:
                return confT[:, c - 4, :]
            if c < 12:
                return attnT[:, c - 8, :]
            if c < 16:
                return xbpad2[:, (c - 12) * T:(c - 11) * T]
            if c < 20:
                return upad2[:, (c - 16) * T:(c - 15) * T]
            return gtail[:, c - 20, :]
        NUR = 2
        ur = ABF(o, [128, NUR, 2, NSEG * FPP]); o += NUR * 2 * NSEG * FPP * 2
        cen = AF32(o, [128, NUR, 4, 512]); o += NUR * 4 * 2048
        gl = ABF(o, [128, 2, 512]); o += 2048
        assert o <= SBYTES, o
        oldk = [("lruT", c_) for c_ in range(4)] + [("confT", c_) for c_ in range(4)] + \
               [("attnT", c_, th_, e__) for c_ in range(4) for th_ in range(2) for e__ in range(2)] + \
               [(nm_, c_, th_) for nm_ in ("xbpad", "xbpad_h", "upad", "upad_h") for c_ in range(4) for th_ in range(2)]
        P.add("pool", lambda e, ur=ur: e.memset(ur, 0.0), w=FW() + oldk + [akey("ur_init")])
        ffn_items = []
        for j in range(11):
            for pi in range(2):
                ffn_items.append((j, pi, j * 2 + pi))

        def ffn_up(item):
            j, pi, c = item
            if pi == 0:
                s_, wk_ = w_next(("up", l, j))
                ffn_up.cur = (slot_view(s_, 8, 512), wk_)
            wv_, wk_ = ffn_up.cur
            slot = c % NUR
            for th in range(2):
                for hv in range(2):
                    bk = (c % 2) * 4 + (hv * 2 + th)
                    co = hv * 256 + pi * 128
                    mm_group(bk, [(wv_[:, kc, co:co + 128], hT[:, kc, th * 512:(th + 1) * 512]) for kc in range(8)],
                             r=wk_ + hkeys(th))
                    src = PB(bk).rearrange("p (s t) -> p s t", s=2)
                    dst = ur[:, slot, hv, :].rearrange("p (s t) -> p s t", s=NSEG)[:, 2 * th:2 * th + 2, 1:1 + SEG]
                    P.add("act", lambda e, dst=dst, src=src: e.activation(out=dst, in_=src, func=AF.Copy),
                          r=[pskey(bk), "ur_init"], w=[akey(("ur", slot, hv, th))])
                    act(cen[:, slot, hv * 2 + th, :], PB(bk), AF.Identity, [pskey(bk), "vecs", "ur_init"],
                        [akey(("cen", slot, hv, th))], bias=V(l, "fcb", hv * 22 + c, 1),
                        scale=V(l, "fcw", 44 + hv * 22 + c, 1))

        def ffn_halo(item):
            j, pi, c = item
            slot = c % NUR
            for hv in range(2):
                uvv = ur[:, slot, hv, :].rearrange("p (s t) -> p s t", s=NSEG)
                rk_ = [("ur", slot, hv, 0), ("ur", slot, hv, 1), "flag"]
                dve_ts(uvv[:, 1:4, 0:1], uvv[:, 0:3, SEG:SEG + 1], flag[:, 0:1], ALU.mult, rk_,
                       [akey(("urh", slot, hv, 0))])
                dve_ts(uvv[:, 0:3, SEG + 1:SEG + 2], uvv[:, 1:4, 1:2], flag[:, 0:1], ALU.mult, rk_,
                       [akey(("urh", slot, hv, 1))])

        def ffn_conv(item):
            j, pi, c = item
            slot = c % NUR
            for k in (0, 2):
                for th in range(2):
                    for hv in range(2):
                        uvv = ur[:, slot, hv, :].rearrange("p (s t) -> p s t", s=NSEG)
                        acc = cen[:, slot, hv * 2 + th, :].rearrange("p (s t) -> p s t", s=2)
                        ck_ = ("cen", slot, hv, th)
                        rk_ = [("ur", slot, hv, 0), ("ur", slot, hv, 1), ("urh", slot, hv, 0), ("urh", slot, hv, 1),
                               ck_, "vecs"]
                        dve_stt(acc, uvv[:, 2 * th:2 * th + 2, k:k + SEG], V(l, "fcw", k * 44 + hv * 22 + c, 1), acc,
                                ALU.mult, ALU.add, rk_, [ck_])
            for th in range(2):
                act(gl[:, th, :], cen[:, slot, th, :], AF.Gelu_apprx_tanh, [("cen", slot, 0, th)], [akey(("gl", th))])
                dve_tt(gTc(c)[:, th * 512:(th + 1) * 512], cen[:, slot, 2 + th, :], gl[:, th, :], ALU.mult,
                       [("cen", slot, 1, th), ("gl", th), "ur_init"], [akey(("gT", c, th))])

        for i in range(len(ffn_items) + 1):
            if i < len(ffn_items):
                ffn_up(ffn_items[i])
            if i >= 1:
                ffn_conv(ffn_items[i - 1])
            if i < len(ffn_items):
                ffn_halo(ffn_items[i])
        rot["i"] = 0
        if phase_end("H%d" % l):
            tap_list.extend([("gtail", gtail, [128, 2, T], BF16), ("lruT", lruT[:], [128, 4, T], BF16)])
            break
        mean = AF32(o, [128, T]); o += 4096
        rstd = AF32(o, [128, T]); o += 4096
        sqb = ABF(o, [128, 2, T]); o += 4096
        cvb = ABF(o, [128, 2, T]); o += 4096
        assert o <= SBYTES, o
        def dn_ln2_thmajor():
            nxt_ = (l + 1 < DEPTH)
            inv = 1.0 / 1024.0
            for th in range(2):
                tsl = slice(th * 512, (th + 1) * 512)
                for n in range(8):
                    s_, wk_ = w_next((("dn" if th == 0 else "dn2"), l, n))
                    wv_ = slot_view(s_, NFF, 128)
                    bk = next_bank()
                    mm_group(bk, [(wv_[:, kc, :], gTc(kc)[:, tsl]) for kc in range(NFF)],
                             r=wk_ + [("gT", kc, th) for kc in range(NFF)])
                    dve_stt(xT[:, n, tsl], PB(bk), der[:, 24 + n:25 + n], xT[:, n, tsl], ALU.mult, ALU.add,
                            [pskey(bk), ("xT", n, th), ("der", 1), ("der", 3)], [("xT", n, th)])
                b1, b2 = next_bank(), next_bank()
                for c in range(8):
                    slot = c % 2
                    cv_ = cvb[:, slot, tsl]
                    sq_ = sqb[:, slot, tsl]
                    xs_ = xT[:, c, tsl]
                    fw_ = FW()
                    P.add("dve", lambda e, cv_=cv_, xs_=xs_: e.tensor_copy(out=cv_, in_=xs_), r=[("xT", c, th)],
                          w=fw_ + [akey(("cvb", slot, th))])
                    act(sq_, xs_, AF.Square, [("xT", c, th)], fw_ + [akey(("sqb", slot, th))])

                    def fn(e, c=c, cv_=cv_, sq_=sq_, b1=b1, b2=b2):
                        e.matmul(PB(b1), onesb[:, :], cv_, start=(c == 0), stop=(c == 7))
                        return e.matmul(PB(b2), onesb[:, :], sq_, start=(c == 0), stop=(c == 7))
                    P.add("pe", fn, r=[("cvb", slot, th), ("sqb", slot, th), "onesb"], w=[pskey(b1), pskey(b2)])
                mk_, rk_ = akey(("mean", th)), akey(("rstd", th))
                act(mean[:, tsl], PB(b1), AF.Copy, [pskey(b1)], [mk_], scale=inv)
                act(rstd[:, tsl], PB(b1), AF.Square, [pskey(b1)], [rk_], scale=inv)
                dve_stt(rstd[:, tsl], PB(b2), inv, rstd[:, tsl], ALU.mult, ALU.subtract, [pskey(b2), rk_], [rk_])
                dve_ts(rstd[:, tsl], rstd[:, tsl], 0.0, ALU.max, [rk_], [rk_], s2=EPS_F, op1=ALU.add)
                act(rstd[:, tsl], rstd[:, tsl], AF.Ln, [rk_], [rk_])
                act(rstd[:, tsl], rstd[:, tsl], AF.Exp, [rk_], [rk_], scale=-0.5)
                for c in range(8):
                    xk = [("xT", c, th)]
                    xc_ = xT[:, c, tsl]
                    dve_tt(xc_, xc_, mean[:, tsl], ALU.subtract, xk + [mk_], xk)
                    dve_tt(xc_, xc_, rstd[:, tsl], ALU.mult, xk + [rk_], xk)
                    act(xc_, xc_, AF.Identity, xk + ["vecs"], xk, bias=V(l, "l2b", c, 1), scale=V(l, "l2g", c, 1))
                    if nxt_:
                        act(hT[:, c, tsl], xc_, AF.Identity, xk + [("der", 0), "modA"], [("hT", c, th)],
                            bias=modT[:, c:c + 1], scale=der[:, c:c + 1])

        dn_ln2_thmajor()
        if phase_end("I%d" % l):
            break

    if not stopped["v"]:
        fence = list(arena_keys)
        del arena_keys[:]
        YT = [AF32(i * 4096, [128, 1024]) for i in range(8)]
        YALL = AF32(0, [128, 8, 1024])
        for c in range(8):
            for tbh in range(2):
                b = next_bank()

                def fn(e, c=c, tbh=tbh, b=b):
                    ins = None
                    for q in range(4):
                        tb = tbh * 4 + q
                        ins = e.transpose(out=PB(b)[:, q * 128:(q + 1) * 128], in_=xT[:, c, tb * 128:(tb + 1) * 128],
                                          identity=ident[:])
                    return ins
                P.add("pe", fn, r=[("xT", c, tbh), "ident"], w=[pskey(b)])
                dst = YALL[:, tbh * 4:(tbh + 1) * 4, c * 128:(c + 1) * 128]
                src = PB(b).rearrange("p (q t) -> p q t", q=4)
                wk = [("yt", tbh, c)] + fence
                if (c + tbh) % 2 == 0:
                    P.add("act", lambda e, dst=dst, src=src: e.activation(out=dst, in_=src, func=AF.Copy),
                          r=[pskey(b)], w=wk)
                else:
                    P.add("dve", lambda e, dst=dst, src=src: e.tensor_copy(out=dst, in_=src), r=[pskey(b)], w=wk)
        for tb in range(8):
            yk = [("yt", tb // 4, c) for c in range(8)]
            dma("sp", y_d[tb * 128:(tb + 1) * 128, :], YT[tb], yk, [], "st_y%d" % tb)
    tap_outs = []
    for ti, (name, ap, shape, dt) in enumerate(tap_list):
        dd = dout("tap_" + name, shape, dt)
        tap_outs.append("tap_" + name)
        P.add("sp", lambda e, dd=dd, ap=ap: e.dma_start(out=dd, in_=ap), r=list(P.kw.keys()), w=[], chan="tap%d" % ti)
    if stopped["v"]:
        dd = dout("tap_xT", [128, 8, T], F32)
        P.add("sp", lambda e, dd=dd: e.dma_start(out=dd, in_=xT[:]), r=list(P.kw.keys()), w=[], chan="tapx")
        dd2 = dout("tap_hT", [128, 8, T], BF16)
        P.add("sp", lambda e, dd2=dd2: e.dma_start(out=dd2, in_=hT[:]), r=list(P.kw.keys()), w=[], chan="taph")
        dd3 = dout("tap_modT", [128, 48], F32)
        P.add("sp", lambda e, dd3=dd3: e.dma_start(out=dd3, in_=modT[:]), r=list(P.kw.keys()), w=[], chan="tapm")
    last_dma = [P.chan_last[c] for c in P.chan_last]
    fin_op = Op()
    fin_op.eng = "sp"
    fin_op.fn = lambda e: e.nop()
    fin_op.chan = None
    fin_op.sig = False
    fin_op.gid = len(P.ops)
    fin_op.deps = set(last_dma)
    P.ops.append(fin_op)

    block = es.enter_context(nc.Block())
    P.emit(nc, es, block)
    es.close()
    return nc


def _fm(v):
    v = np.asarray(v, np.float32)
    return np.ascontiguousarray(v.reshape(-1, 128).T)


def _rope_tables(active):
    c = np.ones((128, T), np.float32)
    s = np.zeros((128, T), np.float32)
    if active:
        t = np.arange(T)
        row = (t // 64).astype(np.float32)
        colp = (t % 64).astype(np.float32)
        inv = (np.float32(10000.0) ** (-np.arange(16, dtype=np.float32) / np.float32(16))).astype(np.float32)
        for p in range(128):
            dd = p % 64
            a = dd // 32
            jj = dd % 32
            f = jj % 16
            pos = row if a == 0 else colp
            ang = (pos * inv[f]).astype(np.float32)
            c[p] = np.cos(ang)
            s[p] = -np.sin(ang) if jj < 16 else np.sin(ang)
    return c, s


def _mask_table(sample):
    m = np.zeros((128, MASK_COLS), np.float32)
    if not sample:
        m[:, 0:512] = NEGM
    k = np.arange(128)[:, None]
    for th in range(2):
        for (kb, jl, jh) in band_tiles(th):
            off = MASK_OFF[(th, kb)]
            for j in range(jl, jh + 1):
                q = np.arange(128)[None, :]
                if sample:
                    ok = np.abs((kb * 128 + k) - (j * 128 + q)) <= 128
                else:
                    ok = np.broadcast_to(np.array(kb // 2 == j // 2), (128, 128))
                m[:, off + (j - jl) * 128: off + (j - jl + 1) * 128] = np.where(ok, 0.0, NEGM)
    return m


def _partner(cols64):
    cols64 = np.asarray(cols64)
    idx = np.arange(64)
    j = idx % 32
    p = np.where(j < 16, idx + 16, idx - 16)
    return cols64[p]


def _prep_shared(inp):
    sh = {}
    w_in = np.asarray(inp["w_in"], np.float32)
    cols = []
    qc = lambda h: np.arange(h * 64, (h + 1) * 64)
    kc = lambda g: 512 + np.arange(g * 64, (g + 1) * 64)
    for name in WIN_CHUNKS:
        if name.startswith("qp"):
            c = int(name[2])
            cols.append(np.concatenate([_partner(qc(2 * c)), _partner(qc(2 * c + 1))]))
        elif name.startswith("q"):
            c = int(name[1])
            cols.append(np.concatenate([qc(2 * c), qc(2 * c + 1)]))
        elif name == "kA":
            cols.append(np.concatenate([kc(0), kc(1)]))
        elif name == "kB":
            cols.append(np.concatenate([kc(1), kc(0)]))
        elif name == "kAp":
            cols.append(np.concatenate([_partner(kc(0)), _partner(kc(1))]))
        elif name == "kBp":
            cols.append(np.concatenate([_partner(kc(1)), _partner(kc(0))]))
        elif name == "v":
            cols.append(640 + np.arange(128))
        elif name.startswith("xb"):
            c = int(name[2])
            cols.append(768 + c * 128 + np.arange(128))
        elif name.startswith("a"):
            c = int(name[1])
            cols.append(1280 + c * 128 + np.arange(128))
        elif name.startswith("g"):
            c = int(name[1])
            cols.append(1280 + 512 + c * 128 + np.arange(128))
    cols = np.concatenate(cols)
    sh["w_in_ext"] = np.ascontiguousarray(w_in[:, :, cols])
    vec = np.zeros((DEPTH, 128, NV), np.float32)
    f = lambda a: np.asarray(a, np.float32)
    for l in range(DEPTH):
        def put(name, arr):
            arr = np.asarray(arr, np.float32)
            vec[l, :, _VOFF[name]:_VOFF[name] + arr.shape[1]] = arr
        put("bmod", _fm(f(inp["b_mod"])[l]))
        put("lcw", np.concatenate([_fm(f(inp["lru_conv_w"])[l, k]) for k in range(4)], axis=1))
        put("lcb", _fm(f(inp["lru_conv_b"])[l]))
        put("lba", np.concatenate([_fm(f(inp["lru_ba"])[l, d]) for d in range(2)], axis=1))
        put("lbx", np.concatenate([_fm(f(inp["lru_bx"])[l, d]) for d in range(2)], axis=1))
        put("llam", np.concatenate([_fm(f(inp["lru_lambda"])[l, d]) for d in range(2)], axis=1))
        put("cw", np.concatenate([_fm(f(inp["conf_dw_w"])[l, k]) for k in range(31)], axis=1))
        put("cb", _fm(f(inp["conf_dw_b"])[l]))
        put("clg", _fm(f(inp["conf_ln_g"])[l]))
        put("clb", _fm(f(inp["conf_ln_b"])[l]))
        put("bm", _fm(f(inp["b_merge"])[l]))
        put("l1g", _fm(f(inp["ln1_g"])[l]))
        put("l1b", _fm(f(inp["ln1_b"])[l]))
        put("fcw", np.concatenate([_fm(f(inp["ffn_conv_w"])[l, k]) for k in range(3)], axis=1))
        put("fcb", _fm(f(inp["ffn_conv_b"])[l]))
        put("l2g", _fm(f(inp["ln2_g"])[l]))
        put("l2b", _fm(f(inp["ln2_b"])[l]))
        put("sink", np.broadcast_to(f(inp["attn_sink"])[l][None, :], (128, 8)))
    sh["vecs"] = vec
    lw = np.zeros((DEPTH, 128, 16, 128), np.float32)
    for l in range(DEPTH):
        for d in range(2):
            for gt, nm in enumerate(("lru_wa", "lru_wx")):
                wsrc = f(inp[nm])[l, d]
                for c in range(4):
                    i = (d * 2 + gt) * 4 + c
                    for bb in range(2):
                        lw[l, bb * 64:(bb + 1) * 64, i, bb * 64:(bb + 1) * 64] = wsrc[2 * c + bb]
    sh["lruw"] = lw
    sh["ident"] = np.eye(128, dtype=np.float32)
    for k in ("w_mod", "w_branch", "w_merge", "w_out", "ffn_w_up", "ffn_w_down"):
        sh[k] = np.ascontiguousarray(np.asarray(inp[k], np.float32))
    return sh


def make_in_maps(inp):
    sh = _prep_shared(inp)
    xs = np.asarray(inp["x_sample"], np.float32)
    xp = np.asarray(inp["x_prompt"], np.float32)
    ck = np.asarray(inp["cache_k"], np.float32)
    cv = np.asarray(inp["cache_v"], np.float32)
    st = np.asarray(inp["state_lru"], np.float32)
    cc = np.asarray(inp["c"], np.float32)
    cctx = np.asarray(inp["c_ctx"], np.float32)
    rc_s, rs_s = _rope_tables(True)
    rc_p, rs_p = _rope_tables(False)
    mk_s = _mask_table(True)
    mk_p = _mask_table(False)
    maps = []
    for core in range(8):
        m = dict(sh)
        if core < 4:
            b = core
            m["x"] = np.ascontiguousarray(xs[b])
            m["cond"] = _fm(cc[b])
            m["ck"] = np.ascontiguousarray(ck[b].reshape(DEPTH, 512, 128))
            m["cv"] = np.ascontiguousarray(cv[b].reshape(DEPTH, 512, 128))
            h0 = np.zeros((128, DEPTH * 8), np.float32)
            for l in range(DEPTH):
                for d in range(2):
                    h0[:, l * 8 + d * 4:l * 8 + d * 4 + 4] = _fm(st[b, l, d])
            m["h0"] = h0
            m["flag"] = np.ones((128, 1), np.float32)
            m["ropec"], m["ropes"], m["maskb"] = rc_s, rs_s, mk_s
        else:
            i = core - 4
            m["x"] = np.ascontiguousarray(xp[4 * i:4 * i + 4].reshape(T, D))
            m["cond"] = _fm(cctx)
            m["ck"] = np.zeros((DEPTH, 512, 128), np.float32)
            m["cv"] = np.zeros((DEPTH, 512, 128), np.float32)
            m["h0"] = np.zeros((128, DEPTH * 8), np.float32)
            m["flag"] = np.zeros((128, 1), np.float32)
            m["ropec"], m["ropes"], m["maskb"] = rc_p, rs_p, mk_p
        maps.append(m)
    return maps


_NC_CACHE = {}


def kernel(**inputs):
    if "nc" not in _NC_CACHE:
        _NC_CACHE["nc"] = build_program()
    nc = _NC_CACHE["nc"]
    maps = make_in_maps(inputs)
    res = run_bass_kernel_spmd(nc, maps, core_ids=list(range(8)))
    R = res.results
    y_prompt = np.zeros((16, 256, D), np.float32)
    y_sample = np.zeros((4, T, D), np.float32)
    nk = np.zeros((16, DEPTH, 256, 2, 64), np.float32)
    nv = np.zeros((16, DEPTH, 256, 2, 64), np.float32)
    nh = np.zeros((16, DEPTH, 2, 512), np.float32)
    for core in range(8):
        r = R[core]
        if core < 4:
            y_sample[core] = r["y"]
        else:
            i = core - 4
            y_prompt[4 * i:4 * i + 4] = r["y"].reshape(4, 256, D)
            for s in range(4):
                for l in range(DEPTH):
                    nk[4 * i + s, l] = r["nk"][l, s * 256:(s + 1) * 256].reshape(256, 2, 64)
                    nv[4 * i + s, l] = r["nv"][l, s * 256:(s + 1) * 256].reshape(256, 2, 64)
                    nh[4 * i + s, l] = r["nh"][l].reshape(4, 2, 512)[s]
    return (y_prompt, y_sample, nk, nv, nh)
```

```python
from contextlib import ExitStack
import numpy as np
import concourse.bass as bass
import concourse.mybir as mybir
from concourse.bass_utils import run_bass_kernel_spmd

F32 = mybir.dt.float32
BF16 = mybir.dt.bfloat16
AF = mybir.ActivationFunctionType
ALU = mybir.AluOpType

D = 1024
T = 1024
DEPTH = 2
NSEG = 4
SEG = 256
DFF = 2816
NFF = 22
ALPHA = (2 * DEPTH) ** 0.25
LN_EPS = 1e-5
EPS_F = LN_EPS / (ALPHA * ALPHA)
NEGM = -30000.0
NSLOT = 4
SLOT_ELEMS = 4096
XBP = 259
UPP = 286
FPP = 258

_VOFF = {}
_o = 0
for _n, _w in [("bmod", 48), ("lcw", 16), ("lcb", 4), ("lba", 8), ("lbx", 8), ("llam", 8), ("cw", 124),
               ("cb", 4), ("clg", 4), ("clb", 4), ("bm", 24), ("l1g", 8), ("l1b", 8), ("fcw", 132),
               ("fcb", 44), ("l2g", 8), ("l2b", 8), ("sink", 8)]:
    _VOFF[_n] = _o
    _o += _w
NV = _o

WIN_CHUNKS = ["q0", "qp0", "q1", "qp1", "q2", "qp2", "q3", "qp3", "kA", "kAp",
              "v", "xb0", "xb1", "xb2", "xb3", "a0", "g0", "a1", "g1", "a2", "g2", "a3", "g3"]
NWIN = len(WIN_CHUNKS) * 128
WIN_TILES = [(0, 512), (512, 512), (1024, 256), (1280, 128), (1408, 512), (1920, 512), (2432, 512)]


def band_tiles(th):
    out = []
    for kb in range(max(0, 4 * th - 1), min(8, 4 * th + 5)):
        jl = max(4 * th, kb - 1)
        jh = min(4 * th + 3, kb + 1)
        out.append((kb, jl, jh))
    return out


MASK_OFF = {}
_m = 512
for _th in range(2):
    for (_kb, _jl, _jh) in band_tiles(_th):
        MASK_OFF[(_th, _kb)] = _m
        _m += 128 * (_jh - _jl + 1)
MASK_COLS = _m


class Op:
    __slots__ = ("eng", "fn", "deps", "chan", "cord", "idx", "sig", "sigcount", "gid")


class Prog:
    ENGS = ("pe", "act", "dve", "pool", "sp")

    def __init__(self):
        self.ops = []
        self.kw = {}
        self.kr = {}
        self.chan_last = {}
        self.chan_count = {}

    def add(self, eng, fn, r=(), w=(), chan=None):
        op = Op()
        op.eng = eng
        op.fn = fn
        op.chan = chan
        op.sig = False
        op.gid = len(self.ops)
        deps = set()
        psr = [k for k in r if isinstance(k, tuple) and k and k[0] == "ps"]
        if psr:
            r = [k for k in r if k not in psr]
            w = list(w) + psr
        for k in r:
            p = self.kw.get(k)
            if p is not None:
                deps.add(p)
        for k in w:
            p = self.kw.get(k)
            if p is not None:
                deps.add(p)
            for q in self.kr.get(k, ()):
                deps.add(q)
        if chan is not None:
            p = self.chan_last.get(chan)
            if p is not None:
                deps.add(p)
            self.chan_last[chan] = op.gid
            self.chan_count[chan] = self.chan_count.get(chan, 0) + 1
            op.cord = self.chan_count[chan]
        for k in r:
            self.kr.setdefault(k, []).append(op.gid)
        for k in w:
            self.kw[k] = op.gid
            self.kr[k] = []
        op.deps = deps
        self.ops.append(op)
        return op.gid

    def emit(self, nc, es, block):
        ops = self.ops
        for op in ops:
            for d in op.deps:
                if ops[d].chan is None:
                    ops[d].sig = True
        cnt = {e: 0 for e in self.ENGS}
        for op in ops:
            if op.chan is None and op.sig:
                cnt[op.eng] += 1
                op.sigcount = cnt[op.eng]
        esem = {e: es.enter_context(nc.semaphore("s_" + e)) for e in ("pe", "act", "dve", "pool")}
        csem = {c: es.enter_context(nc.semaphore("c_" + c)) for c in self.chan_count}
        per_eng = {e: [op for op in ops if op.eng == e] for e in self.ENGS}

        def run(e, name):
            known = {}
            for op in per_eng[name]:
                waits = {}
                for d in op.deps:
                    p = ops[d]
                    if p.chan is not None:
                        key = ("c", p.chan)
                        val = 16 * p.cord
                    else:
                        if p.eng == "pe" and name == "pe":
                            continue
                        key = ("e", p.eng)
                        val = p.sigcount
                    if known.get(key, 0) >= val:
                        continue
                    if waits.get(key, 0) < val:
                        waits[key] = val
                for key, val in waits.items():
                    sem = csem[key[1]] if key[0] == "c" else esem[key[1]]
                    e.wait_ge(sem, val)
                    known[key] = val
                ins = op.fn(e)
                if op.chan is not None:
                    ins.then_inc(csem[op.chan], 16)
                elif op.sig:
                    ins.then_inc(esem[name], 1)

        @block.tensor
        def _(e):
            run(e, "pe")

        @block.scalar
        def _(e):
            run(e, "act")

        @block.vector
        def _(e):
            run(e, "dve")

        @block.gpsimd
        def _(e):
            run(e, "pool")

        @block.sync
        def _(e):
            run(e, "sp")


def build_program(stop_after=None, taps=()):
    nc = bass.Bass("TRN2", target_bir_lowering=False)
    es = ExitStack()
    P = Prog()

    def din(name, shape, dt=F32):
        return nc.dram_tensor(name, list(shape), dt, kind="ExternalInput").ap()

    def dout(name, shape, dt=F32):
        return nc.dram_tensor(name, list(shape), dt, kind="ExternalOutput").ap()

    def sb(name, shape, dt):
        return es.enter_context(nc.sbuf_tensor("sb_" + name, list(shape), dt))

    x_d = din("x", [T, D])
    cond_d = din("cond", [128, 8])
    ck_d = din("ck", [DEPTH, 512, 128])
    cv_d = din("cv", [DEPTH, 512, 128])
    h0_d = din("h0", [128, DEPTH * 8])
    flag_d = din("flag", [128, 1])
    ropec_d = din("ropec", [128, T])
    ropes_d = din("ropes", [128, T])
    maskb_d = din("maskb", [128, MASK_COLS])
    ident_d = din("ident", [128, 128])
    vecs_d = din("vecs", [DEPTH, 128, NV])
    lruw_d = din("lruw", [DEPTH, 128, 16, 128])
    wmod_d = din("w_mod", [DEPTH, D, 6 * D])
    win_d = din("w_in_ext", [DEPTH, D, NWIN])
    wbr_d = din("w_branch", [DEPTH, 3, 512, D])
    wmg_d = din("w_merge", [DEPTH, D, 3 * D])
    wout_d = din("w_out", [DEPTH, D, D])
    wup_d = din("ffn_w_up", [DEPTH, D, 2 * DFF])
    wdn_d = din("ffn_w_down", [DEPTH, DFF, D])

    y_d = dout("y", [T, D])
    nk_d = dout("nk", [DEPTH, T, 128])
    nv_d = dout("nv", [DEPTH, T, 128])
    nh_d = dout("nh", [DEPTH, 32, 128])

    xT = sb("xT", [128, 8, T], F32)
    hT = sb("hT", [128, 8, T], BF16)
    wsl = sb("wsl", [128, NSLOT, SLOT_ELEMS], BF16)
    ident = sb("ident", [128, 128], F32)
    identb = sb("identb", [128, 128], BF16)
    onesb = sb("onesb", [128, 128], BF16)
    vecs = sb("vecs", [128, DEPTH, NV], F32)
    condt = sb("condt", [128, 8], F32)
    scond = sb("scond", [128, 8], BF16)
    flag = sb("flag", [128, 1], F32)
    ctxb = sb("ctxb", [128, 1], F32)
    h0t = sb("h0t", [128, DEPTH * 8], F32)
    modT = sb("modT", [128, 48], F32)
    der = sb("der", [128, 64], F32)
    ropec = sb("ropec", [128, T], BF16)
    ropes = sb("ropes", [128, T], BF16)
    maskb = sb("maskb", [128, MASK_COLS], BF16)
    lruw = sb("lruw", [128, 16, 128], BF16)
    lrudg = sb("lrudg", [128, 16, 128], BF16)
    esink = sb("esink", [128, 8], F32)
    fin = sb("fin", [128, 32], F32)
    fint = sb("fint", [32, 128], F32)
    initb = sb("initb", [128, 8], F32)
    attnT = sb("attnT", [128, 4, T], BF16)
    lruT = sb("lruT", [128, 4, T], BF16)
    confT = sb("confT", [128, 4, T], BF16)
    xbpad2 = sb("xbpad", [128, 4 * NSEG * XBP], BF16)
    xbpad = xbpad2[:, :].rearrange("p (c x) -> p c x", c=4)
    upad2 = sb("upad", [128, 4 * NSEG * UPP], BF16)
    upad = upad2[:, :].rearrange("p (c x) -> p c x", c=4)
    SBYTES = 60 * 1024
    arena = sb("arena", [128, SBYTES // 4], F32)
    arena_b = arena[:].bitcast(BF16) if hasattr(arena[:], "bitcast") else None

    psum = es.enter_context(nc.psum_tensor("ps", [128, 8, 512], F32))

    def AF32(off_bytes, shape):
        n = int(np.prod(shape[1:]))
        o = off_bytes // 4
        ap = arena[0:shape[0], o:o + n]
        return ap if len(shape) == 2 else _reshape(ap, shape[1:])

    def ABF(off_bytes, shape):
        n = int(np.prod(shape[1:]))
        o = off_bytes // 2
        ap = arena_b[0:shape[0], o:o + n]
        return ap if len(shape) == 2 else _reshape(ap, shape[1:])

    def _reshape(ap, free):
        if len(free) == 2:
            return ap.rearrange("p (a b) -> p a b", a=free[0])
        if len(free) == 3:
            return ap.rearrange("p (a b c) -> p a b c", a=free[0], b=free[1])
        raise ValueError

    arena_keys = []

    def akey(name):
        arena_keys.append(name)
        return name

    wtiles = []

    def wt_add(kind, parts):
        wtiles.append((kind, parts))

    def slot_view(s, kc, ncols, np_=128):
        return wsl[0:np_, s, 0:kc * ncols].rearrange("p (k n) -> p k n", k=kc)

    def wt_mod(l, j):
        src = wmod_d[l, :, j * 512:(j + 1) * 512].rearrange("(k p) n -> p k n", p=128)
        wt_add(("mod", l, j), [(lambda s: slot_view(s, 8, 512), src)])

    for l in range(DEPTH):
        if l == 0:
            for j in range(4):
                wt_mod(0, j)
        def wt_win(j):
            c0, ncols = WIN_TILES[j]
            src = win_d[l, :, c0:c0 + ncols].rearrange("(k p) n -> p k n", p=128)
            wt_add(("win", l, j), [(lambda s, ncols=ncols: slot_view(s, 8, ncols), src)])
        for j in range(4):
            wt_win(j)
        for j in range(4, 12):
            wt_mod(l, j)
        for j in range(4, 7):
            wt_win(j)
        for n in range(8):
            parts = []
            for b in range(3):
                src = wmg_d[l, :, b * D + n * 128: b * D + (n + 1) * 128].rearrange("(k p) n -> p k n", p=128)
                parts.append((lambda s, b=b: slot_view(s, 24, 128)[:, b * 8:(b + 1) * 8, :], src))
            wt_add(("mg", l, n), parts)
            parts = []
            for b in range(3):
                src = wbr_d[l, b, :, n * 128:(n + 1) * 128].rearrange("(k p) n -> p k n", p=128)
                parts.append((lambda s, b=b: slot_view(s, 12, 128)[:, 4 * b:4 * b + 4, :], src))
            wt_add(("br", l, n), parts)
            if l + 1 < DEPTH and n % 2 == 1:
                wt_mod(l + 1, n // 2)
        for j in range(2):
            src = wout_d[l, :, j * 512:(j + 1) * 512].rearrange("(k p) n -> p k n", p=128)
            wt_add(("out", l, j), [(lambda s: slot_view(s, 8, 512), src)])
        for j in range(11):
            parts = []
            for hv in range(2):
                c0 = hv * DFF + j * 256
                src = wup_d[l, :, c0:c0 + 256].rearrange("(k p) n -> p k n", p=128)
                parts.append((lambda s, hv=hv: slot_view(s, 8, 512)[:, :, hv * 256:(hv + 1) * 256], src))
            wt_add(("up", l, j), parts)
        for n in range(8):
            src = wdn_d[l, :, n * 128:(n + 1) * 128].rearrange("(k p) n -> p k n", p=128)
            wt_add(("dn", l, n), [(lambda s: slot_view(s, NFF, 128), src)])
        for n in range(8):
            src = wdn_d[l, :, n * 128:(n + 1) * 128].rearrange("(k p) n -> p k n", p=128)
            wt_add(("dn2", l, n), [(lambda s: slot_view(s, NFF, 128), src)])

    wstate = {"issued": 0, "next": 0}

    def w_issue_upto(j):
        while wstate["issued"] <= min(j, len(wtiles) - 1):
            i = wstate["issued"]
            kind, parts = wtiles[i]
            s = i % NSLOT
            for pi, (dst_fn, src) in enumerate(parts):
                dst = dst_fn(s)
                P.add("pool", (lambda e, dst=dst, src=src: e.dma_start(out=dst, in_=src)),
                      w=[("w", s, pi)],
                      chan="w%d_%d" % (s, pi))
            wstate["issued"] += 1

    def w_next(kind, la=NSLOT - 1):
        j = wstate["next"]
        assert wtiles[j][0] == kind, (wtiles[j][0], kind)
        w_issue_upto(j + la)
        wstate["next"] += 1
        s = j % NSLOT
        return s, [("w", s, q) for q in range(3)]

    def PB(b):
        return psum[:, b, :]

    pskey = lambda b: ("ps", b)
    rot = {"i": 0}

    MODB = 7

    def next_bank():
        b = rot["i"] % 7
        rot["i"] += 1
        return b

    def act(out, in_, func, r, w, bias=None, scale=None):
        kw = {}
        if bias is not None:
            kw["bias"] = bias
        if scale is not None:
            kw["scale"] = scale
        return P.add("act", lambda e: e.activation(out=out, in_=in_, func=func, **kw), r=r, w=w)

    def dve_tt(out, in0, in1, op, r, w, eng="dve"):
        return P.add(eng, lambda e: e.tensor_tensor(out=out, in0=in0, in1=in1, op=op), r=r, w=w)

    def dve_ts(out, in0, s1, op0, r, w, s2=None, op1=None, eng="dve"):
        if op1 is None:
            return P.add(eng, lambda e: e.tensor_scalar(out=out, in0=in0, scalar1=s1, scalar2=0.0, op0=op0, op1=ALU.add),
                         r=r, w=w)
        return P.add(eng, lambda e: e.tensor_scalar(out=out, in0=in0, scalar1=s1, scalar2=s2, op0=op0, op1=op1), r=r, w=w)

    def dve_stt(out, in0, scalar, in1, op0, op1, r, w, eng="dve"):
        return P.add(eng, lambda e: e.scalar_tensor_tensor(out=out, in0=in0, scalar=scalar, in1=in1, op0=op0, op1=op1),
                     r=r, w=w)

    def dma(q, out, in_, r, w, chan):
        return P.add(q, lambda e: e.dma_start(out=out, in_=in_), r=r, w=w, chan=chan)

    def mm_group(bank, steps, r, cols=None, extra_w=()):
        outap = PB(bank) if cols is None else PB(bank)[:, cols[0]:cols[1]]

        def fn(e):
            ins = None
            n = len(steps)
            for i, (lt, rh) in enumerate(steps):
                ins = e.matmul(outap, lt, rh, start=(i == 0), stop=(i == n - 1))
            return ins
        return P.add("pe", fn, r=r, w=[pskey(bank)] + list(extra_w))

    stopped = {"v": False}
    tap_list = []

    def phase_end(name):
        if stop_after == name:
            stopped["v"] = True
        return stopped["v"]

    dma("sp", ident[:], ident_d, [], ["ident"], "ld0")
    dma("sp", vecs[:], vecs_d.rearrange("l p n -> p l n"), [], ["vecs"], "ld1")
    dma("sp", condt[:], cond_d, [], ["condt"], "ld2")
    dma("sp", flag[:], flag_d, [], ["flag"], "ld3")
    dma("sp", h0t[:], h0_d, [], ["h0t"], "ld0")

    P.add("dve", lambda e: e.memset(onesb[:], 1.0), w=["onesb"])
    act(identb[:], ident[:], AF.Copy, ["ident"], ["identb"])
    P.add("dve", lambda e: e.tensor_scalar(out=ctxb[:], in0=flag[:], scalar1=-1.0, scalar2=-NEGM, op0=ALU.add,
                                           op1=ALU.mult), r=["flag"], w=["ctxb"])
    act(scond[:], condt[:], AF.Silu, ["condt"], ["scond"])

    XIN = [AF32(i * 4096, [128, 1024]) for i in range(8)]
    for tb in range(8):
        dma("sp", XIN[tb], x_d[tb * 128:(tb + 1) * 128, :], [], [akey(("xin", tb))], "xin%d" % tb)
    for tb in range(8):
        st = XIN[tb]
        k = ("xin", tb)
        for half in range(2):
            b = next_bank()
            def fn(e, st=st, half=half, b=b):
                ins = None
                for q in range(4):
                    c = half * 4 + q
                    ins = e.transpose(out=PB(b)[:, q * 128:(q + 1) * 128], in_=st[:, c * 128:(c + 1) * 128],
                                      identity=ident[:])
                return ins
            P.add("pe", fn, r=[k, "ident"], w=[pskey(b)])
            src = PB(b).rearrange("p (q t) -> p q t", q=4)
            dst = xT[:, half * 4:(half + 1) * 4, tb * 128:(tb + 1) * 128]
            eng = "act" if half == 0 else "dve"
            if eng == "act":
                P.add("act", lambda e, dst=dst, src=src: e.activation(out=dst, in_=src, func=AF.Copy),
                      r=[pskey(b)], w=[("xT", c_, tb // 4) for c_ in range(half * 4, half * 4 + 4)])
            else:
                P.add("dve", lambda e, dst=dst, src=src: e.tensor_copy(out=dst, in_=src),
                      r=[pskey(b)], w=[("xT", c_, tb // 4) for c_ in range(half * 4, half * 4 + 4)])

    def V(l, name, col=0, n=1):
        o = _VOFF[name] + col
        return vecs[:, l, o:o + n]

    for l in range(DEPTH):
        if stopped["v"]:
            break
        def mod_tile(ll, j):
            s, wk = w_next(("mod", ll, j))
            wv = slot_view(s, 8, 512)

            def fn(e, wv=wv, j=j):
                ins = None
                for n4 in range(4):
                    col = j * 4 + n4
                    for kc in range(8):
                        ins = e.matmul(PB(MODB)[:, col:col + 1], wv[:, kc, n4 * 128:(n4 + 1) * 128],
                                       scond[:, kc:kc + 1], start=(kc == 0), stop=(kc == 7))
                return ins
            P.add("pe", fn, r=wk + ["scond"], w=[pskey(MODB)])

        def mod_finish_A(ll):
            dve_tt(modT[:, 0:16], PB(MODB)[:, 0:16], V(ll, "bmod", 0, 16), ALU.add, [pskey(MODB), "vecs"], ["modA"])
            dve_ts(der[:, 0:8], modT[:, 8:16], 1.0, ALU.add, ["modA"], [("der", 0)])

        def mod_finish_B(ll):
            dve_tt(modT[:, 16:48], PB(MODB)[:, 16:48], V(ll, "bmod", 16, 32), ALU.add, [pskey(MODB), "vecs"], ["modB"])
            dve_ts(der[:, 8:16], modT[:, 16:24], 1.0 / ALPHA, ALU.mult, ["modB"], [("der", 1)])
            dve_ts(der[:, 16:24], modT[:, 32:40], 1.0, ALU.add, ["modB"], [("der", 2)])
            dve_ts(der[:, 24:32], modT[:, 40:48], 1.0 / ALPHA, ALU.mult, ["modB"], [("der", 3)])

        if l == 0:
            w_issue_upto(NSLOT - 1)
            for j in range(4):
                mod_tile(0, j)
            mod_finish_A(0)
            dma("pool", ropec[:], ropec_d, [], ["ropec"], "ldr0")
            dma("pool", ropes[:], ropes_d, [], ["ropes"], "ldr1")
            dma("pool", maskb[:], maskb_d, [], ["maskb"], "ldm2")
        dma("pool", lruw[:], lruw_d[l], [], ["lruw"], "ldm")
        act(der[:, 40:48], V(l, "llam", 0, 8), AF.Exp, ["vecs"], [("der", 5)], scale=-1.0)
        dve_ts(der[:, 40:48], der[:, 40:48], 1.0, ALU.add, [("der", 5)], [("der", 5)])
        act(der[:, 32:40], der[:, 40:48], AF.Ln, [("der", 5)], [("der", 4)])
        dve_ts(der[:, 32:40], der[:, 32:40], -4.0, ALU.mult, [("der", 4)], [("der", 4)])
        dve_ts(der[:, 48:64], V(l, "lba", 0, 16), 0.5, ALU.mult, ["vecs"], [("der", 6)])
        act(esink[:], V(l, "sink", 0, 8), AF.Exp, ["vecs"], ["esink"])
        for k in range(4):
            for c in range(4):
                i = k * 4 + c
                dve_ts(lrudg[:, i, :], identb[:], V(l, "lcw", i, 1), ALU.mult, ["identb", "vecs"], [("lrudg", i)],
                       eng="pool")

        for c in range(8 if l == 0 else 0):
            if c % 2 == 0:
                act(hT[:, c, :], xT[:, c, :], AF.Identity, [("xT", c, 0), ("xT", c, 1), ("der", 0), "modA"],
                    [("hT", c, 0), ("hT", c, 1)], bias=modT[:, c:c + 1], scale=der[:, c:c + 1])
            else:
                dve_ts(hT[:, c, :], xT[:, c, :], der[:, c:c + 1], ALU.mult,
                       [("xT", c, 0), ("xT", c, 1), ("der", 0), "modA"], [("hT", c, 0), ("hT", c, 1)],
                       s2=modT[:, c:c + 1], op1=ALU.add)
        gtk = lambda lo, hi: [("gT", c_, th_) for c_ in range(lo, hi) for th_ in range(2)]
        P.add("dve", lambda e: e.memset(xbpad2[:, :], 0.0), r=[("hT", 7, 1)], w=["xbpad_init"] + gtk(12, 16))
        P.add("dve", lambda e: e.memset(upad2[:, :], 0.0), r=[("hT", 7, 1)], w=["upad_init"] + gtk(16, 20))
        if phase_end("A%d" % l):
            break

        fence = list(arena_keys)
        del arena_keys[:]
        o = 0
        qT = ABF(o, [128, 4, T]); o += 8192
        kz = ABF(o, [128, 4, T]); o += 8192
        ktok = AF32(o, [128, 8, 128]); o += 4096
        vtok = AF32(o, [128, 8, 128]); o += 4096
        vaug = ABF(o, [128, 4, 8, 128]); o += 8192
        ckf = AF32(o, [128, 4, 128]); o += 2048
        cksf = AF32(o, [128, 4, 128]); o += 2048
        ckz = ABF(o, [128, 4, 512]); o += 4096
        cvaug = ABF(o, [128, 4, 4, 128]); o += 4096
        NPT = 4
        vraw = AF32(o, [128, T])
        pt = ABF(o, [128, NPT, 512]); o += NPT * 1024
        kraw = AF32(o, [128, T])
        rd = AF32(o, [128, 2, 512]); o += 4096
        rt1 = AF32(o, [128, 2, 512]); o += 4096
        rt2 = AF32(o, [128, 2, 512]); o += 4096
        assert o <= SBYTES, o
        first_fence = {"v": fence}

        def FW():
            f = first_fence["v"]
            first_fence["v"] = []
            return f

        P.add("dve", lambda e, kz=kz: e.memset(kz, 0.0), w=FW() + [akey("kz_init")])
        P.add("dve", lambda e, ckz=ckz: e.memset(ckz, 0.0), r=["kz_init"], w=[akey("ckz_init")])
        P.add("dve", lambda e, vaug=vaug: e.memset(vaug, 1.0), r=["kz_init"], w=[akey("vaug_init")])
        P.add("dve", lambda e, cvaug=cvaug: e.memset(cvaug, 1.0), r=["kz_init"], w=[akey("cvaug_init")])
        dma("sp", ckf, ck_d[l].rearrange("(k p) n -> p k n", p=128), ["kz_init"], [akey("ckf")], "ldc0")
        for g_ in range(2):
            for e__ in range(2):
                dma("pool", cvaug[:, g_ * 2 + e__, :, e__ * 64:(e__ + 1) * 64],
                    cv_d[l][:, g_ * 64:(g_ + 1) * 64].rearrange("(k p) n -> p k n", p=128), ["cvaug_init"],
                    [akey(("cvaug", g_ * 2 + e__))], "ldv%d" % (g_ * 2 + e__))
        hkeys = lambda th: [("hT", c, th) for c in range(8)]
        pend = {}
        wcur = {"j": -1}

        def win_chunk(cname, sgsel=None, ths=(0, 1)):
            ci = WIN_CHUNKS.index(cname)
            col = ci * 128
            j = [i for i, (c0, nc_) in enumerate(WIN_TILES) if c0 <= col < c0 + nc_][0]
            off = col - WIN_TILES[j][0]
            if j != wcur["j"]:
                s_cur, wk_c = w_next(("win", l, j))
                wcur["j"] = j
                wcur["wk"] = wk_c
                wcur["wv"] = slot_view(s_cur, 8, WIN_TILES[j][1])
            wk_cur, wv_cur = wcur["wk"], wcur["wv"]
            banks = {}
            for th in ths:
                b = next_bank()
                banks[th] = b
                mm_group(b, [(wv_cur[:, kc, off:off + 128], hT[:, kc, th * 512:(th + 1) * 512]) for kc in range(8)],
                         r=wk_cur + hkeys(th))
            pend.setdefault(cname, {}).update(banks)
            assert tuple(ths) == (0, 1) or cname[0] == "q"
            if cname.startswith("qp") or cname in ("kAp", "kBp"):
                base = cname.replace("p", "")
                for th in ths:
                    b0 = pend[base][th]
                    b1 = banks[th]
                    tsl = slice(th * 512, (th + 1) * 512)
                    t1 = rt1[:, th, :]
                    t2 = rt2[:, th, :]
                    dve_tt(t1, PB(b0), ropec[:, tsl], ALU.mult, [pskey(b0), "ropec"], [akey(("rt1", th))])
                    dve_tt(t2, PB(b1), ropes[:, tsl], ALU.mult, [pskey(b1), "ropes"], [akey(("rt2", th))])
                    if base.startswith("q"):
                        c = int(base[1])
                        dve_tt(qT[:, c, tsl], t1, t2, ALU.add, [("rt1", th), ("rt2", th)], [akey(("qT", c, th))])
                    else:
                        lo_idx, hi_idx = (0, 3) if base == "kA" else (2, 1)
                        dve_tt(kz[0:64, lo_idx, tsl], t1[0:64, :], t2[0:64, :], ALU.add,
                               [("rt1", th), ("rt2", th), "kz_init"], [akey(("kz", lo_idx, th))])
                        dve_tt(kz[64:128, hi_idx, tsl], t1[64:128, :], t2[64:128, :], ALU.add,
                               [("rt1", th), ("rt2", th), "kz_init"], [akey(("kz", hi_idx, th))])
                    if base == "kA":
                        act(kraw[:, tsl], PB(b0), AF.Copy, [pskey(b0), "kz_init"], [akey(("kraw", th))])
                        for (do_, di_, so_, si_) in ((slice(64, 128), 1, slice(0, 64), 0), (slice(0, 64), 2, slice(64, 128), 3)):
                            dst_ = kz[do_, di_, tsl]
                            src_ = kz[so_, si_, tsl]
                            P.add("dve", lambda e, dst_=dst_, src_=src_: e.tensor_copy(out=dst_, in_=src_),
                                  r=[("kz", si_, th), "kz_init"], w=[akey(("kz", di_, th))])
            elif cname == "v":
                for th in range(2):
                    tsl = slice(th * 512, (th + 1) * 512)
                    act(vraw[:, tsl], PB(banks[th]), AF.Copy, [pskey(banks[th]), "kz_init"], [akey(("vraw", th))])
                for (raw, rkey, tok, tkey) in ((kraw, "kraw", ktok, "ktok"), (vraw, "vraw", vtok, "vtok")):
                    for th in range(2):
                        b = next_bank()

                        def fn(e, raw=raw, th=th, b=b):
                            ins = None
                            for q in range(4):
                                blk = th * 4 + q
                                ins = e.transpose(out=PB(b)[:, q * 128:(q + 1) * 128],
                                                  in_=raw[:, blk * 128:(blk + 1) * 128], identity=ident[:])
                            return ins
                        P.add("pe", fn, r=[(rkey, th), "ident"], w=[pskey(b)])
                        src = PB(b).rearrange("p (q t) -> p q t", q=4)
                        dst = tok[:, th * 4:(th + 1) * 4, :]
                        P.add("dve", lambda e, dst=dst, src=src: e.tensor_copy(out=dst, in_=src),
                              r=[pskey(b), "kz_init"], w=[akey((tkey, th))])
                        if tkey == "vtok":
                            for g_ in range(2):
                                for e__ in range(2):
                                    dstb = vaug[:, g_ * 2 + e__, th * 4:(th + 1) * 4, e__ * 64:(e__ + 1) * 64]
                                    srcb = src[:, :, g_ * 64:(g_ + 1) * 64]
                                    if e__ == 0:
                                        P.add("act", lambda e, dstb=dstb, srcb=srcb: e.activation(
                                            out=dstb, in_=srcb, func=AF.Copy),
                                            r=[pskey(b), "vaug_init"], w=[akey(("vaug", g_ * 2 + e__, th))])
                                    else:
                                        P.add("dve", lambda e, dstb=dstb, srcb=srcb: e.tensor_copy(out=dstb, in_=srcb),
                                              r=[pskey(b), "vaug_init"], w=[akey(("vaug", g_ * 2 + e__, th))])
                dma("sp", nk_d[l].rearrange("(b p) n -> p b n", p=128), ktok, [("ktok", 0), ("ktok", 1)], [], "st_k")
                dma("sp", nv_d[l].rearrange("(b p) n -> p b n", p=128), vtok, [("vtok", 0), ("vtok", 1)], [], "st_v")
            elif cname.startswith("xb"):
                c = int(cname[2])
                for th in range(2):
                    src = PB(banks[th]).rearrange("p (s t) -> p s t", s=2)
                    dst = xbpad[:, c, :].rearrange("p (s t) -> p s t", s=NSEG)[:, 2 * th:2 * th + 2, 2:2 + SEG]
                    P.add("act", lambda e, dst=dst, src=src: e.activation(out=dst, in_=src, func=AF.Copy),
                          r=[pskey(banks[th]), "xbpad_init"], w=[("xbpad", c, th)])
                xv = xbpad[:, c, :].rearrange("p (s t) -> p s t", s=NSEG)
                dve_ts(xv[:, 1:4, 0:2], xv[:, 0:3, SEG:SEG + 2], flag[:, 0:1], ALU.mult,
                       [("xbpad", c, 0), ("xbpad", c, 1), "flag"], [("xbpad_h", c, 0)])
                dve_ts(xv[:, 0:3, SEG + 2:SEG + 3], xv[:, 1:4, 2:3], flag[:, 0:1], ALU.mult,
                       [("xbpad", c, 0), ("xbpad", c, 1), "flag"], [("xbpad_h", c, 1)])
            elif cname.startswith("g"):
                c = int(cname[1])
                ab = pend["a%d" % c]
                for th in range(2):
                    sg, sgk, sgx = sgsel(th)
                    act(sg, PB(banks[th]), AF.Sigmoid, [pskey(banks[th])], sgx + [akey(sgk)])
                    src = PB(ab[th]).rearrange("p (s t) -> p s t", s=2)
                    dst = upad[:, c, :].rearrange("p (s t) -> p s t", s=NSEG)[:, 2 * th:2 * th + 2, 15:15 + SEG]
                    sgv = sg.rearrange("p (s t) -> p s t", s=2)
                    P.add("dve", lambda e, dst=dst, src=src, sgv=sgv: e.tensor_tensor(out=dst, in0=src, in1=sgv,
                                                                                       op=ALU.mult),
                          r=[pskey(ab[th]), sgk, "upad_init"], w=[("upad", c, th)])
                uv = upad[:, c, :].rearrange("p (s t) -> p s t", s=NSEG)
                dve_ts(uv[:, 1:4, 0:15], uv[:, 0:3, SEG:SEG + 15], flag[:, 0:1], ALU.mult,
                       [("upad", c, 0), ("upad", c, 1), "flag"], [("upad_h", c, 0)])
                dve_ts(uv[:, 0:3, SEG + 15:SEG + 30], uv[:, 1:4, 15:30], flag[:, 0:1], ALU.mult,
                       [("upad", c, 0), ("upad", c, 1), "flag"], [("upad_h", c, 1)])
        for tile_chunks in (WIN_CHUNKS[0:4], WIN_CHUNKS[4:8]):
            for th in range(2):
                for cname in tile_chunks:
                    win_chunk(cname, ths=(th,))
        for cname in WIN_CHUNKS[8:11]:
            win_chunk(cname)
        b = next_bank()

        def fn(e, srcb=ckf, b=b):
            ins = None
            for kc in range(4):
                ins = e.transpose(out=PB(b)[:, kc * 128:(kc + 1) * 128], in_=srcb[:, kc, :], identity=ident[:])
            return ins
        P.add("pe", fn, r=["ckf", "ident"], w=[pskey(b)])
        for (do_, di_, so_) in ((slice(0, 64), 0, slice(0, 64)), (slice(64, 128), 3, slice(64, 128)),
                                (slice(64, 128), 1, slice(0, 64)), (slice(0, 64), 2, slice(64, 128))):
            act(ckz[do_, di_, :], PB(b)[so_, :], AF.Copy, [pskey(b), "ckz_init"], [akey(("ckz", di_))])

        if phase_end("B%d" % l):
            tap_list.extend([("qT", qT, [128, 4, T], BF16), ("kz", kz, [128, 4, T], BF16),
                             ("ckz", ckz, [128, 4, 512], BF16)])
            break

        sched = []
        for h in range(8):
            for th in range(2):
                tl = [("ctx", kc) for kc in range(4)] + [("band",) + bt for bt in band_tiles(th)]
                for ti, t_ in enumerate(tl):
                    sched.append((h, th, ti, len(tl), t_))
        LAG = 3
        st_bank = {}
        it_idx = {}
        for i, (h, th) in enumerate([(h, th) for h in range(8) for th in range(2)]):
            it_idx[(h, th)] = i

        def emit_S(gi):
            h, th, ti, ntl, t_ = sched[gi]
            c, e_, g = h // 2, h % 2, h // 4
            b = gi % 4
            st_bank[gi] = b
            if t_[0] == "ctx":
                kc = t_[1]
                lt = ckz[:, g * 2 + e_, kc * 128:(kc + 1) * 128]
                q0, q1 = th * 512, (th + 1) * 512
                mk = None
                rk = [("ckz", g * 2 + e_), "ckz_init"]
            else:
                _, kb, jl, jh = t_
                lt = kz[:, g * 2 + e_, kb * 128:(kb + 1) * 128]
                q0, q1 = jl * 128, (jh + 1) * 128
                mo = MASK_OFF[(th, kb)]
                mk = maskb[:, mo:mo + (q1 - q0)]
                rk = [("kz", g * 2 + e_, kb // 4), "kz_init"]
            n = q1 - q0
            rk += [("qT", c, th), "maskb", "identb"]
            slot = gi % NPT
            if mk is None:
                mm_group(b, [(lt, qT[:, c, q0:q1])], r=rk, cols=(0, n))
                act(pt[:, slot, 0:n], PB(b)[:, 0:n], AF.Exp, [pskey(b), "ctxb"], [akey(("pt", slot))], scale=0.125,
                    bias=ctxb[:, 0:1])
            else:
                mm_group(b, [(lt, qT[:, c, q0:q1]), (identb[:], mk)], r=rk, cols=(0, n))
                act(pt[:, slot, 0:n], PB(b)[:, 0:n], AF.Exp, [pskey(b)], [akey(("pt", slot))], scale=0.125)

        def emit_PV(gi):
            h, th, ti, ntl, t_ = sched[gi]
            c, e_, g = h // 2, h % 2, h // 4
            par = it_idx[(h, th)] % 2
            ab = 4 + par
            slot = gi % NPT
            if t_[0] == "ctx":
                kc = t_[1]
                vv = cvaug[:, g * 2 + e_, kc, :]
                q0, q1 = 0, 512
                rk = [("cvaug", g * 2 + e_), "cvaug_init"]
            else:
                _, kb, jl, jh = t_
                vv = vaug[:, g * 2 + e_, kb, :]
                q0, q1 = jl * 128 - th * 512, (jh + 1) * 128 - th * 512
                rk = [("vaug", g * 2 + e_, kb // 4), "vaug_init"]
            n = q1 - q0
            first, last = (ti == 0), (ti == ntl - 1)
            pr = slice(e_ * 64, e_ * 64 + 64)
            po = slice((1 - e_) * 64, (1 - e_) * 64 + 64)
            ptv = pt[:, slot, 0:n]
            o_a = PB(ab)[:, q0:q1]

            def fn(e, vv=vv, ptv=ptv, o_a=o_a, first=first, last=last):
                return e.matmul(o_a, vv, ptv, start=first, stop=last)
            P.add("pe", fn, r=rk + [("pt", slot)], w=[pskey(ab)])
            if last:
                r_ = rd[pr, par, :]
                dve_ts(r_, PB(ab)[po, :], esink[pr, h:h + 1], ALU.add, [pskey(ab), "esink"], [akey(("rd", par))])
                P.add("dve", lambda e, r_=r_: e.reciprocal(out=r_, in_=r_), r=[("rd", par)], w=[("rd", par)])
                dve_tt(attnT[pr, c, th * 512:(th + 1) * 512], PB(ab)[pr, :], r_, ALU.mult,
                       [pskey(ab), ("rd", par)], [("attnT", c, th, e_)])

        NT = len(sched)
        modj = 4
        for gi in range(NT + LAG):
            if gi < NT:
                emit_S(gi)
                if sched[gi][2] == 0 and it_idx[(sched[gi][0], sched[gi][1])] % 2 == 1 and modj < 12:
                    mod_tile(l, modj)
                    modj += 1
            if gi - LAG >= 0:
                emit_PV(gi - LAG)
        assert modj == 12
        mod_finish_B(l)
        if phase_end("C%d" % l):
            break

        att_fence = list(arena_keys)
        del arena_keys[:]
        first_fence["v"] = []
        o = 0
        xl = AF32(o, [128, T]); o += 4096
        xlb = ABF(o, [128, T]); o += 2048
        Rb = AF32(o, [128, T]); o += 4096
        Ab = AF32(o, [128, T]); o += 4096
        Sb = AF32(o, [128, T]); o += 4096
        Ib = AF32(o, [128, T]); o += 4096
        H0 = AF32(o, [128, T]); o += 4096
        H1 = Rb
        lru_end = o
        cdg = ABF(o, [128, 2, 31, 128]); o += 2 * 31 * 256
        cvo = AF32(o, [128, 4, T]); o += 16384
        assert o <= SBYTES, o

        def cdg_build(c, fence=()):
            o0 = _VOFF["cw"] + c
            wv_ = vecs[:, l, o0:o0 + 4 * 31:4].unsqueeze(2).broadcast_to([128, 31, 128])
            iv_ = identb[:].unsqueeze(1).broadcast_to([128, 31, 128])
            dve_tt(cdg[:, c % 2, :, :], iv_, wv_, ALU.mult, ["identb", "vecs"],
                   list(fence) + [akey(("cdg", c % 2, k)) for k in range(31)], eng="pool")

        def conf_conv(c):
            uv = upad[:, c, :].rearrange("p (s t) -> p s t", s=NSEG)
            for th in range(2):
                b = next_bank()
                mm_group(b, [(cdg[:, c % 2, k, :], uv[:, 2 * th:2 * th + 2, k:k + SEG]) for k in range(31)],
                         r=[("upad", c, 0), ("upad", c, 1), ("upad_h", c, 0), ("upad_h", c, 1)] +
                           [("cdg", c % 2, k) for k in range(31)])
                tsl = slice(th * 512, (th + 1) * 512)
                dve_ts(cvo[:, c, tsl], PB(b), V(l, "cb", c, 1), ALU.add, [pskey(b), "vecs", ("cdg", 0, 0)],
                       [akey(("cvo", c, th))])
            if c + 2 < 4:
                cdg_build(c + 2)

        cdg_build(0, att_fence)
        cdg_build(1)
        cf32 = confT[:].bitcast(F32)
        A1 = cf32[:, 0:2, :].rearrange("p a b -> p (a b)")
        S1 = cf32[:, 2:4, :].rearrange("p a b -> p (a b)")
        sg_first = {"v": True}

        def sgsel(th):
            fx = list(att_fence) if sg_first["v"] else []
            sg_first["v"] = False
            return Sb[:, th * 512:(th + 1) * 512], ("Sg", th), fx + [("S", 0)]
        for cname in ("xb0", "xb1", "xb2", "xb3"):
            win_chunk(cname)
        for c in range(4):
            win_chunk("a%d" % c, sgsel)
            win_chunk("g%d" % c, sgsel)
            xv = xbpad[:, c, :].rearrange("p (s t) -> p s t", s=NSEG)
            cb = []
            for th in range(2):
                b = next_bank()
                cb.append(b)
                mm_group(b, [(lrudg[:, k * 4 + c, :], xv[:, 2 * th:2 * th + 2, k:k + SEG]) for k in range(4)],
                         r=[("xbpad", c, 0), ("xbpad", c, 1), ("xbpad_h", c, 0), ("xbpad_h", c, 1)] +
                           [("lrudg", k * 4 + c) for k in range(4)])
            for th in range(2):
                tsl = slice(th * 512, (th + 1) * 512)
                act(xl[:, tsl], PB(cb[th]), AF.Identity, [pskey(cb[th]), "vecs"],
                    (att_fence if (c == 0 and th == 0) else []) + [akey(("xl", th))], bias=V(l, "lcb", c, 1))
                xs_, xd_ = xl[:, tsl], xlb[:, tsl]
                P.add("dve", lambda e, xs_=xs_, xd_=xd_: e.tensor_copy(out=xd_, in_=xs_), r=[("xl", th)],
                      w=[akey(("xlb", th))])
            for d in range(2):
                Hd = H0 if d == 0 else H1
                hk = "H%d" % d
                gb = {}
                for gt in range(2):
                    for th in range(2):
                        b = next_bank()
                        gb[(gt, th)] = b
                        mm_group(b, [(lruw[:, (d * 2 + gt) * 4 + c, :], xlb[:, th * 512:(th + 1) * 512])],
                                 r=["lruw", ("xlb", th)])
                RK = [akey(("R", 0)), akey(("R", 1))]
                for th in range(2):
                    tsl = slice(th * 512, (th + 1) * 512)
                    act(Rb[:, tsl], PB(gb[(0, th)]), AF.Tanh, [pskey(gb[(0, th)]), ("der", 6)],
                        [("R", th)] + ([("H1", s_) for s_ in range(NSEG)] if th == 0 else []),
                        bias=der[:, 48 + d * 4 + c:49 + d * 4 + c], scale=0.5)
                    act(Ib[:, tsl], PB(gb[(1, th)]), AF.Tanh, [pskey(gb[(1, th)]), ("der", 6)], [akey(("I", th))],
                        bias=der[:, 56 + d * 4 + c:57 + d * 4 + c], scale=0.5)
                clh = der[:, 32 + d * 4 + c:33 + d * 4 + c]
                act(Rb, Rb, AF.Identity, RK + [("der", 4)], RK, bias=clh, scale=clh)
                Ab_, Sb_ = (Ab, Sb) if d == 0 else (A1, S1)
                kA, kS = akey(("A", d)), akey(("S", d))
                act(Ab_, Rb, AF.Exp, RK, [kA] + ([("confT", c_) for c_ in range(4)] if d == 1 else []))
                act(Sb_, Rb, AF.Exp, RK, [kS, ("Sg", 0), ("Sg", 1)], scale=2.0)
                act(Sb_, Sb_, AF.Sqrt, [kS], [kS], bias=0.25, scale=-0.25)
                IK = [("I", 0), ("I", 1)]
                dve_stt(Ib, Ib, 1.0, xl, ALU.add, ALU.mult, IK + [("xl", 0), ("xl", 1)], IK)
                dve_tt(Sb_, Sb_, Ib, ALU.mult, [kS] + IK, [kS])
                order = range(NSEG) if d == 0 else range(NSEG - 1, -1, -1)
                prev = None
                for s_ in order:
                    seg = slice(s_ * SEG, (s_ + 1) * SEG)
                    if prev is None:
                        init = h0t[:, l * 8 + d * 4 + c:l * 8 + d * 4 + c + 1]
                        ik = ["h0t"]
                    else:
                        col = (prev + 1) * SEG - 1 if d == 0 else prev * SEG
                        init = initb[:, s_:s_ + 1] if d == 0 else initb[:, 4 + s_:5 + s_]
                        dve_ts(init, Hd[:, col:col + 1], flag[:, 0:1], ALU.mult, [akey((hk, prev)), "flag"],
                               [("initb", d, s_)])
                        ik = [("initb", d, s_)]
                    if d == 0:
                        o_, a_, b_ = Hd[:, seg], Ab_[:, seg], Sb_[:, seg]
                    else:
                        lo, hi = s_ * SEG, (s_ + 1) * SEG
                        o_ = Hd[:, lo:hi][:, ::-1]
                        a_ = Ab_[:, lo:hi][:, ::-1]
                        b_ = Sb_[:, lo:hi][:, ::-1]
                    P.add("dve", lambda e, o_=o_, a_=a_, b_=b_, init=init: e.tensor_tensor_scan(
                        out=o_, data0=a_, data1=b_, initial=init, op0=ALU.mult, op1=ALU.add),
                        r=[kA, kS] + ik, w=[akey((hk, s_))] + (RK if d == 1 else []))
                    prev = s_
                hv = Hd.rearrange("p (s t) -> p s t", s=NSEG)
                colsel = SEG - 1 if d == 0 else 0
                fv = fin[:, :].rearrange("p (s x) -> p s x", s=NSEG)[:, :, d * 4 + c:d * 4 + c + 1]
                P.add("dve", lambda e, fv=fv, hv=hv, colsel=colsel: e.tensor_copy(out=fv, in_=hv[:, :, colsel:colsel + 1]),
                      r=[(hk, s_) for s_ in range(NSEG)], w=[("fin", d, c)])
            dve_tt(lruT[:, c, :], H0, H1, ALU.add, [("H0", s_) for s_ in range(4)] + [("H1", s_) for s_ in range(4)],
                   [("lruT", c)])
            conf_conv(c)
        fb = next_bank()
        P.add("pe", lambda e, fb=fb: e.transpose(out=PB(fb)[0:32, 0:128], in_=fin[:, :], identity=ident[:]),
              r=[("fin", d, c) for d in range(2) for c in range(4)] + ["ident"], w=[pskey(fb)])
        act(fint[:, :], PB(fb)[0:32, 0:128], AF.Copy, [pskey(fb)], ["fint"])
        dma("sp", nh_d[l], fint[:, :], ["fint"], [], "st_h")
        if phase_end("D%d" % l):
            tap_list.extend([("lruT", lruT[:], [128, 4, T], BF16)])
            break

        lru_fence = [k_ for k_ in arena_keys if not (isinstance(k_, tuple) and k_[0] in ("cvo", "cdg"))]
        first_fence["v"] = lru_fence
        o = 0
        mean = AF32(o, [128, T]); o += 4096
        rstd = AF32(o, [128, T]); o += 4096
        sqb = ABF(o, [128, 2, T]); o += 4096
        cvb = ABF(o, [128, 2, T]); o += 4096
        assert o <= lru_end
        ln_stats_and_norm = None

        def layer_norm(src_fn, nch, keys_fn, eps, tag):
            sb_ = [next_bank(), next_bank(), next_bank(), next_bank()]
            for c in range(nch):
                slot = c % 2
                fw_ = FW()
                cvs_ = cvb[:, slot, :]
                P.add("dve", lambda e, cvs_=cvs_, src_=src_fn(c): e.tensor_copy(out=cvs_, in_=src_),
                      r=keys_fn(c), w=fw_ + [akey(("cvb", slot))])
                act(sqb[:, slot, :], src_fn(c), AF.Square, keys_fn(c), fw_ + [akey(("sqb", slot))])
                for th in range(2):
                    tsl = slice(th * 512, (th + 1) * 512)
                    r1_ = cvb[:, slot, tsl]
                    r2_ = sqb[:, slot, tsl]

                    def fn(e, c=c, th=th, r1_=r1_, r2_=r2_, sb_=sb_):
                        e.matmul(PB(sb_[th]), onesb[:, :], r1_, start=(c == 0), stop=(c == nch - 1))
                        return e.matmul(PB(sb_[2 + th]), onesb[:, :], r2_, start=(c == 0), stop=(c == nch - 1))
                    P.add("pe", fn, r=[("cvb", slot), ("sqb", slot), "onesb"], w=[pskey(sb_[th]), pskey(sb_[2 + th])])
            inv = 1.0 / (nch * 128)
            for th in range(2):
                tsl = slice(th * 512, (th + 1) * 512)
                act(mean[:, tsl], PB(sb_[th]), AF.Copy, [pskey(sb_[th]), ("sqb", 0), ("sqb", 1)], [akey(("mean", th))],
                    scale=inv)
                act(rstd[:, tsl], PB(sb_[th]), AF.Square, [pskey(sb_[th])], [akey(("rstd", th))], scale=inv)
                dve_stt(rstd[:, tsl], PB(sb_[2 + th]), inv, rstd[:, tsl], ALU.mult, ALU.subtract,
                        [pskey(sb_[2 + th]), ("rstd", th)], [("rstd", th)])
                dve_ts(rstd[:, tsl], rstd[:, tsl], 0.0, ALU.max, [("rstd", th)], [("rstd", th)], s2=eps, op1=ALU.add)
                act(rstd[:, tsl], rstd[:, tsl], AF.Ln, [("rstd", th)], [("rstd", th)])
                act(rstd[:, tsl], rstd[:, tsl], AF.Exp, [("rstd", th)], [("rstd", th)], scale=-0.5)

        layer_norm(lambda c: cvo[:, c, :], 4, lambda c: [("cvo", c, 0), ("cvo", c, 1)], LN_EPS, "conf")
        MK = [("mean", 0), ("mean", 1)]
        RSK = [("rstd", 0), ("rstd", 1)]
        for c in range(4):
            ck_ = [("cvo", c, 0), ("cvo", c, 1)]
            dve_tt(cvo[:, c, :], cvo[:, c, :], mean, ALU.subtract, ck_ + MK, ck_)
            dve_tt(cvo[:, c, :], cvo[:, c, :], rstd, ALU.mult, ck_ + RSK, ck_)
            act(confT[:, c, :], cvo[:, c, :], AF.Silu, ck_ + ["vecs"], [("confT", c)], bias=V(l, "clb", c, 1),
                scale=V(l, "clg", c, 1))
        if phase_end("E%d" % l):
            tap_list.extend([("confT", confT[:], [128, 4, T], BF16), ("attnT", attnT[:], [128, 4, T], BF16)])
            break

        fence = list(arena_keys)
        del arena_keys[:]
        first_fence["v"] = fence
        o = 0
        mergeT = ABF(o, [128, 8, T]); o += 16384
        sgb = AF32(o, [128, 2, 3, 512]); o += 12288
        mt = AF32(o, [128, 2, 2, 512]); o += 8192
        it = 0
        for n in range(8):
            sg_, wkg = w_next(("mg", l, n))
            wg = slot_view(sg_, 24, 128)
            sb2, wkb = w_next(("br", l, n), la=NSLOT - 2)
            wb = slot_view(sb2, 12, 128)
            for th in range(2):
                tsl = slice(th * 512, (th + 1) * 512)
                par = it % 2
                it += 1
                G = []
                for b_ in range(3):
                    bk = next_bank()
                    G.append(bk)
                    mm_group(bk, [(wg[:, b_ * 8 + kc, :], hT[:, kc, tsl]) for kc in range(8)], r=wkg + hkeys(th))
                Pb = []
                for b_, (src, skf) in ((0, (attnT, lambda kc: [("attnT", kc, th, 0), ("attnT", kc, th, 1)])),
                                       (1, (lruT, lambda kc: [("lruT", kc)])),
                                       (2, (confT, lambda kc: [("confT", kc)]))):
                    bk = next_bank()
                    Pb.append(bk)
                    mm_group(bk, [(wb[:, 4 * b_ + kc, :], src[:, kc, tsl]) for kc in range(4)],
                             r=wkb + [k_ for kc in range(4) for k_ in skf(kc)])
                for b_ in range(3):
                    act(sgb[:, par, b_, :], PB(G[b_]), AF.Sigmoid, [pskey(G[b_]), "vecs"],
                        FW() + [akey(("sgb", par, b_))], bias=V(l, "bm", b_ * 8 + n, 1))
                dve_tt(mt[:, par, 0, :], PB(Pb[0]), sgb[:, par, 0, :], ALU.mult, [pskey(Pb[0]), ("sgb", par, 0)],
                       [akey(("mt", par, 0))])
                dve_tt(mt[:, par, 1, :], PB(Pb[1]), sgb[:, par, 1, :], ALU.mult, [pskey(Pb[1]), ("sgb", par, 1)],
                       [akey(("mt", par, 1))])
                dve_tt(mt[:, par, 0, :], mt[:, par, 0, :], mt[:, par, 1, :], ALU.add,
                       [("mt", par, 0), ("mt", par, 1)], [("mt", par, 0)])
                dve_tt(mt[:, par, 1, :], PB(Pb[2]), sgb[:, par, 2, :], ALU.mult, [pskey(Pb[2]), ("sgb", par, 2)],
                       [("mt", par, 1)])
                dve_tt(mergeT[:, n, tsl], mt[:, par, 0, :], mt[:, par, 1, :], ALU.add,
                       [("mt", par, 0), ("mt", par, 1)], [akey(("mergeT", n, th))])
            if l + 1 < DEPTH and n % 2 == 1:
                mod_tile(l + 1, n // 2)
        if l + 1 < DEPTH:
            mod_finish_A(l + 1)
        if phase_end("F%d" % l):
            tap_list.extend([("mergeT", mergeT, [128, 8, T], BF16)])
            break

        o = 16384 + 12288 + 8192
        mean = AF32(o, [128, T]); o += 4096
        rstd = AF32(o, [128, T]); o += 4096
        sqb = ABF(o, [128, 2, T]); o += 4096
        cvb = ABF(o, [128, 2, T]); o += 4096
        assert o <= SBYTES

        pending_mod = []

        def proj_residual(kind, src, nk, srckeys, gcol):
            for n in range(8):
                if kind == "out":
                    if n % 4 == 0:
                        s_, wk_ = w_next(("out", l, n // 4))
                        wv_ = slot_view(s_, 8, 512)
                    lts = [wv_[:, kc, (n % 4) * 128:(n % 4 + 1) * 128] for kc in range(nk)]
                else:
                    s_, wk_ = w_next(("dn", l, n))
                    wv_ = slot_view(s_, NFF, 128)
                    lts = [wv_[:, kc, :] for kc in range(nk)]
                    if l + 1 < DEPTH and n % 2 == 1:
                        pending_mod.append(n // 2)
                for th in range(2):
                    tsl = slice(th * 512, (th + 1) * 512)
                    bk = next_bank()
                    rh = (lambda kc: src(kc)[:, tsl]) if callable(src) else (lambda kc: src[:, kc, tsl])
                    mm_group(bk, [(lts[kc], rh(kc)) for kc in range(nk)], r=wk_ + srckeys(th))
                    dve_stt(xT[:, n, tsl], PB(bk), der[:, gcol + n:gcol + n + 1], xT[:, n, tsl], ALU.mult, ALU.add,
                            [pskey(bk), ("xT", n, th), ("der", 1), ("der", 3)], [("xT", n, th)])
                while pending_mod:
                    mod_tile(l + 1, pending_mod.pop(0))


        def ln_apply(gname, bname, nxt=False, h2=False):
            layer_norm(lambda c: xT[:, c, :], 8, lambda c: [("xT", c, 0), ("xT", c, 1)], EPS_F, gname)
            for c in range(8):
                xk = [("xT", c, 0), ("xT", c, 1)]
                dve_tt(xT[:, c, :], xT[:, c, :], mean, ALU.subtract, xk + MK, xk)
                dve_tt(xT[:, c, :], xT[:, c, :], rstd, ALU.mult, xk + RSK, xk)
                act(xT[:, c, :], xT[:, c, :], AF.Identity, xk + ["vecs"], xk, bias=V(l, bname, c, 1),
                    scale=V(l, gname, c, 1))
                if nxt:
                    act(hT[:, c, :], xT[:, c, :], AF.Identity, xk + [("der", 0), "modA"],
                        [("hT", c, 0), ("hT", c, 1)], bias=modT[:, c:c + 1], scale=der[:, c:c + 1])
                if h2:
                    act(hT[:, c, :], xT[:, c, :], AF.Identity, xk + [("der", 2), "modB"],
                        [("hT", c, 0), ("hT", c, 1)], bias=modT[:, 24 + c:25 + c], scale=der[:, 16 + c:17 + c])

        def out_ln1_thmajor():
            s0_, wk0_ = w_next(("out", l, 0))
            s1_, wk1_ = w_next(("out", l, 1), la=NSLOT - 2)
            wvs = [slot_view(s0_, 8, 512), slot_view(s1_, 8, 512)]
            wks = [wk0_, wk1_]
            inv = 1.0 / 1024.0
            for th in range(2):
                tsl = slice(th * 512, (th + 1) * 512)
                for n in range(8):
                    wv_ = wvs[n // 4]
                    bk = next_bank()
                    mm_group(bk, [(wv_[:, kc, (n % 4) * 128:(n % 4 + 1) * 128], mergeT[:, kc, tsl]) for kc in range(8)],
                             r=wks[n // 4] + [("mergeT", kc, th) for kc in range(8)])
                    dve_stt(xT[:, n, tsl], PB(bk), der[:, 8 + n:9 + n], xT[:, n, tsl], ALU.mult, ALU.add,
                            [pskey(bk), ("xT", n, th), ("der", 1), ("der", 3)], [("xT", n, th)])
                b1, b2 = next_bank(), next_bank()
                for c in range(8):
                    slot = c % 2
                    cv_ = cvb[:, slot, tsl]
                    sq_ = sqb[:, slot, tsl]
                    xs_ = xT[:, c, tsl]
                    fw_ = FW()
                    P.add("dve", lambda e, cv_=cv_, xs_=xs_: e.tensor_copy(out=cv_, in_=xs_), r=[("xT", c, th)],
                          w=fw_ + [akey(("cvb", slot, th))])
                    act(sq_, xs_, AF.Square, [("xT", c, th)], fw_ + [akey(("sqb", slot, th))])

                    def fn(e, c=c, cv_=cv_, sq_=sq_, b1=b1, b2=b2):
                        e.matmul(PB(b1), onesb[:, :], cv_, start=(c == 0), stop=(c == 7))
                        return e.matmul(PB(b2), onesb[:, :], sq_, start=(c == 0), stop=(c == 7))
                    P.add("pe", fn, r=[("cvb", slot, th), ("sqb", slot, th), "onesb"], w=[pskey(b1), pskey(b2)])
                mk_, rk_ = akey(("mean", th)), akey(("rstd", th))
                act(mean[:, tsl], PB(b1), AF.Copy, [pskey(b1)], [mk_], scale=inv)
                act(rstd[:, tsl], PB(b1), AF.Square, [pskey(b1)], [rk_], scale=inv)
                dve_stt(rstd[:, tsl], PB(b2), inv, rstd[:, tsl], ALU.mult, ALU.subtract, [pskey(b2), rk_], [rk_])
                dve_ts(rstd[:, tsl], rstd[:, tsl], 0.0, ALU.max, [rk_], [rk_], s2=EPS_F, op1=ALU.add)
                act(rstd[:, tsl], rstd[:, tsl], AF.Ln, [rk_], [rk_])
                act(rstd[:, tsl], rstd[:, tsl], AF.Exp, [rk_], [rk_], scale=-0.5)
                for c in range(8):
                    xk = [("xT", c, th)]
                    xc_ = xT[:, c, tsl]
                    dve_tt(xc_, xc_, mean[:, tsl], ALU.subtract, xk + [mk_], xk)
                    dve_tt(xc_, xc_, rstd[:, tsl], ALU.mult, xk + [rk_], xk)
                    act(xc_, xc_, AF.Identity, xk + ["vecs"], xk, bias=V(l, "l1b", c, 1), scale=V(l, "l1g", c, 1))
                    act(hT[:, c, tsl], xc_, AF.Identity, xk + [("der", 2), "modB"], [("hT", c, th)],
                        bias=modT[:, 24 + c:25 + c], scale=der[:, 16 + c:17 + c])

        out_ln1_thmajor()
        if phase_end("G%d" % l):
            break

        fence = list(arena_keys)
        del arena_keys[:]
        first_fence["v"] = fence
        o = 0
        gtail = ABF(o, [128, 2, T]); o += 4096

        def gTc(c):
            if c < 4:
                return lruT[:, c, :]
            if c < 8:
                return confT[:, c - 4, :]
            if c < 12:
                return attnT[:, c - 8, :]
            if c < 16:
                return xbpad2[:, (c - 12) * T:(c - 11) * T]
            if c < 20:
                return upad2[:, (c - 16) * T:(c - 15) * T]
            return gtail[:, c - 20, :]
        NUR = 2
        ur = ABF(o, [128, NUR, 2, NSEG * FPP]); o += NUR * 2 * NSEG * FPP * 2
        cen = AF32(o, [128, NUR, 4, 512]); o += NUR * 4 * 2048
        gl = ABF(o, [128, 2, 512]); o += 2048
        assert o <= SBYTES, o
        oldk = [("lruT", c_) for c_ in range(4)] + [("confT", c_) for c_ in range(4)] + \
               [("attnT", c_, th_, e__) for c_ in range(4) for th_ in range(2) for e__ in range(2)] + \
               [(nm_, c_, th_) for nm_ in ("xbpad", "xbpad_h", "upad", "upad_h") for c_ in range(4) for th_ in range(2)]
        P.add("pool", lambda e, ur=ur: e.memset(ur, 0.0), w=FW() + oldk + [akey("ur_init")])
        ffn_items = []
        for j in range(11):
            for pi in range(2):
                ffn_items.append((j, pi, j * 2 + pi))

        def ffn_up(item):
            j, pi, c = item
            if pi == 0:
                s_, wk_ = w_next(("up", l, j))
                ffn_up.cur = (slot_view(s_, 8, 512), wk_)
            wv_, wk_ = ffn_up.cur
            slot = c % NUR
            for th in range(2):
                for hv in range(2):
                    bk = (c % 2) * 4 + (hv * 2 + th)
                    co = hv * 256 + pi * 128
                    mm_group(bk, [(wv_[:, kc, co:co + 128], hT[:, kc, th * 512:(th + 1) * 512]) for kc in range(8)],
                             r=wk_ + hkeys(th))
                    src = PB(bk).rearrange("p (s t) -> p s t", s=2)
                    dst = ur[:, slot, hv, :].rearrange("p (s t) -> p s t", s=NSEG)[:, 2 * th:2 * th + 2, 1:1 + SEG]
                    P.add("act", lambda e, dst=dst, src=src: e.activation(out=dst, in_=src, func=AF.Copy),
                          r=[pskey(bk), "ur_init"], w=[akey(("ur", slot, hv, th))])
                    act(cen[:, slot, hv * 2 + th, :], PB(bk), AF.Identity, [pskey(bk), "vecs", "ur_init"],
                        [akey(("cen", slot, hv, th))], bias=V(l, "fcb", hv * 22 + c, 1),
                        scale=V(l, "fcw", 44 + hv * 22 + c, 1))

        def ffn_halo(item):
            j, pi, c = item
            slot = c % NUR
            for hv in range(2):
                uvv = ur[:, slot, hv, :].rearrange("p (s t) -> p s t", s=NSEG)
                rk_ = [("ur", slot, hv, 0), ("ur", slot, hv, 1), "flag"]
                dve_ts(uvv[:, 1:4, 0:1], uvv[:, 0:3, SEG:SEG + 1], flag[:, 0:1], ALU.mult, rk_,
                       [akey(("urh", slot, hv, 0))])
                dve_ts(uvv[:, 0:3, SEG + 1:SEG + 2], uvv[:, 1:4, 1:2], flag[:, 0:1], ALU.mult, rk_,
                       [akey(("urh", slot, hv, 1))])

        def ffn_conv(item):
            j, pi, c = item
            slot = c % NUR
            for k in (0, 2):
                for th in range(2):
                    for hv in range(2):
                        uvv = ur[:, slot, hv, :].rearrange("p (s t) -> p s t", s=NSEG)
                        acc = cen[:, slot, hv * 2 + th, :].rearrange("p (s t) -> p s t", s=2)
                        ck_ = ("cen", slot, hv, th)
                        rk_ = [("ur", slot, hv, 0), ("ur", slot, hv, 1), ("urh", slot, hv, 0), ("urh", slot, hv, 1),
                               ck_, "vecs"]
                        dve_stt(acc, uvv[:, 2 * th:2 * th + 2, k:k + SEG], V(l, "fcw", k * 44 + hv * 22 + c, 1), acc,
                                ALU.mult, ALU.add, rk_, [ck_])
            for th in range(2):
                act(gl[:, th, :], cen[:, slot, th, :], AF.Gelu_apprx_tanh, [("cen", slot, 0, th)], [akey(("gl", th))])
                dve_tt(gTc(c)[:, th * 512:(th + 1) * 512], cen[:, slot, 2 + th, :], gl[:, th, :], ALU.mult,
                       [("cen", slot, 1, th), ("gl", th), "ur_init"], [akey(("gT", c, th))])

        for i in range(len(ffn_items) + 1):
            if i < len(ffn_items):
                ffn_up(ffn_items[i])
            if i >= 1:
                ffn_conv(ffn_items[i - 1])
            if i < len(ffn_items):
                ffn_halo(ffn_items[i])
        rot["i"] = 0
        if phase_end("H%d" % l):
            tap_list.extend([("gtail", gtail, [128, 2, T], BF16), ("lruT", lruT[:], [128, 4, T], BF16)])
            break
        mean = AF32(o, [128, T]); o += 4096
        rstd = AF32(o, [128, T]); o += 4096
        sqb = ABF(o, [128, 2, T]); o += 4096
        cvb = ABF(o, [128, 2, T]); o += 4096
        assert o <= SBYTES, o
        def dn_ln2_thmajor():
            nxt_ = (l + 1 < DEPTH)
            inv = 1.0 / 1024.0
            for th in range(2):
                tsl = slice(th * 512, (th + 1) * 512)
                for n in range(8):
                    s_, wk_ = w_next((("dn" if th == 0 else "dn2"), l, n))
                    wv_ = slot_view(s_, NFF, 128)
                    bk = next_bank()
                    mm_group(bk, [(wv_[:, kc, :], gTc(kc)[:, tsl]) for kc in range(NFF)],
                             r=wk_ + [("gT", kc, th) for kc in range(NFF)])
                    dve_stt(xT[:, n, tsl], PB(bk), der[:, 24 + n:25 + n], xT[:, n, tsl], ALU.mult, ALU.add,
                            [pskey(bk), ("xT", n, th), ("der", 1), ("der", 3)], [("xT", n, th)])
                b1, b2 = next_bank(), next_bank()
                for c in range(8):
                    slot = c % 2
                    cv_ = cvb[:, slot, tsl]
                    sq_ = sqb[:, slot, tsl]
                    xs_ = xT[:, c, tsl]
                    fw_ = FW()
                    P.add("dve", lambda e, cv_=cv_, xs_=xs_: e.tensor_copy(out=cv_, in_=xs_), r=[("xT", c, th)],
                          w=fw_ + [akey(("cvb", slot, th))])
                    act(sq_, xs_, AF.Square, [("xT", c, th)], fw_ + [akey(("sqb", slot, th))])

                    def fn(e, c=c, cv_=cv_, sq_=sq_, b1=b1, b2=b2):
                        e.matmul(PB(b1), onesb[:, :], cv_, start=(c == 0), stop=(c == 7))
                        return e.matmul(PB(b2), onesb[:, :], sq_, start=(c == 0), stop=(c == 7))
                    P.add("pe", fn, r=[("cvb", slot, th), ("sqb", slot, th), "onesb"], w=[pskey(b1), pskey(b2)])
                mk_, rk_ = akey(("mean", th)), akey(("rstd", th))
                act(mean[:, tsl], PB(b1), AF.Copy, [pskey(b1)], [mk_], scale=inv)
                act(rstd[:, tsl], PB(b1), AF.Square, [pskey(b1)], [rk_], scale=inv)
                dve_stt(rstd[:, tsl], PB(b2), inv, rstd[:, tsl], ALU.mult, ALU.subtract, [pskey(b2), rk_], [rk_])
                dve_ts(rstd[:, tsl], rstd[:, tsl], 0.0, ALU.max, [rk_], [rk_], s2=EPS_F, op1=ALU.add)
                act(rstd[:, tsl], rstd[:, tsl], AF.Ln, [rk_], [rk_])
                act(rstd[:, tsl], rstd[:, tsl], AF.Exp, [rk_], [rk_], scale=-0.5)
                for c in range(8):
                    xk = [("xT", c, th)]
                    xc_ = xT[:, c, tsl]
                    dve_tt(xc_, xc_, mean[:, tsl], ALU.subtract, xk + [mk_], xk)
                    dve_tt(xc_, xc_, rstd[:, tsl], ALU.mult, xk + [rk_], xk)
                    act(xc_, xc_, AF.Identity, xk + ["vecs"], xk, bias=V(l, "l2b", c, 1), scale=V(l, "l2g", c, 1))
                    if nxt_:
                        act(hT[:, c, tsl], xc_, AF.Identity, xk + [("der", 0), "modA"], [("hT", c, th)],
                            bias=modT[:, c:c + 1], scale=der[:, c:c + 1])

        dn_ln2_thmajor()
        if phase_end("I%d" % l):
            break

    if not stopped["v"]:
        fence = list(arena_keys)
        del arena_keys[:]
        YT = [AF32(i * 4096, [128, 1024]) for i in range(8)]
        YALL = AF32(0, [128, 8, 1024])
        for tbh in range(2):
            for c in range(8):
                b = next_bank()

                def fn(e, c=c, tbh=tbh, b=b):
                    ins = None
                    for q in range(4):
                        tb = tbh * 4 + q
                        ins = e.transpose(out=PB(b)[:, q * 128:(q + 1) * 128], in_=xT[:, c, tb * 128:(tb + 1) * 128],
                                          identity=ident[:])
                    return ins
                P.add("pe", fn, r=[("xT", c, tbh), "ident"], w=[pskey(b)])
                dst = YALL[:, tbh * 4:(tbh + 1) * 4, c * 128:(c + 1) * 128]
                src = PB(b).rearrange("p (q t) -> p q t", q=4)
                wk = [("yt", tbh, c)] + fence
                if c % 2 == 0:
                    P.add("act", lambda e, dst=dst, src=src: e.activation(out=dst, in_=src, func=AF.Copy),
                          r=[pskey(b)], w=wk)
                else:
                    P.add("dve", lambda e, dst=dst, src=src: e.tensor_copy(out=dst, in_=src), r=[pskey(b)], w=wk)
            for tb in range(tbh * 4, tbh * 4 + 4):
                yk = [("yt", tbh, c) for c in range(8)]
                dma("sp", y_d[tb * 128:(tb + 1) * 128, :], YT[tb], yk, [], "st_y%d" % tb)
    tap_outs = []
    for ti, (name, ap, shape, dt) in enumerate(tap_list):
        dd = dout("tap_" + name, shape, dt)
        tap_outs.append("tap_" + name)
        P.add("sp", lambda e, dd=dd, ap=ap: e.dma_start(out=dd, in_=ap), r=list(P.kw.keys()), w=[], chan="tap%d" % ti)
    if stopped["v"]:
        dd = dout("tap_xT", [128, 8, T], F32)
        P.add("sp", lambda e, dd=dd: e.dma_start(out=dd, in_=xT[:]), r=list(P.kw.keys()), w=[], chan="tapx")
        dd2 = dout("tap_hT", [128, 8, T], BF16)
        P.add("sp", lambda e, dd2=dd2: e.dma_start(out=dd2, in_=hT[:]), r=list(P.kw.keys()), w=[], chan="taph")
        dd3 = dout("tap_modT", [128, 48], F32)
        P.add("sp", lambda e, dd3=dd3: e.dma_start(out=dd3, in_=modT[:]), r=list(P.kw.keys()), w=[], chan="tapm")
    last_dma = [P.chan_last[c] for c in P.chan_last]
    fin_op = Op()
    fin_op.eng = "sp"
    fin_op.fn = lambda e: e.nop()
    fin_op.chan = None
    fin_op.sig = False
    fin_op.gid = len(P.ops)
    fin_op.deps = set(last_dma)
    P.ops.append(fin_op)

    block = es.enter_context(nc.Block())
    P.emit(nc, es, block)
    es.close()
    return nc


def _fm(v):
    v = np.asarray(v, np.float32)
    return np.ascontiguousarray(v.reshape(-1, 128).T)


def _rope_tables(active):
    c = np.ones((128, T), np.float32)
    s = np.zeros((128, T), np.float32)
    if active:
        t = np.arange(T)
        row = (t // 64).astype(np.float32)
        colp = (t % 64).astype(np.float32)
        inv = (np.float32(10000.0) ** (-np.arange(16, dtype=np.float32) / np.float32(16))).astype(np.float32)
        for p in range(128):
            dd = p % 64
            a = dd // 32
            jj = dd % 32
            f = jj % 16
            pos = row if a == 0 else colp
            ang = (pos * inv[f]).astype(np.float32)
            c[p] = np.cos(ang)
            s[p] = -np.sin(ang) if jj < 16 else np.sin(ang)
    return c, s


def _mask_table(sample):
    m = np.zeros((128, MASK_COLS), np.float32)
    if not sample:
        m[:, 0:512] = NEGM
    k = np.arange(128)[:, None]
    for th in range(2):
        for (kb, jl, jh) in band_tiles(th):
            off = MASK_OFF[(th, kb)]
            for j in range(jl, jh + 1):
                q = np.arange(128)[None, :]
                if sample:
                    ok = np.abs((kb * 128 + k) - (j * 128 + q)) <= 128
                else:
                    ok = np.broadcast_to(np.array(kb // 2 == j // 2), (128, 128))
                m[:, off + (j - jl) * 128: off + (j - jl + 1) * 128] = np.where(ok, 0.0, NEGM)
    return m


def _partner(cols64):
    cols64 = np.asarray(cols64)
    idx = np.arange(64)
    j = idx % 32
    p = np.where(j < 16, idx + 16, idx - 16)
    return cols64[p]


def _prep_shared(inp):
    sh = {}
    w_in = np.asarray(inp["w_in"], np.float32)
    cols = []
    qc = lambda h: np.arange(h * 64, (h + 1) * 64)
    kc = lambda g: 512 + np.arange(g * 64, (g + 1) * 64)
    for name in WIN_CHUNKS:
        if name.startswith("qp"):
            c = int(name[2])
            cols.append(np.concatenate([_partner(qc(2 * c)), _partner(qc(2 * c + 1))]))
        elif name.startswith("q"):
            c = int(name[1])
            cols.append(np.concatenate([qc(2 * c), qc(2 * c + 1)]))
        elif name == "kA":
            cols.append(np.concatenate([kc(0), kc(1)]))
        elif name == "kB":
            cols.append(np.concatenate([kc(1), kc(0)]))
        elif name == "kAp":
            cols.append(np.concatenate([_partner(kc(0)), _partner(kc(1))]))
        elif name == "kBp":
            cols.append(np.concatenate([_partner(kc(1)), _partner(kc(0))]))
        elif name == "v":
            cols.append(640 + np.arange(128))
        elif name.startswith("xb"):
            c = int(name[2])
            cols.append(768 + c * 128 + np.arange(128))
        elif name.startswith("a"):
            c = int(name[1])
            cols.append(1280 + c * 128 + np.arange(128))
        elif name.startswith("g"):
            c = int(name[1])
            cols.append(1280 + 512 + c * 128 + np.arange(128))
    cols = np.concatenate(cols)
    sh["w_in_ext"] = np.ascontiguousarray(w_in[:, :, cols])
    vec = np.zeros((DEPTH, 128, NV), np.float32)
    f = lambda a: np.asarray(a, np.float32)
    for l in range(DEPTH):
        def put(name, arr):
            arr = np.asarray(arr, np.float32)
            vec[l, :, _VOFF[name]:_VOFF[name] + arr.shape[1]] = arr
        put("bmod", _fm(f(inp["b_mod"])[l]))
        put("lcw", np.concatenate([_fm(f(inp["lru_conv_w"])[l, k]) for k in range(4)], axis=1))
        put("lcb", _fm(f(inp["lru_conv_b"])[l]))
        put("lba", np.concatenate([_fm(f(inp["lru_ba"])[l, d]) for d in range(2)], axis=1))
        put("lbx", np.concatenate([_fm(f(inp["lru_bx"])[l, d]) for d in range(2)], axis=1))
        put("llam", np.concatenate([_fm(f(inp["lru_lambda"])[l, d]) for d in range(2)], axis=1))
        put("cw", np.concatenate([_fm(f(inp["conf_dw_w"])[l, k]) for k in range(31)], axis=1))
        put("cb", _fm(f(inp["conf_dw_b"])[l]))
        put("clg", _fm(f(inp["conf_ln_g"])[l]))
        put("clb", _fm(f(inp["conf_ln_b"])[l]))
        put("bm", _fm(f(inp["b_merge"])[l]))
        put("l1g", _fm(f(inp["ln1_g"])[l]))
        put("l1b", _fm(f(inp["ln1_b"])[l]))
        put("fcw", np.concatenate([_fm(f(inp["ffn_conv_w"])[l, k]) for k in range(3)], axis=1))
        put("fcb", _fm(f(inp["ffn_conv_b"])[l]))
        put("l2g", _fm(f(inp["ln2_g"])[l]))
        put("l2b", _fm(f(inp["ln2_b"])[l]))
        put("sink", np.broadcast_to(f(inp["attn_sink"])[l][None, :], (128, 8)))
    sh["vecs"] = vec
    lw = np.zeros((DEPTH, 128, 16, 128), np.float32)
    for l in range(DEPTH):
        for d in range(2):
            for gt, nm in enumerate(("lru_wa", "lru_wx")):
                wsrc = f(inp[nm])[l, d]
                for c in range(4):
                    i = (d * 2 + gt) * 4 + c
                    for bb in range(2):
                        lw[l, bb * 64:(bb + 1) * 64, i, bb * 64:(bb + 1) * 64] = wsrc[2 * c + bb]
    sh["lruw"] = lw
    sh["ident"] = np.eye(128, dtype=np.float32)
    for k in ("w_mod", "w_branch", "w_merge", "w_out", "ffn_w_up", "ffn_w_down"):
        sh[k] = np.ascontiguousarray(np.asarray(inp[k], np.float32))
    return sh


def make_in_maps(inp):
    sh = _prep_shared(inp)
    xs = np.asarray(inp["x_sample"], np.float32)
    xp = np.asarray(inp["x_prompt"], np.float32)
    ck = np.asarray(inp["cache_k"], np.float32)
    cv = np.asarray(inp["cache_v"], np.float32)
    st = np.asarray(inp["state_lru"], np.float32)
    cc = np.asarray(inp["c"], np.float32)
    cctx = np.asarray(inp["c_ctx"], np.float32)
    rc_s, rs_s = _rope_tables(True)
    rc_p, rs_p = _rope_tables(False)
    mk_s = _mask_table(True)
    mk_p = _mask_table(False)
    maps = []
    for core in range(8):
        m = dict(sh)
        if core < 4:
            b = core
            m["x"] = np.ascontiguousarray(xs[b])
            m["cond"] = _fm(cc[b])
            m["ck"] = np.ascontiguousarray(ck[b].reshape(DEPTH, 512, 128))
            m["cv"] = np.ascontiguousarray(cv[b].reshape(DEPTH, 512, 128))
            h0 = np.zeros((128, DEPTH * 8), np.float32)
            for l in range(DEPTH):
                for d in range(2):
                    h0[:, l * 8 + d * 4:l * 8 + d * 4 + 4] = _fm(st[b, l, d])
            m["h0"] = h0
            m["flag"] = np.ones((128, 1), np.float32)
            m["ropec"], m["ropes"], m["maskb"] = rc_s, rs_s, mk_s
        else:
            i = core - 4
            m["x"] = np.ascontiguousarray(xp[4 * i:4 * i + 4].reshape(T, D))
            m["cond"] = _fm(cctx)
            m["ck"] = np.zeros((DEPTH, 512, 128), np.float32)
            m["cv"] = np.zeros((DEPTH, 512, 128), np.float32)
            m["h0"] = np.zeros((128, DEPTH * 8), np.float32)
            m["flag"] = np.zeros((128, 1), np.float32)
            m["ropec"], m["ropes"], m["maskb"] = rc_p, rs_p, mk_p
        maps.append(m)
    return maps


_NC_CACHE = {}


def kernel(**inputs):
    if "nc" not in _NC_CACHE:
        _NC_CACHE["nc"] = build_program()
    nc = _NC_CACHE["nc"]
    maps = make_in_maps(inputs)
    res = run_bass_kernel_spmd(nc, maps, core_ids=list(range(8)))
    R = res.results
    y_prompt = np.zeros((16, 256, D), np.float32)
    y_sample = np.zeros((4, T, D), np.float32)
    nk = np.zeros((16, DEPTH, 256, 2, 64), np.float32)
    nv = np.zeros((16, DEPTH, 256, 2, 64), np.float32)
    nh = np.zeros((16, DEPTH, 2, 512), np.float32)
    for core in range(8):
        r = R[core]
        if core < 4:
            y_sample[core] = r["y"]
        else:
            i = core - 4
            y_prompt[4 * i:4 * i + 4] = r["y"].reshape(4, 256, D)
            for s in range(4):
                for l in range(DEPTH):
                    nk[4 * i + s, l] = r["nk"][l, s * 256:(s + 1) * 256].reshape(256, 2, 64)
                    nv[4 * i + s, l] = r["nv"][l, s * 256:(s + 1) * 256].reshape(256, 2, 64)
                    nh[4 * i + s, l] = r["nh"][l].reshape(4, 2, 512)[s]
    return (y_prompt, y_sample, nk, nv, nh)
```

```python
from contextlib import ExitStack
import numpy as np
import concourse.bass as bass
import concourse.mybir as mybir
from concourse.bass_utils import run_bass_kernel_spmd

F32 = mybir.dt.float32
BF16 = mybir.dt.bfloat16
AF = mybir.ActivationFunctionType
ALU = mybir.AluOpType

D = 1024
T = 1024
DEPTH = 2
NSEG = 4
SEG = 256
DFF = 2816
NFF = 22
ALPHA = (2 * DEPTH) ** 0.25
LN_EPS = 1e-5
EPS_F = LN_EPS / (ALPHA * ALPHA)
NEGM = -30000.0
NSLOT = 4
SLOT_ELEMS = 4096
XBP = 259
UPP = 286
FPP = 258

_VOFF = {}
_o = 0
for _n, _w in [("bmod", 48), ("lcw", 16), ("lcb", 4), ("lba", 8), ("lbx", 8), ("llam", 8), ("cw", 124),
               ("cb", 4), ("clg", 4), ("clb", 4), ("bm", 24), ("l1g", 8), ("l1b", 8), ("fcw", 132),
               ("fcb", 44), ("l2g", 8), ("l2b", 8), ("sink", 8)]:
    _VOFF[_n] = _o
    _o += _w
NV = _o

WIN_CHUNKS = ["q0", "qp0", "q1", "qp1", "q2", "qp2", "q3", "qp3", "kA", "kAp",
              "v", "xb0", "xb1", "xb2", "xb3", "a0", "g0", "a1", "g1", "a2", "g2", "a3", "g3"]
NWIN = len(WIN_CHUNKS) * 128
WIN_TILES = [(0, 512), (512, 512), (1024, 256), (1280, 128), (1408, 512), (1920, 512), (2432, 512)]


def band_tiles(th):
    out = []
    for kb in range(max(0, 4 * th - 1), min(8, 4 * th + 5)):
        jl = max(4 * th, kb - 1)
        jh = min(4 * th + 3, kb + 1)
        out.append((kb, jl, jh))
    return out


MASK_OFF = {}
_m = 512
for _th in range(2):
    for (_kb, _jl, _jh) in band_tiles(_th):
        MASK_OFF[(_th, _kb)] = _m
        _m += 128 * (_jh - _jl + 1)
MASK_COLS = _m


class Op:
    __slots__ = ("eng", "fn", "deps", "chan", "cord", "idx", "sig", "sigcount", "gid")


class Prog:
    ENGS = ("pe", "act", "dve", "pool", "sp")

    def __init__(self):
        self.ops = []
        self.kw = {}
        self.kr = {}
        self.chan_last = {}
        self.chan_count = {}

    def add(self, eng, fn, r=(), w=(), chan=None):
        op = Op()
        op.eng = eng
        op.fn = fn
        op.chan = chan
        op.sig = False
        op.gid = len(self.ops)
        deps = set()
        psr = [k for k in r if isinstance(k, tuple) and k and k[0] == "ps"]
        if psr:
            r = [k for k in r if k not in psr]
            w = list(w) + psr
        for k in r:
            p = self.kw.get(k)
            if p is not None:
                deps.add(p)
        for k in w:
            p = self.kw.get(k)
            if p is not None:
                deps.add(p)
            for q in self.kr.get(k, ()):
                deps.add(q)
        if chan is not None:
            p = self.chan_last.get(chan)
            if p is not None:
                deps.add(p)
            self.chan_last[chan] = op.gid
            self.chan_count[chan] = self.chan_count.get(chan, 0) + 1
            op.cord = self.chan_count[chan]
        for k in r:
            self.kr.setdefault(k, []).append(op.gid)
        for k in w:
            self.kw[k] = op.gid
            self.kr[k] = []
        op.deps = deps
        self.ops.append(op)
        return op.gid

    def emit(self, nc, es, block):
        ops = self.ops
        for op in ops:
            for d in op.deps:
                if ops[d].chan is None:
                    ops[d].sig = True
        cnt = {e: 0 for e in self.ENGS}
        for op in ops:
            if op.chan is None and op.sig:
                cnt[op.eng] += 1
                op.sigcount = cnt[op.eng]
        esem = {e: es.enter_context(nc.semaphore("s_" + e)) for e in ("pe", "act", "dve", "pool")}
        csem = {c: es.enter_context(nc.semaphore("c_" + c)) for c in self.chan_count}
        per_eng = {e: [op for op in ops if op.eng == e] for e in self.ENGS}

        def run(e, name):
            known = {}
            for op in per_eng[name]:
                waits = {}
                for d in op.deps:
                    p = ops[d]
                    if p.chan is not None:
                        key = ("c", p.chan)
                        val = 16 * p.cord
                    else:
                        if p.eng == "pe" and name == "pe":
                            continue
                        key = ("e", p.eng)
                        val = p.sigcount
                    if known.get(key, 0) >= val:
                        continue
                    if waits.get(key, 0) < val:
                        waits[key] = val
                for key, val in waits.items():
                    sem = csem[key[1]] if key[0] == "c" else esem[key[1]]
                    e.wait_ge(sem, val)
                    known[key] = val
                ins = op.fn(e)
                if op.chan is not None:
                    ins.then_inc(csem[op.chan], 16)
                elif op.sig:
                    ins.then_inc(esem[name], 1)

        @block.tensor
        def _(e):
            run(e, "pe")

        @block.scalar
        def _(e):
            run(e, "act")

        @block.vector
        def _(e):
            run(e, "dve")

        @block.gpsimd
        def _(e):
            run(e, "pool")

        @block.sync
        def _(e):
            run(e, "sp")


def build_program(stop_after=None, taps=()):
    nc = bass.Bass("TRN2", target_bir_lowering=False)
    es = ExitStack()
    P = Prog()

    def din(name, shape, dt=F32):
        return nc.dram_tensor(name, list(shape), dt, kind="ExternalInput").ap()

    def dout(name, shape, dt=F32):
        return nc.dram_tensor(name, list(shape), dt, kind="ExternalOutput").ap()

    def sb(name, shape, dt):
        return es.enter_context(nc.sbuf_tensor("sb_" + name, list(shape), dt))

    x_d = din("x", [T, D])
    cond_d = din("cond", [128, 8])
    ck_d = din("ck", [DEPTH, 512, 128])
    cv_d = din("cv", [DEPTH, 512, 128])
    h0_d = din("h0", [128, DEPTH * 8])
    flag_d = din("flag", [128, 1])
    ropec_d = din("ropec", [128, T])
    ropes_d = din("ropes", [128, T])
    maskb_d = din("maskb", [128, MASK_COLS])
    ident_d = din("ident", [128, 128])
    vecs_d = din("vecs", [DEPTH, 128, NV])
    lruw_d = din("lruw", [DEPTH, 128, 16, 128])
    wmod_d = din("w_mod", [DEPTH, D, 6 * D])
    win_d = din("w_in_ext", [DEPTH, D, NWIN])
    wbr_d = din("w_branch", [DEPTH, 3, 512, D])
    wmg_d = din("w_merge", [DEPTH, D, 3 * D])
    wout_d = din("w_out", [DEPTH, D, D])
    wup_d = din("ffn_w_up", [DEPTH, D, 2 * DFF])
    wdn_d = din("ffn_w_down", [DEPTH, DFF, D])

    y_d = dout("y", [T, D])
    nk_d = dout("nk", [DEPTH, T, 128])
    nv_d = dout("nv", [DEPTH, T, 128])
    nh_d = dout("nh", [DEPTH, 32, 128])

    xT = sb("xT", [128, 8, T], F32)
    hT = sb("hT", [128, 8, T], BF16)
    wsl = sb("wsl", [128, NSLOT, SLOT_ELEMS], BF16)
    ident = sb("ident", [128, 128], F32)
    identb = sb("identb", [128, 128], BF16)
    onesb = sb("onesb", [128, 128], BF16)
    vecs = sb("vecs", [128, DEPTH, NV], F32)
    condt = sb("condt", [128, 8], F32)
    scond = sb("scond", [128, 8], BF16)
    flag = sb("flag", [128, 1], F32)
    ctxb = sb("ctxb", [128, 1], F32)
    h0t = sb("h0t", [128, DEPTH * 8], F32)
    modT = sb("modT", [128, 48], F32)
    der = sb("der", [128, 64], F32)
    ropec = sb("ropec", [128, T], BF16)
    ropes = sb("ropes", [128, T], BF16)
    maskb = sb("maskb", [128, MASK_COLS], BF16)
    lruw = sb("lruw", [128, 16, 128], BF16)
    lrudg = sb("lrudg", [128, 16, 128], BF16)
    esink = sb("esink", [128, 8], F32)
    fin = sb("fin", [128, 32], F32)
    fint = sb("fint", [32, 128], F32)
    initb = sb("initb", [128, 8], F32)
    attnT = sb("attnT", [128, 4, T], BF16)
    lruT = sb("lruT", [128, 4, T], BF16)
    confT = sb("confT", [128, 4, T], BF16)
    xbpad2 = sb("xbpad", [128, 4 * NSEG * XBP], BF16)
    xbpad = xbpad2[:, :].rearrange("p (c x) -> p c x", c=4)
    upad2 = sb("upad", [128, 4 * NSEG * UPP], BF16)
    upad = upad2[:, :].rearrange("p (c x) -> p c x", c=4)
    SBYTES = 60 * 1024
    arena = sb("arena", [128, SBYTES // 4], F32)
    arena_b = arena[:].bitcast(BF16) if hasattr(arena[:], "bitcast") else None

    psum = es.enter_context(nc.psum_tensor("ps", [128, 8, 512], F32))

    def AF32(off_bytes, shape):
        n = int(np.prod(shape[1:]))
        o = off_bytes // 4
        ap = arena[0:shape[0], o:o + n]
        return ap if len(shape) == 2 else _reshape(ap, shape[1:])

    def ABF(off_bytes, shape):
        n = int(np.prod(shape[1:]))
        o = off_bytes // 2
        ap = arena_b[0:shape[0], o:o + n]
        return ap if len(shape) == 2 else _reshape(ap, shape[1:])

    def _reshape(ap, free):
        if len(free) == 2:
            return ap.rearrange("p (a b) -> p a b", a=free[0])
        if len(free) == 3:
            return ap.rearrange("p (a b c) -> p a b c", a=free[0], b=free[1])
        raise ValueError

    arena_keys = []

    def akey(name):
        arena_keys.append(name)
        return name

    wtiles = []

    def wt_add(kind, parts):
        wtiles.append((kind, parts))

    def slot_view(s, kc, ncols, np_=128):
        return wsl[0:np_, s, 0:kc * ncols].rearrange("p (k n) -> p k n", k=kc)

    def wt_mod(l, j):
        src = wmod_d[l, :, j * 512:(j + 1) * 512].rearrange("(k p) n -> p k n", p=128)
        wt_add(("mod", l, j), [(lambda s: slot_view(s, 8, 512), src)])

    for l in range(DEPTH):
        if l == 0:
            for j in range(4):
                wt_mod(0, j)
        def wt_win(j):
            c0, ncols = WIN_TILES[j]
            src = win_d[l, :, c0:c0 + ncols].rearrange("(k p) n -> p k n", p=128)
            wt_add(("win", l, j), [(lambda s, ncols=ncols: slot_view(s, 8, ncols), src)])
        for j in range(4):
            wt_win(j)
        for j in range(4, 12):
            wt_mod(l, j)
        for j in range(4, 7):
            wt_win(j)
        for n in range(8):
            parts = []
            for b in range(3):
                src = wmg_d[l, :, b * D + n * 128: b * D + (n + 1) * 128].rearrange("(k p) n -> p k n", p=128)
                parts.append((lambda s, b=b: slot_view(s, 24, 128)[:, b * 8:(b + 1) * 8, :], src))
            wt_add(("mg", l, n), parts)
            parts = []
            for b in range(3):
                src = wbr_d[l, b, :, n * 128:(n + 1) * 128].rearrange("(k p) n -> p k n", p=128)
                parts.append((lambda s, b=b: slot_view(s, 12, 128)[:, 4 * b:4 * b + 4, :], src))
            wt_add(("br", l, n), parts)
            if l + 1 < DEPTH and n % 2 == 1:
                wt_mod(l + 1, n // 2)
        for j in range(2):
            src = wout_d[l, :, j * 512:(j + 1) * 512].rearrange("(k p) n -> p k n", p=128)
            wt_add(("out", l, j), [(lambda s: slot_view(s, 8, 512), src)])
        for j in range(11):
            parts = []
            for hv in range(2):
                c0 = hv * DFF + j * 256
                src = wup_d[l, :, c0:c0 + 256].rearrange("(k p) n -> p k n", p=128)
                parts.append((lambda s, hv=hv: slot_view(s, 8, 512)[:, :, hv * 256:(hv + 1) * 256], src))
            wt_add(("up", l, j), parts)
        for n in range(8):
            src = wdn_d[l, :, n * 128:(n + 1) * 128].rearrange("(k p) n -> p k n", p=128)
            wt_add(("dn", l, n), [(lambda s: slot_view(s, NFF, 128), src)])
        for n in range(8):
            src = wdn_d[l, :, n * 128:(n + 1) * 128].rearrange("(k p) n -> p k n", p=128)
            wt_add(("dn2", l, n), [(lambda s: slot_view(s, NFF, 128), src)])

    wstate = {"issued": 0, "next": 0}

    def w_issue_upto(j):
        while wstate["issued"] <= min(j, len(wtiles) - 1):
            i = wstate["issued"]
            kind, parts = wtiles[i]
            s = i % NSLOT
            for pi, (dst_fn, src) in enumerate(parts):
                dst = dst_fn(s)
                P.add("pool", (lambda e, dst=dst, src=src: e.dma_start(out=dst, in_=src)),
                      w=[("w", s, pi)],
                      chan="w%d_%d" % (s, pi))
            wstate["issued"] += 1

    def w_next(kind, la=NSLOT - 1):
        j = wstate["next"]
        assert wtiles[j][0] == kind, (wtiles[j][0], kind)
        w_issue_upto(j + la)
        wstate["next"] += 1
        s = j % NSLOT
        return s, [("w", s, q) for q in range(3)]

    def PB(b):
        return psum[:, b, :]

    pskey = lambda b: ("ps", b)
    rot = {"i": 0}

    MODB = 7

    def next_bank():
        b = rot["i"] % 7
        rot["i"] += 1
        return b

    def act(out, in_, func, r, w, bias=None, scale=None):
        kw = {}
        if bias is not None:
            kw["bias"] = bias
        if scale is not None:
            kw["scale"] = scale
        return P.add("act", lambda e: e.activation(out=out, in_=in_, func=func, **kw), r=r, w=w)

    def dve_tt(out, in0, in1, op, r, w, eng="dve"):
        return P.add(eng, lambda e: e.tensor_tensor(out=out, in0=in0, in1=in1, op=op), r=r, w=w)

    def dve_ts(out, in0, s1, op0, r, w, s2=None, op1=None, eng="dve"):
        if op1 is None:
            return P.add(eng, lambda e: e.tensor_scalar(out=out, in0=in0, scalar1=s1, scalar2=0.0, op0=op0, op1=ALU.add),
                         r=r, w=w)
        return P.add(eng, lambda e: e.tensor_scalar(out=out, in0=in0, scalar1=s1, scalar2=s2, op0=op0, op1=op1), r=r, w=w)

    def dve_stt(out, in0, scalar, in1, op0, op1, r, w, eng="dve"):
        return P.add(eng, lambda e: e.scalar_tensor_tensor(out=out, in0=in0, scalar=scalar, in1=in1, op0=op0, op1=op1),
                     r=r, w=w)

    def dma(q, out, in_, r, w, chan):
        return P.add(q, lambda e: e.dma_start(out=out, in_=in_), r=r, w=w, chan=chan)

    def mm_group(bank, steps, r, cols=None, extra_w=()):
        outap = PB(bank) if cols is None else PB(bank)[:, cols[0]:cols[1]]

        def fn(e):
            ins = None
            n = len(steps)
            for i, (lt, rh) in enumerate(steps):
                ins = e.matmul(outap, lt, rh, start=(i == 0), stop=(i == n - 1))
            return ins
        return P.add("pe", fn, r=r, w=[pskey(bank)] + list(extra_w))

    stopped = {"v": False}
    tap_list = []

    def phase_end(name):
        if stop_after == name:
            stopped["v"] = True
        return stopped["v"]

    dma("sp", ident[:], ident_d, [], ["ident"], "ld0")
    dma("sp", vecs[:], vecs_d.rearrange("l p n -> p l n"), [], ["vecs"], "ld1")
    dma("sp", condt[:], cond_d, [], ["condt"], "ld2")
    dma("sp", flag[:], flag_d, [], ["flag"], "ld3")
    dma("sp", h0t[:], h0_d, [], ["h0t"], "ld0")

    P.add("dve", lambda e: e.memset(onesb[:], 1.0), w=["onesb"])
    act(identb[:], ident[:], AF.Copy, ["ident"], ["identb"])
    P.add("dve", lambda e: e.tensor_scalar(out=ctxb[:], in0=flag[:], scalar1=-1.0, scalar2=-NEGM, op0=ALU.add,
                                           op1=ALU.mult), r=["flag"], w=["ctxb"])
    act(scond[:], condt[:], AF.Silu, ["condt"], ["scond"])

    XIN = [AF32(i * 4096, [128, 1024]) for i in range(8)]
    for tb in range(8):
        dma("sp", XIN[tb], x_d[tb * 128:(tb + 1) * 128, :], [], [akey(("xin", tb))], "xin%d" % tb)
    for tb in range(8):
        st = XIN[tb]
        k = ("xin", tb)
        for half in range(2):
            b = next_bank()
            def fn(e, st=st, half=half, b=b):
                ins = None
                for q in range(4):
                    c = half * 4 + q
                    ins = e.transpose(out=PB(b)[:, q * 128:(q + 1) * 128], in_=st[:, c * 128:(c + 1) * 128],
                                      identity=ident[:])
                return ins
            P.add("pe", fn, r=[k, "ident"], w=[pskey(b)])
            src = PB(b).rearrange("p (q t) -> p q t", q=4)
            dst = xT[:, half * 4:(half + 1) * 4, tb * 128:(tb + 1) * 128]
            eng = "act" if half == 0 else "dve"
            if eng == "act":
                P.add("act", lambda e, dst=dst, src=src: e.activation(out=dst, in_=src, func=AF.Copy),
                      r=[pskey(b)], w=[("xT", c_, tb // 4) for c_ in range(half * 4, half * 4 + 4)])
            else:
                P.add("dve", lambda e, dst=dst, src=src: e.tensor_copy(out=dst, in_=src),
                      r=[pskey(b)], w=[("xT", c_, tb // 4) for c_ in range(half * 4, half * 4 + 4)])

    def V(l, name, col=0, n=1):
        o = _VOFF[name] + col
        return vecs[:, l, o:o + n]

    for l in range(DEPTH):
        if stopped["v"]:
            break
        def mod_tile(ll, j):
            s, wk = w_next(("mod", ll, j))
            wv = slot_view(s, 8, 512)

            def fn(e, wv=wv, j=j):
                ins = None
                for n4 in range(4):
                    col = j * 4 + n4
                    for kc in range(8):
                        ins = e.matmul(PB(MODB)[:, col:col + 1], wv[:, kc, n4 * 128:(n4 + 1) * 128],
                                       scond[:, kc:kc + 1], start=(kc == 0), stop=(kc == 7))
                return ins
            P.add("pe", fn, r=wk + ["scond"], w=[pskey(MODB)])

        def mod_finish_A(ll):
            dve_tt(modT[:, 0:16], PB(MODB)[:, 0:16], V(ll, "bmod", 0, 16), ALU.add, [pskey(MODB), "vecs"], ["modA"])
            dve_ts(der[:, 0:8], modT[:, 8:16], 1.0, ALU.add, ["modA"], [("der", 0)])

        def mod_finish_B(ll):
            dve_tt(modT[:, 16:48], PB(MODB)[:, 16:48], V(ll, "bmod", 16, 32), ALU.add, [pskey(MODB), "vecs"], ["modB"])
            dve_ts(der[:, 8:16], modT[:, 16:24], 1.0 / ALPHA, ALU.mult, ["modB"], [("der", 1)])
            dve_ts(der[:, 16:24], modT[:, 32:40], 1.0, ALU.add, ["modB"], [("der", 2)])
            dve_ts(der[:, 24:32], modT[:, 40:48], 1.0 / ALPHA, ALU.mult, ["modB"], [("der", 3)])

        if l == 0:
            w_issue_upto(NSLOT - 1)
            for j in range(4):
                mod_tile(0, j)
            mod_finish_A(0)
            dma("pool", ropec[:], ropec_d, [], ["ropec"], "ldr0")
            dma("pool", ropes[:], ropes_d, [], ["ropes"], "ldr1")
            dma("pool", maskb[:], maskb_d, [], ["maskb"], "ldm2")
        dma("pool", lruw[:], lruw_d[l], [], ["lruw"], "ldm")
        act(der[:, 40:48], V(l, "llam", 0, 8), AF.Exp, ["vecs"], [("der", 5)], scale=-1.0)
        dve_ts(der[:, 40:48], der[:, 40:48], 1.0, ALU.add, [("der", 5)], [("der", 5)])
        act(der[:, 32:40], der[:, 40:48], AF.Ln, [("der", 5)], [("der", 4)])
        dve_ts(der[:, 32:40], der[:, 32:40], -4.0, ALU.mult, [("der", 4)], [("der", 4)])
        dve_ts(der[:, 48:64], V(l, "lba", 0, 16), 0.5, ALU.mult, ["vecs"], [("der", 6)])
        act(esink[:], V(l, "sink", 0, 8), AF.Exp, ["vecs"], ["esink"])
        for k in range(4):
            for c in range(4):
                i = k * 4 + c
                dve_ts(lrudg[:, i, :], identb[:], V(l, "lcw", i, 1), ALU.mult, ["identb", "vecs"], [("lrudg", i)],
                       eng="pool")

        for c in range(8 if l == 0 else 0):
            if c % 2 == 0:
                act(hT[:, c, :], xT[:, c, :], AF.Identity, [("xT", c, 0), ("xT", c, 1), ("der", 0), "modA"],
                    [("hT", c, 0), ("hT", c, 1)], bias=modT[:, c:c + 1], scale=der[:, c:c + 1])
            else:
                dve_ts(hT[:, c, :], xT[:, c, :], der[:, c:c + 1], ALU.mult,
                       [("xT", c, 0), ("xT", c, 1), ("der", 0), "modA"], [("hT", c, 0), ("hT", c, 1)],
                       s2=modT[:, c:c + 1], op1=ALU.add)
        gtk = lambda lo, hi: [("gT", c_, th_) for c_ in range(lo, hi) for th_ in range(2)]
        P.add("dve", lambda e: e.memset(xbpad2[:, :], 0.0), r=[("hT", 7, 1)], w=["xbpad_init"] + gtk(12, 16))
        P.add("dve", lambda e: e.memset(upad2[:, :], 0.0), r=[("hT", 7, 1)], w=["upad_init"] + gtk(16, 20))
        if phase_end("A%d" % l):
            break

        fence = list(arena_keys)
        del arena_keys[:]
        o = 0
        qT = ABF(o, [128, 4, T]); o += 8192
        kz = ABF(o, [128, 4, T]); o += 8192
        ktok = AF32(o, [128, 8, 128]); o += 4096
        vtok = AF32(o, [128, 8, 128]); o += 4096
        vaug = ABF(o, [128, 4, 8, 128]); o += 8192
        ckf = AF32(o, [128, 4, 128]); o += 2048
        cksf = AF32(o, [128, 4, 128]); o += 2048
        ckz = ABF(o, [128, 4, 512]); o += 4096
        cvaug = ABF(o, [128, 4, 4, 128]); o += 4096
        NPT = 4
        vraw = AF32(o, [128, T])
        pt = ABF(o, [128, NPT, 512]); o += NPT * 1024
        kraw = AF32(o, [128, T])
        rd = AF32(o, [128, 2, 512]); o += 4096
        rt1 = AF32(o, [128, 2, 512]); o += 4096
        rt2 = AF32(o, [128, 2, 512]); o += 4096
        assert o <= SBYTES, o
        first_fence = {"v": fence}

        def FW():
            f = first_fence["v"]
            first_fence["v"] = []
            return f

        P.add("dve", lambda e, kz=kz: e.memset(kz, 0.0), w=FW() + [akey("kz_init")])
        P.add("dve", lambda e, ckz=ckz: e.memset(ckz, 0.0), r=["kz_init"], w=[akey("ckz_init")])
        P.add("dve", lambda e, vaug=vaug: e.memset(vaug, 1.0), r=["kz_init"], w=[akey("vaug_init")])
        P.add("dve", lambda e, cvaug=cvaug: e.memset(cvaug, 1.0), r=["kz_init"], w=[akey("cvaug_init")])
        dma("sp", ckf, ck_d[l].rearrange("(k p) n -> p k n", p=128), ["kz_init"], [akey("ckf")], "ldc0")
        for g_ in range(2):
            for e__ in range(2):
                dma("pool", cvaug[:, g_ * 2 + e__, :, e__ * 64:(e__ + 1) * 64],
                    cv_d[l][:, g_ * 64:(g_ + 1) * 64].rearrange("(k p) n -> p k n", p=128), ["cvaug_init"],
                    [akey(("cvaug", g_ * 2 + e__))], "ldv%d" % (g_ * 2 + e__))
        hkeys = lambda th: [("hT", c, th) for c in range(8)]
        pend = {}
        wcur = {"j": -1}

        def win_chunk(cname, sgsel=None):
            ci = WIN_CHUNKS.index(cname)
            col = ci * 128
            j = [i for i, (c0, nc_) in enumerate(WIN_TILES) if c0 <= col < c0 + nc_][0]
            off = col - WIN_TILES[j][0]
            if j != wcur["j"]:
                s_cur, wk_c = w_next(("win", l, j))
                wcur["j"] = j
                wcur["wk"] = wk_c
                wcur["wv"] = slot_view(s_cur, 8, WIN_TILES[j][1])
            wk_cur, wv_cur = wcur["wk"], wcur["wv"]
            banks = []
            for th in range(2):
                b = next_bank()
                banks.append(b)
                mm_group(b, [(wv_cur[:, kc, off:off + 128], hT[:, kc, th * 512:(th + 1) * 512]) for kc in range(8)],
                         r=wk_cur + hkeys(th))
            pend[cname] = banks
            if cname.startswith("qp") or cname in ("kAp", "kBp"):
                base = cname.replace("p", "")
                for th in range(2):
                    b0 = pend[base][th]
                    b1 = banks[th]
                    tsl = slice(th * 512, (th + 1) * 512)
                    t1 = rt1[:, th, :]
                    t2 = rt2[:, th, :]
                    dve_tt(t1, PB(b0), ropec[:, tsl], ALU.mult, [pskey(b0), "ropec"], [akey(("rt1", th))])
                    dve_tt(t2, PB(b1), ropes[:, tsl], ALU.mult, [pskey(b1), "ropes"], [akey(("rt2", th))])
                    if base.startswith("q"):
                        c = int(base[1])
                        dve_tt(qT[:, c, tsl], t1, t2, ALU.add, [("rt1", th), ("rt2", th)], [akey(("qT", c, th))])
                    else:
                        lo_idx, hi_idx = (0, 3) if base == "kA" else (2, 1)
                        dve_tt(kz[0:64, lo_idx, tsl], t1[0:64, :], t2[0:64, :], ALU.add,
                               [("rt1", th), ("rt2", th), "kz_init"], [akey(("kz", lo_idx, th))])
                        dve_tt(kz[64:128, hi_idx, tsl], t1[64:128, :], t2[64:128, :], ALU.add,
                               [("rt1", th), ("rt2", th), "kz_init"], [akey(("kz", hi_idx, th))])
                    if base == "kA":
                        act(kraw[:, tsl], PB(b0), AF.Copy, [pskey(b0), "kz_init"], [akey(("kraw", th))])
                        for (do_, di_, so_, si_) in ((slice(64, 128), 1, slice(0, 64), 0), (slice(0, 64), 2, slice(64, 128), 3)):
                            dst_ = kz[do_, di_, tsl]
                            src_ = kz[so_, si_, tsl]
                            P.add("dve", lambda e, dst_=dst_, src_=src_: e.tensor_copy(out=dst_, in_=src_),
                                  r=[("kz", si_, th), "kz_init"], w=[akey(("kz", di_, th))])
            elif cname == "v":
                for th in range(2):
                    tsl = slice(th * 512, (th + 1) * 512)
                    act(vraw[:, tsl], PB(banks[th]), AF.Copy, [pskey(banks[th]), "kz_init"], [akey(("vraw", th))])
                for (raw, rkey, tok, tkey) in ((kraw, "kraw", ktok, "ktok"), (vraw, "vraw", vtok, "vtok")):
                    for th in range(2):
                        b = next_bank()

                        def fn(e, raw=raw, th=th, b=b):
                            ins = None
                            for q in range(4):
                                blk = th * 4 + q
                                ins = e.transpose(out=PB(b)[:, q * 128:(q + 1) * 128],
                                                  in_=raw[:, blk * 128:(blk + 1) * 128], identity=ident[:])
                            return ins
                        P.add("pe", fn, r=[(rkey, th), "ident"], w=[pskey(b)])
                        src = PB(b).rearrange("p (q t) -> p q t", q=4)
                        dst = tok[:, th * 4:(th + 1) * 4, :]
                        P.add("dve", lambda e, dst=dst, src=src: e.tensor_copy(out=dst, in_=src),
                              r=[pskey(b), "kz_init"], w=[akey((tkey, th))])
                        if tkey == "vtok":
                            for g_ in range(2):
                                for e__ in range(2):
                                    dstb = vaug[:, g_ * 2 + e__, th * 4:(th + 1) * 4, e__ * 64:(e__ + 1) * 64]
                                    srcb = src[:, :, g_ * 64:(g_ + 1) * 64]
                                    if e__ == 0:
                                        P.add("act", lambda e, dstb=dstb, srcb=srcb: e.activation(
                                            out=dstb, in_=srcb, func=AF.Copy),
                                            r=[pskey(b), "vaug_init"], w=[akey(("vaug", g_ * 2 + e__, th))])
                                    else:
                                        P.add("dve", lambda e, dstb=dstb, srcb=srcb: e.tensor_copy(out=dstb, in_=srcb),
                                              r=[pskey(b), "vaug_init"], w=[akey(("vaug", g_ * 2 + e__, th))])
                dma("sp", nk_d[l].rearrange("(b p) n -> p b n", p=128), ktok, [("ktok", 0), ("ktok", 1)], [], "st_k")
                dma("sp", nv_d[l].rearrange("(b p) n -> p b n", p=128), vtok, [("vtok", 0), ("vtok", 1)], [], "st_v")
            elif cname.startswith("xb"):
                c = int(cname[2])
                for th in range(2):
                    src = PB(banks[th]).rearrange("p (s t) -> p s t", s=2)
                    dst = xbpad[:, c, :].rearrange("p (s t) -> p s t", s=NSEG)[:, 2 * th:2 * th + 2, 2:2 + SEG]
                    P.add("act", lambda e, dst=dst, src=src: e.activation(out=dst, in_=src, func=AF.Copy),
                          r=[pskey(banks[th]), "xbpad_init"], w=[("xbpad", c, th)])
                xv = xbpad[:, c, :].rearrange("p (s t) -> p s t", s=NSEG)
                dve_ts(xv[:, 1:4, 0:2], xv[:, 0:3, SEG:SEG + 2], flag[:, 0:1], ALU.mult,
                       [("xbpad", c, 0), ("xbpad", c, 1), "flag"], [("xbpad_h", c, 0)])
                dve_ts(xv[:, 0:3, SEG + 2:SEG + 3], xv[:, 1:4, 2:3], flag[:, 0:1], ALU.mult,
                       [("xbpad", c, 0), ("xbpad", c, 1), "flag"], [("xbpad_h", c, 1)])
            elif cname.startswith("g"):
                c = int(cname[1])
                ab = pend["a%d" % c]
                for th in range(2):
                    sg, sgk, sgx = sgsel(th)
                    act(sg, PB(banks[th]), AF.Sigmoid, [pskey(banks[th])], sgx + [akey(sgk)])
                    src = PB(ab[th]).rearrange("p (s t) -> p s t", s=2)
                    dst = upad[:, c, :].rearrange("p (s t) -> p s t", s=NSEG)[:, 2 * th:2 * th + 2, 15:15 + SEG]
                    sgv = sg.rearrange("p (s t) -> p s t", s=2)
                    P.add("dve", lambda e, dst=dst, src=src, sgv=sgv: e.tensor_tensor(out=dst, in0=src, in1=sgv,
                                                                                       op=ALU.mult),
                          r=[pskey(ab[th]), sgk, "upad_init"], w=[("upad", c, th)])
                uv = upad[:, c, :].rearrange("p (s t) -> p s t", s=NSEG)
                dve_ts(uv[:, 1:4, 0:15], uv[:, 0:3, SEG:SEG + 15], flag[:, 0:1], ALU.mult,
                       [("upad", c, 0), ("upad", c, 1), "flag"], [("upad_h", c, 0)])
                dve_ts(uv[:, 0:3, SEG + 15:SEG + 30], uv[:, 1:4, 15:30], flag[:, 0:1], ALU.mult,
                       [("upad", c, 0), ("upad", c, 1), "flag"], [("upad_h", c, 1)])
        for cname in WIN_CHUNKS[:11]:
            win_chunk(cname)
        b = next_bank()

        def fn(e, srcb=ckf, b=b):
            ins = None
            for kc in range(4):
                ins = e.transpose(out=PB(b)[:, kc * 128:(kc + 1) * 128], in_=srcb[:, kc, :], identity=ident[:])
            return ins
        P.add("pe", fn, r=["ckf", "ident"], w=[pskey(b)])
        for (do_, di_, so_) in ((slice(0, 64), 0, slice(0, 64)), (slice(64, 128), 3, slice(64, 128)),
                                (slice(64, 128), 1, slice(0, 64)), (slice(0, 64), 2, slice(64, 128))):
            act(ckz[do_, di_, :], PB(b)[so_, :], AF.Copy, [pskey(b), "ckz_init"], [akey(("ckz", di_))])

        if phase_end("B%d" % l):
            tap_list.extend([("qT", qT, [128, 4, T], BF16), ("kz", kz, [128, 4, T], BF16),
                             ("ckz", ckz, [128, 4, 512], BF16)])
            break

        sched = []
        for h in range(8):
            for th in range(2):
                tl = [("ctx", kc) for kc in range(4)] + [("band",) + bt for bt in band_tiles(th)]
                for ti, t_ in enumerate(tl):
                    sched.append((h, th, ti, len(tl), t_))
        LAG = 3
        st_bank = {}
        it_idx = {}
        for i, (h, th) in enumerate([(h, th) for h in range(8) for th in range(2)]):
            it_idx[(h, th)] = i

        def emit_S(gi):
            h, th, ti, ntl, t_ = sched[gi]
            c, e_, g = h // 2, h % 2, h // 4
            b = gi % 4
            st_bank[gi] = b
            if t_[0] == "ctx":
                kc = t_[1]
                lt = ckz[:, g * 2 + e_, kc * 128:(kc + 1) * 128]
                q0, q1 = th * 512, (th + 1) * 512
                mk = None
                rk = [("ckz", g * 2 + e_), "ckz_init"]
            else:
                _, kb, jl, jh = t_
                lt = kz[:, g * 2 + e_, kb * 128:(kb + 1) * 128]
                q0, q1 = jl * 128, (jh + 1) * 128
                mo = MASK_OFF[(th, kb)]
                mk = maskb[:, mo:mo + (q1 - q0)]
                rk = [("kz", g * 2 + e_, kb // 4), "kz_init"]
            n = q1 - q0
            rk += [("qT", c, th), "maskb", "identb"]
            slot = gi % NPT
            if mk is None:
                mm_group(b, [(lt, qT[:, c, q0:q1])], r=rk, cols=(0, n))
                act(pt[:, slot, 0:n], PB(b)[:, 0:n], AF.Exp, [pskey(b), "ctxb"], [akey(("pt", slot))], scale=0.125,
                    bias=ctxb[:, 0:1])
            else:
                mm_group(b, [(lt, qT[:, c, q0:q1]), (identb[:], mk)], r=rk, cols=(0, n))
                act(pt[:, slot, 0:n], PB(b)[:, 0:n], AF.Exp, [pskey(b)], [akey(("pt", slot))], scale=0.125)

        def emit_PV(gi):
            h, th, ti, ntl, t_ = sched[gi]
            c, e_, g = h // 2, h % 2, h // 4
            par = it_idx[(h, th)] % 2
            ab = 4 + par
            slot = gi % NPT
            if t_[0] == "ctx":
                kc = t_[1]
                vv = cvaug[:, g * 2 + e_, kc, :]
                q0, q1 = 0, 512
                rk = [("cvaug", g * 2 + e_), "cvaug_init"]
            else:
                _, kb, jl, jh = t_
                vv = vaug[:, g * 2 + e_, kb, :]
                q0, q1 = jl * 128 - th * 512, (jh + 1) * 128 - th * 512
                rk = [("vaug", g * 2 + e_, kb // 4), "vaug_init"]
            n = q1 - q0
            first, last = (ti == 0), (ti == ntl - 1)
            pr = slice(e_ * 64, e_ * 64 + 64)
            po = slice((1 - e_) * 64, (1 - e_) * 64 + 64)
            ptv = pt[:, slot, 0:n]
            o_a = PB(ab)[:, q0:q1]

            def fn(e, vv=vv, ptv=ptv, o_a=o_a, first=first, last=last):
                return e.matmul(o_a, vv, ptv, start=first, stop=last)
            P.add("pe", fn, r=rk + [("pt", slot)], w=[pskey(ab)])
            if last:
                r_ = rd[pr, par, :]
                dve_ts(r_, PB(ab)[po, :], esink[pr, h:h + 1], ALU.add, [pskey(ab), "esink"], [akey(("rd", par))])
                P.add("dve", lambda e, r_=r_: e.reciprocal(out=r_, in_=r_), r=[("rd", par)], w=[("rd", par)])
                dve_tt(attnT[pr, c, th * 512:(th + 1) * 512], PB(ab)[pr, :], r_, ALU.mult,
                       [pskey(ab), ("rd", par)], [("attnT", c, th, e_)])

        NT = len(sched)
        modj = 4
        for gi in range(NT + LAG):
            if gi < NT:
                emit_S(gi)
                if sched[gi][2] == 0 and it_idx[(sched[gi][0], sched[gi][1])] % 2 == 1 and modj < 12:
                    mod_tile(l, modj)
                    modj += 1
            if gi - LAG >= 0:
                emit_PV(gi - LAG)
        assert modj == 12
        mod_finish_B(l)
        if phase_end("C%d" % l):
            break

        att_fence = list(arena_keys)
        del arena_keys[:]
        first_fence["v"] = []
        o = 0
        xl = AF32(o, [128, T]); o += 4096
        xlb = ABF(o, [128, T]); o += 2048
        Rb = AF32(o, [128, T]); o += 4096
        Ab = AF32(o, [128, T]); o += 4096
        Sb = AF32(o, [128, T]); o += 4096
        Ib = AF32(o, [128, T]); o += 4096
        H0 = AF32(o, [128, T]); o += 4096
        H1 = Rb
        lru_end = o
        cdg = ABF(o, [128, 2, 31, 128]); o += 2 * 31 * 256
        cvo = AF32(o, [128, 4, T]); o += 16384
        assert o <= SBYTES, o

        def cdg_build(c, fence=()):
            o0 = _VOFF["cw"] + c
            wv_ = vecs[:, l, o0:o0 + 4 * 31:4].unsqueeze(2).broadcast_to([128, 31, 128])
            iv_ = identb[:].unsqueeze(1).broadcast_to([128, 31, 128])
            dve_tt(cdg[:, c % 2, :, :], iv_, wv_, ALU.mult, ["identb", "vecs"],
                   list(fence) + [akey(("cdg", c % 2, k)) for k in range(31)], eng="pool")

        def conf_conv(c):
            uv = upad[:, c, :].rearrange("p (s t) -> p s t", s=NSEG)
            for th in range(2):
                b = next_bank()
                mm_group(b, [(cdg[:, c % 2, k, :], uv[:, 2 * th:2 * th + 2, k:k + SEG]) for k in range(31)],
                         r=[("upad", c, 0), ("upad", c, 1), ("upad_h", c, 0), ("upad_h", c, 1)] +
                           [("cdg", c % 2, k) for k in range(31)])
                tsl = slice(th * 512, (th + 1) * 512)
                dve_ts(cvo[:, c, tsl], PB(b), V(l, "cb", c, 1), ALU.add, [pskey(b), "vecs", ("cdg", 0, 0)],
                       [akey(("cvo", c, th))])
            if c + 2 < 4:
                cdg_build(c + 2)

        cdg_build(0, att_fence)
        cdg_build(1)
        cf32 = confT[:].bitcast(F32)
        A1 = cf32[:, 0:2, :].rearrange("p a b -> p (a b)")
        S1 = cf32[:, 2:4, :].rearrange("p a b -> p (a b)")
        sg_first = {"v": True}

        def sgsel(th):
            fx = list(att_fence) if sg_first["v"] else []
            sg_first["v"] = False
            return Sb[:, th * 512:(th + 1) * 512], ("Sg", th), fx + [("S", 0)]
        for cname in ("xb0", "xb1", "xb2", "xb3"):
            win_chunk(cname)
        for c in range(4):
            win_chunk("a%d" % c, sgsel)
            win_chunk("g%d" % c, sgsel)
            xv = xbpad[:, c, :].rearrange("p (s t) -> p s t", s=NSEG)
            cb = []
            for th in range(2):
                b = next_bank()
                cb.append(b)
                mm_group(b, [(lrudg[:, k * 4 + c, :], xv[:, 2 * th:2 * th + 2, k:k + SEG]) for k in range(4)],
                         r=[("xbpad", c, 0), ("xbpad", c, 1), ("xbpad_h", c, 0), ("xbpad_h", c, 1)] +
                           [("lrudg", k * 4 + c) for k in range(4)])
            for th in range(2):
                tsl = slice(th * 512, (th + 1) * 512)
                act(xl[:, tsl], PB(cb[th]), AF.Identity, [pskey(cb[th]), "vecs"],
                    (att_fence if (c == 0 and th == 0) else []) + [akey(("xl", th))], bias=V(l, "lcb", c, 1))
                xs_, xd_ = xl[:, tsl], xlb[:, tsl]
                P.add("dve", lambda e, xs_=xs_, xd_=xd_: e.tensor_copy(out=xd_, in_=xs_), r=[("xl", th)],
                      w=[akey(("xlb", th))])
            for d in range(2):
                Hd = H0 if d == 0 else H1
                hk = "H%d" % d
                gb = {}
                for gt in range(2):
                    for th in range(2):
                        b = next_bank()
                        gb[(gt, th)] = b
                        mm_group(b, [(lruw[:, (d * 2 + gt) * 4 + c, :], xlb[:, th * 512:(th + 1) * 512])],
                                 r=["lruw", ("xlb", th)])
                RK = [akey(("R", 0)), akey(("R", 1))]
                for th in range(2):
                    tsl = slice(th * 512, (th + 1) * 512)
                    act(Rb[:, tsl], PB(gb[(0, th)]), AF.Tanh, [pskey(gb[(0, th)]), ("der", 6)],
                        [("R", th)] + ([("H1", s_) for s_ in range(NSEG)] if th == 0 else []),
                        bias=der[:, 48 + d * 4 + c:49 + d * 4 + c], scale=0.5)
                    act(Ib[:, tsl], PB(gb[(1, th)]), AF.Tanh, [pskey(gb[(1, th)]), ("der", 6)], [akey(("I", th))],
                        bias=der[:, 56 + d * 4 + c:57 + d * 4 + c], scale=0.5)
                clh = der[:, 32 + d * 4 + c:33 + d * 4 + c]
                act(Rb, Rb, AF.Identity, RK + [("der", 4)], RK, bias=clh, scale=clh)
                Ab_, Sb_ = (Ab, Sb) if d == 0 else (A1, S1)
                kA, kS = akey(("A", d)), akey(("S", d))
                act(Ab_, Rb, AF.Exp, RK, [kA] + ([("confT", c_) for c_ in range(4)] if d == 1 else []))
                act(Sb_, Rb, AF.Exp, RK, [kS, ("Sg", 0), ("Sg", 1)], scale=2.0)
                act(Sb_, Sb_, AF.Sqrt, [kS], [kS], bias=0.25, scale=-0.25)
                IK = [("I", 0), ("I", 1)]
                dve_stt(Ib, Ib, 1.0, xl, ALU.add, ALU.mult, IK + [("xl", 0), ("xl", 1)], IK)
                dve_tt(Sb_, Sb_, Ib, ALU.mult, [kS] + IK, [kS])
                order = range(NSEG) if d == 0 else range(NSEG - 1, -1, -1)
                prev = None
                for s_ in order:
                    seg = slice(s_ * SEG, (s_ + 1) * SEG)
                    if prev is None:
                        init = h0t[:, l * 8 + d * 4 + c:l * 8 + d * 4 + c + 1]
                        ik = ["h0t"]
                    else:
                        col = (prev + 1) * SEG - 1 if d == 0 else prev * SEG
                        init = initb[:, s_:s_ + 1] if d == 0 else initb[:, 4 + s_:5 + s_]
                        dve_ts(init, Hd[:, col:col + 1], flag[:, 0:1], ALU.mult, [akey((hk, prev)), "flag"],
                               [("initb", d, s_)])
                        ik = [("initb", d, s_)]
                    if d == 0:
                        o_, a_, b_ = Hd[:, seg], Ab_[:, seg], Sb_[:, seg]
                    else:
                        lo, hi = s_ * SEG, (s_ + 1) * SEG
                        o_ = Hd[:, lo:hi][:, ::-1]
                        a_ = Ab_[:, lo:hi][:, ::-1]
                        b_ = Sb_[:, lo:hi][:, ::-1]
                    P.add("dve", lambda e, o_=o_, a_=a_, b_=b_, init=init: e.tensor_tensor_scan(
                        out=o_, data0=a_, data1=b_, initial=init, op0=ALU.mult, op1=ALU.add),
                        r=[kA, kS] + ik, w=[akey((hk, s_))] + (RK if d == 1 else []))
                    prev = s_
                hv = Hd.rearrange("p (s t) -> p s t", s=NSEG)
                colsel = SEG - 1 if d == 0 else 0
                fv = fin[:, :].rearrange("p (s x) -> p s x", s=NSEG)[:, :, d * 4 + c:d * 4 + c + 1]
                P.add("dve", lambda e, fv=fv, hv=hv, colsel=colsel: e.tensor_copy(out=fv, in_=hv[:, :, colsel:colsel + 1]),
                      r=[(hk, s_) for s_ in range(NSEG)], w=[("fin", d, c)])
            dve_tt(lruT[:, c, :], H0, H1, ALU.add, [("H0", s_) for s_ in range(4)] + [("H1", s_) for s_ in range(4)],
                   [("lruT", c)])
            conf_conv(c)
        fb = next_bank()
        P.add("pe", lambda e, fb=fb: e.transpose(out=PB(fb)[0:32, 0:128], in_=fin[:, :], identity=ident[:]),
              r=[("fin", d, c) for d in range(2) for c in range(4)] + ["ident"], w=[pskey(fb)])
        act(fint[:, :], PB(fb)[0:32, 0:128], AF.Copy, [pskey(fb)], ["fint"])
        dma("sp", nh_d[l], fint[:, :], ["fint"], [], "st_h")
        if phase_end("D%d" % l):
            tap_list.extend([("lruT", lruT[:], [128, 4, T], BF16)])
            break

        lru_fence = [k_ for k_ in arena_keys if not (isinstance(k_, tuple) and k_[0] in ("cvo", "cdg"))]
        first_fence["v"] = lru_fence
        o = 0
        mean = AF32(o, [128, T]); o += 4096
        rstd = AF32(o, [128, T]); o += 4096
        sqb = ABF(o, [128, 2, T]); o += 4096
        cvb = ABF(o, [128, 2, T]); o += 4096
        assert o <= lru_end
        ln_stats_and_norm = None

        def layer_norm(src_fn, nch, keys_fn, eps, tag):
            sb_ = [next_bank(), next_bank(), next_bank(), next_bank()]
            for c in range(nch):
                slot = c % 2
                fw_ = FW()
                cvs_ = cvb[:, slot, :]
                P.add("dve", lambda e, cvs_=cvs_, src_=src_fn(c): e.tensor_copy(out=cvs_, in_=src_),
                      r=keys_fn(c), w=fw_ + [akey(("cvb", slot))])
                act(sqb[:, slot, :], src_fn(c), AF.Square, keys_fn(c), fw_ + [akey(("sqb", slot))])
                for th in range(2):
                    tsl = slice(th * 512, (th + 1) * 512)
                    r1_ = cvb[:, slot, tsl]
                    r2_ = sqb[:, slot, tsl]

                    def fn(e, c=c, th=th, r1_=r1_, r2_=r2_, sb_=sb_):
                        e.matmul(PB(sb_[th]), onesb[:, :], r1_, start=(c == 0), stop=(c == nch - 1))
                        return e.matmul(PB(sb_[2 + th]), onesb[:, :], r2_, start=(c == 0), stop=(c == nch - 1))
                    P.add("pe", fn, r=[("cvb", slot), ("sqb", slot), "onesb"], w=[pskey(sb_[th]), pskey(sb_[2 + th])])
            inv = 1.0 / (nch * 128)
            for th in range(2):
                tsl = slice(th * 512, (th + 1) * 512)
                act(mean[:, tsl], PB(sb_[th]), AF.Copy, [pskey(sb_[th]), ("sqb", 0), ("sqb", 1)], [akey(("mean", th))],
                    scale=inv)
                act(rstd[:, tsl], PB(sb_[th]), AF.Square, [pskey(sb_[th])], [akey(("rstd", th))], scale=inv)
                dve_stt(rstd[:, tsl], PB(sb_[2 + th]), inv, rstd[:, tsl], ALU.mult, ALU.subtract,
                        [pskey(sb_[2 + th]), ("rstd", th)], [("rstd", th)])
                dve_ts(rstd[:, tsl], rstd[:, tsl], 0.0, ALU.max, [("rstd", th)], [("rstd", th)], s2=eps, op1=ALU.add)
                act(rstd[:, tsl], rstd[:, tsl], AF.Ln, [("rstd", th)], [("rstd", th)])
                act(rstd[:, tsl], rstd[:, tsl], AF.Exp, [("rstd", th)], [("rstd", th)], scale=-0.5)

        layer_norm(lambda c: cvo[:, c, :], 4, lambda c: [("cvo", c, 0), ("cvo", c, 1)], LN_EPS, "conf")
        MK = [("mean", 0), ("mean", 1)]
        RSK = [("rstd", 0), ("rstd", 1)]
        for c in range(4):
            ck_ = [("cvo", c, 0), ("cvo", c, 1)]
            dve_tt(cvo[:, c, :], cvo[:, c, :], mean, ALU.subtract, ck_ + MK, ck_)
            dve_tt(cvo[:, c, :], cvo[:, c, :], rstd, ALU.mult, ck_ + RSK, ck_)
            act(confT[:, c, :], cvo[:, c, :], AF.Silu, ck_ + ["vecs"], [("confT", c)], bias=V(l, "clb", c, 1),
                scale=V(l, "clg", c, 1))
        if phase_end("E%d" % l):
            tap_list.extend([("confT", confT[:], [128, 4, T], BF16), ("attnT", attnT[:], [128, 4, T], BF16)])
            break

        fence = list(arena_keys)
        del arena_keys[:]
        first_fence["v"] = fence
        o = 0
        mergeT = ABF(o, [128, 8, T]); o += 16384
        sgb = AF32(o, [128, 2, 3, 512]); o += 12288
        mt = AF32(o, [128, 2, 2, 512]); o += 8192
        it = 0
        for n in range(8):
            sg_, wkg = w_next(("mg", l, n))
            wg = slot_view(sg_, 24, 128)
            sb2, wkb = w_next(("br", l, n), la=NSLOT - 2)
            wb = slot_view(sb2, 12, 128)
            for th in range(2):
                tsl = slice(th * 512, (th + 1) * 512)
                par = it % 2
                it += 1
                G = []
                for b_ in range(3):
                    bk = next_bank()
                    G.append(bk)
                    mm_group(bk, [(wg[:, b_ * 8 + kc, :], hT[:, kc, tsl]) for kc in range(8)], r=wkg + hkeys(th))
                Pb = []
                for b_, (src, skf) in ((0, (attnT, lambda kc: [("attnT", kc, th, 0), ("attnT", kc, th, 1)])),
                                       (1, (lruT, lambda kc: [("lruT", kc)])),
                                       (2, (confT, lambda kc: [("confT", kc)]))):
                    bk = next_bank()
                    Pb.append(bk)
                    mm_group(bk, [(wb[:, 4 * b_ + kc, :], src[:, kc, tsl]) for kc in range(4)],
                             r=wkb + [k_ for kc in range(4) for k_ in skf(kc)])
                for b_ in range(3):
                    act(sgb[:, par, b_, :], PB(G[b_]), AF.Sigmoid, [pskey(G[b_]), "vecs"],
                        FW() + [akey(("sgb", par, b_))], bias=V(l, "bm", b_ * 8 + n, 1))
                dve_tt(mt[:, par, 0, :], PB(Pb[0]), sgb[:, par, 0, :], ALU.mult, [pskey(Pb[0]), ("sgb", par, 0)],
                       [akey(("mt", par, 0))])
                dve_tt(mt[:, par, 1, :], PB(Pb[1]), sgb[:, par, 1, :], ALU.mult, [pskey(Pb[1]), ("sgb", par, 1)],
                       [akey(("mt", par, 1))])
                dve_tt(mt[:, par, 0, :], mt[:, par, 0, :], mt[:, par, 1, :], ALU.add,
                       [("mt", par, 0), ("mt", par, 1)], [("mt", par, 0)])
                dve_tt(mt[:, par, 1, :], PB(Pb[2]), sgb[:, par, 2, :], ALU.mult, [pskey(Pb[2]), ("sgb", par, 2)],
                       [("mt", par, 1)])
                dve_tt(mergeT[:, n, tsl], mt[:, par, 0, :], mt[:, par, 1, :], ALU.add,
                       [("mt", par, 0), ("mt", par, 1)], [akey(("mergeT", n, th))])
            if l + 1 < DEPTH and n % 2 == 1:
                mod_tile(l + 1, n // 2)
        if l + 1 < DEPTH:
            mod_finish_A(l + 1)
        if phase_end("F%d" % l):
            tap_list.extend([("mergeT", mergeT, [128, 8, T], BF16)])
            break

        o = 16384 + 12288 + 8192
        mean = AF32(o, [128, T]); o += 4096
        rstd = AF32(o, [128, T]); o += 4096
        sqb = ABF(o, [128, 2, T]); o += 4096
        cvb = ABF(o, [128, 2, T]); o += 4096
        assert o <= SBYTES

        pending_mod = []

        def proj_residual(kind, src, nk, srckeys, gcol):
            for n in range(8):
                if kind == "out":
                    if n % 4 == 0:
                        s_, wk_ = w_next(("out", l, n // 4))
                        wv_ = slot_view(s_, 8, 512)
                    lts = [wv_[:, kc, (n % 4) * 128:(n % 4 + 1) * 128] for kc in range(nk)]
                else:
                    s_, wk_ = w_next(("dn", l, n))
                    wv_ = slot_view(s_, NFF, 128)
                    lts = [wv_[:, kc, :] for kc in range(nk)]
                    if l + 1 < DEPTH and n % 2 == 1:
                        pending_mod.append(n // 2)
                for th in range(2):
                    tsl = slice(th * 512, (th + 1) * 512)
                    bk = next_bank()
                    rh = (lambda kc: src(kc)[:, tsl]) if callable(src) else (lambda kc: src[:, kc, tsl])
                    mm_group(bk, [(lts[kc], rh(kc)) for kc in range(nk)], r=wk_ + srckeys(th))
                    dve_stt(xT[:, n, tsl], PB(bk), der[:, gcol + n:gcol + n + 1], xT[:, n, tsl], ALU.mult, ALU.add,
                            [pskey(bk), ("xT", n, th), ("der", 1), ("der", 3)], [("xT", n, th)])
                while pending_mod:
                    mod_tile(l + 1, pending_mod.pop(0))


        def ln_apply(gname, bname, nxt=False, h2=False):
            layer_norm(lambda c: xT[:, c, :], 8, lambda c: [("xT", c, 0), ("xT", c, 1)], EPS_F, gname)
            for c in range(8):
                xk = [("xT", c, 0), ("xT", c, 1)]
                dve_tt(xT[:, c, :], xT[:, c, :], mean, ALU.subtract, xk + MK, xk)
                dve_tt(xT[:, c, :], xT[:, c, :], rstd, ALU.mult, xk + RSK, xk)
                act(xT[:, c, :], xT[:, c, :], AF.Identity, xk + ["vecs"], xk, bias=V(l, bname, c, 1),
                    scale=V(l, gname, c, 1))
                if nxt:
                    act(hT[:, c, :], xT[:, c, :], AF.Identity, xk + [("der", 0), "modA"],
                        [("hT", c, 0), ("hT", c, 1)], bias=modT[:, c:c + 1], scale=der[:, c:c + 1])
                if h2:
                    act(hT[:, c, :], xT[:, c, :], AF.Identity, xk + [("der", 2), "modB"],
                        [("hT", c, 0), ("hT", c, 1)], bias=modT[:, 24 + c:25 + c], scale=der[:, 16 + c:17 + c])

        def out_ln1_thmajor():
            s0_, wk0_ = w_next(("out", l, 0))
            s1_, wk1_ = w_next(("out", l, 1), la=NSLOT - 2)
            wvs = [slot_view(s0_, 8, 512), slot_view(s1_, 8, 512)]
            wks = [wk0_, wk1_]
            inv = 1.0 / 1024.0
            for th in range(2):
                tsl = slice(th * 512, (th + 1) * 512)
                for n in range(8):
                    wv_ = wvs[n // 4]
                    bk = next_bank()
                    mm_group(bk, [(wv_[:, kc, (n % 4) * 128:(n % 4 + 1) * 128], mergeT[:, kc, tsl]) for kc in range(8)],
                             r=wks[n // 4] + [("mergeT", kc, th) for kc in range(8)])
                    dve_stt(xT[:, n, tsl], PB(bk), der[:, 8 + n:9 + n], xT[:, n, tsl], ALU.mult, ALU.add,
                            [pskey(bk), ("xT", n, th), ("der", 1), ("der", 3)], [("xT", n, th)])
                b1, b2 = next_bank(), next_bank()
                for c in range(8):
                    slot = c % 2
                    cv_ = cvb[:, slot, tsl]
                    sq_ = sqb[:, slot, tsl]
                    xs_ = xT[:, c, tsl]
                    fw_ = FW()
                    P.add("dve", lambda e, cv_=cv_, xs_=xs_: e.tensor_copy(out=cv_, in_=xs_), r=[("xT", c, th)],
                          w=fw_ + [akey(("cvb", slot, th))])
                    act(sq_, xs_, AF.Square, [("xT", c, th)], fw_ + [akey(("sqb", slot, th))])

                    def fn(e, c=c, cv_=cv_, sq_=sq_, b1=b1, b2=b2):
                        e.matmul(PB(b1), onesb[:, :], cv_, start=(c == 0), stop=(c == 7))
                        return e.matmul(PB(b2), onesb[:, :], sq_, start=(c == 0), stop=(c == 7))
                    P.add("pe", fn, r=[("cvb", slot, th), ("sqb", slot, th), "onesb"], w=[pskey(b1), pskey(b2)])
                mk_, rk_ = akey(("mean", th)), akey(("rstd", th))
                act(mean[:, tsl], PB(b1), AF.Copy, [pskey(b1)], [mk_], scale=inv)
                act(rstd[:, tsl], PB(b1), AF.Square, [pskey(b1)], [rk_], scale=inv)
                dve_stt(rstd[:, tsl], PB(b2), inv, rstd[:, tsl], ALU.mult, ALU.subtract, [pskey(b2), rk_], [rk_])
                dve_ts(rstd[:, tsl], rstd[:, tsl], 0.0, ALU.max, [rk_], [rk_], s2=EPS_F, op1=ALU.add)
                act(rstd[:, tsl], rstd[:, tsl], AF.Ln, [rk_], [rk_])
                act(rstd[:, tsl], rstd[:, tsl], AF.Exp, [rk_], [rk_], scale=-0.5)
                for c in range(8):
                    xk = [("xT", c, th)]
                    xc_ = xT[:, c, tsl]
                    dve_tt(xc_, xc_, mean[:, tsl], ALU.subtract, xk + [mk_], xk)
                    dve_tt(xc_, xc_, rstd[:, tsl], ALU.mult, xk + [rk_], xk)
                    act(xc_, xc_, AF.Identity, xk + ["vecs"], xk, bias=V(l, "l1b", c, 1), scale=V(l, "l1g", c, 1))
                    act(hT[:, c, tsl], xc_, AF.Identity, xk + [("der", 2), "modB"], [("hT", c, th)],
                        bias=modT[:, 24 + c:25 + c], scale=der[:, 16 + c:17 + c])

        out_ln1_thmajor()
        if phase_end("G%d" % l):
            break

        fence = list(arena_keys)
        del arena_keys[:]
        first_fence["v"] = fence
        o = 0
        gtail = ABF(o, [128, 2, T]); o += 4096

        def gTc(c):
            if c < 4:
                return lruT[:, c, :]
            if c < 8:
                return confT[:, c - 4, :]
            if c < 12:
                return attnT[:, c - 8, :]
            if c < 16:
                return xbpad2[:, (c - 12) * T:(c - 11) * T]
            if c < 20:
                return upad2[:, (c - 16) * T:(c - 15) * T]
            return gtail[:, c - 20, :]
        NUR = 2
        ur = ABF(o, [128, NUR, 2, NSEG * FPP]); o += NUR * 2 * NSEG * FPP * 2
        cen = AF32(o, [128, NUR, 4, 512]); o += NUR * 4 * 2048
        gl = ABF(o, [128, 2, 512]); o += 2048
        assert o <= SBYTES, o
        oldk = [("lruT", c_) for c_ in range(4)] + [("confT", c_) for c_ in range(4)] + \
               [("attnT", c_, th_, e__) for c_ in range(4) for th_ in range(2) for e__ in range(2)] + \
               [(nm_, c_, th_) for nm_ in ("xbpad", "xbpad_h", "upad", "upad_h") for c_ in range(4) for th_ in range(2)]
        P.add("pool", lambda e, ur=ur: e.memset(ur, 0.0), w=FW() + oldk + [akey("ur_init")])
        ffn_items = []
        for j in range(11):
            for pi in range(2):
                ffn_items.append((j, pi, j * 2 + pi))

        def ffn_up(item):
            j, pi, c = item
            if pi == 0:
                s_, wk_ = w_next(("up", l, j))
                ffn_up.cur = (slot_view(s_, 8, 512), wk_)
            wv_, wk_ = ffn_up.cur
            slot = c % NUR
            for th in range(2):
                for hv in range(2):
                    bk = (c % 2) * 4 + (hv * 2 + th)
                    co = hv * 256 + pi * 128
                    mm_group(bk, [(wv_[:, kc, co:co + 128], hT[:, kc, th * 512:(th + 1) * 512]) for kc in range(8)],
                             r=wk_ + hkeys(th))
                    src = PB(bk).rearrange("p (s t) -> p s t", s=2)
                    dst = ur[:, slot, hv, :].rearrange("p (s t) -> p s t", s=NSEG)[:, 2 * th:2 * th + 2, 1:1 + SEG]
                    P.add("act", lambda e, dst=dst, src=src: e.activation(out=dst, in_=src, func=AF.Copy),
                          r=[pskey(bk), "ur_init"], w=[akey(("ur", slot, hv, th))])
                    act(cen[:, slot, hv * 2 + th, :], PB(bk), AF.Identity, [pskey(bk), "vecs", "ur_init"],
                        [akey(("cen", slot, hv, th))], bias=V(l, "fcb", hv * 22 + c, 1),
                        scale=V(l, "fcw", 44 + hv * 22 + c, 1))

        def ffn_halo(item):
            j, pi, c = item
            slot = c % NUR
            for hv in range(2):
                uvv = ur[:, slot, hv, :].rearrange("p (s t) -> p s t", s=NSEG)
                rk_ = [("ur", slot, hv, 0), ("ur", slot, hv, 1), "flag"]
                dve_ts(uvv[:, 1:4, 0:1], uvv[:, 0:3, SEG:SEG + 1], flag[:, 0:1], ALU.mult, rk_,
                       [akey(("urh", slot, hv, 0))])
                dve_ts(uvv[:, 0:3, SEG + 1:SEG + 2], uvv[:, 1:4, 1:2], flag[:, 0:1], ALU.mult, rk_,
                       [akey(("urh", slot, hv, 1))])

        def ffn_conv(item):
            j, pi, c = item
            slot = c % NUR
            for k in (0, 2):
                for th in range(2):
                    for hv in range(2):
                        uvv = ur[:, slot, hv, :].rearrange("p (s t) -> p s t", s=NSEG)
                        acc = cen[:, slot, hv * 2 + th, :].rearrange("p (s t) -> p s t", s=2)
                        ck_ = ("cen", slot, hv, th)
                        rk_ = [("ur", slot, hv, 0), ("ur", slot, hv, 1), ("urh", slot, hv, 0), ("urh", slot, hv, 1),
                               ck_, "vecs"]
                        dve_stt(acc, uvv[:, 2 * th:2 * th + 2, k:k + SEG], V(l, "fcw", k * 44 + hv * 22 + c, 1), acc,
                                ALU.mult, ALU.add, rk_, [ck_])
            for th in range(2):
                act(gl[:, th, :], cen[:, slot, th, :], AF.Gelu_apprx_tanh, [("cen", slot, 0, th)], [akey(("gl", th))])
                dve_tt(gTc(c)[:, th * 512:(th + 1) * 512], cen[:, slot, 2 + th, :], gl[:, th, :], ALU.mult,
                       [("cen", slot, 1, th), ("gl", th), "ur_init"], [akey(("gT", c, th))])

        for i in range(len(ffn_items) + 1):
            if i < len(ffn_items):
                ffn_up(ffn_items[i])
            if i >= 1:
                ffn_conv(ffn_items[i - 1])
            if i < len(ffn_items):
                ffn_halo(ffn_items[i])
        rot["i"] = 0
        if phase_end("H%d" % l):
            tap_list.extend([("gtail", gtail, [128, 2, T], BF16), ("lruT", lruT[:], [128, 4, T], BF16)])
            break
        mean = AF32(o, [128, T]); o += 4096
        rstd = AF32(o, [128, T]); o += 4096
        sqb = ABF(o, [128, 2, T]); o += 4096
        cvb = ABF(o, [128, 2, T]); o += 4096
        assert o <= SBYTES, o
        def dn_ln2_thmajor():
            nxt_ = (l + 1 < DEPTH)
            inv = 1.0 / 1024.0
            for th in range(2):
                tsl = slice(th * 512, (th + 1) * 512)
                early = {}
                if th == 0:
                    KE = NFF - 2
                    for n in range(2):
                        s_, wk_ = w_next(("dn", l, n), la=(NSLOT - 1 if n == 0 else NSLOT - 2))
                        wv_ = slot_view(s_, NFF, 128)
                        bk = next_bank()
                        oa_ = PB(bk)

                        def fnA(e, wv_=wv_, oa_=oa_, tsl=tsl):
                            ins = None
                            for kc in range(KE):
                                ins = e.matmul(oa_, wv_[:, kc, :], gTc(kc)[:, tsl], start=(kc == 0), stop=False)
                            return ins
                        P.add("pe", fnA, r=wk_ + [("gT", kc, th) for kc in range(KE)], w=[pskey(bk)])
                        early[n] = (wv_, wk_, bk)
                for n in range(8):
                    if n in early:
                        wv_, wk_, bk = early[n]
                        oa_ = PB(bk)

                        def fnB(e, wv_=wv_, oa_=oa_, tsl=tsl):
                            ins = None
                            for kc in range(NFF - 2, NFF):
                                ins = e.matmul(oa_, wv_[:, kc, :], gTc(kc)[:, tsl], start=False, stop=(kc == NFF - 1))
                            return ins
                        P.add("pe", fnB, r=wk_ + [("gT", kc, th) for kc in range(NFF - 2, NFF)], w=[pskey(bk)])
                    else:
                        s_, wk_ = w_next((("dn" if th == 0 else "dn2"), l, n))
                        wv_ = slot_view(s_, NFF, 128)
                        bk = next_bank()
                        mm_group(bk, [(wv_[:, kc, :], gTc(kc)[:, tsl]) for kc in range(NFF)],
                                 r=wk_ + [("gT", kc, th) for kc in range(NFF)])
                    dve_stt(xT[:, n, tsl], PB(bk), der[:, 24 + n:25 + n], xT[:, n, tsl], ALU.mult, ALU.add,
                            [pskey(bk), ("xT", n, th), ("der", 1), ("der", 3)], [("xT", n, th)])
                b1, b2 = next_bank(), next_bank()
                for c in range(8):
                    slot = c % 2
                    cv_ = cvb[:, slot, tsl]
                    sq_ = sqb[:, slot, tsl]
                    xs_ = xT[:, c, tsl]
                    fw_ = FW()
                    P.add("dve", lambda e, cv_=cv_, xs_=xs_: e.tensor_copy(out=cv_, in_=xs_), r=[("xT", c, th)],
                          w=fw_ + [akey(("cvb", slot, th))])
                    act(sq_, xs_, AF.Square, [("xT", c, th)], fw_ + [akey(("sqb", slot, th))])

                    def fn(e, c=c, cv_=cv_, sq_=sq_, b1=b1, b2=b2):
                        e.matmul(PB(b1), onesb[:, :], cv_, start=(c == 0), stop=(c == 7))
                        return e.matmul(PB(b2), onesb[:, :], sq_, start=(c == 0), stop=(c == 7))
                    P.add("pe", fn, r=[("cvb", slot, th), ("sqb", slot, th), "onesb"], w=[pskey(b1), pskey(b2)])
                mk_, rk_ = akey(("mean", th)), akey(("rstd", th))
                act(mean[:, tsl], PB(b1), AF.Copy, [pskey(b1)], [mk_], scale=inv)
                act(rstd[:, tsl], PB(b1), AF.Square, [pskey(b1)], [rk_], scale=inv)
                dve_stt(rstd[:, tsl], PB(b2), inv, rstd[:, tsl], ALU.mult, ALU.subtract, [pskey(b2), rk_], [rk_])
                dve_ts(rstd[:, tsl], rstd[:, tsl], 0.0, ALU.max, [rk_], [rk_], s2=EPS_F, op1=ALU.add)
                act(rstd[:, tsl], rstd[:, tsl], AF.Ln, [rk_], [rk_])
                act(rstd[:, tsl], rstd[:, tsl], AF.Exp, [rk_], [rk_], scale=-0.5)
                for c in range(8):
                    xk = [("xT", c, th)]
                    xc_ = xT[:, c, tsl]
                    dve_tt(xc_, xc_, mean[:, tsl], ALU.subtract, xk + [mk_], xk)
                    dve_tt(xc_, xc_, rstd[:, tsl], ALU.mult, xk + [rk_], xk)
                    act(xc_, xc_, AF.Identity, xk + ["vecs"], xk, bias=V(l, "l2b", c, 1), scale=V(l, "l2g", c, 1))
                    if nxt_:
                        act(hT[:, c, tsl], xc_, AF.Identity, xk + [("der", 0), "modA"], [("hT", c, th)],
                            bias=modT[:, c:c + 1], scale=der[:, c:c + 1])

        dn_ln2_thmajor()
        if phase_end("I%d" % l):
            break

    if not stopped["v"]:
        fence = list(arena_keys)
        del arena_keys[:]
        YT = [AF32(i * 4096, [128, 1024]) for i in range(8)]
        YALL = AF32(0, [128, 8, 1024])
        for tbh in range(2):
            for c in range(8):
                b = next_bank()

                def fn(e, c=c, tbh=tbh, b=b):
                    ins = None
                    for q in range(4):
                        tb = tbh * 4 + q
                        ins = e.transpose(out=PB(b)[:, q * 128:(q + 1) * 128], in_=xT[:, c, tb * 128:(tb + 1) * 128],
                                          identity=ident[:])
                    return ins
                P.add("pe", fn, r=[("xT", c, tbh), "ident"], w=[pskey(b)])
                dst = YALL[:, tbh * 4:(tbh + 1) * 4, c * 128:(c + 1) * 128]
                src = PB(b).rearrange("p (q t) -> p q t", q=4)
                wk = [("yt", tbh, c)] + fence
                if c % 2 == 0:
                    P.add("act", lambda e, dst=dst, src=src: e.activation(out=dst, in_=src, func=AF.Copy),
                          r=[pskey(b)], w=wk)
                else:
                    P.add("dve", lambda e, dst=dst, src=src: e.tensor_copy(out=dst, in_=src), r=[pskey(b)], w=wk)
            for tb in range(tbh * 4, tbh * 4 + 4):
                yk = [("yt", tbh, c) for c in range(8)]
                dma("sp", y_d[tb * 128:(tb + 1) * 128, :], YT[tb], yk, [], "st_y%d" % tb)
    tap_outs = []
    for ti, (name, ap, shape, dt) in enumerate(tap_list):
        dd = dout("tap_" + name, shape, dt)
        tap_outs.append("tap_" + name)
        P.add("sp", lambda e, dd=dd, ap=ap: e.dma_start(out=dd, in_=ap), r=list(P.kw.keys()), w=[], chan="tap%d" % ti)
    if stopped["v"]:
        dd = dout("tap_xT", [128, 8, T], F32)
        P.add("sp", lambda e, dd=dd: e.dma_start(out=dd, in_=xT[:]), r=list(P.kw.keys()), w=[], chan="tapx")
        dd2 = dout("tap_hT", [128, 8, T], BF16)
        P.add("sp", lambda e, dd2=dd2: e.dma_start(out=dd2, in_=hT[:]), r=list(P.kw.keys()), w=[], chan="taph")
        dd3 = dout("tap_modT", [128, 48], F32)
        P.add("sp", lambda e, dd3=dd3: e.dma_start(out=dd3, in_=modT[:]), r=list(P.kw.keys()), w=[], chan="tapm")
    last_dma = [P.chan_last[c] for c in P.chan_last]
    fin_op = Op()
    fin_op.eng = "sp"
    fin_op.fn = lambda e: e.nop()
    fin_op.chan = None
    fin_op.sig = False
    fin_op.gid = len(P.ops)
    fin_op.deps = set(last_dma)
    P.ops.append(fin_op)

    block = es.enter_context(nc.Block())
    P.emit(nc, es, block)
    es.close()
    return nc


def _fm(v):
    v = np.asarray(v, np.float32)
    return np.ascontiguousarray(v.reshape(-1, 128).T)


def _rope_tables(active):
    c = np.ones((128, T), np.float32)
    s = np.zeros((128, T), np.float32)
    if active:
        t = np.arange(T)
        row = (t // 64).astype(np.float32)
        colp = (t % 64).astype(np.float32)
        inv = (np.float32(10000.0) ** (-np.arange(16, dtype=np.float32) / np.float32(16))).astype(np.float32)
        for p in range(128):
            dd = p % 64
            a = dd // 32
            jj = dd % 32
            f = jj % 16
            pos = row if a == 0 else colp
            ang = (pos * inv[f]).astype(np.float32)
            c[p] = np.cos(ang)
            s[p] = -np.sin(ang) if jj < 16 else np.sin(ang)
    return c, s


def _mask_table(sample):
    m = np.zeros((128, MASK_COLS), np.float32)
    if not sample:
        m[:, 0:512] = NEGM
    k = np.arange(128)[:, None]
    for th in range(2):
        for (kb, jl, jh) in band_tiles(th):
            off = MASK_OFF[(th, kb)]
            for j in range(jl, jh + 1):
                q = np.arange(128)[None, :]
                if sample:
                    ok = np.abs((kb * 128 + k) - (j * 128 + q)) <= 128
                else:
                    ok = np.broadcast_to(np.array(kb // 2 == j // 2), (128, 128))
                m[:, off + (j - jl) * 128: off + (j - jl + 1) * 128] = np.where(ok, 0.0, NEGM)
    return m


def _partner(cols64):
    cols64 = np.asarray(cols64)
    idx = np.arange(64)
    j = idx % 32
    p = np.where(j < 16, idx + 16, idx - 16)
    return cols64[p]


def _prep_shared(inp):
    sh = {}
    w_in = np.asarray(inp["w_in"], np.float32)
    cols = []
    qc = lambda h: np.arange(h * 64, (h + 1) * 64)
    kc = lambda g: 512 + np.arange(g * 64, (g + 1) * 64)
    for name in WIN_CHUNKS:
        if name.startswith("qp"):
            c = int(name[2])
            cols.append(np.concatenate([_partner(qc(2 * c)), _partner(qc(2 * c + 1))]))
        elif name.startswith("q"):
            c = int(name[1])
            cols.append(np.concatenate([qc(2 * c), qc(2 * c + 1)]))
        elif name == "kA":
            cols.append(np.concatenate([kc(0), kc(1)]))
        elif name == "kB":
            cols.append(np.concatenate([kc(1), kc(0)]))
        elif name == "kAp":
            cols.append(np.concatenate([_partner(kc(0)), _partner(kc(1))]))
        elif name == "kBp":
            cols.append(np.concatenate([_partner(kc(1)), _partner(kc(0))]))
        elif name == "v":
            cols.append(640 + np.arange(128))
        elif name.startswith("xb"):
            c = int(name[2])
            cols.append(768 + c * 128 + np.arange(128))
        elif name.startswith("a"):
            c = int(name[1])
            cols.append(1280 + c * 128 + np.arange(128))
        elif name.startswith("g"):
            c = int(name[1])
            cols.append(1280 + 512 + c * 128 + np.arange(128))
    cols = np.concatenate(cols)
    sh["w_in_ext"] = np.ascontiguousarray(w_in[:, :, cols])
    vec = np.zeros((DEPTH, 128, NV), np.float32)
    f = lambda a: np.asarray(a, np.float32)
    for l in range(DEPTH):
        def put(name, arr):
            arr = np.asarray(arr, np.float32)
            vec[l, :, _VOFF[name]:_VOFF[name] + arr.shape[1]] = arr
        put("bmod", _fm(f(inp["b_mod"])[l]))
        put("lcw", np.concatenate([_fm(f(inp["lru_conv_w"])[l, k]) for k in range(4)], axis=1))
        put("lcb", _fm(f(inp["lru_conv_b"])[l]))
        put("lba", np.concatenate([_fm(f(inp["lru_ba"])[l, d]) for d in range(2)], axis=1))
        put("lbx", np.concatenate([_fm(f(inp["lru_bx"])[l, d]) for d in range(2)], axis=1))
        put("llam", np.concatenate([_fm(f(inp["lru_lambda"])[l, d]) for d in range(2)], axis=1))
        put("cw", np.concatenate([_fm(f(inp["conf_dw_w"])[l, k]) for k in range(31)], axis=1))
        put("cb", _fm(f(inp["conf_dw_b"])[l]))
        put("clg", _fm(f(inp["conf_ln_g"])[l]))
        put("clb", _fm(f(inp["conf_ln_b"])[l]))
        put("bm", _fm(f(inp["b_merge"])[l]))
        put("l1g", _fm(f(inp["ln1_g"])[l]))
        put("l1b", _fm(f(inp["ln1_b"])[l]))
        put("fcw", np.concatenate([_fm(f(inp["ffn_conv_w"])[l, k]) for k in range(3)], axis=1))
        put("fcb", _fm(f(inp["ffn_conv_b"])[l]))
        put("l2g", _fm(f(inp["ln2_g"])[l]))
        put("l2b", _fm(f(inp["ln2_b"])[l]))
        put("sink", np.broadcast_to(f(inp["attn_sink"])[l][None, :], (128, 8)))
    sh["vecs"] = vec
    lw = np.zeros((DEPTH, 128, 16, 128), np.float32)
    for l in range(DEPTH):
        for d in range(2):
            for gt, nm in enumerate(("lru_wa", "lru_wx")):
                wsrc = f(inp[nm])[l, d]
                for c in range(4):
                    i = (d * 2 + gt) * 4 + c
                    for bb in range(2):
                        lw[l, bb * 64:(bb + 1) * 64, i, bb * 64:(bb + 1) * 64] = wsrc[2 * c + bb]
    sh["lruw"] = lw
    sh["ident"] = np.eye(128, dtype=np.float32)
    for k in ("w_mod", "w_branch", "w_merge", "w_out", "ffn_w_up", "ffn_w_down"):
        sh[k] = np.ascontiguousarray(np.asarray(inp[k], np.float32))
    return sh


def make_in_maps(inp):
    sh = _prep_shared(inp)
    xs = np.asarray(inp["x_sample"], np.float32)
    xp = np.asarray(inp["x_prompt"], np.float32)
    ck = np.asarray(inp["cache_k"], np.float32)
    cv = np.asarray(inp["cache_v"], np.float32)
    st = np.asarray(inp["state_lru"], np.float32)
    cc = np.asarray(inp["c"], np.float32)
    cctx = np.asarray(inp["c_ctx"], np.float32)
    rc_s, rs_s = _rope_tables(True)
    rc_p, rs_p = _rope_tables(False)
    mk_s = _mask_table(True)
    mk_p = _mask_table(False)
    maps = []
    for core in range(8):
        m = dict(sh)
        if core < 4:
            b = core
            m["x"] = np.ascontiguousarray(xs[b])
            m["cond"] = _fm(cc[b])
            m["ck"] = np.ascontiguousarray(ck[b].reshape(DEPTH, 512, 128))
            m["cv"] = np.ascontiguousarray(cv[b].reshape(DEPTH, 512, 128))
            h0 = np.zeros((128, DEPTH * 8), np.float32)
            for l in range(DEPTH):
                for d in range(2):
                    h0[:, l * 8 + d * 4:l * 8 + d * 4 + 4] = _fm(st[b, l, d])
            m["h0"] = h0
            m["flag"] = np.ones((128, 1), np.float32)
            m["ropec"], m["ropes"], m["maskb"] = rc_s, rs_s, mk_s
        else:
            i = core - 4
            m["x"] = np.ascontiguousarray(xp[4 * i:4 * i + 4].reshape(T, D))
            m["cond"] = _fm(cctx)
            m["ck"] = np.zeros((DEPTH, 512, 128), np.float32)
            m["cv"] = np.zeros((DEPTH, 512, 128), np.float32)
            m["h0"] = np.zeros((128, DEPTH * 8), np.float32)
            m["flag"] = np.zeros((128, 1), np.float32)
            m["ropec"], m["ropes"], m["maskb"] = rc_p, rs_p, mk_p
        maps.append(m)
    return maps


_NC_CACHE = {}


def kernel(**inputs):
    if "nc" not in _NC_CACHE:
        _NC_CACHE["nc"] = build_program()
    nc = _NC_CACHE["nc"]
    maps = make_in_maps(inputs)
    res = run_bass_kernel_spmd(nc, maps, core_ids=list(range(8)))
    R = res.results
    y_prompt = np.zeros((16, 256, D), np.float32)
    y_sample = np.zeros((4, T, D), np.float32)
    nk = np.zeros((16, DEPTH, 256, 2, 64), np.float32)
    nv = np.zeros((16, DEPTH, 256, 2, 64), np.float32)
    nh = np.zeros((16, DEPTH, 2, 512), np.float32)
    for core in range(8):
        r = R[core]
        if core < 4:
            y_sample[core] = r["y"]
        else:
            i = core - 4
            y_prompt[4 * i:4 * i + 4] = r["y"].reshape(4, 256, D)
            for s in range(4):
                for l in range(DEPTH):
                    nk[4 * i + s, l] = r["nk"][l, s * 256:(s + 1) * 256].reshape(256, 2, 64)
                    nv[4 * i + s, l] = r["nv"][l, s * 256:(s + 1) * 256].reshape(256, 2, 64)
                    nh[4 * i + s, l] = r["nh"][l].reshape(4, 2, 512)[s]
    return (y_prompt, y_sample, nk, nv, nh)
```

```python
from contextlib import ExitStack
import numpy as np
import concourse.bass as bass
import concourse.mybir as mybir
from concourse.bass_utils import run_bass_kernel_spmd

F32 = mybir.dt.float32
BF16 = mybir.dt.bfloat16
AF = mybir.ActivationFunctionType
ALU = mybir.AluOpType

D = 1024
T = 1024
DEPTH = 2
NSEG = 4
SEG = 256
DFF = 2816
NFF = 22
ALPHA = (2 * DEPTH) ** 0.25
LN_EPS = 1e-5
EPS_F = LN_EPS / (ALPHA * ALPHA)
NEGM = -30000.0
NSLOT = 4
SLOT_ELEMS = 4096
XBP = 259
UPP = 286
FPP = 258

_VOFF = {}
_o = 0
for _n, _w in [("bmod", 48), ("lcw", 16), ("lcb", 4), ("lba", 8), ("lbx", 8), ("llam", 8), ("cw", 124),
               ("cb", 4), ("clg", 4), ("clb", 4), ("bm", 24), ("l1g", 8), ("l1b", 8), ("fcw", 132),
               ("fcb", 44), ("l2g", 8), ("l2b", 8), ("sink", 8)]:
    _VOFF[_n] = _o
    _o += _w
NV = _o

WIN_CHUNKS = ["q0", "qp0", "q1", "qp1", "q2", "qp2", "q3", "qp3", "kA", "kAp",
              "v", "xb0", "xb1", "xb2", "xb3", "a0", "g0", "a1", "g1", "a2", "g2", "a3", "g3"]
NWIN = len(WIN_CHUNKS) * 128
WIN_TILES = [(0, 512), (512, 512), (1024, 256), (1280, 128), (1408, 512), (1920, 512), (2432, 512)]


def band_tiles(th):
    out = []
    for kb in range(max(0, 4 * th - 1), min(8, 4 * th + 5)):
        jl = max(4 * th, kb - 1)
        jh = min(4 * th + 3, kb + 1)
        out.append((kb, jl, jh))
    return out


MASK_OFF = {}
_m = 512
for _th in range(2):
    for (_kb, _jl, _jh) in band_tiles(_th):
        MASK_OFF[(_th, _kb)] = _m
        _m += 128 * (_jh - _jl + 1)
MASK_COLS = _m


class Op:
    __slots__ = ("eng", "fn", "deps", "chan", "cord", "idx", "sig", "sigcount", "gid")


class Prog:
    ENGS = ("pe", "act", "dve", "pool", "sp")

    def __init__(self):
        self.ops = []
        self.kw = {}
        self.kr = {}
        self.chan_last = {}
        self.chan_count = {}

    def add(self, eng, fn, r=(), w=(), chan=None):
        op = Op()
        op.eng = eng
        op.fn = fn
        op.chan = chan
        op.sig = False
        op.gid = len(self.ops)
        deps = set()
        psr = [k for k in r if isinstance(k, tuple) and k and k[0] == "ps"]
        if psr:
            r = [k for k in r if k not in psr]
            w = list(w) + psr
        for k in r:
            p = self.kw.get(k)
            if p is not None:
                deps.add(p)
        for k in w:
            p = self.kw.get(k)
            if p is not None:
                deps.add(p)
            for q in self.kr.get(k, ()):
                deps.add(q)
        if chan is not None:
            p = self.chan_last.get(chan)
            if p is not None:
                deps.add(p)
            self.chan_last[chan] = op.gid
            self.chan_count[chan] = self.chan_count.get(chan, 0) + 1
            op.cord = self.chan_count[chan]
        for k in r:
            self.kr.setdefault(k, []).append(op.gid)
        for k in w:
            self.kw[k] = op.gid
            self.kr[k] = []
        op.deps = deps
        self.ops.append(op)
        return op.gid

    def emit(self, nc, es, block):
        ops = self.ops
        for op in ops:
            for d in op.deps:
                if ops[d].chan is None:
                    ops[d].sig = True
        cnt = {e: 0 for e in self.ENGS}
        for op in ops:
            if op.chan is None and op.sig:
                cnt[op.eng] += 1
                op.sigcount = cnt[op.eng]
        esem = {e: es.enter_context(nc.semaphore("s_" + e)) for e in ("pe", "act", "dve", "pool")}
        csem = {c: es.enter_context(nc.semaphore("c_" + c)) for c in self.chan_count}
        per_eng = {e: [op for op in ops if op.eng == e] for e in self.ENGS}

        def run(e, name):
            known = {}
            for op in per_eng[name]:
                waits = {}
                for d in op.deps:
                    p = ops[d]
                    if p.chan is not None:
                        key = ("c", p.chan)
                        val = 16 * p.cord
                    else:
                        if p.eng == "pe" and name == "pe":
                            continue
                        key = ("e", p.eng)
                        val = p.sigcount
                    if known.get(key, 0) >= val:
                        continue
                    if waits.get(key, 0) < val:
                        waits[key] = val
                for key, val in waits.items():
                    sem = csem[key[1]] if key[0] == "c" else esem[key[1]]
                    e.wait_ge(sem, val)
                    known[key] = val
                ins = op.fn(e)
                if op.chan is not None:
                    ins.then_inc(csem[op.chan], 16)
                elif op.sig:
                    ins.then_inc(esem[name], 1)

        @block.tensor
        def _(e):
            run(e, "pe")

        @block.scalar
        def _(e):
            run(e, "act")

        @block.vector
        def _(e):
            run(e, "dve")

        @block.gpsimd
        def _(e):
            run(e, "pool")

        @block.sync
        def _(e):
            run(e, "sp")


def build_program(stop_after=None, taps=()):
    nc = bass.Bass("TRN2", target_bir_lowering=False)
    es = ExitStack()
    P = Prog()

    def din(name, shape, dt=F32):
        return nc.dram_tensor(name, list(shape), dt, kind="ExternalInput").ap()

    def dout(name, shape, dt=F32):
        return nc.dram_tensor(name, list(shape), dt, kind="ExternalOutput").ap()

    def sb(name, shape, dt):
        return es.enter_context(nc.sbuf_tensor("sb_" + name, list(shape), dt))

    x_d = din("x", [T, D])
    cond_d = din("cond", [128, 8])
    ck_d = din("ck", [DEPTH, 512, 128])
    cv_d = din("cv", [DEPTH, 512, 128])
    h0_d = din("h0", [128, DEPTH * 8])
    flag_d = din("flag", [128, 1])
    ropec_d = din("ropec", [128, T])
    ropes_d = din("ropes", [128, T])
    maskb_d = din("maskb", [128, MASK_COLS])
    ident_d = din("ident", [128, 128])
    vecs_d = din("vecs", [DEPTH, 128, NV])
    lruw_d = din("lruw", [DEPTH, 128, 16, 128])
    wmod_d = din("w_mod", [DEPTH, D, 6 * D])
    win_d = din("w_in_ext", [DEPTH, D, NWIN])
    wbr_d = din("w_branch", [DEPTH, 3, 512, D])
    wmg_d = din("w_merge", [DEPTH, D, 3 * D])
    wout_d = din("w_out", [DEPTH, D, D])
    wup_d = din("ffn_w_up", [DEPTH, D, 2 * DFF])
    wdn_d = din("ffn_w_down", [DEPTH, DFF, D])

    y_d = dout("y", [T, D])
    nk_d = dout("nk", [DEPTH, T, 128])
    nv_d = dout("nv", [DEPTH, T, 128])
    nh_d = dout("nh", [DEPTH, 32, 128])

    xT = sb("xT", [128, 8, T], F32)
    hT = sb("hT", [128, 8, T], BF16)
    wsl = sb("wsl", [128, NSLOT, SLOT_ELEMS], BF16)
    ident = sb("ident", [128, 128], F32)
    identb = sb("identb", [128, 128], BF16)
    onesb = sb("onesb", [128, 128], BF16)
    vecs = sb("vecs", [128, DEPTH, NV], F32)
    condt = sb("condt", [128, 8], F32)
    scond = sb("scond", [128, 8], BF16)
    flag = sb("flag", [128, 1], F32)
    ctxb = sb("ctxb", [128, 1], F32)
    h0t = sb("h0t", [128, DEPTH * 8], F32)
    modT = sb("modT", [128, 48], F32)
    der = sb("der", [128, 64], F32)
    ropec = sb("ropec", [128, T], BF16)
    ropes = sb("ropes", [128, T], BF16)
    maskb = sb("maskb", [128, MASK_COLS], BF16)
    lruw = sb("lruw", [128, 16, 128], BF16)
    lrudg = sb("lrudg", [128, 16, 128], BF16)
    esink = sb("esink", [128, 8], F32)
    fin = sb("fin", [128, 32], F32)
    fint = sb("fint", [32, 128], F32)
    initb = sb("initb", [128, 8], F32)
    attnT = sb("attnT", [128, 4, T], BF16)
    lruT = sb("lruT", [128, 4, T], BF16)
    confT = sb("confT", [128, 4, T], BF16)
    xbpad2 = sb("xbpad", [128, 4 * NSEG * XBP], BF16)
    xbpad = xbpad2[:, :].rearrange("p (c x) -> p c x", c=4)
    upad2 = sb("upad", [128, 4 * NSEG * UPP], BF16)
    upad = upad2[:, :].rearrange("p (c x) -> p c x", c=4)
    SBYTES = 60 * 1024
    arena = sb("arena", [128, SBYTES // 4], F32)
    arena_b = arena[:].bitcast(BF16) if hasattr(arena[:], "bitcast") else None

    psum = es.enter_context(nc.psum_tensor("ps", [128, 8, 512], F32))

    def AF32(off_bytes, shape):
        n = int(np.prod(shape[1:]))
        o = off_bytes // 4
        ap = arena[0:shape[0], o:o + n]
        return ap if len(shape) == 2 else _reshape(ap, shape[1:])

    def ABF(off_bytes, shape):
        n = int(np.prod(shape[1:]))
        o = off_bytes // 2
        ap = arena_b[0:shape[0], o:o + n]
        return ap if len(shape) == 2 else _reshape(ap, shape[1:])

    def _reshape(ap, free):
        if len(free) == 2:
            return ap.rearrange("p (a b) -> p a b", a=free[0])
        if len(free) == 3:
            return ap.rearrange("p (a b c) -> p a b c", a=free[0], b=free[1])
        raise ValueError

    arena_keys = []

    def akey(name):
        arena_keys.append(name)
        return name

    wtiles = []

    def wt_add(kind, parts):
        wtiles.append((kind, parts))

    def slot_view(s, kc, ncols, np_=128):
        return wsl[0:np_, s, 0:kc * ncols].rearrange("p (k n) -> p k n", k=kc)

    def wt_mod(l, j):
        src = wmod_d[l, :, j * 512:(j + 1) * 512].rearrange("(k p) n -> p k n", p=128)
        wt_add(("mod", l, j), [(lambda s: slot_view(s, 8, 512), src)])

    for l in range(DEPTH):
        if l == 0:
            for j in range(4):
                wt_mod(0, j)
        def wt_win(j):
            c0, ncols = WIN_TILES[j]
            src = win_d[l, :, c0:c0 + ncols].rearrange("(k p) n -> p k n", p=128)
            wt_add(("win", l, j), [(lambda s, ncols=ncols: slot_view(s, 8, ncols), src)])
        for j in range(4):
            wt_win(j)
        for j in range(4, 12):
            wt_mod(l, j)
        for j in range(4, 7):
            wt_win(j)
        for n in range(8):
            parts = []
            for b in range(3):
                src = wmg_d[l, :, b * D + n * 128: b * D + (n + 1) * 128].rearrange("(k p) n -> p k n", p=128)
                parts.append((lambda s, b=b: slot_view(s, 24, 128)[:, b * 8:(b + 1) * 8, :], src))
            wt_add(("mg", l, n), parts)
            parts = []
            for b in range(3):
                src = wbr_d[l, b, :, n * 128:(n + 1) * 128].rearrange("(k p) n -> p k n", p=128)
                parts.append((lambda s, b=b: slot_view(s, 12, 128)[:, 4 * b:4 * b + 4, :], src))
            wt_add(("br", l, n), parts)
            if l + 1 < DEPTH and n % 2 == 1:
                wt_mod(l + 1, n // 2)
        for j in range(2):
            src = wout_d[l, :, j * 512:(j + 1) * 512].rearrange("(k p) n -> p k n", p=128)
            wt_add(("out", l, j), [(lambda s: slot_view(s, 8, 512), src)])
        for j in range(11):
            parts = []
            for hv in range(2):
                c0 = hv * DFF + j * 256
                src = wup_d[l, :, c0:c0 + 256].rearrange("(k p) n -> p k n", p=128)
                parts.append((lambda s, hv=hv: slot_view(s, 8, 512)[:, :, hv * 256:(hv + 1) * 256], src))
            wt_add(("up", l, j), parts)
        for n in range(8):
            src = wdn_d[l, :, n * 128:(n + 1) * 128].rearrange("(k p) n -> p k n", p=128)
            wt_add(("dn", l, n), [(lambda s: slot_view(s, NFF, 128), src)])
        for n in range(8):
            src = wdn_d[l, :, n * 128:(n + 1) * 128].rearrange("(k p) n -> p k n", p=128)
            wt_add(("dn2", l, n), [(lambda s: slot_view(s, NFF, 128), src)])

    wstate = {"issued": 0, "next": 0}

    def w_issue_upto(j):
        while wstate["issued"] <= min(j, len(wtiles) - 1):
            i = wstate["issued"]
            kind, parts = wtiles[i]
            s = i % NSLOT
            for pi, (dst_fn, src) in enumerate(parts):
                dst = dst_fn(s)
                P.add("pool", (lambda e, dst=dst, src=src: e.dma_start(out=dst, in_=src)),
                      w=[("w", s, pi)],
                      chan="w%d_%d" % (s, pi))
            wstate["issued"] += 1

    def w_next(kind, la=NSLOT - 1):
        j = wstate["next"]
        assert wtiles[j][0] == kind, (wtiles[j][0], kind)
        w_issue_upto(j + la)
        wstate["next"] += 1
        s = j % NSLOT
        return s, [("w", s, q) for q in range(3)]

    def PB(b):
        return psum[:, b, :]

    pskey = lambda b: ("ps", b)
    rot = {"i": 0}

    MODB = 7

    def next_bank():
        b = rot["i"] % 7
        rot["i"] += 1
        return b

    def act(out, in_, func, r, w, bias=None, scale=None):
        kw = {}
        if bias is not None:
            kw["bias"] = bias
        if scale is not None:
            kw["scale"] = scale
        return P.add("act", lambda e: e.activation(out=out, in_=in_, func=func, **kw), r=r, w=w)

    def dve_tt(out, in0, in1, op, r, w, eng="dve"):
        return P.add(eng, lambda e: e.tensor_tensor(out=out, in0=in0, in1=in1, op=op), r=r, w=w)

    def dve_ts(out, in0, s1, op0, r, w, s2=None, op1=None, eng="dve"):
        if op1 is None:
            return P.add(eng, lambda e: e.tensor_scalar(out=out, in0=in0, scalar1=s1, scalar2=0.0, op0=op0, op1=ALU.add),
                         r=r, w=w)
        return P.add(eng, lambda e: e.tensor_scalar(out=out, in0=in0, scalar1=s1, scalar2=s2, op0=op0, op1=op1), r=r, w=w)

    def dve_stt(out, in0, scalar, in1, op0, op1, r, w, eng="dve"):
        return P.add(eng, lambda e: e.scalar_tensor_tensor(out=out, in0=in0, scalar=scalar, in1=in1, op0=op0, op1=op1),
                     r=r, w=w)

    def dma(q, out, in_, r, w, chan):
        return P.add(q, lambda e: e.dma_start(out=out, in_=in_), r=r, w=w, chan=chan)

    def mm_group(bank, steps, r, cols=None, extra_w=()):
        outap = PB(bank) if cols is None else PB(bank)[:, cols[0]:cols[1]]

        def fn(e):
            ins = None
            n = len(steps)
            for i, (lt, rh) in enumerate(steps):
                ins = e.matmul(outap, lt, rh, start=(i == 0), stop=(i == n - 1))
            return ins
        return P.add("pe", fn, r=r, w=[pskey(bank)] + list(extra_w))

    stopped = {"v": False}
    tap_list = []

    def phase_end(name):
        if stop_after == name:
            stopped["v"] = True
        return stopped["v"]

    dma("sp", ident[:], ident_d, [], ["ident"], "ld0")
    dma("sp", vecs[:], vecs_d.rearrange("l p n -> p l n"), [], ["vecs"], "ld1")
    dma("sp", condt[:], cond_d, [], ["condt"], "ld2")
    dma("sp", flag[:], flag_d, [], ["flag"], "ld3")
    dma("sp", h0t[:], h0_d, [], ["h0t"], "ld0")

    P.add("dve", lambda e: e.memset(onesb[:], 1.0), w=["onesb"])
    act(identb[:], ident[:], AF.Copy, ["ident"], ["identb"])
    P.add("dve", lambda e: e.tensor_scalar(out=ctxb[:], in0=flag[:], scalar1=-1.0, scalar2=-NEGM, op0=ALU.add,
                                           op1=ALU.mult), r=["flag"], w=["ctxb"])
    act(scond[:], condt[:], AF.Silu, ["condt"], ["scond"])

    XIN = [AF32(i * 4096, [128, 1024]) for i in range(8)]
    for tb in range(8):
        dma("sp", XIN[tb], x_d[tb * 128:(tb + 1) * 128, :], [], [akey(("xin", tb))], "xin%d" % tb)
    for tb in range(8):
        st = XIN[tb]
        k = ("xin", tb)
        for half in range(2):
            b = next_bank()
            def fn(e, st=st, half=half, b=b):
                ins = None
                for q in range(4):
                    c = half * 4 + q
                    ins = e.transpose(out=PB(b)[:, q * 128:(q + 1) * 128], in_=st[:, c * 128:(c + 1) * 128],
                                      identity=ident[:])
                return ins
            P.add("pe", fn, r=[k, "ident"], w=[pskey(b)])
            src = PB(b).rearrange("p (q t) -> p q t", q=4)
            dst = xT[:, half * 4:(half + 1) * 4, tb * 128:(tb + 1) * 128]
            eng = "act" if half == 0 else "dve"
            if eng == "act":
                P.add("act", lambda e, dst=dst, src=src: e.activation(out=dst, in_=src, func=AF.Copy),
                      r=[pskey(b)], w=[("xT", c_, tb // 4) for c_ in range(half * 4, half * 4 + 4)])
            else:
                P.add("dve", lambda e, dst=dst, src=src: e.tensor_copy(out=dst, in_=src),
                      r=[pskey(b)], w=[("xT", c_, tb // 4) for c_ in range(half * 4, half * 4 + 4)])

    def V(l, name, col=0, n=1):
        o = _VOFF[name] + col
        return vecs[:, l, o:o + n]

    for l in range(DEPTH):
        if stopped["v"]:
            break
        def mod_tile(ll, j):
            s, wk = w_next(("mod", ll, j))
            wv = slot_view(s, 8, 512)

            def fn(e, wv=wv, j=j):
                ins = None
                for n4 in range(4):
                    col = j * 4 + n4
                    for kc in range(8):
                        ins = e.matmul(PB(MODB)[:, col:col + 1], wv[:, kc, n4 * 128:(n4 + 1) * 128],
                                       scond[:, kc:kc + 1], start=(kc == 0), stop=(kc == 7))
                return ins
            P.add("pe", fn, r=wk + ["scond"], w=[pskey(MODB)])

        def mod_finish_A(ll):
            dve_tt(modT[:, 0:16], PB(MODB)[:, 0:16], V(ll, "bmod", 0, 16), ALU.add, [pskey(MODB), "vecs"], ["modA"])
            dve_ts(der[:, 0:8], modT[:, 8:16], 1.0, ALU.add, ["modA"], [("der", 0)])

        def mod_finish_B(ll):
            dve_tt(modT[:, 16:48], PB(MODB)[:, 16:48], V(ll, "bmod", 16, 32), ALU.add, [pskey(MODB), "vecs"], ["modB"])
            dve_ts(der[:, 8:16], modT[:, 16:24], 1.0 / ALPHA, ALU.mult, ["modB"], [("der", 1)])
            dve_ts(der[:, 16:24], modT[:, 32:40], 1.0, ALU.add, ["modB"], [("der", 2)])
            dve_ts(der[:, 24:32], modT[:, 40:48], 1.0 / ALPHA, ALU.mult, ["modB"], [("der", 3)])

        if l == 0:
            w_issue_upto(NSLOT - 1)
            for j in range(4):
                mod_tile(0, j)
            mod_finish_A(0)
            dma("pool", ropec[:], ropec_d, [], ["ropec"], "ldr0")
            dma("pool", ropes[:], ropes_d, [], ["ropes"], "ldr1")
            dma("pool", maskb[:], maskb_d, [], ["maskb"], "ldm2")
        dma("pool", lruw[:], lruw_d[l], [], ["lruw"], "ldm")
        act(der[:, 40:48], V(l, "llam", 0, 8), AF.Exp, ["vecs"], [("der", 5)], scale=-1.0)
        dve_ts(der[:, 40:48], der[:, 40:48], 1.0, ALU.add, [("der", 5)], [("der", 5)])
        act(der[:, 32:40], der[:, 40:48], AF.Ln, [("der", 5)], [("der", 4)])
        dve_ts(der[:, 32:40], der[:, 32:40], -4.0, ALU.mult, [("der", 4)], [("der", 4)])
        dve_ts(der[:, 48:64], V(l, "lba", 0, 16), 0.5, ALU.mult, ["vecs"], [("der", 6)])
        act(esink[:], V(l, "sink", 0, 8), AF.Exp, ["vecs"], ["esink"])
        for k in range(4):
            for c in range(4):
                i = k * 4 + c
                dve_ts(lrudg[:, i, :], identb[:], V(l, "lcw", i, 1), ALU.mult, ["identb", "vecs"], [("lrudg", i)],
                       eng="pool")

        for c in range(8 if l == 0 else 0):
            if c % 2 == 0:
                act(hT[:, c, :], xT[:, c, :], AF.Identity, [("xT", c, 0), ("xT", c, 1), ("der", 0), "modA"],
                    [("hT", c, 0), ("hT", c, 1)], bias=modT[:, c:c + 1], scale=der[:, c:c + 1])
            else:
                dve_ts(hT[:, c, :], xT[:, c, :], der[:, c:c + 1], ALU.mult,
                       [("xT", c, 0), ("xT", c, 1), ("der", 0), "modA"], [("hT", c, 0), ("hT", c, 1)],
                       s2=modT[:, c:c + 1], op1=ALU.add)
        gtk = lambda lo, hi: [("gT", c_, th_) for c_ in range(lo, hi) for th_ in range(2)]
        P.add("dve", lambda e: e.memset(xbpad2[:, :], 0.0), r=[("hT", 7, 1)], w=["xbpad_init"] + gtk(12, 16))
        P.add("dve", lambda e: e.memset(upad2[:, :], 0.0), r=[("hT", 7, 1)], w=["upad_init"] + gtk(16, 20))
        if phase_end("A%d" % l):
            break

        fence = list(arena_keys)
        del arena_keys[:]
        o = 0
        qT = ABF(o, [128, 4, T]); o += 8192
        kz = ABF(o, [128, 4, T]); o += 8192
        ktok = AF32(o, [128, 8, 128]); o += 4096
        vtok = AF32(o, [128, 8, 128]); o += 4096
        vaug = ABF(o, [128, 4, 8, 128]); o += 8192
        ckf = AF32(o, [128, 4, 128]); o += 2048
        cksf = AF32(o, [128, 4, 128]); o += 2048
        ckz = ABF(o, [128, 4, 512]); o += 4096
        cvaug = ABF(o, [128, 4, 4, 128]); o += 4096
        NPT = 4
        vraw = AF32(o, [128, T])
        pt = ABF(o, [128, NPT, 512]); o += NPT * 1024
        kraw = AF32(o, [128, T])
        rd = AF32(o, [128, 2, 512]); o += 4096
        rt1 = AF32(o, [128, 2, 512]); o += 4096
        rt2 = AF32(o, [128, 2, 512]); o += 4096
        assert o <= SBYTES, o
        first_fence = {"v": fence}

        def FW():
            f = first_fence["v"]
            first_fence["v"] = []
            return f

        P.add("dve", lambda e, kz=kz: e.memset(kz, 0.0), w=FW() + [akey("kz_init")])
        P.add("dve", lambda e, ckz=ckz: e.memset(ckz, 0.0), r=["kz_init"], w=[akey("ckz_init")])
        P.add("dve", lambda e, vaug=vaug: e.memset(vaug, 1.0), r=["kz_init"], w=[akey("vaug_init")])
        P.add("dve", lambda e, cvaug=cvaug: e.memset(cvaug, 1.0), r=["kz_init"], w=[akey("cvaug_init")])
        dma("sp", ckf, ck_d[l].rearrange("(k p) n -> p k n", p=128), ["kz_init"], [akey("ckf")], "ldc0")
        for g_ in range(2):
            for e__ in range(2):
                dma("pool", cvaug[:, g_ * 2 + e__, :, e__ * 64:(e__ + 1) * 64],
                    cv_d[l][:, g_ * 64:(g_ + 1) * 64].rearrange("(k p) n -> p k n", p=128), ["cvaug_init"],
                    [akey(("cvaug", g_ * 2 + e__))], "ldv%d" % (g_ * 2 + e__))
        hkeys = lambda th: [("hT", c, th) for c in range(8)]
        pend = {}
        wcur = {"j": -1}

        def win_chunk(cname, sgsel=None):
            ci = WIN_CHUNKS.index(cname)
            col = ci * 128
            j = [i for i, (c0, nc_) in enumerate(WIN_TILES) if c0 <= col < c0 + nc_][0]
            off = col - WIN_TILES[j][0]
            if j != wcur["j"]:
                s_cur, wk_c = w_next(("win", l, j))
                wcur["j"] = j
                wcur["wk"] = wk_c
                wcur["wv"] = slot_view(s_cur, 8, WIN_TILES[j][1])
            wk_cur, wv_cur = wcur["wk"], wcur["wv"]
            banks = []
            for th in range(2):
                b = next_bank()
                banks.append(b)
                mm_group(b, [(wv_cur[:, kc, off:off + 128], hT[:, kc, th * 512:(th + 1) * 512]) for kc in range(8)],
                         r=wk_cur + hkeys(th))
            pend[cname] = banks
            if cname.startswith("qp") or cname in ("kAp", "kBp"):
                base = cname.replace("p", "")
                for th in range(2):
                    b0 = pend[base][th]
                    b1 = banks[th]
                    tsl = slice(th * 512, (th + 1) * 512)
                    t1 = rt1[:, th, :]
                    t2 = rt2[:, th, :]
                    dve_tt(t1, PB(b0), ropec[:, tsl], ALU.mult, [pskey(b0), "ropec"], [akey(("rt1", th))])
                    dve_tt(t2, PB(b1), ropes[:, tsl], ALU.mult, [pskey(b1), "ropes"], [akey(("rt2", th))])
                    if base.startswith("q"):
                        c = int(base[1])
                        dve_tt(qT[:, c, tsl], t1, t2, ALU.add, [("rt1", th), ("rt2", th)], [akey(("qT", c, th))])
                    else:
                        lo_idx, hi_idx = (0, 3) if base == "kA" else (2, 1)
                        dve_tt(kz[0:64, lo_idx, tsl], t1[0:64, :], t2[0:64, :], ALU.add,
                               [("rt1", th), ("rt2", th), "kz_init"], [akey(("kz", lo_idx, th))])
                        dve_tt(kz[64:128, hi_idx, tsl], t1[64:128, :], t2[64:128, :], ALU.add,
                               [("rt1", th), ("rt2", th), "kz_init"], [akey(("kz", hi_idx, th))])
                    if base == "kA":
                        act(kraw[:, tsl], PB(b0), AF.Copy, [pskey(b0), "kz_init"], [akey(("kraw", th))])
                        for (do_, di_, so_, si_) in ((slice(64, 128), 1, slice(0, 64), 0), (slice(0, 64), 2, slice(64, 128), 3)):
                            dst_ = kz[do_, di_, tsl]
                            src_ = kz[so_, si_, tsl]
                            P.add("dve", lambda e, dst_=dst_, src_=src_: e.tensor_copy(out=dst_, in_=src_),
                                  r=[("kz", si_, th), "kz_init"], w=[akey(("kz", di_, th))])
            elif cname == "v":
                for th in range(2):
                    tsl = slice(th * 512, (th + 1) * 512)
                    act(vraw[:, tsl], PB(banks[th]), AF.Copy, [pskey(banks[th]), "kz_init"], [akey(("vraw", th))])
                for (raw, rkey, tok, tkey) in ((kraw, "kraw", ktok, "ktok"), (vraw, "vraw", vtok, "vtok")):
                    for th in range(2):
                        b = next_bank()

                        def fn(e, raw=raw, th=th, b=b):
                            ins = None
                            for q in range(4):
                                blk = th * 4 + q
                                ins = e.transpose(out=PB(b)[:, q * 128:(q + 1) * 128],
                                                  in_=raw[:, blk * 128:(blk + 1) * 128], identity=ident[:])
                            return ins
                        P.add("pe", fn, r=[(rkey, th), "ident"], w=[pskey(b)])
                        src = PB(b).rearrange("p (q t) -> p q t", q=4)
                        dst = tok[:, th * 4:(th + 1) * 4, :]
                        P.add("dve", lambda e, dst=dst, src=src: e.tensor_copy(out=dst, in_=src),
                              r=[pskey(b), "kz_init"], w=[akey((tkey, th))])
                        if tkey == "vtok":
                            for g_ in range(2):
                                for e__ in range(2):
                                    dstb = vaug[:, g_ * 2 + e__, th * 4:(th + 1) * 4, e__ * 64:(e__ + 1) * 64]
                                    srcb = src[:, :, g_ * 64:(g_ + 1) * 64]
                                    if e__ == 0:
                                        P.add("act", lambda e, dstb=dstb, srcb=srcb: e.activation(
                                            out=dstb, in_=srcb, func=AF.Copy),
                                            r=[pskey(b), "vaug_init"], w=[akey(("vaug", g_ * 2 + e__, th))])
                                    else:
                                        P.add("dve", lambda e, dstb=dstb, srcb=srcb: e.tensor_copy(out=dstb, in_=srcb),
                                              r=[pskey(b), "vaug_init"], w=[akey(("vaug", g_ * 2 + e__, th))])
                dma("sp", nk_d[l].rearrange("(b p) n -> p b n", p=128), ktok, [("ktok", 0), ("ktok", 1)], [], "st_k")
                dma("sp", nv_d[l].rearrange("(b p) n -> p b n", p=128), vtok, [("vtok", 0), ("vtok", 1)], [], "st_v")
            elif cname.startswith("xb"):
                c = int(cname[2])
                for th in range(2):
                    src = PB(banks[th]).rearrange("p (s t) -> p s t", s=2)
                    dst = xbpad[:, c, :].rearrange("p (s t) -> p s t", s=NSEG)[:, 2 * th:2 * th + 2, 2:2 + SEG]
                    P.add("act", lambda e, dst=dst, src=src: e.activation(out=dst, in_=src, func=AF.Copy),
                          r=[pskey(banks[th]), "xbpad_init"], w=[("xbpad", c, th)])
                xv = xbpad[:, c, :].rearrange("p (s t) -> p s t", s=NSEG)
                dve_ts(xv[:, 1:4, 0:2], xv[:, 0:3, SEG:SEG + 2], flag[:, 0:1], ALU.mult,
                       [("xbpad", c, 0), ("xbpad", c, 1), "flag"], [("xbpad_h", c, 0)])
                dve_ts(xv[:, 0:3, SEG + 2:SEG + 3], xv[:, 1:4, 2:3], flag[:, 0:1], ALU.mult,
                       [("xbpad", c, 0), ("xbpad", c, 1), "flag"], [("xbpad_h", c, 1)])
            elif cname.startswith("g"):
                c = int(cname[1])
                ab = pend["a%d" % c]
                for th in range(2):
                    sg, sgk, sgx = sgsel(th)
                    act(sg, PB(banks[th]), AF.Sigmoid, [pskey(banks[th])], sgx + [akey(sgk)])
                    src = PB(ab[th]).rearrange("p (s t) -> p s t", s=2)
                    dst = upad[:, c, :].rearrange("p (s t) -> p s t", s=NSEG)[:, 2 * th:2 * th + 2, 15:15 + SEG]
                    sgv = sg.rearrange("p (s t) -> p s t", s=2)
                    P.add("dve", lambda e, dst=dst, src=src, sgv=sgv: e.tensor_tensor(out=dst, in0=src, in1=sgv,
                                                                                       op=ALU.mult),
                          r=[pskey(ab[th]), sgk, "upad_init"], w=[("upad", c, th)])
                uv = upad[:, c, :].rearrange("p (s t) -> p s t", s=NSEG)
                dve_ts(uv[:, 1:4, 0:15], uv[:, 0:3, SEG:SEG + 15], flag[:, 0:1], ALU.mult,
                       [("upad", c, 0), ("upad", c, 1), "flag"], [("upad_h", c, 0)])
                dve_ts(uv[:, 0:3, SEG + 15:SEG + 30], uv[:, 1:4, 15:30], flag[:, 0:1], ALU.mult,
                       [("upad", c, 0), ("upad", c, 1), "flag"], [("upad_h", c, 1)])
        for cname in WIN_CHUNKS[:11]:
            win_chunk(cname)
        b = next_bank()

        def fn(e, srcb=ckf, b=b):
            ins = None
            for kc in range(4):
                ins = e.transpose(out=PB(b)[:, kc * 128:(kc + 1) * 128], in_=srcb[:, kc, :], identity=ident[:])
            return ins
        P.add("pe", fn, r=["ckf", "ident"], w=[pskey(b)])
        for (do_, di_, so_) in ((slice(0, 64), 0, slice(0, 64)), (slice(64, 128), 3, slice(64, 128)),
                                (slice(64, 128), 1, slice(0, 64)), (slice(0, 64), 2, slice(64, 128))):
            act(ckz[do_, di_, :], PB(b)[so_, :], AF.Copy, [pskey(b), "ckz_init"], [akey(("ckz", di_))])

        if phase_end("B%d" % l):
            tap_list.extend([("qT", qT, [128, 4, T], BF16), ("kz", kz, [128, 4, T], BF16),
                             ("ckz", ckz, [128, 4, 512], BF16)])
            break

        sched = []
        for h in range(8):
            for th in range(2):
                tl = [("ctx", kc) for kc in range(4)] + [("band",) + bt for bt in band_tiles(th)]
                for ti, t_ in enumerate(tl):
                    sched.append((h, th, ti, len(tl), t_))
        LAG = 3
        st_bank = {}
        it_idx = {}
        for i, (h, th) in enumerate([(h, th) for h in range(8) for th in range(2)]):
            it_idx[(h, th)] = i

        def emit_S(gi):
            h, th, ti, ntl, t_ = sched[gi]
            c, e_, g = h // 2, h % 2, h // 4
            b = gi % 4
            st_bank[gi] = b
            if t_[0] == "ctx":
                kc = t_[1]
                lt = ckz[:, g * 2 + e_, kc * 128:(kc + 1) * 128]
                q0, q1 = th * 512, (th + 1) * 512
                mk = None
                rk = [("ckz", g * 2 + e_), "ckz_init"]
            else:
                _, kb, jl, jh = t_
                lt = kz[:, g * 2 + e_, kb * 128:(kb + 1) * 128]
                q0, q1 = jl * 128, (jh + 1) * 128
                mo = MASK_OFF[(th, kb)]
                mk = maskb[:, mo:mo + (q1 - q0)]
                rk = [("kz", g * 2 + e_, kb // 4), "kz_init"]
            n = q1 - q0
            rk += [("qT", c, th), "maskb", "identb"]
            slot = gi % NPT
            if mk is None:
                mm_group(b, [(lt, qT[:, c, q0:q1])], r=rk, cols=(0, n))
                act(pt[:, slot, 0:n], PB(b)[:, 0:n], AF.Exp, [pskey(b), "ctxb"], [akey(("pt", slot))], scale=0.125,
                    bias=ctxb[:, 0:1])
            else:
                mm_group(b, [(lt, qT[:, c, q0:q1]), (identb[:], mk)], r=rk, cols=(0, n))
                act(pt[:, slot, 0:n], PB(b)[:, 0:n], AF.Exp, [pskey(b)], [akey(("pt", slot))], scale=0.125)

        def emit_PV(gi):
            h, th, ti, ntl, t_ = sched[gi]
            c, e_, g = h // 2, h % 2, h // 4
            par = it_idx[(h, th)] % 2
            ab = 4 + par
            slot = gi % NPT
            if t_[0] == "ctx":
                kc = t_[1]
                vv = cvaug[:, g * 2 + e_, kc, :]
                q0, q1 = 0, 512
                rk = [("cvaug", g * 2 + e_), "cvaug_init"]
            else:
                _, kb, jl, jh = t_
                vv = vaug[:, g * 2 + e_, kb, :]
                q0, q1 = jl * 128 - th * 512, (jh + 1) * 128 - th * 512
                rk = [("vaug", g * 2 + e_, kb // 4), "vaug_init"]
            n = q1 - q0
            first, last = (ti == 0), (ti == ntl - 1)
            pr = slice(e_ * 64, e_ * 64 + 64)
            po = slice((1 - e_) * 64, (1 - e_) * 64 + 64)
            ptv = pt[:, slot, 0:n]
            o_a = PB(ab)[:, q0:q1]

            def fn(e, vv=vv, ptv=ptv, o_a=o_a, first=first, last=last):
                return e.matmul(o_a, vv, ptv, start=first, stop=last)
            P.add("pe", fn, r=rk + [("pt", slot)], w=[pskey(ab)])
            if last:
                r_ = rd[pr, par, :]
                dve_ts(r_, PB(ab)[po, :], esink[pr, h:h + 1], ALU.add, [pskey(ab), "esink"], [akey(("rd", par))])
                P.add("dve", lambda e, r_=r_: e.reciprocal(out=r_, in_=r_), r=[("rd", par)], w=[("rd", par)])
                dve_tt(attnT[pr, c, th * 512:(th + 1) * 512], PB(ab)[pr, :], r_, ALU.mult,
                       [pskey(ab), ("rd", par)], [("attnT", c, th, e_)])

        NT = len(sched)
        modj = 4
        for gi in range(NT + LAG):
            if gi < NT:
                emit_S(gi)
                if sched[gi][2] == 0 and it_idx[(sched[gi][0], sched[gi][1])] % 2 == 1 and modj < 12:
                    mod_tile(l, modj)
                    modj += 1
            if gi - LAG >= 0:
                emit_PV(gi - LAG)
        assert modj == 12
        mod_finish_B(l)
        if phase_end("C%d" % l):
            break

        att_fence = list(arena_keys)
        del arena_keys[:]
        first_fence["v"] = []
        o = 0
        xl = AF32(o, [128, T]); o += 4096
        xlb = ABF(o, [128, T]); o += 2048
        Rb = AF32(o, [128, T]); o += 4096
        Ab = AF32(o, [128, T]); o += 4096
        Sb = AF32(o, [128, T]); o += 4096
        Ib = AF32(o, [128, T]); o += 4096
        H0 = AF32(o, [128, T]); o += 4096
        H1 = Rb
        lru_end = o
        cdg = ABF(o, [128, 2, 31, 128]); o += 2 * 31 * 256
        cvo = AF32(o, [128, 4, T]); o += 16384
        assert o <= SBYTES, o

        def cdg_build(c, fence=()):
            o0 = _VOFF["cw"] + c
            wv_ = vecs[:, l, o0:o0 + 4 * 31:4].unsqueeze(2).broadcast_to([128, 31, 128])
            iv_ = identb[:].unsqueeze(1).broadcast_to([128, 31, 128])
            dve_tt(cdg[:, c % 2, :, :], iv_, wv_, ALU.mult, ["identb", "vecs"],
                   list(fence) + [akey(("cdg", c % 2, k)) for k in range(31)], eng="pool")

        def conf_conv(c):
            uv = upad[:, c, :].rearrange("p (s t) -> p s t", s=NSEG)
            for th in range(2):
                b = next_bank()
                mm_group(b, [(cdg[:, c % 2, k, :], uv[:, 2 * th:2 * th + 2, k:k + SEG]) for k in range(31)],
                         r=[("upad", c, 0), ("upad", c, 1), ("upad_h", c, 0), ("upad_h", c, 1)] +
                           [("cdg", c % 2, k) for k in range(31)])
                tsl = slice(th * 512, (th + 1) * 512)
                dve_ts(cvo[:, c, tsl], PB(b), V(l, "cb", c, 1), ALU.add, [pskey(b), "vecs", ("cdg", 0, 0)],
                       [akey(("cvo", c, th))])
            if c + 2 < 4:
                cdg_build(c + 2)

        cdg_build(0, att_fence)
        cdg_build(1)
        cf32 = confT[:].bitcast(F32)
        A1 = cf32[:, 0:2, :].rearrange("p a b -> p (a b)")
        S1 = cf32[:, 2:4, :].rearrange("p a b -> p (a b)")
        sg_first = {"v": True}

        def sgsel(th):
            fx = list(att_fence) if sg_first["v"] else []
            sg_first["v"] = False
            return Sb[:, th * 512:(th + 1) * 512], ("Sg", th), fx + [("S", 0)]
        for cname in ("xb0", "xb1", "xb2", "xb3"):
            win_chunk(cname)
        for c in range(4):
            win_chunk("a%d" % c, sgsel)
            win_chunk("g%d" % c, sgsel)
            xv = xbpad[:, c, :].rearrange("p (s t) -> p s t", s=NSEG)
            cb = []
            for th in range(2):
                b = next_bank()
                cb.append(b)
                mm_group(b, [(lrudg[:, k * 4 + c, :], xv[:, 2 * th:2 * th + 2, k:k + SEG]) for k in range(4)],
                         r=[("xbpad", c, 0), ("xbpad", c, 1), ("xbpad_h", c, 0), ("xbpad_h", c, 1)] +
                           [("lrudg", k * 4 + c) for k in range(4)])
            for th in range(2):
                tsl = slice(th * 512, (th + 1) * 512)
                act(xl[:, tsl], PB(cb[th]), AF.Identity, [pskey(cb[th]), "vecs"],
                    (att_fence if (c == 0 and th == 0) else []) + [akey(("xl", th))], bias=V(l, "lcb", c, 1))
                xs_, xd_ = xl[:, tsl], xlb[:, tsl]
                P.add("dve", lambda e, xs_=xs_, xd_=xd_: e.tensor_copy(out=xd_, in_=xs_), r=[("xl", th)],
                      w=[akey(("xlb", th))])
            for d in range(2):
                Hd = H0 if d == 0 else H1
                hk = "H%d" % d
                gb = {}
                for gt in range(2):
                    for th in range(2):
                        b = next_bank()
                        gb[(gt, th)] = b
                        mm_group(b, [(lruw[:, (d * 2 + gt) * 4 + c, :], xlb[:, th * 512:(th + 1) * 512])],
                                 r=["lruw", ("xlb", th)])
                RK = [akey(("R", 0)), akey(("R", 1))]
                for th in range(2):
                    tsl = slice(th * 512, (th + 1) * 512)
                    act(Rb[:, tsl], PB(gb[(0, th)]), AF.Tanh, [pskey(gb[(0, th)]), ("der", 6)],
                        [("R", th)] + ([("H1", s_) for s_ in range(NSEG)] if th == 0 else []),
                        bias=der[:, 48 + d * 4 + c:49 + d * 4 + c], scale=0.5)
                    act(Ib[:, tsl], PB(gb[(1, th)]), AF.Tanh, [pskey(gb[(1, th)]), ("der", 6)], [akey(("I", th))],
                        bias=der[:, 56 + d * 4 + c:57 + d * 4 + c], scale=0.5)
                clh = der[:, 32 + d * 4 + c:33 + d * 4 + c]
                act(Rb, Rb, AF.Identity, RK + [("der", 4)], RK, bias=clh, scale=clh)
                Ab_, Sb_ = (Ab, Sb) if d == 0 else (A1, S1)
                kA, kS = akey(("A", d)), akey(("S", d))
                act(Ab_, Rb, AF.Exp, RK, [kA] + ([("confT", c_) for c_ in range(4)] if d == 1 else []))
                act(Sb_, Rb, AF.Exp, RK, [kS, ("Sg", 0), ("Sg", 1)], scale=2.0)
                act(Sb_, Sb_, AF.Sqrt, [kS], [kS], bias=0.25, scale=-0.25)
                IK = [("I", 0), ("I", 1)]
                dve_stt(Ib, Ib, 1.0, xl, ALU.add, ALU.mult, IK + [("xl", 0), ("xl", 1)], IK)
                dve_tt(Sb_, Sb_, Ib, ALU.mult, [kS] + IK, [kS])
                order = range(NSEG) if d == 0 else range(NSEG - 1, -1, -1)
                prev = None
                for s_ in order:
                    seg = slice(s_ * SEG, (s_ + 1) * SEG)
                    if prev is None:
                        init = h0t[:, l * 8 + d * 4 + c:l * 8 + d * 4 + c + 1]
                        ik = ["h0t"]
                    else:
                        col = (prev + 1) * SEG - 1 if d == 0 else prev * SEG
                        init = initb[:, s_:s_ + 1] if d == 0 else initb[:, 4 + s_:5 + s_]
                        dve_ts(init, Hd[:, col:col + 1], flag[:, 0:1], ALU.mult, [akey((hk, prev)), "flag"],
                               [("initb", d, s_)])
                        ik = [("initb", d, s_)]
                    if d == 0:
                        o_, a_, b_ = Hd[:, seg], Ab_[:, seg], Sb_[:, seg]
                    else:
                        lo, hi = s_ * SEG, (s_ + 1) * SEG
                        o_ = Hd[:, lo:hi][:, ::-1]
                        a_ = Ab_[:, lo:hi][:, ::-1]
                        b_ = Sb_[:, lo:hi][:, ::-1]
                    P.add("dve", lambda e, o_=o_, a_=a_, b_=b_, init=init: e.tensor_tensor_scan(
                        out=o_, data0=a_, data1=b_, initial=init, op0=ALU.mult, op1=ALU.add),
                        r=[kA, kS] + ik, w=[akey((hk, s_))] + (RK if d == 1 else []))
                    prev = s_
                hv = Hd.rearrange("p (s t) -> p s t", s=NSEG)
                colsel = SEG - 1 if d == 0 else 0
                fv = fin[:, :].rearrange("p (s x) -> p s x", s=NSEG)[:, :, d * 4 + c:d * 4 + c + 1]
                P.add("dve", lambda e, fv=fv, hv=hv, colsel=colsel: e.tensor_copy(out=fv, in_=hv[:, :, colsel:colsel + 1]),
                      r=[(hk, s_) for s_ in range(NSEG)], w=[("fin", d, c)])
            dve_tt(lruT[:, c, :], H0, H1, ALU.add, [("H0", s_) for s_ in range(4)] + [("H1", s_) for s_ in range(4)],
                   [("lruT", c)])
            conf_conv(c)
        fb = next_bank()
        P.add("pe", lambda e, fb=fb: e.transpose(out=PB(fb)[0:32, 0:128], in_=fin[:, :], identity=ident[:]),
              r=[("fin", d, c) for d in range(2) for c in range(4)] + ["ident"], w=[pskey(fb)])
        act(fint[:, :], PB(fb)[0:32, 0:128], AF.Copy, [pskey(fb)], ["fint"])
        dma("sp", nh_d[l], fint[:, :], ["fint"], [], "st_h")
        if phase_end("D%d" % l):
            tap_list.extend([("lruT", lruT[:], [128, 4, T], BF16)])
            break

        lru_fence = [k_ for k_ in arena_keys if not (isinstance(k_, tuple) and k_[0] in ("cvo", "cdg"))]
        first_fence["v"] = lru_fence
        o = 0
        mean = AF32(o, [128, T]); o += 4096
        rstd = AF32(o, [128, T]); o += 4096
        sqb = ABF(o, [128, 2, T]); o += 4096
        cvb = ABF(o, [128, 2, T]); o += 4096
        assert o <= lru_end
        ln_stats_and_norm = None

        def layer_norm(src_fn, nch, keys_fn, eps, tag):
            sb_ = [next_bank(), next_bank(), next_bank(), next_bank()]
            for c in range(nch):
                slot = c % 2
                fw_ = FW()
                cvs_ = cvb[:, slot, :]
                P.add("dve", lambda e, cvs_=cvs_, src_=src_fn(c): e.tensor_copy(out=cvs_, in_=src_),
                      r=keys_fn(c), w=fw_ + [akey(("cvb", slot))])
                act(sqb[:, slot, :], src_fn(c), AF.Square, keys_fn(c), fw_ + [akey(("sqb", slot))])
                for th in range(2):
                    tsl = slice(th * 512, (th + 1) * 512)
                    r1_ = cvb[:, slot, tsl]
                    r2_ = sqb[:, slot, tsl]

                    def fn(e, c=c, th=th, r1_=r1_, r2_=r2_, sb_=sb_):
                        e.matmul(PB(sb_[th]), onesb[:, :], r1_, start=(c == 0), stop=(c == nch - 1))
                        return e.matmul(PB(sb_[2 + th]), onesb[:, :], r2_, start=(c == 0), stop=(c == nch - 1))
                    P.add("pe", fn, r=[("cvb", slot), ("sqb", slot), "onesb"], w=[pskey(sb_[th]), pskey(sb_[2 + th])])
            inv = 1.0 / (nch * 128)
            for th in range(2):
                tsl = slice(th * 512, (th + 1) * 512)
                act(mean[:, tsl], PB(sb_[th]), AF.Copy, [pskey(sb_[th]), ("sqb", 0), ("sqb", 1)], [akey(("mean", th))],
                    scale=inv)
                act(rstd[:, tsl], PB(sb_[th]), AF.Square, [pskey(sb_[th])], [akey(("rstd", th))], scale=inv)
                dve_stt(rstd[:, tsl], PB(sb_[2 + th]), inv, rstd[:, tsl], ALU.mult, ALU.subtract,
                        [pskey(sb_[2 + th]), ("rstd", th)], [("rstd", th)])
                dve_ts(rstd[:, tsl], rstd[:, tsl], 0.0, ALU.max, [("rstd", th)], [("rstd", th)], s2=eps, op1=ALU.add)
                act(rstd[:, tsl], rstd[:, tsl], AF.Ln, [("rstd", th)], [("rstd", th)])
                act(rstd[:, tsl], rstd[:, tsl], AF.Exp, [("rstd", th)], [("rstd", th)], scale=-0.5)

        layer_norm(lambda c: cvo[:, c, :], 4, lambda c: [("cvo", c, 0), ("cvo", c, 1)], LN_EPS, "conf")
        MK = [("mean", 0), ("mean", 1)]
        RSK = [("rstd", 0), ("rstd", 1)]
        for c in range(4):
            ck_ = [("cvo", c, 0), ("cvo", c, 1)]
            dve_tt(cvo[:, c, :], cvo[:, c, :], mean, ALU.subtract, ck_ + MK, ck_)
            dve_tt(cvo[:, c, :], cvo[:, c, :], rstd, ALU.mult, ck_ + RSK, ck_)
            act(confT[:, c, :], cvo[:, c, :], AF.Silu, ck_ + ["vecs"], [("confT", c)], bias=V(l, "clb", c, 1),
                scale=V(l, "clg", c, 1))
        if phase_end("E%d" % l):
            tap_list.extend([("confT", confT[:], [128, 4, T], BF16), ("attnT", attnT[:], [128, 4, T], BF16)])
            break

        fence = list(arena_keys)
        del arena_keys[:]
        first_fence["v"] = fence
        o = 0
        mergeT = ABF(o, [128, 8, T]); o += 16384
        sgb = AF32(o, [128, 2, 3, 512]); o += 12288
        mt = AF32(o, [128, 2, 2, 512]); o += 8192
        it = 0
        for n in range(8):
            sg_, wkg = w_next(("mg", l, n))
            wg = slot_view(sg_, 24, 128)
            sb2, wkb = w_next(("br", l, n), la=NSLOT - 2)
            wb = slot_view(sb2, 12, 128)
            for th in range(2):
                tsl = slice(th * 512, (th + 1) * 512)
                par = it % 2
                it += 1
                G = []
                for b_ in range(3):
                    bk = next_bank()
                    G.append(bk)
                    mm_group(bk, [(wg[:, b_ * 8 + kc, :], hT[:, kc, tsl]) for kc in range(8)], r=wkg + hkeys(th))
                Pb = []
                for b_, (src, skf) in ((0, (attnT, lambda kc: [("attnT", kc, th, 0), ("attnT", kc, th, 1)])),
                                       (1, (lruT, lambda kc: [("lruT", kc)])),
                                       (2, (confT, lambda kc: [("confT", kc)]))):
                    bk = next_bank()
                    Pb.append(bk)
                    mm_group(bk, [(wb[:, 4 * b_ + kc, :], src[:, kc, tsl]) for kc in range(4)],
                             r=wkb + [k_ for kc in range(4) for k_ in skf(kc)])
                for b_ in range(3):
                    act(sgb[:, par, b_, :], PB(G[b_]), AF.Sigmoid, [pskey(G[b_]), "vecs"],
                        FW() + [akey(("sgb", par, b_))], bias=V(l, "bm", b_ * 8 + n, 1))
                dve_tt(mt[:, par, 0, :], PB(Pb[0]), sgb[:, par, 0, :], ALU.mult, [pskey(Pb[0]), ("sgb", par, 0)],
                       [akey(("mt", par, 0))])
                dve_tt(mt[:, par, 1, :], PB(Pb[1]), sgb[:, par, 1, :], ALU.mult, [pskey(Pb[1]), ("sgb", par, 1)],
                       [akey(("mt", par, 1))])
                dve_tt(mt[:, par, 0, :], mt[:, par, 0, :], mt[:, par, 1, :], ALU.add,
                       [("mt", par, 0), ("mt", par, 1)], [("mt", par, 0)])
                dve_tt(mt[:, par, 1, :], PB(Pb[2]), sgb[:, par, 2, :], ALU.mult, [pskey(Pb[2]), ("sgb", par, 2)],
                       [("mt", par, 1)])
                dve_tt(mergeT[:, n, tsl], mt[:, par, 0, :], mt[:, par, 1, :], ALU.add,
                       [("mt", par, 0), ("mt", par, 1)], [akey(("mergeT", n, th))])
            if l + 1 < DEPTH and n % 2 == 1:
                mod_tile(l + 1, n // 2)
        if l + 1 < DEPTH:
            mod_finish_A(l + 1)
        if phase_end("F%d" % l):
            tap_list.extend([("mergeT", mergeT, [128, 8, T], BF16)])
            break

        o = 16384 + 12288 + 8192
        mean = AF32(o, [128, T]); o += 4096
        rstd = AF32(o, [128, T]); o += 4096
        sqb = ABF(o, [128, 2, T]); o += 4096
        cvb = ABF(o, [128, 2, T]); o += 4096
        assert o <= SBYTES

        pending_mod = []

        def proj_residual(kind, src, nk, srckeys, gcol):
            for n in range(8):
                if kind == "out":
                    if n % 4 == 0:
                        s_, wk_ = w_next(("out", l, n // 4))
                        wv_ = slot_view(s_, 8, 512)
                    lts = [wv_[:, kc, (n % 4) * 128:(n % 4 + 1) * 128] for kc in range(nk)]
                else:
                    s_, wk_ = w_next(("dn", l, n))
                    wv_ = slot_view(s_, NFF, 128)
                    lts = [wv_[:, kc, :] for kc in range(nk)]
                    if l + 1 < DEPTH and n % 2 == 1:
                        pending_mod.append(n // 2)
                for th in range(2):
                    tsl = slice(th * 512, (th + 1) * 512)
                    bk = next_bank()
                    rh = (lambda kc: src(kc)[:, tsl]) if callable(src) else (lambda kc: src[:, kc, tsl])
                    mm_group(bk, [(lts[kc], rh(kc)) for kc in range(nk)], r=wk_ + srckeys(th))
                    dve_stt(xT[:, n, tsl], PB(bk), der[:, gcol + n:gcol + n + 1], xT[:, n, tsl], ALU.mult, ALU.add,
                            [pskey(bk), ("xT", n, th), ("der", 1), ("der", 3)], [("xT", n, th)])
                while pending_mod:
                    mod_tile(l + 1, pending_mod.pop(0))


        def ln_apply(gname, bname, nxt=False, h2=False):
            layer_norm(lambda c: xT[:, c, :], 8, lambda c: [("xT", c, 0), ("xT", c, 1)], EPS_F, gname)
            for c in range(8):
                xk = [("xT", c, 0), ("xT", c, 1)]
                dve_tt(xT[:, c, :], xT[:, c, :], mean, ALU.subtract, xk + MK, xk)
                dve_tt(xT[:, c, :], xT[:, c, :], rstd, ALU.mult, xk + RSK, xk)
                act(xT[:, c, :], xT[:, c, :], AF.Identity, xk + ["vecs"], xk, bias=V(l, bname, c, 1),
                    scale=V(l, gname, c, 1))
                if nxt:
                    act(hT[:, c, :], xT[:, c, :], AF.Identity, xk + [("der", 0), "modA"],
                        [("hT", c, 0), ("hT", c, 1)], bias=modT[:, c:c + 1], scale=der[:, c:c + 1])
                if h2:
                    act(hT[:, c, :], xT[:, c, :], AF.Identity, xk + [("der", 2), "modB"],
                        [("hT", c, 0), ("hT", c, 1)], bias=modT[:, 24 + c:25 + c], scale=der[:, 16 + c:17 + c])

        def out_ln1_thmajor():
            s0_, wk0_ = w_next(("out", l, 0))
            s1_, wk1_ = w_next(("out", l, 1), la=NSLOT - 2)
            wvs = [slot_view(s0_, 8, 512), slot_view(s1_, 8, 512)]
            wks = [wk0_, wk1_]
            inv = 1.0 / 1024.0
            for th in range(2):
                tsl = slice(th * 512, (th + 1) * 512)
                for n in range(8):
                    wv_ = wvs[n // 4]
                    bk = next_bank()
                    mm_group(bk, [(wv_[:, kc, (n % 4) * 128:(n % 4 + 1) * 128], mergeT[:, kc, tsl]) for kc in range(8)],
                             r=wks[n // 4] + [("mergeT", kc, th) for kc in range(8)])
                    dve_stt(xT[:, n, tsl], PB(bk), der[:, 8 + n:9 + n], xT[:, n, tsl], ALU.mult, ALU.add,
                            [pskey(bk), ("xT", n, th), ("der", 1), ("der", 3)], [("xT", n, th)])
                b1, b2 = next_bank(), next_bank()
                for c in range(8):
                    slot = c % 2
                    cv_ = cvb[:, slot, tsl]
                    sq_ = sqb[:, slot, tsl]
                    xs_ = xT[:, c, tsl]
                    fw_ = FW()
                    P.add("dve", lambda e, cv_=cv_, xs_=xs_: e.tensor_copy(out=cv_, in_=xs_), r=[("xT", c, th)],
                          w=fw_ + [akey(("cvb", slot, th))])
                    act(sq_, xs_, AF.Square, [("xT", c, th)], fw_ + [akey(("sqb", slot, th))])

                    def fn(e, c=c, cv_=cv_, sq_=sq_, b1=b1, b2=b2):
                        e.matmul(PB(b1), onesb[:, :], cv_, start=(c == 0), stop=(c == 7))
                        return e.matmul(PB(b2), onesb[:, :], sq_, start=(c == 0), stop=(c == 7))
                    P.add("pe", fn, r=[("cvb", slot, th), ("sqb", slot, th), "onesb"], w=[pskey(b1), pskey(b2)])
                mk_, rk_ = akey(("mean", th)), akey(("rstd", th))
                act(mean[:, tsl], PB(b1), AF.Copy, [pskey(b1)], [mk_], scale=inv)
                act(rstd[:, tsl], PB(b1), AF.Square, [pskey(b1)], [rk_], scale=inv)
                dve_stt(rstd[:, tsl], PB(b2), inv, rstd[:, tsl], ALU.mult, ALU.subtract, [pskey(b2), rk_], [rk_])
                dve_ts(rstd[:, tsl], rstd[:, tsl], 0.0, ALU.max, [rk_], [rk_], s2=EPS_F, op1=ALU.add)
                act(rstd[:, tsl], rstd[:, tsl], AF.Ln, [rk_], [rk_])
                act(rstd[:, tsl], rstd[:, tsl], AF.Exp, [rk_], [rk_], scale=-0.5)
                for c in range(8):
                    xk = [("xT", c, th)]
                    xc_ = xT[:, c, tsl]
                    dve_tt(xc_, xc_, mean[:, tsl], ALU.subtract, xk + [mk_], xk)
                    dve_tt(xc_, xc_, rstd[:, tsl], ALU.mult, xk + [rk_], xk)
                    act(xc_, xc_, AF.Identity, xk + ["vecs"], xk, bias=V(l, "l1b", c, 1), scale=V(l, "l1g", c, 1))
                    act(hT[:, c, tsl], xc_, AF.Identity, xk + [("der", 2), "modB"], [("hT", c, th)],
                        bias=modT[:, 24 + c:25 + c], scale=der[:, 16 + c:17 + c])

        out_ln1_thmajor()
        if phase_end("G%d" % l):
            break

        fence = list(arena_keys)
        del arena_keys[:]
        first_fence["v"] = fence
        o = 0
        gtail = ABF(o, [128, 2, T]); o += 4096

        def gTc(c):
            if c < 4:
                return lruT[:, c, :]
            if c < 8:
                return confT[:, c - 4, :]
            if c < 12:
                return attnT[:, c - 8, :]
            if c < 16:
                return xbpad2[:, (c - 12) * T:(c - 11) * T]
            if c < 20:
                return upad2[:, (c - 16) * T:(c - 15) * T]
            return gtail[:, c - 20, :]
        NUR = 2
        ur = ABF(o, [128, NUR, 2, NSEG * FPP]); o += NUR * 2 * NSEG * FPP * 2
        cen = AF32(o, [128, NUR, 4, 512]); o += NUR * 4 * 2048
        gl = ABF(o, [128, 2, 512]); o += 2048
        assert o <= SBYTES, o
        oldk = [("lruT", c_) for c_ in range(4)] + [("confT", c_) for c_ in range(4)] + \
               [("attnT", c_, th_, e__) for c_ in range(4) for th_ in range(2) for e__ in range(2)] + \
               [(nm_, c_, th_) for nm_ in ("xbpad", "xbpad_h", "upad", "upad_h") for c_ in range(4) for th_ in range(2)]
        P.add("pool", lambda e, ur=ur: e.memset(ur, 0.0), w=FW() + oldk + [akey("ur_init")])
        ffn_items = []
        for j in range(11):
            for pi in range(2):
                ffn_items.append((j, pi, j * 2 + pi))

        def ffn_up(item, ths=(0, 1)):
            j, pi, c = item
            if pi == 0 and ths[0] == 0:
                s_, wk_ = w_next(("up", l, j))
                ffn_up.cur = (slot_view(s_, 8, 512), wk_)
            wv_, wk_ = ffn_up.cur
            slot = c % NUR
            for th in ths:
                for hv in range(2):
                    bk = (c % 2) * 4 + (hv * 2 + th)
                    co = hv * 256 + pi * 128
                    mm_group(bk, [(wv_[:, kc, co:co + 128], hT[:, kc, th * 512:(th + 1) * 512]) for kc in range(8)],
                             r=wk_ + hkeys(th))
                    src = PB(bk).rearrange("p (s t) -> p s t", s=2)
                    dst = ur[:, slot, hv, :].rearrange("p (s t) -> p s t", s=NSEG)[:, 2 * th:2 * th + 2, 1:1 + SEG]
                    P.add("act", lambda e, dst=dst, src=src: e.activation(out=dst, in_=src, func=AF.Copy),
                          r=[pskey(bk), "ur_init"], w=[akey(("ur", slot, hv, th))])
                    act(cen[:, slot, hv * 2 + th, :], PB(bk), AF.Identity, [pskey(bk), "vecs", "ur_init"],
                        [akey(("cen", slot, hv, th))], bias=V(l, "fcb", hv * 22 + c, 1),
                        scale=V(l, "fcw", 44 + hv * 22 + c, 1))

        def ffn_halo(item):
            j, pi, c = item
            slot = c % NUR
            for hv in range(2):
                uvv = ur[:, slot, hv, :].rearrange("p (s t) -> p s t", s=NSEG)
                rk_ = [("ur", slot, hv, 0), ("ur", slot, hv, 1), "flag"]
                dve_ts(uvv[:, 1:4, 0:1], uvv[:, 0:3, SEG:SEG + 1], flag[:, 0:1], ALU.mult, rk_,
                       [akey(("urh", slot, hv, 0))])
                dve_ts(uvv[:, 0:3, SEG + 1:SEG + 2], uvv[:, 1:4, 1:2], flag[:, 0:1], ALU.mult, rk_,
                       [akey(("urh", slot, hv, 1))])

        def ffn_conv(item):
            j, pi, c = item
            slot = c % NUR
            for k in (0, 2):
                for th in range(2):
                    for hv in range(2):
                        uvv = ur[:, slot, hv, :].rearrange("p (s t) -> p s t", s=NSEG)
                        acc = cen[:, slot, hv * 2 + th, :].rearrange("p (s t) -> p s t", s=2)
                        ck_ = ("cen", slot, hv, th)
                        rk_ = [("ur", slot, hv, 0), ("ur", slot, hv, 1), ("urh", slot, hv, 0), ("urh", slot, hv, 1),
                               ck_, "vecs"]
                        dve_stt(acc, uvv[:, 2 * th:2 * th + 2, k:k + SEG], V(l, "fcw", k * 44 + hv * 22 + c, 1), acc,
                                ALU.mult, ALU.add, rk_, [ck_])
            for th in range(2):
                act(gl[:, th, :], cen[:, slot, th, :], AF.Gelu_apprx_tanh, [("cen", slot, 0, th)], [akey(("gl", th))])
                dve_tt(gTc(c)[:, th * 512:(th + 1) * 512], cen[:, slot, 2 + th, :], gl[:, th, :], ALU.mult,
                       [("cen", slot, 1, th), ("gl", th), "ur_init"], [akey(("gT", c, th))])

        ffn_up(ffn_items[0], ths=(0,))
        ffn_up(ffn_items[1], ths=(0,))
        ffn_up(ffn_items[0], ths=(1,))
        ffn_halo(ffn_items[0])
        ffn_up(ffn_items[1], ths=(1,))
        ffn_conv(ffn_items[0])
        ffn_halo(ffn_items[1])
        for i in range(2, len(ffn_items) + 1):
            if i < len(ffn_items):
                ffn_up(ffn_items[i])
            if i >= 1:
                ffn_conv(ffn_items[i - 1])
            if i < len(ffn_items):
                ffn_halo(ffn_items[i])
        rot["i"] = 0
        if phase_end("H%d" % l):
            tap_list.extend([("gtail", gtail, [128, 2, T], BF16), ("lruT", lruT[:], [128, 4, T], BF16)])
            break
        mean = AF32(o, [128, T]); o += 4096
        rstd = AF32(o, [128, T]); o += 4096
        sqb = ABF(o, [128, 2, T]); o += 4096
        cvb = ABF(o, [128, 2, T]); o += 4096
        assert o <= SBYTES, o
        def dn_ln2_thmajor():
            nxt_ = (l + 1 < DEPTH)
            inv = 1.0 / 1024.0
            for th in range(2):
                tsl = slice(th * 512, (th + 1) * 512)
                early = {}
                if th == 0:
                    KE = NFF - 2
                    for n in range(2):
                        s_, wk_ = w_next(("dn", l, n), la=(NSLOT - 1 if n == 0 else NSLOT - 2))
                        wv_ = slot_view(s_, NFF, 128)
                        bk = next_bank()
                        oa_ = PB(bk)

                        def fnA(e, wv_=wv_, oa_=oa_, tsl=tsl):
                            ins = None
                            for kc in range(KE):
                                ins = e.matmul(oa_, wv_[:, kc, :], gTc(kc)[:, tsl], start=(kc == 0), stop=False)
                            return ins
                        P.add("pe", fnA, r=wk_ + [("gT", kc, th) for kc in range(KE)], w=[pskey(bk)])
                        early[n] = (wv_, wk_, bk)
                for n in range(8):
                    if n in early:
                        wv_, wk_, bk = early[n]
                        oa_ = PB(bk)

                        def fnB(e, wv_=wv_, oa_=oa_, tsl=tsl):
                            ins = None
                            for kc in range(NFF - 2, NFF):
                                ins = e.matmul(oa_, wv_[:, kc, :], gTc(kc)[:, tsl], start=False, stop=(kc == NFF - 1))
                            return ins
                        P.add("pe", fnB, r=wk_ + [("gT", kc, th) for kc in range(NFF - 2, NFF)], w=[pskey(bk)])
                    else:
                        s_, wk_ = w_next((("dn" if th == 0 else "dn2"), l, n))
                        wv_ = slot_view(s_, NFF, 128)
                        bk = next_bank()
                        mm_group(bk, [(wv_[:, kc, :], gTc(kc)[:, tsl]) for kc in range(NFF)],
                                 r=wk_ + [("gT", kc, th) for kc in range(NFF)])
                    dve_stt(xT[:, n, tsl], PB(bk), der[:, 24 + n:25 + n], xT[:, n, tsl], ALU.mult, ALU.add,
                            [pskey(bk), ("xT", n, th), ("der", 1), ("der", 3)], [("xT", n, th)])
                b1, b2 = next_bank(), next_bank()
                for c in range(8):
                    slot = c % 2
                    cv_ = cvb[:, slot, tsl]
                    sq_ = sqb[:, slot, tsl]
                    xs_ = xT[:, c, tsl]
                    fw_ = FW()
                    P.add("dve", lambda e, cv_=cv_, xs_=xs_: e.tensor_copy(out=cv_, in_=xs_), r=[("xT", c, th)],
                          w=fw_ + [akey(("cvb", slot, th))])
                    act(sq_, xs_, AF.Square, [("xT", c, th)], fw_ + [akey(("sqb", slot, th))])

                    def fn(e, c=c, cv_=cv_, sq_=sq_, b1=b1, b2=b2):
                        e.matmul(PB(b1), onesb[:, :], cv_, start=(c == 0), stop=(c == 7))
                        return e.matmul(PB(b2), onesb[:, :], sq_, start=(c == 0), stop=(c == 7))
                    P.add("pe", fn, r=[("cvb", slot, th), ("sqb", slot, th), "onesb"], w=[pskey(b1), pskey(b2)])
                mk_, rk_ = akey(("mean", th)), akey(("rstd", th))
                act(mean[:, tsl], PB(b1), AF.Copy, [pskey(b1)], [mk_], scale=inv)
                act(rstd[:, tsl], PB(b1), AF.Square, [pskey(b1)], [rk_], scale=inv)
                dve_stt(rstd[:, tsl], PB(b2), inv, rstd[:, tsl], ALU.mult, ALU.subtract, [pskey(b2), rk_], [rk_])
                dve_ts(rstd[:, tsl], rstd[:, tsl], 0.0, ALU.max, [rk_], [rk_], s2=EPS_F, op1=ALU.add)
                act(rstd[:, tsl], rstd[:, tsl], AF.Ln, [rk_], [rk_])
                act(rstd[:, tsl], rstd[:, tsl], AF.Exp, [rk_], [rk_], scale=-0.5)
                for c in range(8):
                    xk = [("xT", c, th)]
                    xc_ = xT[:, c, tsl]
                    dve_tt(xc_, xc_, mean[:, tsl], ALU.subtract, xk + [mk_], xk)
                    dve_tt(xc_, xc_, rstd[:, tsl], ALU.mult, xk + [rk_], xk)
                    act(xc_, xc_, AF.Identity, xk + ["vecs"], xk, bias=V(l, "l2b", c, 1), scale=V(l, "l2g", c, 1))
                    if nxt_:
                        act(hT[:, c, tsl], xc_, AF.Identity, xk + [("der", 0), "modA"], [("hT", c, th)],
                            bias=modT[:, c:c + 1], scale=der[:, c:c + 1])

        dn_ln2_thmajor()
        if phase_end("I%d" % l):
            break

    if not stopped["v"]:
        fence = list(arena_keys)
        del arena_keys[:]
        YT = [AF32(i * 4096, [128, 1024]) for i in range(8)]
        YALL = AF32(0, [128, 8, 1024])
        for tbh in range(2):
            for c in range(8):
                b = next_bank()

                def fn(e, c=c, tbh=tbh, b=b):
                    ins = None
                    for q in range(4):
                        tb = tbh * 4 + q
                        ins = e.transpose(out=PB(b)[:, q * 128:(q + 1) * 128], in_=xT[:, c, tb * 128:(tb + 1) * 128],
                                          identity=ident[:])
                    return ins
                P.add("pe", fn, r=[("xT", c, tbh), "ident"], w=[pskey(b)])
                dst = YALL[:, tbh * 4:(tbh + 1) * 4, c * 128:(c + 1) * 128]
                src = PB(b).rearrange("p (q t) -> p q t", q=4)
                wk = [("yt", tbh, c)] + fence
                if c % 2 == 0:
                    P.add("act", lambda e, dst=dst, src=src: e.activation(out=dst, in_=src, func=AF.Copy),
                          r=[pskey(b)], w=wk)
                else:
                    P.add("dve", lambda e, dst=dst, src=src: e.tensor_copy(out=dst, in_=src), r=[pskey(b)], w=wk)
            for tb in range(tbh * 4, tbh * 4 + 4):
                yk = [("yt", tbh, c) for c in range(8)]
                dma("sp", y_d[tb * 128:(tb + 1) * 128, :], YT[tb], yk, [], "st_y%d" % tb)
    tap_outs = []
    for ti, (name, ap, shape, dt) in enumerate(tap_list):
        dd = dout("tap_" + name, shape, dt)
        tap_outs.append("tap_" + name)
        P.add("sp", lambda e, dd=dd, ap=ap: e.dma_start(out=dd, in_=ap), r=list(P.kw.keys()), w=[], chan="tap%d" % ti)
    if stopped["v"]:
        dd = dout("tap_xT", [128, 8, T], F32)
        P.add("sp", lambda e, dd=dd: e.dma_start(out=dd, in_=xT[:]), r=list(P.kw.keys()), w=[], chan="tapx")
        dd2 = dout("tap_hT", [128, 8, T], BF16)
        P.add("sp", lambda e, dd2=dd2: e.dma_start(out=dd2, in_=hT[:]), r=list(P.kw.keys()), w=[], chan="taph")
        dd3 = dout("tap_modT", [128, 48], F32)
        P.add("sp", lambda e, dd3=dd3: e.dma_start(out=dd3, in_=modT[:]), r=list(P.kw.keys()), w=[], chan="tapm")
    last_dma = [P.chan_last[c] for c in P.chan_last]
    fin_op = Op()
    fin_op.eng = "sp"
    fin_op.fn = lambda e: e.nop()
    fin_op.chan = None
    fin_op.sig = False
    fin_op.gid = len(P.ops)
    fin_op.deps = set(last_dma)
    P.ops.append(fin_op)

    block = es.enter_context(nc.Block())
    P.emit(nc, es, block)
    es.close()
    return nc


def _fm(v):
    v = np.asarray(v, np.float32)
    return np.ascontiguousarray(v.reshape(-1, 128).T)


def _rope_tables(active):
    c = np.ones((128, T), np.float32)
    s = np.zeros((128, T), np.float32)
    if active:
        t = np.arange(T)
        row = (t // 64).astype(np.float32)
        colp = (t % 64).astype(np.float32)
        inv = (np.float32(10000.0) ** (-np.arange(16, dtype=np.float32) / np.float32(16))).astype(np.float32)
        for p in range(128):
            dd = p % 64
            a = dd // 32
            jj = dd % 32
            f = jj % 16
            pos = row if a == 0 else colp
            ang = (pos * inv[f]).astype(np.float32)
            c[p] = np.cos(ang)
            s[p] = -np.sin(ang) if jj < 16 else np.sin(ang)
    return c, s


def _mask_table(sample):
    m = np.zeros((128, MASK_COLS), np.float32)
    if not sample:
        m[:, 0:512] = NEGM
    k = np.arange(128)[:, None]
    for th in range(2):
        for (kb, jl, jh) in band_tiles(th):
            off = MASK_OFF[(th, kb)]
            for j in range(jl, jh + 1):
                q = np.arange(128)[None, :]
                if sample:
                    ok = np.abs((kb * 128 + k) - (j * 128 + q)) <= 128
                else:
                    ok = np.broadcast_to(np.array(kb // 2 == j // 2), (128, 128))
                m[:, off + (j - jl) * 128: off + (j - jl + 1) * 128] = np.where(ok, 0.0, NEGM)
    return m


def _partner(cols64):
    cols64 = np.asarray(cols64)
    idx = np.arange(64)
    j = idx % 32
    p = np.where(j < 16, idx + 16, idx - 16)
    return cols64[p]


def _prep_shared(inp):
    sh = {}
    w_in = np.asarray(inp["w_in"], np.float32)
    cols = []
    qc = lambda h: np.arange(h * 64, (h + 1) * 64)
    kc = lambda g: 512 + np.arange(g * 64, (g + 1) * 64)
    for name in WIN_CHUNKS:
        if name.startswith("qp"):
            c = int(name[2])
            cols.append(np.concatenate([_partner(qc(2 * c)), _partner(qc(2 * c + 1))]))
        elif name.startswith("q"):
            c = int(name[1])
            cols.append(np.concatenate([qc(2 * c), qc(2 * c + 1)]))
        elif name == "kA":
            cols.append(np.concatenate([kc(0), kc(1)]))
        elif name == "kB":
            cols.append(np.concatenate([kc(1), kc(0)]))
        elif name == "kAp":
            cols.append(np.concatenate([_partner(kc(0)), _partner(kc(1))]))
        elif name == "kBp":
            cols.append(np.concatenate([_partner(kc(1)), _partner(kc(0))]))
        elif name == "v":
            cols.append(640 + np.arange(128))
        elif name.startswith("xb"):
            c = int(name[2])
            cols.append(768 + c * 128 + np.arange(128))
        elif name.startswith("a"):
            c = int(name[1])
            cols.append(1280 + c * 128 + np.arange(128))
        elif name.startswith("g"):
            c = int(name[1])
            cols.append(1280 + 512 + c * 128 + np.arange(128))
    cols = np.concatenate(cols)
    sh["w_in_ext"] = np.ascontiguousarray(w_in[:, :, cols])
    vec = np.zeros((DEPTH, 128, NV), np.float32)
    f = lambda a: np.asarray(a, np.float32)
    for l in range(DEPTH):
        def put(name, arr):
            arr = np.asarray(arr, np.float32)
            vec[l, :, _VOFF[name]:_VOFF[name] + arr.shape[1]] = arr
        put("bmod", _fm(f(inp["b_mod"])[l]))
        put("lcw", np.concatenate([_fm(f(inp["lru_conv_w"])[l, k]) for k in range(4)], axis=1))
        put("lcb", _fm(f(inp["lru_conv_b"])[l]))
        put("lba", np.concatenate([_fm(f(inp["lru_ba"])[l, d]) for d in range(2)], axis=1))
        put("lbx", np.concatenate([_fm(f(inp["lru_bx"])[l, d]) for d in range(2)], axis=1))
        put("llam", np.concatenate([_fm(f(inp["lru_lambda"])[l, d]) for d in range(2)], axis=1))
        put("cw", np.concatenate([_fm(f(inp["conf_dw_w"])[l, k]) for k in range(31)], axis=1))
        put("cb", _fm(f(inp["conf_dw_b"])[l]))
        put("clg", _fm(f(inp["conf_ln_g"])[l]))
        put("clb", _fm(f(inp["conf_ln_b"])[l]))
        put("bm", _fm(f(inp["b_merge"])[l]))
        put("l1g", _fm(f(inp["ln1_g"])[l]))
        put("l1b", _fm(f(inp["ln1_b"])[l]))
        put("fcw", np.concatenate([_fm(f(inp["ffn_conv_w"])[l, k]) for k in range(3)], axis=1))
        put("fcb", _fm(f(inp["ffn_conv_b"])[l]))
        put("l2g", _fm(f(inp["ln2_g"])[l]))
        put("l2b", _fm(f(inp["ln2_b"])[l]))
        put("sink", np.broadcast_to(f(inp["attn_sink"])[l][None, :], (128, 8)))
    sh["vecs"] = vec
    lw = np.zeros((DEPTH, 128, 16, 128), np.float32)
    for l in range(DEPTH):
        for d in range(2):
            for gt, nm in enumerate(("lru_wa", "lru_wx")):
                wsrc = f(inp[nm])[l, d]
                for c in range(4):
                    i = (d * 2 + gt) * 4 + c
                    for bb in range(2):
                        lw[l, bb * 64:(bb + 1) * 64, i, bb * 64:(bb + 1) * 64] = wsrc[2 * c + bb]
    sh["lruw"] = lw
    sh["ident"] = np.eye(128, dtype=np.float32)
    for k in ("w_mod", "w_branch", "w_merge", "w_out", "ffn_w_up", "ffn_w_down"):
        sh[k] = np.ascontiguousarray(np.asarray(inp[k], np.float32))
    return sh


def make_in_maps(inp):
    sh = _prep_shared(inp)
    xs = np.asarray(inp["x_sample"], np.float32)
    xp = np.asarray(inp["x_prompt"], np.float32)
    ck = np.asarray(inp["cache_k"], np.float32)
    cv = np.asarray(inp["cache_v"], np.float32)
    st = np.asarray(inp["state_lru"], np.float32)
    cc = np.asarray(inp["c"], np.float32)
    cctx = np.asarray(inp["c_ctx"], np.float32)
    rc_s, rs_s = _rope_tables(True)
    rc_p, rs_p = _rope_tables(False)
    mk_s = _mask_table(True)
    mk_p = _mask_table(False)
    maps = []
    for core in range(8):
        m = dict(sh)
        if core < 4:
            b = core
            m["x"] = np.ascontiguousarray(xs[b])
            m["cond"] = _fm(cc[b])
            m["ck"] = np.ascontiguousarray(ck[b].reshape(DEPTH, 512, 128))
            m["cv"] = np.ascontiguousarray(cv[b].reshape(DEPTH, 512, 128))
            h0 = np.zeros((128, DEPTH * 8), np.float32)
            for l in range(DEPTH):
                for d in range(2):
                    h0[:, l * 8 + d * 4:l * 8 + d * 4 + 4] = _fm(st[b, l, d])
            m["h0"] = h0
            m["flag"] = np.ones((128, 1), np.float32)
            m["ropec"], m["ropes"], m["maskb"] = rc_s, rs_s, mk_s
        else:
            i = core - 4
            m["x"] = np.ascontiguousarray(xp[4 * i:4 * i + 4].reshape(T, D))
            m["cond"] = _fm(cctx)
            m["ck"] = np.zeros((DEPTH, 512, 128), np.float32)
            m["cv"] = np.zeros((DEPTH, 512, 128), np.float32)
            m["h0"] = np.zeros((128, DEPTH * 8), np.float32)
            m["flag"] = np.zeros((128, 1), np.float32)
            m["ropec"], m["ropes"], m["maskb"] = rc_p, rs_p, mk_p
        maps.append(m)
    return maps


_NC_CACHE = {}


def kernel(**inputs):
    if "nc" not in _NC_CACHE:
        _NC_CACHE["nc"] = build_program()
    nc = _NC_CACHE["nc"]
    maps = make_in_maps(inputs)
    res = run_bass_kernel_spmd(nc, maps, core_ids=list(range(8)))
    R = res.results
    y_prompt = np.zeros((16, 256, D), np.float32)
    y_sample = np.zeros((4, T, D), np.float32)
    nk = np.zeros((16, DEPTH, 256, 2, 64), np.float32)
    nv = np.zeros((16, DEPTH, 256, 2, 64), np.float32)
    nh = np.zeros((16, DEPTH, 2, 512), np.float32)
    for core in range(8):
        r = R[core]
        if core < 4:
            y_sample[core] = r["y"]
        else:
            i = core - 4
            y_prompt[4 * i:4 * i + 4] = r["y"].reshape(4, 256, D)
            for s in range(4):
                for l in range(DEPTH):
                    nk[4 * i + s, l] = r["nk"][l, s * 256:(s + 1) * 256].reshape(256, 2, 64)
                    nv[4 * i + s, l] = r["nv"][l, s * 256:(s + 1) * 256].reshape(256, 2, 64)
                    nh[4 * i + s, l] = r["nh"][l].reshape(4, 2, 512)[s]
    return (y_prompt, y_sample, nk, nv, nh)
```

```python
from contextlib import ExitStack
import numpy as np
import concourse.bass as bass
import concourse.mybir as mybir
from concourse.bass_utils import run_bass_kernel_spmd

F32 = mybir.dt.float32
BF16 = mybir.dt.bfloat16
AF = mybir.ActivationFunctionType
ALU = mybir.AluOpType

D = 1024
T = 1024
DEPTH = 2
NSEG = 4
SEG = 256
DFF = 2816
NFF = 22
ALPHA = (2 * DEPTH) ** 0.25
LN_EPS = 1e-5
EPS_F = LN_EPS / (ALPHA * ALPHA)
NEGM = -30000.0
NSLOT = 4
SLOT_ELEMS = 4096
XBP = 259
UPP = 286
FPP = 258

_VOFF = {}
_o = 0
for _n, _w in [("bmod", 48), ("lcw", 16), ("lcb", 4), ("lba", 8), ("lbx", 8), ("llam", 8), ("cw", 124),
               ("cb", 4), ("clg", 4), ("clb", 4), ("bm", 24), ("l1g", 8), ("l1b", 8), ("fcw", 132),
               ("fcb", 44), ("l2g", 8), ("l2b", 8), ("sink", 8)]:
    _VOFF[_n] = _o
    _o += _w
NV = _o

WIN_CHUNKS = ["q0", "qp0", "q1", "qp1", "q2", "qp2", "q3", "qp3", "kA", "kAp",
              "v", "xb0", "xb1", "xb2", "xb3", "a0", "g0", "a1", "g1", "a2", "g2", "a3", "g3"]
NWIN = len(WIN_CHUNKS) * 128
WIN_TILES = [(0, 512), (512, 512), (1024, 256), (1280, 128), (1408, 512), (1920, 512), (2432, 512)]


def band_tiles(th):
    out = []
    for kb in range(max(0, 4 * th - 1), min(8, 4 * th + 5)):
        jl = max(4 * th, kb - 1)
        jh = min(4 * th + 3, kb + 1)
        out.append((kb, jl, jh))
    return out


MASK_OFF = {}
_m = 512
for _th in range(2):
    for (_kb, _jl, _jh) in band_tiles(_th):
        MASK_OFF[(_th, _kb)] = _m
        _m += 128 * (_jh - _jl + 1)
MASK_COLS = _m


class Op:
    __slots__ = ("eng", "fn", "deps", "chan", "cord", "idx", "sig", "sigcount", "gid")


class Prog:
    ENGS = ("pe", "act", "dve", "pool", "sp")

    def __init__(self):
        self.ops = []
        self.kw = {}
        self.kr = {}
        self.chan_last = {}
        self.chan_count = {}

    def add(self, eng, fn, r=(), w=(), chan=None):
        op = Op()
        op.eng = eng
        op.fn = fn
        op.chan = chan
        op.sig = False
        op.gid = len(self.ops)
        deps = set()
        psr = [k for k in r if isinstance(k, tuple) and k and k[0] == "ps"]
        if psr:
            r = [k for k in r if k not in psr]
            w = list(w) + psr
        for k in r:
            p = self.kw.get(k)
            if p is not None:
                deps.add(p)
        for k in w:
            p = self.kw.get(k)
            if p is not None:
                deps.add(p)
            for q in self.kr.get(k, ()):
                deps.add(q)
        if chan is not None:
            p = self.chan_last.get(chan)
            if p is not None:
                deps.add(p)
            self.chan_last[chan] = op.gid
            self.chan_count[chan] = self.chan_count.get(chan, 0) + 1
            op.cord = self.chan_count[chan]
        for k in r:
            self.kr.setdefault(k, []).append(op.gid)
        for k in w:
            self.kw[k] = op.gid
            self.kr[k] = []
        op.deps = deps
        self.ops.append(op)
        return op.gid

    def emit(self, nc, es, block):
        ops = self.ops
        for op in ops:
            for d in op.deps:
                if ops[d].chan is None:
                    ops[d].sig = True
        cnt = {e: 0 for e in self.ENGS}
        for op in ops:
            if op.chan is None and op.sig:
                cnt[op.eng] += 1
                op.sigcount = cnt[op.eng]
        esem = {e: es.enter_context(nc.semaphore("s_" + e)) for e in ("pe", "act", "dve", "pool")}
        csem = {c: es.enter_context(nc.semaphore("c_" + c)) for c in self.chan_count}
        per_eng = {e: [op for op in ops if op.eng == e] for e in self.ENGS}

        def run(e, name):
            known = {}
            for op in per_eng[name]:
                waits = {}
                for d in op.deps:
                    p = ops[d]
                    if p.chan is not None:
                        key = ("c", p.chan)
                        val = 16 * p.cord
                    else:
                        if p.eng == "pe" and name == "pe":
                            continue
                        key = ("e", p.eng)
                        val = p.sigcount
                    if known.get(key, 0) >= val:
                        continue
                    if waits.get(key, 0) < val:
                        waits[key] = val
                for key, val in waits.items():
                    sem = csem[key[1]] if key[0] == "c" else esem[key[1]]
                    e.wait_ge(sem, val)
                    known[key] = val
                ins = op.fn(e)
                if op.chan is not None:
                    ins.then_inc(csem[op.chan], 16)
                elif op.sig:
                    ins.then_inc(esem[name], 1)

        @block.tensor
        def _(e):
            run(e, "pe")

        @block.scalar
        def _(e):
            run(e, "act")

        @block.vector
        def _(e):
            run(e, "dve")

        @block.gpsimd
        def _(e):
            run(e, "pool")

        @block.sync
        def _(e):
            run(e, "sp")


def build_program(stop_after=None, taps=()):
    nc = bass.Bass("TRN2", target_bir_lowering=False)
    es = ExitStack()
    P = Prog()

    def din(name, shape, dt=F32):
        return nc.dram_tensor(name, list(shape), dt, kind="ExternalInput").ap()

    def dout(name, shape, dt=F32):
        return nc.dram_tensor(name, list(shape), dt, kind="ExternalOutput").ap()

    def sb(name, shape, dt):
        return es.enter_context(nc.sbuf_tensor("sb_" + name, list(shape), dt))

    x_d = din("x", [T, D])
    cond_d = din("cond", [128, 8])
    ck_d = din("ck", [DEPTH, 512, 128])
    cv_d = din("cv", [DEPTH, 512, 128])
    h0_d = din("h0", [128, DEPTH * 8])
    flag_d = din("flag", [128, 1])
    ropec_d = din("ropec", [128, T])
    ropes_d = din("ropes", [128, T])
    maskb_d = din("maskb", [128, MASK_COLS])
    ident_d = din("ident", [128, 128])
    vecs_d = din("vecs", [DEPTH, 128, NV])
    lruw_d = din("lruw", [DEPTH, 128, 16, 128])
    wmod_d = din("w_mod", [DEPTH, D, 6 * D])
    win_d = din("w_in_ext", [DEPTH, D, NWIN])
    wbr_d = din("w_branch", [DEPTH, 3, 512, D])
    wmg_d = din("w_merge", [DEPTH, D, 3 * D])
    wout_d = din("w_out", [DEPTH, D, D])
    wup_d = din("ffn_w_up", [DEPTH, D, 2 * DFF])
    wdn_d = din("ffn_w_down", [DEPTH, DFF, D])

    y_d = dout("y", [T, D])
    nk_d = dout("nk", [DEPTH, T, 128])
    nv_d = dout("nv", [DEPTH, T, 128])
    nh_d = dout("nh", [DEPTH, 32, 128])

    xT = sb("xT", [128, 8, T], F32)
    hT = sb("hT", [128, 8, T], BF16)
    wsl = sb("wsl", [128, NSLOT, SLOT_ELEMS], BF16)
    ident = sb("ident", [128, 128], F32)
    identb = sb("identb", [128, 128], BF16)
    onesb = sb("onesb", [128, 128], BF16)
    vecs = sb("vecs", [128, DEPTH, NV], F32)
    condt = sb("condt", [128, 8], F32)
    scond = sb("scond", [128, 8], BF16)
    flag = sb("flag", [128, 1], F32)
    ctxb = sb("ctxb", [128, 1], F32)
    h0t = sb("h0t", [128, DEPTH * 8], F32)
    modT = sb("modT", [128, 48], F32)
    der = sb("der", [128, 64], F32)
    ropec = sb("ropec", [128, T], BF16)
    ropes = sb("ropes", [128, T], BF16)
    maskb = sb("maskb", [128, MASK_COLS], BF16)
    lruw = sb("lruw", [128, 16, 128], BF16)
    lrudg = sb("lrudg", [128, 16, 128], BF16)
    esink = sb("esink", [128, 8], F32)
    fin = sb("fin", [128, 32], F32)
    fint = sb("fint", [32, 128], F32)
    initb = sb("initb", [128, 8], F32)
    attnT = sb("attnT", [128, 4, T], BF16)
    lruT = sb("lruT", [128, 4, T], BF16)
    confT = sb("confT", [128, 4, T], BF16)
    xbpad2 = sb("xbpad", [128, 4 * NSEG * XBP], BF16)
    xbpad = xbpad2[:, :].rearrange("p (c x) -> p c x", c=4)
    upad2 = sb("upad", [128, 4 * NSEG * UPP], BF16)
    upad = upad2[:, :].rearrange("p (c x) -> p c x", c=4)
    SBYTES = 60 * 1024
    arena = sb("arena", [128, SBYTES // 4], F32)
    arena_b = arena[:].bitcast(BF16) if hasattr(arena[:], "bitcast") else None

    psum = es.enter_context(nc.psum_tensor("ps", [128, 8, 512], F32))

    def AF32(off_bytes, shape):
        n = int(np.prod(shape[1:]))
        o = off_bytes // 4
        ap = arena[0:shape[0], o:o + n]
        return ap if len(shape) == 2 else _reshape(ap, shape[1:])

    def ABF(off_bytes, shape):
        n = int(np.prod(shape[1:]))
        o = off_bytes // 2
        ap = arena_b[0:shape[0], o:o + n]
        return ap if len(shape) == 2 else _reshape(ap, shape[1:])

    def _reshape(ap, free):
        if len(free) == 2:
            return ap.rearrange("p (a b) -> p a b", a=free[0])
        if len(free) == 3:
            return ap.rearrange("p (a b c) -> p a b c", a=free[0], b=free[1])
        raise ValueError

    arena_keys = []

    def akey(name):
        arena_keys.append(name)
        return name

    wtiles = []

    def wt_add(kind, parts):
        wtiles.append((kind, parts))

    def slot_view(s, kc, ncols, np_=128):
        return wsl[0:np_, s, 0:kc * ncols].rearrange("p (k n) -> p k n", k=kc)

    def wt_mod(l, j):
        src = wmod_d[l, :, j * 512:(j + 1) * 512].rearrange("(k p) n -> p k n", p=128)
        wt_add(("mod", l, j), [(lambda s: slot_view(s, 8, 512), src)])

    for l in range(DEPTH):
        if l == 0:
            for j in range(4):
                wt_mod(0, j)
        def wt_win(j):
            c0, ncols = WIN_TILES[j]
            src = win_d[l, :, c0:c0 + ncols].rearrange("(k p) n -> p k n", p=128)
            wt_add(("win", l, j), [(lambda s, ncols=ncols: slot_view(s, 8, ncols), src)])
        for j in range(4):
            wt_win(j)
        for j in range(4, 12):
            wt_mod(l, j)
        for j in range(4, 7):
            wt_win(j)
        for n in range(8):
            parts = []
            for b in range(3):
                src = wmg_d[l, :, b * D + n * 128: b * D + (n + 1) * 128].rearrange("(k p) n -> p k n", p=128)
                parts.append((lambda s, b=b: slot_view(s, 24, 128)[:, b * 8:(b + 1) * 8, :], src))
            wt_add(("mg", l, n), parts)
            parts = []
            for b in range(3):
                src = wbr_d[l, b, :, n * 128:(n + 1) * 128].rearrange("(k p) n -> p k n", p=128)
                parts.append((lambda s, b=b: slot_view(s, 12, 128)[:, 4 * b:4 * b + 4, :], src))
            wt_add(("br", l, n), parts)
            if l + 1 < DEPTH and n % 2 == 1:
                wt_mod(l + 1, n // 2)
        for j in range(2):
            src = wout_d[l, :, j * 512:(j + 1) * 512].rearrange("(k p) n -> p k n", p=128)
            wt_add(("out", l, j), [(lambda s: slot_view(s, 8, 512), src)])
        for j in range(11):
            parts = []
            for hv in range(2):
                c0 = hv * DFF + j * 256
                src = wup_d[l, :, c0:c0 + 256].rearrange("(k p) n -> p k n", p=128)
                parts.append((lambda s, hv=hv: slot_view(s, 8, 512)[:, :, hv * 256:(hv + 1) * 256], src))
            wt_add(("up", l, j), parts)
        for n in range(8):
            src = wdn_d[l, :, n * 128:(n + 1) * 128].rearrange("(k p) n -> p k n", p=128)
            wt_add(("dn", l, n), [(lambda s: slot_view(s, NFF, 128), src)])
        for n in range(8):
            src = wdn_d[l, :, n * 128:(n + 1) * 128].rearrange("(k p) n -> p k n", p=128)
            wt_add(("dn2", l, n), [(lambda s: slot_view(s, NFF, 128), src)])

    wstate = {"issued": 0, "next": 0}

    def w_issue_upto(j):
        while wstate["issued"] <= min(j, len(wtiles) - 1):
            i = wstate["issued"]
            kind, parts = wtiles[i]
            s = i % NSLOT
            for pi, (dst_fn, src) in enumerate(parts):
                dst = dst_fn(s)
                P.add("pool", (lambda e, dst=dst, src=src: e.dma_start(out=dst, in_=src)),
                      w=[("w", s, pi)],
                      chan="w%d_%d" % (s, pi))
            wstate["issued"] += 1

    def w_next(kind, la=NSLOT - 1):
        j = wstate["next"]
        assert wtiles[j][0] == kind, (wtiles[j][0], kind)
        w_issue_upto(j + la)
        wstate["next"] += 1
        s = j % NSLOT
        return s, [("w", s, q) for q in range(3)]

    def PB(b):
        return psum[:, b, :]

    pskey = lambda b: ("ps", b)
    rot = {"i": 0}

    MODB = 7

    def next_bank():
        b = rot["i"] % 7
        rot["i"] += 1
        return b

    def act(out, in_, func, r, w, bias=None, scale=None):
        kw = {}
        if bias is not None:
            kw["bias"] = bias
        if scale is not None:
            kw["scale"] = scale
        return P.add("act", lambda e: e.activation(out=out, in_=in_, func=func, **kw), r=r, w=w)

    def dve_tt(out, in0, in1, op, r, w, eng="dve"):
        return P.add(eng, lambda e: e.tensor_tensor(out=out, in0=in0, in1=in1, op=op), r=r, w=w)

    def dve_ts(out, in0, s1, op0, r, w, s2=None, op1=None, eng="dve"):
        if op1 is None:
            return P.add(eng, lambda e: e.tensor_scalar(out=out, in0=in0, scalar1=s1, scalar2=0.0, op0=op0, op1=ALU.add),
                         r=r, w=w)
        return P.add(eng, lambda e: e.tensor_scalar(out=out, in0=in0, scalar1=s1, scalar2=s2, op0=op0, op1=op1), r=r, w=w)

    def dve_stt(out, in0, scalar, in1, op0, op1, r, w, eng="dve"):
        return P.add(eng, lambda e: e.scalar_tensor_tensor(out=out, in0=in0, scalar=scalar, in1=in1, op0=op0, op1=op1),
                     r=r, w=w)

    def dma(q, out, in_, r, w, chan):
        return P.add(q, lambda e: e.dma_start(out=out, in_=in_), r=r, w=w, chan=chan)

    def mm_group(bank, steps, r, cols=None, extra_w=()):
        outap = PB(bank) if cols is None else PB(bank)[:, cols[0]:cols[1]]

        def fn(e):
            ins = None
            n = len(steps)
            for i, (lt, rh) in enumerate(steps):
                ins = e.matmul(outap, lt, rh, start=(i == 0), stop=(i == n - 1))
            return ins
        return P.add("pe", fn, r=r, w=[pskey(bank)] + list(extra_w))

    stopped = {"v": False}
    tap_list = []

    def phase_end(name):
        if stop_after == name:
            stopped["v"] = True
        return stopped["v"]

    dma("sp", ident[:], ident_d, [], ["ident"], "ld0")
    dma("sp", vecs[:], vecs_d.rearrange("l p n -> p l n"), [], ["vecs"], "ld1")
    dma("sp", condt[:], cond_d, [], ["condt"], "ld2")
    dma("sp", flag[:], flag_d, [], ["flag"], "ld3")
    dma("sp", h0t[:], h0_d, [], ["h0t"], "ld0")

    P.add("dve", lambda e: e.memset(onesb[:], 1.0), w=["onesb"])
    act(identb[:], ident[:], AF.Copy, ["ident"], ["identb"])
    P.add("dve", lambda e: e.tensor_scalar(out=ctxb[:], in0=flag[:], scalar1=-1.0, scalar2=-NEGM, op0=ALU.add,
                                           op1=ALU.mult), r=["flag"], w=["ctxb"])
    act(scond[:], condt[:], AF.Silu, ["condt"], ["scond"])

    XIN = [AF32(i * 4096, [128, 1024]) for i in range(8)]
    for tb in range(8):
        dma("sp", XIN[tb], x_d[tb * 128:(tb + 1) * 128, :], [], [akey(("xin", tb))], "xin%d" % tb)
    for tb in range(8):
        st = XIN[tb]
        k = ("xin", tb)
        for half in range(2):
            b = next_bank()
            def fn(e, st=st, half=half, b=b):
                ins = None
                for q in range(4):
                    c = half * 4 + q
                    ins = e.transpose(out=PB(b)[:, q * 128:(q + 1) * 128], in_=st[:, c * 128:(c + 1) * 128],
                                      identity=ident[:])
                return ins
            P.add("pe", fn, r=[k, "ident"], w=[pskey(b)])
            src = PB(b).rearrange("p (q t) -> p q t", q=4)
            dst = xT[:, half * 4:(half + 1) * 4, tb * 128:(tb + 1) * 128]
            eng = "act" if half == 0 else "dve"
            if eng == "act":
                P.add("act", lambda e, dst=dst, src=src: e.activation(out=dst, in_=src, func=AF.Copy),
                      r=[pskey(b)], w=[("xT", c_, tb // 4) for c_ in range(half * 4, half * 4 + 4)])
            else:
                P.add("dve", lambda e, dst=dst, src=src: e.tensor_copy(out=dst, in_=src),
                      r=[pskey(b)], w=[("xT", c_, tb // 4) for c_ in range(half * 4, half * 4 + 4)])

    def V(l, name, col=0, n=1):
        o = _VOFF[name] + col
        return vecs[:, l, o:o + n]

    for l in range(DEPTH):
        if stopped["v"]:
            break
        def mod_tile(ll, j):
            s, wk = w_next(("mod", ll, j))
            wv = slot_view(s, 8, 512)

            def fn(e, wv=wv, j=j):
                ins = None
                for n4 in range(4):
                    col = j * 4 + n4
                    for kc in range(8):
                        ins = e.matmul(PB(MODB)[:, col:col + 1], wv[:, kc, n4 * 128:(n4 + 1) * 128],
                                       scond[:, kc:kc + 1], start=(kc == 0), stop=(kc == 7))
                return ins
            P.add("pe", fn, r=wk + ["scond"], w=[pskey(MODB)])

        def mod_finish_A(ll):
            dve_tt(modT[:, 0:16], PB(MODB)[:, 0:16], V(ll, "bmod", 0, 16), ALU.add, [pskey(MODB), "vecs"], ["modA"])
            dve_ts(der[:, 0:8], modT[:, 8:16], 1.0, ALU.add, ["modA"], [("der", 0)])

        def mod_finish_B(ll):
            dve_tt(modT[:, 16:48], PB(MODB)[:, 16:48], V(ll, "bmod", 16, 32), ALU.add, [pskey(MODB), "vecs"], ["modB"])
            dve_ts(der[:, 8:16], modT[:, 16:24], 1.0 / ALPHA, ALU.mult, ["modB"], [("der", 1)])
            dve_ts(der[:, 16:24], modT[:, 32:40], 1.0, ALU.add, ["modB"], [("der", 2)])
            dve_ts(der[:, 24:32], modT[:, 40:48], 1.0 / ALPHA, ALU.mult, ["modB"], [("der", 3)])

        if l == 0:
            w_issue_upto(NSLOT - 1)
            for j in range(4):
                mod_tile(0, j)
            mod_finish_A(0)
            dma("pool", ropec[:], ropec_d, [], ["ropec"], "ldr0")
            dma("pool", ropes[:], ropes_d, [], ["ropes"], "ldr1")
            dma("pool", maskb[:], maskb_d, [], ["maskb"], "ldm2")
        dma("pool", lruw[:], lruw_d[l], [], ["lruw"], "ldm")
        act(der[:, 40:48], V(l, "llam", 0, 8), AF.Exp, ["vecs"], [("der", 5)], scale=-1.0)
        dve_ts(der[:, 40:48], der[:, 40:48], 1.0, ALU.add, [("der", 5)], [("der", 5)])
        act(der[:, 32:40], der[:, 40:48], AF.Ln, [("der", 5)], [("der", 4)])
        dve_ts(der[:, 32:40], der[:, 32:40], -4.0, ALU.mult, [("der", 4)], [("der", 4)])
        dve_ts(der[:, 48:64], V(l, "lba", 0, 16), 0.5, ALU.mult, ["vecs"], [("der", 6)])
        act(esink[:], V(l, "sink", 0, 8), AF.Exp, ["vecs"], ["esink"])
        for k in range(4):
            for c in range(4):
                i = k * 4 + c
                dve_ts(lrudg[:, i, :], identb[:], V(l, "lcw", i, 1), ALU.mult, ["identb", "vecs"], [("lrudg", i)],
                       eng="pool")

        for c in range(8 if l == 0 else 0):
            if c % 2 == 0:
                act(hT[:, c, :], xT[:, c, :], AF.Identity, [("xT", c, 0), ("xT", c, 1), ("der", 0), "modA"],
                    [("hT", c, 0), ("hT", c, 1)], bias=modT[:, c:c + 1], scale=der[:, c:c + 1])
            else:
                dve_ts(hT[:, c, :], xT[:, c, :], der[:, c:c + 1], ALU.mult,
                       [("xT", c, 0), ("xT", c, 1), ("der", 0), "modA"], [("hT", c, 0), ("hT", c, 1)],
                       s2=modT[:, c:c + 1], op1=ALU.add)
        gtk = lambda lo, hi: [("gT", c_, th_) for c_ in range(lo, hi) for th_ in range(2)]
        P.add("dve", lambda e: e.memset(xbpad2[:, :], 0.0), r=[("hT", 7, 1)], w=["xbpad_init"] + gtk(12, 16))
        P.add("dve", lambda e: e.memset(upad2[:, :], 0.0), r=[("hT", 7, 1)], w=["upad_init"] + gtk(16, 20))
        if phase_end("A%d" % l):
            break

        fence = list(arena_keys)
        del arena_keys[:]
        o = 0
        qT = ABF(o, [128, 4, T]); o += 8192
        kz = ABF(o, [128, 4, T]); o += 8192
        ktok = AF32(o, [128, 8, 128]); o += 4096
        vtok = AF32(o, [128, 8, 128]); o += 4096
        vaug = ABF(o, [128, 4, 8, 128]); o += 8192
        ckf = AF32(o, [128, 4, 128]); o += 2048
        cksf = AF32(o, [128, 4, 128]); o += 2048
        ckz = ABF(o, [128, 4, 512]); o += 4096
        cvaug = ABF(o, [128, 4, 4, 128]); o += 4096
        NPT = 4
        vraw = AF32(o, [128, T])
        pt = ABF(o, [128, NPT, 512]); o += NPT * 1024
        kraw = AF32(o, [128, T])
        rd = AF32(o, [128, 2, 512]); o += 4096
        rt1 = AF32(o, [128, 2, 512]); o += 4096
        rt2 = AF32(o, [128, 2, 512]); o += 4096
        assert o <= SBYTES, o
        first_fence = {"v": fence}

        def FW():
            f = first_fence["v"]
            first_fence["v"] = []
            return f

        P.add("dve", lambda e, kz=kz: e.memset(kz, 0.0), w=FW() + [akey("kz_init")])
        P.add("dve", lambda e, ckz=ckz: e.memset(ckz, 0.0), r=["kz_init"], w=[akey("ckz_init")])
        P.add("dve", lambda e, vaug=vaug: e.memset(vaug, 1.0), r=["kz_init"], w=[akey("vaug_init")])
        P.add("dve", lambda e, cvaug=cvaug: e.memset(cvaug, 1.0), r=["kz_init"], w=[akey("cvaug_init")])
        dma("sp", ckf, ck_d[l].rearrange("(k p) n -> p k n", p=128), ["kz_init"], [akey("ckf")], "ldc0")
        for g_ in range(2):
            for e__ in range(2):
                dma("pool", cvaug[:, g_ * 2 + e__, :, e__ * 64:(e__ + 1) * 64],
                    cv_d[l][:, g_ * 64:(g_ + 1) * 64].rearrange("(k p) n -> p k n", p=128), ["cvaug_init"],
                    [akey(("cvaug", g_ * 2 + e__))], "ldv%d" % (g_ * 2 + e__))
        hkeys = lambda th: [("hT", c, th) for c in range(8)]
        pend = {}
        wcur = {"j": -1}

        def win_chunk(cname, sgsel=None):
            ci = WIN_CHUNKS.index(cname)
            col = ci * 128
            j = [i for i, (c0, nc_) in enumerate(WIN_TILES) if c0 <= col < c0 + nc_][0]
            off = col - WIN_TILES[j][0]
            if j != wcur["j"]:
                s_cur, wk_c = w_next(("win", l, j))
                wcur["j"] = j
                wcur["wk"] = wk_c
                wcur["wv"] = slot_view(s_cur, 8, WIN_TILES[j][1])
            wk_cur, wv_cur = wcur["wk"], wcur["wv"]
            banks = []
            for th in range(2):
                b = next_bank()
                banks.append(b)
                mm_group(b, [(wv_cur[:, kc, off:off + 128], hT[:, kc, th * 512:(th + 1) * 512]) for kc in range(8)],
                         r=wk_cur + hkeys(th))
            pend[cname] = banks
            if cname.startswith("qp") or cname in ("kAp", "kBp"):
                base = cname.replace("p", "")
                for th in range(2):
                    b0 = pend[base][th]
                    b1 = banks[th]
                    tsl = slice(th * 512, (th + 1) * 512)
                    t1 = rt1[:, th, :]
                    t2 = rt2[:, th, :]
                    dve_tt(t1, PB(b0), ropec[:, tsl], ALU.mult, [pskey(b0), "ropec"], [akey(("rt1", th))])
                    dve_tt(t2, PB(b1), ropes[:, tsl], ALU.mult, [pskey(b1), "ropes"], [akey(("rt2", th))])
                    if base.startswith("q"):
                        c = int(base[1])
                        dve_tt(qT[:, c, tsl], t1, t2, ALU.add, [("rt1", th), ("rt2", th)], [akey(("qT", c, th))])
                    else:
                        lo_idx, hi_idx = (0, 3) if base == "kA" else (2, 1)
                        dve_tt(kz[0:64, lo_idx, tsl], t1[0:64, :], t2[0:64, :], ALU.add,
                               [("rt1", th), ("rt2", th), "kz_init"], [akey(("kz", lo_idx, th))])
                        dve_tt(kz[64:128, hi_idx, tsl], t1[64:128, :], t2[64:128, :], ALU.add,
                               [("rt1", th), ("rt2", th), "kz_init"], [akey(("kz", hi_idx, th))])
                    if base == "kA":
                        act(kraw[:, tsl], PB(b0), AF.Copy, [pskey(b0), "kz_init"], [akey(("kraw", th))])
                        for (do_, di_, so_, si_) in ((slice(64, 128), 1, slice(0, 64), 0), (slice(0, 64), 2, slice(64, 128), 3)):
                            dst_ = kz[do_, di_, tsl]
                            src_ = kz[so_, si_, tsl]
                            P.add("dve", lambda e, dst_=dst_, src_=src_: e.tensor_copy(out=dst_, in_=src_),
                                  r=[("kz", si_, th), "kz_init"], w=[akey(("kz", di_, th))])
            elif cname == "v":
                for th in range(2):
                    tsl = slice(th * 512, (th + 1) * 512)
                    act(vraw[:, tsl], PB(banks[th]), AF.Copy, [pskey(banks[th]), "kz_init"], [akey(("vraw", th))])
                for (raw, rkey, tok, tkey) in ((kraw, "kraw", ktok, "ktok"), (vraw, "vraw", vtok, "vtok")):
                    for th in range(2):
                        b = next_bank()

                        def fn(e, raw=raw, th=th, b=b):
                            ins = None
                            for q in range(4):
                                blk = th * 4 + q
                                ins = e.transpose(out=PB(b)[:, q * 128:(q + 1) * 128],
                                                  in_=raw[:, blk * 128:(blk + 1) * 128], identity=ident[:])
                            return ins
                        P.add("pe", fn, r=[(rkey, th), "ident"], w=[pskey(b)])
                        src = PB(b).rearrange("p (q t) -> p q t", q=4)
                        dst = tok[:, th * 4:(th + 1) * 4, :]
                        P.add("dve", lambda e, dst=dst, src=src: e.tensor_copy(out=dst, in_=src),
                              r=[pskey(b), "kz_init"], w=[akey((tkey, th))])
                        if tkey == "vtok":
                            for g_ in range(2):
                                for e__ in range(2):
                                    dstb = vaug[:, g_ * 2 + e__, th * 4:(th + 1) * 4, e__ * 64:(e__ + 1) * 64]
                                    srcb = src[:, :, g_ * 64:(g_ + 1) * 64]
                                    if e__ == 0:
                                        P.add("act", lambda e, dstb=dstb, srcb=srcb: e.activation(
                                            out=dstb, in_=srcb, func=AF.Copy),
                                            r=[pskey(b), "vaug_init"], w=[akey(("vaug", g_ * 2 + e__, th))])
                                    else:
                                        P.add("dve", lambda e, dstb=dstb, srcb=srcb: e.tensor_copy(out=dstb, in_=srcb),
                                              r=[pskey(b), "vaug_init"], w=[akey(("vaug", g_ * 2 + e__, th))])
                dma("sp", nk_d[l].rearrange("(b p) n -> p b n", p=128), ktok, [("ktok", 0), ("ktok", 1)], [], "st_k")
                dma("sp", nv_d[l].rearrange("(b p) n -> p b n", p=128), vtok, [("vtok", 0), ("vtok", 1)], [], "st_v")
            elif cname.startswith("xb"):
                c = int(cname[2])
                for th in range(2):
                    src = PB(banks[th]).rearrange("p (s t) -> p s t", s=2)
                    dst = xbpad[:, c, :].rearrange("p (s t) -> p s t", s=NSEG)[:, 2 * th:2 * th + 2, 2:2 + SEG]
                    P.add("act", lambda e, dst=dst, src=src: e.activation(out=dst, in_=src, func=AF.Copy),
                          r=[pskey(banks[th]), "xbpad_init"], w=[("xbpad", c, th)])
                xv = xbpad[:, c, :].rearrange("p (s t) -> p s t", s=NSEG)
                dve_ts(xv[:, 1:4, 0:2], xv[:, 0:3, SEG:SEG + 2], flag[:, 0:1], ALU.mult,
                       [("xbpad", c, 0), ("xbpad", c, 1), "flag"], [("xbpad_h", c, 0)])
                dve_ts(xv[:, 0:3, SEG + 2:SEG + 3], xv[:, 1:4, 2:3], flag[:, 0:1], ALU.mult,
                       [("xbpad", c, 0), ("xbpad", c, 1), "flag"], [("xbpad_h", c, 1)])
            elif cname.startswith("g"):
                c = int(cname[1])
                ab = pend["a%d" % c]
                for th in range(2):
                    sg, sgk, sgx = sgsel(th)
                    act(sg, PB(banks[th]), AF.Sigmoid, [pskey(banks[th])], sgx + [akey(sgk)])
                    src = PB(ab[th]).rearrange("p (s t) -> p s t", s=2)
                    dst = upad[:, c, :].rearrange("p (s t) -> p s t", s=NSEG)[:, 2 * th:2 * th + 2, 15:15 + SEG]
                    sgv = sg.rearrange("p (s t) -> p s t", s=2)
                    P.add("dve", lambda e, dst=dst, src=src, sgv=sgv: e.tensor_tensor(out=dst, in0=src, in1=sgv,
                                                                                       op=ALU.mult),
                          r=[pskey(ab[th]), sgk, "upad_init"], w=[("upad", c, th)])
                uv = upad[:, c, :].rearrange("p (s t) -> p s t", s=NSEG)
                dve_ts(uv[:, 1:4, 0:15], uv[:, 0:3, SEG:SEG + 15], flag[:, 0:1], ALU.mult,
                       [("upad", c, 0), ("upad", c, 1), "flag"], [("upad_h", c, 0)])
                dve_ts(uv[:, 0:3, SEG + 15:SEG + 30], uv[:, 1:4, 15:30], flag[:, 0:1], ALU.mult,
                       [("upad", c, 0), ("upad", c, 1), "flag"], [("upad_h", c, 1)])
        for cname in WIN_CHUNKS[:11]:
            win_chunk(cname)
        b = next_bank()

        def fn(e, srcb=ckf, b=b):
            ins = None
            for kc in range(4):
                ins = e.transpose(out=PB(b)[:, kc * 128:(kc + 1) * 128], in_=srcb[:, kc, :], identity=ident[:])
            return ins
        P.add("pe", fn, r=["ckf", "ident"], w=[pskey(b)])
        for (do_, di_, so_) in ((slice(0, 64), 0, slice(0, 64)), (slice(64, 128), 3, slice(64, 128)),
                                (slice(64, 128), 1, slice(0, 64)), (slice(0, 64), 2, slice(64, 128))):
            act(ckz[do_, di_, :], PB(b)[so_, :], AF.Copy, [pskey(b), "ckz_init"], [akey(("ckz", di_))])

        if phase_end("B%d" % l):
            tap_list.extend([("qT", qT, [128, 4, T], BF16), ("kz", kz, [128, 4, T], BF16),
                             ("ckz", ckz, [128, 4, 512], BF16)])
            break

        sched = []
        for h in range(8):
            for th in range(2):
                tl = [("ctx", kc) for kc in range(4)] + [("band",) + bt for bt in band_tiles(th)]
                for ti, t_ in enumerate(tl):
                    sched.append((h, th, ti, len(tl), t_))
        LAG = 3
        st_bank = {}
        it_idx = {}
        for i, (h, th) in enumerate([(h, th) for h in range(8) for th in range(2)]):
            it_idx[(h, th)] = i

        def emit_S(gi):
            h, th, ti, ntl, t_ = sched[gi]
            c, e_, g = h // 2, h % 2, h // 4
            b = gi % 4
            st_bank[gi] = b
            if t_[0] == "ctx":
                kc = t_[1]
                lt = ckz[:, g * 2 + e_, kc * 128:(kc + 1) * 128]
                q0, q1 = th * 512, (th + 1) * 512
                mk = None
                rk = [("ckz", g * 2 + e_), "ckz_init"]
            else:
                _, kb, jl, jh = t_
                lt = kz[:, g * 2 + e_, kb * 128:(kb + 1) * 128]
                q0, q1 = jl * 128, (jh + 1) * 128
                mo = MASK_OFF[(th, kb)]
                mk = maskb[:, mo:mo + (q1 - q0)]
                rk = [("kz", g * 2 + e_, kb // 4), "kz_init"]
            n = q1 - q0
            rk += [("qT", c, th), "maskb", "identb"]
            slot = gi % NPT
            if mk is None:
                mm_group(b, [(lt, qT[:, c, q0:q1])], r=rk, cols=(0, n))
                act(pt[:, slot, 0:n], PB(b)[:, 0:n], AF.Exp, [pskey(b), "ctxb"], [akey(("pt", slot))], scale=0.125,
                    bias=ctxb[:, 0:1])
            else:
                mm_group(b, [(lt, qT[:, c, q0:q1]), (identb[:], mk)], r=rk, cols=(0, n))
                act(pt[:, slot, 0:n], PB(b)[:, 0:n], AF.Exp, [pskey(b)], [akey(("pt", slot))], scale=0.125)

        def emit_PV(gi):
            h, th, ti, ntl, t_ = sched[gi]
            c, e_, g = h // 2, h % 2, h // 4
            par = it_idx[(h, th)] % 2
            ab = 4 + par
            slot = gi % NPT
            if t_[0] == "ctx":
                kc = t_[1]
                vv = cvaug[:, g * 2 + e_, kc, :]
                q0, q1 = 0, 512
                rk = [("cvaug", g * 2 + e_), "cvaug_init"]
            else:
                _, kb, jl, jh = t_
                vv = vaug[:, g * 2 + e_, kb, :]
                q0, q1 = jl * 128 - th * 512, (jh + 1) * 128 - th * 512
                rk = [("vaug", g * 2 + e_, kb // 4), "vaug_init"]
            n = q1 - q0
            first, last = (ti == 0), (ti == ntl - 1)
            pr = slice(e_ * 64, e_ * 64 + 64)
            po = slice((1 - e_) * 64, (1 - e_) * 64 + 64)
            ptv = pt[:, slot, 0:n]
            o_a = PB(ab)[:, q0:q1]

            def fn(e, vv=vv, ptv=ptv, o_a=o_a, first=first, last=last):
                return e.matmul(o_a, vv, ptv, start=first, stop=last)
            P.add("pe", fn, r=rk + [("pt", slot)], w=[pskey(ab)])
            if last:
                r_ = rd[pr, par, :]
                dve_ts(r_, PB(ab)[po, :], esink[pr, h:h + 1], ALU.add, [pskey(ab), "esink"], [akey(("rd", par))])
                P.add("dve", lambda e, r_=r_: e.reciprocal(out=r_, in_=r_), r=[("rd", par)], w=[("rd", par)])
                dve_tt(attnT[pr, c, th * 512:(th + 1) * 512], PB(ab)[pr, :], r_, ALU.mult,
                       [pskey(ab), ("rd", par)], [("attnT", c, th, e_)])

        NT = len(sched)
        modj = 4
        for gi in range(NT + LAG):
            if gi < NT:
                emit_S(gi)
                if sched[gi][2] == 0 and it_idx[(sched[gi][0], sched[gi][1])] % 2 == 1 and modj < 12:
                    mod_tile(l, modj)
                    modj += 1
            if gi - LAG >= 0:
                emit_PV(gi - LAG)
        assert modj == 12
        mod_finish_B(l)
        if phase_end("C%d" % l):
            break

        att_fence = list(arena_keys)
        del arena_keys[:]
        first_fence["v"] = []
        o = 0
        xl = AF32(o, [128, T]); o += 4096
        xlb = ABF(o, [128, T]); o += 2048
        Rb = AF32(o, [128, T]); o += 4096
        Ab = AF32(o, [128, T]); o += 4096
        Sb = AF32(o, [128, T]); o += 4096
        Ib = AF32(o, [128, T]); o += 4096
        H0 = AF32(o, [128, T]); o += 4096
        H1 = Rb
        lru_end = o
        cdg = ABF(o, [128, 2, 31, 128]); o += 2 * 31 * 256
        cvo = AF32(o, [128, 4, T]); o += 16384
        assert o <= SBYTES, o

        def cdg_build(c, fence=()):
            o0 = _VOFF["cw"] + c
            wv_ = vecs[:, l, o0:o0 + 4 * 31:4].unsqueeze(2).broadcast_to([128, 31, 128])
            iv_ = identb[:].unsqueeze(1).broadcast_to([128, 31, 128])
            dve_tt(cdg[:, c % 2, :, :], iv_, wv_, ALU.mult, ["identb", "vecs"],
                   list(fence) + [akey(("cdg", c % 2, k)) for k in range(31)], eng="pool")

        def conf_conv(c):
            uv = upad[:, c, :].rearrange("p (s t) -> p s t", s=NSEG)
            for th in range(2):
                b = next_bank()
                mm_group(b, [(cdg[:, c % 2, k, :], uv[:, 2 * th:2 * th + 2, k:k + SEG]) for k in range(31)],
                         r=[("upad", c, 0), ("upad", c, 1), ("upad_h", c, 0), ("upad_h", c, 1)] +
                           [("cdg", c % 2, k) for k in range(31)])
                tsl = slice(th * 512, (th + 1) * 512)
                dve_ts(cvo[:, c, tsl], PB(b), V(l, "cb", c, 1), ALU.add, [pskey(b), "vecs", ("cdg", 0, 0)],
                       [akey(("cvo", c, th))])
            if c + 2 < 4:
                cdg_build(c + 2)

        cdg_build(0, att_fence)
        cdg_build(1)
        cf32 = confT[:].bitcast(F32)
        A1 = cf32[:, 0:2, :].rearrange("p a b -> p (a b)")
        S1 = cf32[:, 2:4, :].rearrange("p a b -> p (a b)")
        sg_first = {"v": True}

        def sgsel(th):
            fx = list(att_fence) if sg_first["v"] else []
            sg_first["v"] = False
            return Sb[:, th * 512:(th + 1) * 512], ("Sg", th), fx + [("S", 0)]
        for cname in ("xb0", "xb1", "xb2", "xb3"):
            win_chunk(cname)
        for c in range(4):
            win_chunk("a%d" % c, sgsel)
            win_chunk("g%d" % c, sgsel)
            xv = xbpad[:, c, :].rearrange("p (s t) -> p s t", s=NSEG)
            cb = []
            for th in range(2):
                b = next_bank()
                cb.append(b)
                mm_group(b, [(lrudg[:, k * 4 + c, :], xv[:, 2 * th:2 * th + 2, k:k + SEG]) for k in range(4)],
                         r=[("xbpad", c, 0), ("xbpad", c, 1), ("xbpad_h", c, 0), ("xbpad_h", c, 1)] +
                           [("lrudg", k * 4 + c) for k in range(4)])
            for th in range(2):
                tsl = slice(th * 512, (th + 1) * 512)
                act(xl[:, tsl], PB(cb[th]), AF.Identity, [pskey(cb[th]), "vecs"],
                    (att_fence if (c == 0 and th == 0) else []) + [akey(("xl", th))], bias=V(l, "lcb", c, 1))
                xs_, xd_ = xl[:, tsl], xlb[:, tsl]
                P.add("dve", lambda e, xs_=xs_, xd_=xd_: e.tensor_copy(out=xd_, in_=xs_), r=[("xl", th)],
                      w=[akey(("xlb", th))])
            for d in range(2):
                Hd = H0 if d == 0 else H1
                hk = "H%d" % d
                gb = {}
                for gt in range(2):
                    for th in range(2):
                        b = next_bank()
                        gb[(gt, th)] = b
                        mm_group(b, [(lruw[:, (d * 2 + gt) * 4 + c, :], xlb[:, th * 512:(th + 1) * 512])],
                                 r=["lruw", ("xlb", th)])
                RK = [akey(("R", 0)), akey(("R", 1))]
                for th in range(2):
                    tsl = slice(th * 512, (th + 1) * 512)
                    act(Rb[:, tsl], PB(gb[(0, th)]), AF.Tanh, [pskey(gb[(0, th)]), ("der", 6)],
                        [("R", th)] + ([("H1", s_) for s_ in range(NSEG)] if th == 0 else []),
                        bias=der[:, 48 + d * 4 + c:49 + d * 4 + c], scale=0.5)
                    act(Ib[:, tsl], PB(gb[(1, th)]), AF.Tanh, [pskey(gb[(1, th)]), ("der", 6)], [akey(("I", th))],
                        bias=der[:, 56 + d * 4 + c:57 + d * 4 + c], scale=0.5)
                clh = der[:, 32 + d * 4 + c:33 + d * 4 + c]
                act(Rb, Rb, AF.Identity, RK + [("der", 4)], RK, bias=clh, scale=clh)
                Ab_, Sb_ = (Ab, Sb) if d == 0 else (A1, S1)
                kA, kS = akey(("A", d)), akey(("S", d))
                act(Ab_, Rb, AF.Exp, RK, [kA] + ([("confT", c_) for c_ in range(4)] if d == 1 else []))
                act(Sb_, Rb, AF.Exp, RK, [kS, ("Sg", 0), ("Sg", 1)], scale=2.0)
                act(Sb_, Sb_, AF.Sqrt, [kS], [kS], bias=0.25, scale=-0.25)
                IK = [("I", 0), ("I", 1)]
                dve_stt(Ib, Ib, 1.0, xl, ALU.add, ALU.mult, IK + [("xl", 0), ("xl", 1)], IK)
                dve_tt(Sb_, Sb_, Ib, ALU.mult, [kS] + IK, [kS])
                order = range(NSEG) if d == 0 else range(NSEG - 1, -1, -1)
                prev = None
                for s_ in order:
                    seg = slice(s_ * SEG, (s_ + 1) * SEG)
                    if prev is None:
                        init = h0t[:, l * 8 + d * 4 + c:l * 8 + d * 4 + c + 1]
                        ik = ["h0t"]
                    else:
                        col = (prev + 1) * SEG - 1 if d == 0 else prev * SEG
                        init = initb[:, s_:s_ + 1] if d == 0 else initb[:, 4 + s_:5 + s_]
                        dve_ts(init, Hd[:, col:col + 1], flag[:, 0:1], ALU.mult, [akey((hk, prev)), "flag"],
                               [("initb", d, s_)])
                        ik = [("initb", d, s_)]
                    if d == 0:
                        o_, a_, b_ = Hd[:, seg], Ab_[:, seg], Sb_[:, seg]
                    else:
                        lo, hi = s_ * SEG, (s_ + 1) * SEG
                        o_ = Hd[:, lo:hi][:, ::-1]
                        a_ = Ab_[:, lo:hi][:, ::-1]
                        b_ = Sb_[:, lo:hi][:, ::-1]
                    P.add("dve", lambda e, o_=o_, a_=a_, b_=b_, init=init: e.tensor_tensor_scan(
                        out=o_, data0=a_, data1=b_, initial=init, op0=ALU.mult, op1=ALU.add),
                        r=[kA, kS] + ik, w=[akey((hk, s_))] + (RK if d == 1 else []))
                    prev = s_
                hv = Hd.rearrange("p (s t) -> p s t", s=NSEG)
                colsel = SEG - 1 if d == 0 else 0
                fv = fin[:, :].rearrange("p (s x) -> p s x", s=NSEG)[:, :, d * 4 + c:d * 4 + c + 1]
                P.add("dve", lambda e, fv=fv, hv=hv, colsel=colsel: e.tensor_copy(out=fv, in_=hv[:, :, colsel:colsel + 1]),
                      r=[(hk, s_) for s_ in range(NSEG)], w=[("fin", d, c)])
            dve_tt(lruT[:, c, :], H0, H1, ALU.add, [("H0", s_) for s_ in range(4)] + [("H1", s_) for s_ in range(4)],
                   [("lruT", c)])
            conf_conv(c)
        fb = next_bank()
        P.add("pe", lambda e, fb=fb: e.transpose(out=PB(fb)[0:32, 0:128], in_=fin[:, :], identity=ident[:]),
              r=[("fin", d, c) for d in range(2) for c in range(4)] + ["ident"], w=[pskey(fb)])
        act(fint[:, :], PB(fb)[0:32, 0:128], AF.Copy, [pskey(fb)], ["fint"])
        dma("sp", nh_d[l], fint[:, :], ["fint"], [], "st_h")
        if phase_end("D%d" % l):
            tap_list.extend([("lruT", lruT[:], [128, 4, T], BF16)])
            break

        lru_fence = [k_ for k_ in arena_keys if not (isinstance(k_, tuple) and k_[0] in ("cvo", "cdg"))]
        first_fence["v"] = lru_fence
        o = 0
        mean = AF32(o, [128, T]); o += 4096
        rstd = AF32(o, [128, T]); o += 4096
        sqb = ABF(o, [128, 2, T]); o += 4096
        cvb = ABF(o, [128, 2, T]); o += 4096
        assert o <= lru_end
        ln_stats_and_norm = None

        def layer_norm(src_fn, nch, keys_fn, eps, tag):
            sb_ = [next_bank(), next_bank(), next_bank(), next_bank()]
            for c in range(nch):
                slot = c % 2
                fw_ = FW()
                cvs_ = cvb[:, slot, :]
                P.add("dve", lambda e, cvs_=cvs_, src_=src_fn(c): e.tensor_copy(out=cvs_, in_=src_),
                      r=keys_fn(c), w=fw_ + [akey(("cvb", slot))])
                act(sqb[:, slot, :], src_fn(c), AF.Square, keys_fn(c), fw_ + [akey(("sqb", slot))])
                for th in range(2):
                    tsl = slice(th * 512, (th + 1) * 512)
                    r1_ = cvb[:, slot, tsl]
                    r2_ = sqb[:, slot, tsl]

                    def fn(e, c=c, th=th, r1_=r1_, r2_=r2_, sb_=sb_):
                        e.matmul(PB(sb_[th]), onesb[:, :], r1_, start=(c == 0), stop=(c == nch - 1))
                        return e.matmul(PB(sb_[2 + th]), onesb[:, :], r2_, start=(c == 0), stop=(c == nch - 1))
                    P.add("pe", fn, r=[("cvb", slot), ("sqb", slot), "onesb"], w=[pskey(sb_[th]), pskey(sb_[2 + th])])
            inv = 1.0 / (nch * 128)
            for th in range(2):
                tsl = slice(th * 512, (th + 1) * 512)
                act(mean[:, tsl], PB(sb_[th]), AF.Copy, [pskey(sb_[th]), ("sqb", 0), ("sqb", 1)], [akey(("mean", th))],
                    scale=inv)
                act(rstd[:, tsl], PB(sb_[th]), AF.Square, [pskey(sb_[th])], [akey(("rstd", th))], scale=inv)
                dve_stt(rstd[:, tsl], PB(sb_[2 + th]), inv, rstd[:, tsl], ALU.mult, ALU.subtract,
                        [pskey(sb_[2 + th]), ("rstd", th)], [("rstd", th)])
                dve_ts(rstd[:, tsl], rstd[:, tsl], 0.0, ALU.max, [("rstd", th)], [("rstd", th)], s2=eps, op1=ALU.add)
                act(rstd[:, tsl], rstd[:, tsl], AF.Ln, [("rstd", th)], [("rstd", th)])
                act(rstd[:, tsl], rstd[:, tsl], AF.Exp, [("rstd", th)], [("rstd", th)], scale=-0.5)

        layer_norm(lambda c: cvo[:, c, :], 4, lambda c: [("cvo", c, 0), ("cvo", c, 1)], LN_EPS, "conf")
        MK = [("mean", 0), ("mean", 1)]
        RSK = [("rstd", 0), ("rstd", 1)]
        for c in range(4):
            ck_ = [("cvo", c, 0), ("cvo", c, 1)]
            dve_tt(cvo[:, c, :], cvo[:, c, :], mean, ALU.subtract, ck_ + MK, ck_)
            dve_tt(cvo[:, c, :], cvo[:, c, :], rstd, ALU.mult, ck_ + RSK, ck_)
            act(confT[:, c, :], cvo[:, c, :], AF.Silu, ck_ + ["vecs"], [("confT", c)], bias=V(l, "clb", c, 1),
                scale=V(l, "clg", c, 1))
        if phase_end("E%d" % l):
            tap_list.extend([("confT", confT[:], [128, 4, T], BF16), ("attnT", attnT[:], [128, 4, T], BF16)])
            break

        fence = list(arena_keys)
        del arena_keys[:]
        first_fence["v"] = fence
        o = 0
        mergeT = ABF(o, [128, 8, T]); o += 16384
        sgb = AF32(o, [128, 2, 3, 512]); o += 12288
        mt = AF32(o, [128, 2, 2, 512]); o += 8192
        it = 0
        for n in range(8):
            sg_, wkg = w_next(("mg", l, n))
            wg = slot_view(sg_, 24, 128)
            sb2, wkb = w_next(("br", l, n), la=NSLOT - 2)
            wb = slot_view(sb2, 12, 128)
            for th in range(2):
                tsl = slice(th * 512, (th + 1) * 512)
                par = it % 2
                it += 1
                G = []
                for b_ in range(3):
                    bk = next_bank()
                    G.append(bk)
                    mm_group(bk, [(wg[:, b_ * 8 + kc, :], hT[:, kc, tsl]) for kc in range(8)], r=wkg + hkeys(th))
                Pb = []
                for b_, (src, skf) in ((0, (attnT, lambda kc: [("attnT", kc, th, 0), ("attnT", kc, th, 1)])),
                                       (1, (lruT, lambda kc: [("lruT", kc)])),
                                       (2, (confT, lambda kc: [("confT", kc)]))):
                    bk = next_bank()
                    Pb.append(bk)
                    mm_group(bk, [(wb[:, 4 * b_ + kc, :], src[:, kc, tsl]) for kc in range(4)],
                             r=wkb + [k_ for kc in range(4) for k_ in skf(kc)])
                for b_ in range(3):
                    act(sgb[:, par, b_, :], PB(G[b_]), AF.Sigmoid, [pskey(G[b_]), "vecs"],
                        FW() + [akey(("sgb", par, b_))], bias=V(l, "bm", b_ * 8 + n, 1))
                dve_tt(mt[:, par, 0, :], PB(Pb[0]), sgb[:, par, 0, :], ALU.mult, [pskey(Pb[0]), ("sgb", par, 0)],
                       [akey(("mt", par, 0))])
                dve_tt(mt[:, par, 1, :], PB(Pb[1]), sgb[:, par, 1, :], ALU.mult, [pskey(Pb[1]), ("sgb", par, 1)],
                       [akey(("mt", par, 1))])
                dve_tt(mt[:, par, 0, :], mt[:, par, 0, :], mt[:, par, 1, :], ALU.add,
                       [("mt", par, 0), ("mt", par, 1)], [("mt", par, 0)])
                dve_tt(mt[:, par, 1, :], PB(Pb[2]), sgb[:, par, 2, :], ALU.mult, [pskey(Pb[2]), ("sgb", par, 2)],
                       [("mt", par, 1)])
                dve_tt(mergeT[:, n, tsl], mt[:, par, 0, :], mt[:, par, 1, :], ALU.add,
                       [("mt", par, 0), ("mt", par, 1)], [akey(("mergeT", n, th))])
            if l + 1 < DEPTH and n % 2 == 1:
                mod_tile(l + 1, n // 2)
        if l + 1 < DEPTH:
            mod_finish_A(l + 1)
        if phase_end("F%d" % l):
            tap_list.extend([("mergeT", mergeT, [128, 8, T], BF16)])
            break

        o = 16384 + 12288 + 8192
        mean = AF32(o, [128, T]); o += 4096
        rstd = AF32(o, [128, T]); o += 4096
        sqb = ABF(o, [128, 2, T]); o += 4096
        cvb = ABF(o, [128, 2, T]); o += 4096
        assert o <= SBYTES

        pending_mod = []

        def proj_residual(kind, src, nk, srckeys, gcol):
            for n in range(8):
                if kind == "out":
                    if n % 4 == 0:
                        s_, wk_ = w_next(("out", l, n // 4))
                        wv_ = slot_view(s_, 8, 512)
                    lts = [wv_[:, kc, (n % 4) * 128:(n % 4 + 1) * 128] for kc in range(nk)]
                else:
                    s_, wk_ = w_next(("dn", l, n))
                    wv_ = slot_view(s_, NFF, 128)
                    lts = [wv_[:, kc, :] for kc in range(nk)]
                    if l + 1 < DEPTH and n % 2 == 1:
                        pending_mod.append(n // 2)
                for th in range(2):
                    tsl = slice(th * 512, (th + 1) * 512)
                    bk = next_bank()
                    rh = (lambda kc: src(kc)[:, tsl]) if callable(src) else (lambda kc: src[:, kc, tsl])
                    mm_group(bk, [(lts[kc], rh(kc)) for kc in range(nk)], r=wk_ + srckeys(th))
                    dve_stt(xT[:, n, tsl], PB(bk), der[:, gcol + n:gcol + n + 1], xT[:, n, tsl], ALU.mult, ALU.add,
                            [pskey(bk), ("xT", n, th), ("der", 1), ("der", 3)], [("xT", n, th)])
                while pending_mod:
                    mod_tile(l + 1, pending_mod.pop(0))


        def ln_apply(gname, bname, nxt=False, h2=False):
            layer_norm(lambda c: xT[:, c, :], 8, lambda c: [("xT", c, 0), ("xT", c, 1)], EPS_F, gname)
            for c in range(8):
                xk = [("xT", c, 0), ("xT", c, 1)]
                dve_tt(xT[:, c, :], xT[:, c, :], mean, ALU.subtract, xk + MK, xk)
                dve_tt(xT[:, c, :], xT[:, c, :], rstd, ALU.mult, xk + RSK, xk)
                act(xT[:, c, :], xT[:, c, :], AF.Identity, xk + ["vecs"], xk, bias=V(l, bname, c, 1),
                    scale=V(l, gname, c, 1))
                if nxt:
                    act(hT[:, c, :], xT[:, c, :], AF.Identity, xk + [("der", 0), "modA"],
                        [("hT", c, 0), ("hT", c, 1)], bias=modT[:, c:c + 1], scale=der[:, c:c + 1])
                if h2:
                    act(hT[:, c, :], xT[:, c, :], AF.Identity, xk + [("der", 2), "modB"],
                        [("hT", c, 0), ("hT", c, 1)], bias=modT[:, 24 + c:25 + c], scale=der[:, 16 + c:17 + c])

        def out_ln1_thmajor():
            s0_, wk0_ = w_next(("out", l, 0))
            s1_, wk1_ = w_next(("out", l, 1), la=NSLOT - 2)
            wvs = [slot_view(s0_, 8, 512), slot_view(s1_, 8, 512)]
            wks = [wk0_, wk1_]
            inv = 1.0 / 1024.0
            for th in range(2):
                tsl = slice(th * 512, (th + 1) * 512)
                for n in range(8):
                    wv_ = wvs[n // 4]
                    bk = next_bank()
                    mm_group(bk, [(wv_[:, kc, (n % 4) * 128:(n % 4 + 1) * 128], mergeT[:, kc, tsl]) for kc in range(8)],
                             r=wks[n // 4] + [("mergeT", kc, th) for kc in range(8)])
                    dve_stt(xT[:, n, tsl], PB(bk), der[:, 8 + n:9 + n], xT[:, n, tsl], ALU.mult, ALU.add,
                            [pskey(bk), ("xT", n, th), ("der", 1), ("der", 3)], [("xT", n, th)])
                b1, b2 = next_bank(), next_bank()
                for c in range(8):
                    slot = c % 2
                    cv_ = cvb[:, slot, tsl]
                    sq_ = sqb[:, slot, tsl]
                    xs_ = xT[:, c, tsl]
                    fw_ = FW()
                    P.add("dve", lambda e, cv_=cv_, xs_=xs_: e.tensor_copy(out=cv_, in_=xs_), r=[("xT", c, th)],
                          w=fw_ + [akey(("cvb", slot, th))])
                    act(sq_, xs_, AF.Square, [("xT", c, th)], fw_ + [akey(("sqb", slot, th))])

                    def fn(e, c=c, cv_=cv_, sq_=sq_, b1=b1, b2=b2):
                        e.matmul(PB(b1), onesb[:, :], cv_, start=(c == 0), stop=(c == 7))
                        return e.matmul(PB(b2), onesb[:, :], sq_, start=(c == 0), stop=(c == 7))
                    P.add("pe", fn, r=[("cvb", slot, th), ("sqb", slot, th), "onesb"], w=[pskey(b1), pskey(b2)])
                mk_, rk_ = akey(("mean", th)), akey(("rstd", th))
                act(mean[:, tsl], PB(b1), AF.Copy, [pskey(b1)], [mk_], scale=inv)
                act(rstd[:, tsl], PB(b1), AF.Square, [pskey(b1)], [rk_], scale=inv)
                dve_stt(rstd[:, tsl], PB(b2), inv, rstd[:, tsl], ALU.mult, ALU.subtract, [pskey(b2), rk_], [rk_])
                dve_ts(rstd[:, tsl], rstd[:, tsl], 0.0, ALU.max, [rk_], [rk_], s2=EPS_F, op1=ALU.add)
                act(rstd[:, tsl], rstd[:, tsl], AF.Ln, [rk_], [rk_])
                act(rstd[:, tsl], rstd[:, tsl], AF.Exp, [rk_], [rk_], scale=-0.5)
                for c in range(8):
                    xk = [("xT", c, th)]
                    xc_ = xT[:, c, tsl]
                    dve_tt(xc_, xc_, mean[:, tsl], ALU.subtract, xk + [mk_], xk)
                    dve_tt(xc_, xc_, rstd[:, tsl], ALU.mult, xk + [rk_], xk)
                    act(xc_, xc_, AF.Identity, xk + ["vecs"], xk, bias=V(l, "l1b", c, 1), scale=V(l, "l1g", c, 1))
                    act(hT[:, c, tsl], xc_, AF.Identity, xk + [("der", 2), "modB"], [("hT", c, th)],
                        bias=modT[:, 24 + c:25 + c], scale=der[:, 16 + c:17 + c])

        out_ln1_thmajor()
        if phase_end("G%d" % l):
            break

        fence = list(arena_keys)
        del arena_keys[:]
        first_fence["v"] = fence
        o = 0
        gtail = ABF(o, [128, 2, T]); o += 4096

        def gTc(c):
            if c < 4:
                return lruT[:, c, :]
            if c < 8:
                return confT[:, c - 4, :]
            if c < 12:
                return attnT[:, c - 8, :]
            if c < 16:
                return xbpad2[:, (c - 12) * T:(c - 11) * T]
            if c < 20:
                return upad2[:, (c - 16) * T:(c - 15) * T]
            return gtail[:, c - 20, :]
        NUR = 2
        ur = ABF(o, [128, NUR, 2, NSEG * FPP]); o += NUR * 2 * NSEG * FPP * 2
        cen = AF32(o, [128, NUR, 4, 512]); o += NUR * 4 * 2048
        gl = ABF(o, [128, 2, 512]); o += 2048
        assert o <= SBYTES, o
        oldk = [("lruT", c_) for c_ in range(4)] + [("confT", c_) for c_ in range(4)] + \
               [("attnT", c_, th_, e__) for c_ in range(4) for th_ in range(2) for e__ in range(2)] + \
               [(nm_, c_, th_) for nm_ in ("xbpad", "xbpad_h", "upad", "upad_h") for c_ in range(4) for th_ in range(2)]
        P.add("pool", lambda e, ur=ur: e.memset(ur, 0.0), w=FW() + oldk + [akey("ur_init")])
        ffn_items = []
        for j in range(11):
            for pi in range(2):
                ffn_items.append((j, pi, j * 2 + pi))

        def ffn_up(item):
            j, pi, c = item
            if pi == 0:
                s_, wk_ = w_next(("up", l, j))
                ffn_up.cur = (slot_view(s_, 8, 512), wk_)
            wv_, wk_ = ffn_up.cur
            slot = c % NUR
            for th in range(2):
                for hv in range(2):
                    bk = (c % 2) * 4 + (hv * 2 + th)
                    co = hv * 256 + pi * 128
                    mm_group(bk, [(wv_[:, kc, co:co + 128], hT[:, kc, th * 512:(th + 1) * 512]) for kc in range(8)],
                             r=wk_ + hkeys(th))
                    src = PB(bk).rearrange("p (s t) -> p s t", s=2)
                    dst = ur[:, slot, hv, :].rearrange("p (s t) -> p s t", s=NSEG)[:, 2 * th:2 * th + 2, 1:1 + SEG]
                    P.add("act", lambda e, dst=dst, src=src: e.activation(out=dst, in_=src, func=AF.Copy),
                          r=[pskey(bk), "ur_init"], w=[akey(("ur", slot, hv, th))])
                    act(cen[:, slot, hv * 2 + th, :], PB(bk), AF.Identity, [pskey(bk), "vecs", "ur_init"],
                        [akey(("cen", slot, hv, th))], bias=V(l, "fcb", hv * 22 + c, 1),
                        scale=V(l, "fcw", 44 + hv * 22 + c, 1))

        def ffn_halo(item):
            j, pi, c = item
            slot = c % NUR
            for hv in range(2):
                uvv = ur[:, slot, hv, :].rearrange("p (s t) -> p s t", s=NSEG)
                rk_ = [("ur", slot, hv, 0), ("ur", slot, hv, 1), "flag"]
                dve_ts(uvv[:, 1:4, 0:1], uvv[:, 0:3, SEG:SEG + 1], flag[:, 0:1], ALU.mult, rk_,
                       [akey(("urh", slot, hv, 0))])
                dve_ts(uvv[:, 0:3, SEG + 1:SEG + 2], uvv[:, 1:4, 1:2], flag[:, 0:1], ALU.mult, rk_,
                       [akey(("urh", slot, hv, 1))])

        def ffn_conv(item):
            j, pi, c = item
            slot = c % NUR
            for k in (0, 2):
                for th in range(2):
                    for hv in range(2):
                        uvv = ur[:, slot, hv, :].rearrange("p (s t) -> p s t", s=NSEG)
                        acc = cen[:, slot, hv * 2 + th, :].rearrange("p (s t) -> p s t", s=2)
                        ck_ = ("cen", slot, hv, th)
                        rk_ = [("ur", slot, hv, 0), ("ur", slot, hv, 1), ("urh", slot, hv, 0), ("urh", slot, hv, 1),
                               ck_, "vecs"]
                        dve_stt(acc, uvv[:, 2 * th:2 * th + 2, k:k + SEG], V(l, "fcw", k * 44 + hv * 22 + c, 1), acc,
                                ALU.mult, ALU.add, rk_, [ck_])
            for th in range(2):
                act(gl[:, th, :], cen[:, slot, th, :], AF.Gelu_apprx_tanh, [("cen", slot, 0, th)], [akey(("gl", th))])
                dve_tt(gTc(c)[:, th * 512:(th + 1) * 512], cen[:, slot, 2 + th, :], gl[:, th, :], ALU.mult,
                       [("cen", slot, 1, th), ("gl", th), "ur_init"], [akey(("gT", c, th))])

        for i in range(len(ffn_items) + 1):
            if i < len(ffn_items):
                ffn_up(ffn_items[i])
            if i >= 1:
                ffn_conv(ffn_items[i - 1])
            if i < len(ffn_items):
                ffn_halo(ffn_items[i])
        rot["i"] = 0
        if phase_end("H%d" % l):
            tap_list.extend([("gtail", gtail, [128, 2, T], BF16), ("lruT", lruT[:], [128, 4, T], BF16)])
            break
        mean = AF32(o, [128, T]); o += 4096
        rstd = AF32(o, [128, T]); o += 4096
        sqb = ABF(o, [128, 2, T]); o += 4096
        cvb = ABF(o, [128, 2, T]); o += 4096
        assert o <= SBYTES, o
        def dn_ln2_thmajor():
            nxt_ = (l + 1 < DEPTH)
            inv = 1.0 / 1024.0
            for th in range(2):
                tsl = slice(th * 512, (th + 1) * 512)
                early = {}
                if th == 0:
                    KE = NFF - 2
                    for n in range(3):
                        s_, wk_ = w_next(("dn", l, n), la=NSLOT - 1 - n)
                        wv_ = slot_view(s_, NFF, 128)
                        bk = next_bank()
                        oa_ = PB(bk)

                        def fnA(e, wv_=wv_, oa_=oa_, tsl=tsl):
                            ins = None
                            for kc in range(KE):
                                ins = e.matmul(oa_, wv_[:, kc, :], gTc(kc)[:, tsl], start=(kc == 0), stop=False)
                            return ins
                        P.add("pe", fnA, r=wk_ + [("gT", kc, th) for kc in range(KE)], w=[pskey(bk)])
                        early[n] = (wv_, wk_, bk)
                for n in range(8):
                    if n in early:
                        wv_, wk_, bk = early[n]
                        oa_ = PB(bk)

                        def fnB(e, wv_=wv_, oa_=oa_, tsl=tsl):
                            ins = None
                            for kc in range(NFF - 2, NFF):
                                ins = e.matmul(oa_, wv_[:, kc, :], gTc(kc)[:, tsl], start=False, stop=(kc == NFF - 1))
                            return ins
                        P.add("pe", fnB, r=wk_ + [("gT", kc, th) for kc in range(NFF - 2, NFF)], w=[pskey(bk)])
                    else:
                        s_, wk_ = w_next((("dn" if th == 0 else "dn2"), l, n))
                        wv_ = slot_view(s_, NFF, 128)
                        bk = next_bank()
                        mm_group(bk, [(wv_[:, kc, :], gTc(kc)[:, tsl]) for kc in range(NFF)],
                                 r=wk_ + [("gT", kc, th) for kc in range(NFF)])
                    dve_stt(xT[:, n, tsl], PB(bk), der[:, 24 + n:25 + n], xT[:, n, tsl], ALU.mult, ALU.add,
                            [pskey(bk), ("xT", n, th), ("der", 1), ("der", 3)], [("xT", n, th)])
                b1, b2 = next_bank(), next_bank()
                for c in range(8):
                    slot = c % 2
                    cv_ = cvb[:, slot, tsl]
                    sq_ = sqb[:, slot, tsl]
                    xs_ = xT[:, c, tsl]
                    fw_ = FW()
                    P.add("dve", lambda e, cv_=cv_, xs_=xs_: e.tensor_copy(out=cv_, in_=xs_), r=[("xT", c, th)],
                          w=fw_ + [akey(("cvb", slot, th))])
                    act(sq_, xs_, AF.Square, [("xT", c, th)], fw_ + [akey(("sqb", slot, th))])

                    def fn(e, c=c, cv_=cv_, sq_=sq_, b1=b1, b2=b2):
                        e.matmul(PB(b1), onesb[:, :], cv_, start=(c == 0), stop=(c == 7))
                        return e.matmul(PB(b2), onesb[:, :], sq_, start=(c == 0), stop=(c == 7))
                    P.add("pe", fn, r=[("cvb", slot, th), ("sqb", slot, th), "onesb"], w=[pskey(b1), pskey(b2)])
                mk_, rk_ = akey(("mean", th)), akey(("rstd", th))
                act(mean[:, tsl], PB(b1), AF.Copy, [pskey(b1)], [mk_], scale=inv)
                act(rstd[:, tsl], PB(b1), AF.Square, [pskey(b1)], [rk_], scale=inv)
                dve_stt(rstd[:, tsl], PB(b2), inv, rstd[:, tsl], ALU.mult, ALU.subtract, [pskey(b2), rk_], [rk_])
                dve_ts(rstd[:, tsl], rstd[:, tsl], 0.0, ALU.max, [rk_], [rk_], s2=EPS_F, op1=ALU.add)
                act(rstd[:, tsl], rstd[:, tsl], AF.Ln, [rk_], [rk_])
                act(rstd[:, tsl], rstd[:, tsl], AF.Exp, [rk_], [rk_], scale=-0.5)
                for c in range(8):
                    xk = [("xT", c, th)]
                    xc_ = xT[:, c, tsl]
                    dve_tt(xc_, xc_, mean[:, tsl], ALU.subtract, xk + [mk_], xk)
                    dve_tt(xc_, xc_, rstd[:, tsl], ALU.mult, xk + [rk_], xk)
                    act(xc_, xc_, AF.Identity, xk + ["vecs"], xk, bias=V(l, "l2b", c, 1), scale=V(l, "l2g", c, 1))
                    if nxt_:
                        act(hT[:, c, tsl], xc_, AF.Identity, xk + [("der", 0), "modA"], [("hT", c, th)],
                            bias=modT[:, c:c + 1], scale=der[:, c:c + 1])

        dn_ln2_thmajor()
        if phase_end("I%d" % l):
            break

    if not stopped["v"]:
        fence = list(arena_keys)
        del arena_keys[:]
        YT = [AF32(i * 4096, [128, 1024]) for i in range(8)]
        YALL = AF32(0, [128, 8, 1024])
        for tbh in range(2):
            for c in range(8):
                b = next_bank()

                def fn(e, c=c, tbh=tbh, b=b):
                    ins = None
                    for q in range(4):
                        tb = tbh * 4 + q
                        ins = e.transpose(out=PB(b)[:, q * 128:(q + 1) * 128], in_=xT[:, c, tb * 128:(tb + 1) * 128],
                                          identity=ident[:])
                    return ins
                P.add("pe", fn, r=[("xT", c, tbh), "ident"], w=[pskey(b)])
                dst = YALL[:, tbh * 4:(tbh + 1) * 4, c * 128:(c + 1) * 128]
                src = PB(b).rearrange("p (q t) -> p q t", q=4)
                wk = [("yt", tbh, c)] + fence
                if c % 2 == 0:
                    P.add("act", lambda e, dst=dst, src=src: e.activation(out=dst, in_=src, func=AF.Copy),
                          r=[pskey(b)], w=wk)
                else:
                    P.add("dve", lambda e, dst=dst, src=src: e.tensor_copy(out=dst, in_=src), r=[pskey(b)], w=wk)
            for tb in range(tbh * 4, tbh * 4 + 4):
                yk = [("yt", tbh, c) for c in range(8)]
                dma("sp", y_d[tb * 128:(tb + 1) * 128, :], YT[tb], yk, [], "st_y%d" % tb)
    tap_outs = []
    for ti, (name, ap, shape, dt) in enumerate(tap_list):
        dd = dout("tap_" + name, shape, dt)
        tap_outs.append("tap_" + name)
        P.add("sp", lambda e, dd=dd, ap=ap: e.dma_start(out=dd, in_=ap), r=list(P.kw.keys()), w=[], chan="tap%d" % ti)
    if stopped["v"]:
        dd = dout("tap_xT", [128, 8, T], F32)
        P.add("sp", lambda e, dd=dd: e.dma_start(out=dd, in_=xT[:]), r=list(P.kw.keys()), w=[], chan="tapx")
        dd2 = dout("tap_hT", [128, 8, T], BF16)
        P.add("sp", lambda e, dd2=dd2: e.dma_start(out=dd2, in_=hT[:]), r=list(P.kw.keys()), w=[], chan="taph")
        dd3 = dout("tap_modT", [128, 48], F32)
        P.add("sp", lambda e, dd3=dd3: e.dma_start(out=dd3, in_=modT[:]), r=list(P.kw.keys()), w=[], chan="tapm")
    last_dma = [P.chan_last[c] for c in P.chan_last]
    fin_op = Op()
    fin_op.eng = "sp"
    fin_op.fn = lambda e: e.nop()
    fin_op.chan = None
    fin_op.sig = False
    fin_op.gid = len(P.ops)
    fin_op.deps = set(last_dma)
    P.ops.append(fin_op)

    block = es.enter_context(nc.Block())
    P.emit(nc, es, block)
    es.close()
    return nc


def _fm(v):
    v = np.asarray(v, np.float32)
    return np.ascontiguousarray(v.reshape(-1, 128).T)


def _rope_tables(active):
    c = np.ones((128, T), np.float32)
    s = np.zeros((128, T), np.float32)
    if active:
        t = np.arange(T)
        row = (t // 64).astype(np.float32)
        colp = (t % 64).astype(np.float32)
        inv = (np.float32(10000.0) ** (-np.arange(16, dtype=np.float32) / np.float32(16))).astype(np.float32)
        for p in range(128):
            dd = p % 64
            a = dd // 32
            jj = dd % 32
            f = jj % 16
            pos = row if a == 0 else colp
            ang = (pos * inv[f]).astype(np.float32)
            c[p] = np.cos(ang)
            s[p] = -np.sin(ang) if jj < 16 else np.sin(ang)
    return c, s


def _mask_table(sample):
    m = np.zeros((128, MASK_COLS), np.float32)
    if not sample:
        m[:, 0:512] = NEGM
    k = np.arange(128)[:, None]
    for th in range(2):
        for (kb, jl, jh) in band_tiles(th):
            off = MASK_OFF[(th, kb)]
            for j in range(jl, jh + 1):
                q = np.arange(128)[None, :]
                if sample:
                    ok = np.abs((kb * 128 + k) - (j * 128 + q)) <= 128
                else:
                    ok = np.broadcast_to(np.array(kb // 2 == j // 2), (128, 128))
                m[:, off + (j - jl) * 128: off + (j - jl + 1) * 128] = np.where(ok, 0.0, NEGM)
    return m


def _partner(cols64):
    cols64 = np.asarray(cols64)
    idx = np.arange(64)
    j = idx % 32
    p = np.where(j < 16, idx + 16, idx - 16)
    return cols64[p]


def _prep_shared(inp):
    sh = {}
    w_in = np.asarray(inp["w_in"], np.float32)
    cols = []
    qc = lambda h: np.arange(h * 64, (h + 1) * 64)
    kc = lambda g: 512 + np.arange(g * 64, (g + 1) * 64)
    for name in WIN_CHUNKS:
        if name.startswith("qp"):
            c = int(name[2])
            cols.append(np.concatenate([_partner(qc(2 * c)), _partner(qc(2 * c + 1))]))
        elif name.startswith("q"):
            c = int(name[1])
            cols.append(np.concatenate([qc(2 * c), qc(2 * c + 1)]))
        elif name == "kA":
            cols.append(np.concatenate([kc(0), kc(1)]))
        elif name == "kB":
            cols.append(np.concatenate([kc(1), kc(0)]))
        elif name == "kAp":
            cols.append(np.concatenate([_partner(kc(0)), _partner(kc(1))]))
        elif name == "kBp":
            cols.append(np.concatenate([_partner(kc(1)), _partner(kc(0))]))
        elif name == "v":
            cols.append(640 + np.arange(128))
        elif name.startswith("xb"):
            c = int(name[2])
            cols.append(768 + c * 128 + np.arange(128))
        elif name.startswith("a"):
            c = int(name[1])
            cols.append(1280 + c * 128 + np.arange(128))
        elif name.startswith("g"):
            c = int(name[1])
            cols.append(1280 + 512 + c * 128 + np.arange(128))
    cols = np.concatenate(cols)
    sh["w_in_ext"] = np.ascontiguousarray(w_in[:, :, cols])
    vec = np.zeros((DEPTH, 128, NV), np.float32)
    f = lambda a: np.asarray(a, np.float32)
    for l in range(DEPTH):
        def put(name, arr):
            arr = np.asarray(arr, np.float32)
            vec[l, :, _VOFF[name]:_VOFF[name] + arr.shape[1]] = arr
        put("bmod", _fm(f(inp["b_mod"])[l]))
        put("lcw", np.concatenate([_fm(f(inp["lru_conv_w"])[l, k]) for k in range(4)], axis=1))
        put("lcb", _fm(f(inp["lru_conv_b"])[l]))
        put("lba", np.concatenate([_fm(f(inp["lru_ba"])[l, d]) for d in range(2)], axis=1))
        put("lbx", np.concatenate([_fm(f(inp["lru_bx"])[l, d]) for d in range(2)], axis=1))
        put("llam", np.concatenate([_fm(f(inp["lru_lambda"])[l, d]) for d in range(2)], axis=1))
        put("cw", np.concatenate([_fm(f(inp["conf_dw_w"])[l, k]) for k in range(31)], axis=1))
        put("cb", _fm(f(inp["conf_dw_b"])[l]))
        put("clg", _fm(f(inp["conf_ln_g"])[l]))
        put("clb", _fm(f(inp["conf_ln_b"])[l]))
        put("bm", _fm(f(inp["b_merge"])[l]))
        put("l1g", _fm(f(inp["ln1_g"])[l]))
        put("l1b", _fm(f(inp["ln1_b"])[l]))
        put("fcw", np.concatenate([_fm(f(inp["ffn_conv_w"])[l, k]) for k in range(3)], axis=1))
        put("fcb", _fm(f(inp["ffn_conv_b"])[l]))
        put("l2g", _fm(f(inp["ln2_g"])[l]))
        put("l2b", _fm(f(inp["ln2_b"])[l]))
        put("sink", np.broadcast_to(f(inp["attn_sink"])[l][None, :], (128, 8)))
    sh["vecs"] = vec
    lw = np.zeros((DEPTH, 128, 16, 128), np.float32)
    for l in range(DEPTH):
        for d in range(2):
            for gt, nm in enumerate(("lru_wa", "lru_wx")):
                wsrc = f(inp[nm])[l, d]
                for c in range(4):
                    i = (d * 2 + gt) * 4 + c
                    for bb in range(2):
                        lw[l, bb * 64:(bb + 1) * 64, i, bb * 64:(bb + 1) * 64] = wsrc[2 * c + bb]
    sh["lruw"] = lw
    sh["ident"] = np.eye(128, dtype=np.float32)
    for k in ("w_mod", "w_branch", "w_merge", "w_out", "ffn_w_up", "ffn_w_down"):
        sh[k] = np.ascontiguousarray(np.asarray(inp[k], np.float32))
    return sh


def make_in_maps(inp):
    sh = _prep_shared(inp)
    xs = np.asarray(inp["x_sample"], np.float32)
    xp = np.asarray(inp["x_prompt"], np.float32)
    ck = np.asarray(inp["cache_k"], np.float32)
    cv = np.asarray(inp["cache_v"], np.float32)
    st = np.asarray(inp["state_lru"], np.float32)
    cc = np.asarray(inp["c"], np.float32)
    cctx = np.asarray(inp["c_ctx"], np.float32)
    rc_s, rs_s = _rope_tables(True)
    rc_p, rs_p = _rope_tables(False)
    mk_s = _mask_table(True)
    mk_p = _mask_table(False)
    maps = []
    for core in range(8):
        m = dict(sh)
        if core < 4:
            b = core
            m["x"] = np.ascontiguousarray(xs[b])
            m["cond"] = _fm(cc[b])
            m["ck"] = np.ascontiguousarray(ck[b].reshape(DEPTH, 512, 128))
            m["cv"] = np.ascontiguousarray(cv[b].reshape(DEPTH, 512, 128))
            h0 = np.zeros((128, DEPTH * 8), np.float32)
            for l in range(DEPTH):
                for d in range(2):
                    h0[:, l * 8 + d * 4:l * 8 + d * 4 + 4] = _fm(st[b, l, d])
            m["h0"] = h0
            m["flag"] = np.ones((128, 1), np.float32)
            m["ropec"], m["ropes"], m["maskb"] = rc_s, rs_s, mk_s
        else:
            i = core - 4
            m["x"] = np.ascontiguousarray(xp[4 * i:4 * i + 4].reshape(T, D))
            m["cond"] = _fm(cctx)
            m["ck"] = np.zeros((DEPTH, 512, 128), np.float32)
            m["cv"] = np.zeros((DEPTH, 512, 128), np.float32)
            m["h0"] = np.zeros((128, DEPTH * 8), np.float32)
            m["flag"] = np.zeros((128, 1), np.float32)
            m["ropec"], m["ropes"], m["maskb"] = rc_p, rs_p, mk_p
        maps.append(m)
    return maps


_NC_CACHE = {}


def kernel(**inputs):
    if "nc" not in _NC_CACHE:
        _NC_CACHE["nc"] = build_program()
    nc = _NC_CACHE["nc"]
    maps = make_in_maps(inputs)
    res = run_bass_kernel_spmd(nc, maps, core_ids=list(range(8)))
    R = res.results
    y_prompt = np.zeros((16, 256, D), np.float32)
    y_sample = np.zeros((4, T, D), np.float32)
    nk = np.zeros((16, DEPTH, 256, 2, 64), np.float32)
    nv = np.zeros((16, DEPTH, 256, 2, 64), np.float32)
    nh = np.zeros((16, DEPTH, 2, 512), np.float32)
    for core in range(8):
        r = R[core]
        if core < 4:
            y_sample[core] = r["y"]
        else:
            i = core - 4
            y_prompt[4 * i:4 * i + 4] = r["y"].reshape(4, 256, D)
            for s in range(4):
                for l in range(DEPTH):
                    nk[4 * i + s, l] = r["nk"][l, s * 256:(s + 1) * 256].reshape(256, 2, 64)
                    nv[4 * i + s, l] = r["nv"][l, s * 256:(s + 1) * 256].reshape(256, 2, 64)
                    nh[4 * i + s, l] = r["nh"][l].reshape(4, 2, 512)[s]
    return (y_prompt, y_sample, nk, nv, nh)
```
